# Optimizing a Trainium2 kernel written in Bass

```python
import math
import jax, jax.numpy as jnp
from jax import lax
import numpy as np

D_MODEL = 1024
BATCH = 2
SEQ = 8192
DEPTH = 4

N_MIXERS = 3
N_A = (DEPTH + 2) // 3
N_B = (DEPTH + 1) // 3
N_C = DEPTH // 3
D_FF = ((8 * D_MODEL // 3 + 127) // 128) * 128
NORM_EPS = 1e-6
SC_WIDTH = 3
HY_SHORT = 3
HY_EMB = 33
HY_BANDS = (HY_EMB - 1) // 2
HY_ORDER = 64
HY_TARGET = 1e-2
HY_FAST_PCT = 0.3
HY_SLOW_PCT = 1.5
HY_MAX_DECAY = math.log(HY_TARGET) / HY_FAST_PCT
HY_MIN_DECAY = math.log(HY_TARGET) / HY_SLOW_PCT
GD_HEADS = 8
GD_DK = D_MODEL // GD_HEADS
GD_DV = D_MODEL // GD_HEADS
GD_CONV = 3
GD_CHUNK = 64
GD_QKV = 2 * GD_HEADS * GD_DK + GD_HEADS * GD_DV

kernel_name = 'hybrid_shortconv_hyena_gdn_encoder'


def rms_norm(x, g):
    xf = x.astype(jnp.float32)
    y = xf * lax.rsqrt(jnp.mean(xf * xf, axis=-1, keepdims=True) + NORM_EPS)
    return (y * g.astype(jnp.float32)).astype(x.dtype)


def centred_dwconv(x, w):
    K = w.shape[0]
    p = K // 2
    L = x.shape[1]
    xp = jnp.pad(x, ((0, 0), (p, p), (0, 0)))
    return sum(xp[:, j:j + L] * w[j] for j in range(K))


def swiglu(x, w_in, w_out):
    g, u = jnp.split(x @ w_in, 2, axis=-1)
    return (jax.nn.silu(g) * u) @ w_out


def short_conv_mixer(x, w_in, conv_w, w_out):
    b, c, h = jnp.split(x @ w_in, 3, axis=-1)
    return (b * centred_dwconv(c * h, conv_w)) @ w_out


def hyena_filters(L, w1, b1, w2, b2, w3, freq):
    f32 = jnp.float32
    t = jnp.linspace(0.0, 1.0, L, dtype=f32)[:, None]
    w = 2.0 * math.pi * jnp.arange(L, dtype=f32)[:, None] / L
    f = jnp.linspace(1e-4, HY_BANDS - 1, HY_BANDS, dtype=f32)[None, :]
    z = jnp.concatenate([t, jnp.cos(f * w), -jnp.sin(f * w)], axis=-1)
    fr = freq.astype(f32)
    h = jnp.sin(fr * (z @ w1.astype(f32) + b1.astype(f32)))
    h = jnp.sin(fr * (h @ w2.astype(f32) + b2.astype(f32)))
    h = h @ w3.astype(f32)
    D = h.shape[-1] // 2
    deltas = jnp.linspace(HY_MIN_DECAY, HY_MAX_DECAY, D, dtype=f32)
    h = h * jnp.exp(-t * jnp.abs(jnp.concatenate([deltas, deltas])))[None] [0]
    return h[:, :D], h[:, D:]


def two_sided_fft_conv(u, h_f, h_b):
    L = u.shape[1]
    zero = jnp.zeros((1, h_f.shape[1]), h_f.dtype)
    h_circ = jnp.concatenate([h_f, zero, h_b[1:][::-1]], axis=0)
    uf = jnp.fft.rfft(u, n=2 * L, axis=1)
    hf = jnp.fft.rfft(h_circ, axis=0)
    return jnp.fft.irfft(uf * hf[None], n=2 * L, axis=1)[:, :L]


def hyena_mixer(x, w_in, b_in, conv_w, conv_b, f_w1, f_b1, f_w2, f_b2, f_w3, f_freq, d_bias, w_out, b_out):
    L = x.shape[1]
    u = centred_dwconv(x @ w_in + b_in, conv_w) + conv_b
    x0, x1, v = jnp.split(u, 3, axis=-1)
    h_f, h_b = hyena_filters(L, f_w1, f_b1, f_w2, f_b2, f_w3, f_freq)
    v = (v * x1).astype(jnp.float32)
    y = two_sided_fft_conv(v, h_f, h_b) + v * d_bias.astype(jnp.float32)
    return (y.astype(x.dtype) * x0) @ w_out + b_out


def chunk_gated_delta_rule(q, k, v, g, beta):
    Bt, L, H, dk = q.shape
    dv = v.shape[-1]
    C = GD_CHUNK
    N = L // C

    def blocks(t):
        t = t.reshape((Bt, N, C, H) + t.shape[3:])
        return jnp.moveaxis(t, 3, 1)

    q, k, v, g, beta = blocks(q), blocks(k), blocks(v), blocks(g), blocks(beta)
    gc = jnp.cumsum(g, axis=-1)
    idx = jnp.arange(C)
    incl = idx[:, None] >= idx[None, :]
    strict = idx[:, None] > idx[None, :]
    decay = jnp.exp(jnp.where(incl, gc[..., :, None] - gc[..., None, :], -jnp.inf))
    kb = k * beta[..., None]
    vb = v * beta[..., None]
    lower = jnp.where(strict, jnp.einsum('bhncd,bhnsd->bhncs', kb, k) * decay, 0.0)
    eye = jnp.eye(C, dtype=q.dtype)
    rhs = jnp.concatenate([vb, kb * jnp.exp(gc)[..., None]], axis=-1)
    sol = lax.linalg.triangular_solve(lower + eye, rhs, left_side=True, lower=True, unit_diagonal=True)
    u, w = sol[..., :dv], sol[..., dv:]
    attn = jnp.einsum('bhncd,bhnsd->bhncs', q, k) * decay
    q_dec = q * jnp.exp(gc)[..., None]
    k_dec = k * jnp.exp(gc[..., -1:] - gc)[..., None]
    g_tot = jnp.exp(gc[..., -1])

    def step(S, inp):
        u_n, w_n, a_n, qd_n, kd_n, gt_n = inp
        v_new = u_n - jnp.einsum('bhcd,bhde->bhce', w_n, S)
        o_n = jnp.einsum('bhcd,bhde->bhce', qd_n, S) + jnp.einsum('bhcs,bhse->bhce', a_n, v_new)
        S = S * gt_n[..., None, None] + jnp.einsum('bhcd,bhce->bhde', kd_n, v_new)
        return S, o_n

    xs = tuple(jnp.moveaxis(t, 2, 0) for t in (u, w, attn, q_dec, k_dec, g_tot))
    S0 = jnp.zeros((Bt, H, dk, dv), jnp.float32)
    _, o = lax.scan(step, S0, xs)
    o = jnp.moveaxis(o, 0, 2)
    return jnp.moveaxis(o, 1, 3).reshape(Bt, L, H, dv)


def l2norm(t):
    return t * lax.rsqrt(jnp.sum(t * t, axis=-1, keepdims=True) + 1e-6)


def gated_deltanet_mixer(x, w_in, conv_w, a_log, dt_bias, norm_g, w_out):
    B, L, _ = x.shape
    H = GD_HEADS
    f32 = jnp.float32
    proj = x @ w_in
    qkv, z, a, b = jnp.split(proj, [GD_QKV, GD_QKV + H * GD_DV, GD_QKV + H * GD_DV + 2 * H], axis=-1)
    qkv = jax.nn.silu(centred_dwconv(qkv, conv_w)).astype(f32)
    q, k, v = jnp.split(qkv, [H * GD_DK, 2 * H * GD_DK], axis=-1)
    q = l2norm(q.reshape(B, L, H, GD_DK)) * (GD_DK ** -0.5)
    k = l2norm(k.reshape(B, L, H, GD_DK))
    v = v.reshape(B, L, H, GD_DV)
    a = a.astype(f32).reshape(B, L, 2, H)
    b = b.astype(f32).reshape(B, L, 2, H)
    g = -jnp.exp(a_log.astype(f32)) * jax.nn.softplus(a + dt_bias.astype(f32))
    beta = jax.nn.sigmoid(b)
    flip = lambda t: jnp.flip(t, axis=1)
    q2 = jnp.concatenate([q, flip(q)], axis=0)
    k2 = jnp.concatenate([k, flip(k)], axis=0)
    v2 = jnp.concatenate([v, flip(v)], axis=0)
    g2 = jnp.concatenate([g[:, :, 0], flip(g[:, :, 1])], axis=0)
    b2 = jnp.concatenate([beta[:, :, 0], flip(beta[:, :, 1])], axis=0)
    o = chunk_gated_delta_rule(q2, k2, v2, g2, b2)
    o = o[:B] + flip(o[B:])
    o = rms_norm(o, norm_g) * jax.nn.silu(z.astype(f32).reshape(B, L, H, GD_DV))
    return o.reshape(B, L, H * GD_DV).astype(x.dtype) @ w_out


def setup_inputs(seed: int = 0) -> dict:
    key = jax.random.key(seed)
    ks = jax.random.split(key, 32)
    f32 = jnp.float32
    D, F, H = D_MODEL, D_FF, GD_HEADS

    def dense(k, shape, fan_in, scale=1.0):
        return jax.random.normal(k, shape, f32) * (scale * fan_in ** -0.5)

    def small(k, shape, scale=0.01):
        return jax.random.normal(k, shape, f32) * scale

    dt = jnp.exp(jax.random.uniform(ks[25], (N_C, 2, H), f32, math.log(1e-3), math.log(1e-1)))
    return {
        'x': jax.random.normal(ks[0], (BATCH, SEQ, D), f32),
        'norms': 1.0 + small(ks[1], (DEPTH, 3, D)),
        'final_norm': 1.0 + small(ks[2], (D,)),
        'ffn_w_in': dense(ks[3], (DEPTH, 2, D, 2 * F), D),
        'ffn_w_out': dense(ks[4], (DEPTH, 2, F, D), F),
        'sc_w_in': dense(ks[5], (N_A, D, 3 * D), D),
        'sc_conv': dense(ks[6], (N_A, SC_WIDTH, D), SC_WIDTH),
        'sc_w_out': dense(ks[7], (N_A, D, D), D),
        'hy_w_in': dense(ks[8], (N_B, D, 3 * D), D),
        'hy_b_in': small(ks[9], (N_B, 3 * D)),
        'hy_conv': dense(ks[10], (N_B, HY_SHORT, 3 * D), HY_SHORT),
        'hy_conv_b': small(ks[11], (N_B, 3 * D)),
        'hy_f_w1': dense(ks[12], (N_B, HY_EMB, HY_ORDER), HY_EMB),
        'hy_f_b1': small(ks[13], (N_B, HY_ORDER), 0.1),
        'hy_f_w2': dense(ks[14], (N_B, HY_ORDER, HY_ORDER), HY_ORDER),
        'hy_f_b2': small(ks[15], (N_B, HY_ORDER), 0.1),
        'hy_f_w3': dense(ks[16], (N_B, HY_ORDER, 2 * D), HY_ORDER, 0.03),
        'hy_f_freq': 1.0 + small(ks[17], (N_B, HY_ORDER)),
        'hy_d': small(ks[18], (N_B, D), 0.5),
        'hy_w_out': dense(ks[19], (N_B, D, D), D),
        'hy_b_out': small(ks[20], (N_B, D)),
        'gd_w_in': dense(ks[21], (N_C, D, GD_QKV + H * GD_DV + 4 * H), D),
        'gd_conv': dense(ks[22], (N_C, GD_CONV, GD_QKV), GD_CONV),
        'gd_a_log': jnp.log(jax.random.uniform(ks[23], (N_C, 2, H), f32, 1.0, 16.0)),
        'gd_dt_bias': dt + jnp.log(-jnp.expm1(-dt)),
        'gd_norm': 1.0 + small(ks[24], (N_C, GD_DV)),
        'gd_w_out': dense(ks[26], (N_C, H * GD_DV, D), H * GD_DV),
    }


def reference(x, norms, final_norm, ffn_w_in, ffn_w_out,
              sc_w_in, sc_conv, sc_w_out,
              hy_w_in, hy_b_in, hy_conv, hy_conv_b, hy_f_w1, hy_f_b1, hy_f_w2, hy_f_b2,
              hy_f_w3, hy_f_freq, hy_d, hy_w_out, hy_b_out,
              gd_w_in, gd_conv, gd_a_log, gd_dt_bias, gd_norm, gd_w_out):
    for i in range(DEPTH):
        m, j = i % N_MIXERS, i // N_MIXERS
        x = x + 0.5 * swiglu(rms_norm(x, norms[i, 0]), ffn_w_in[i, 0], ffn_w_out[i, 0])
        hn = rms_norm(x, norms[i, 1])
        if m == 0:
            y = short_conv_mixer(hn, sc_w_in[j], sc_conv[j], sc_w_out[j])
        elif m == 1:
            y = hyena_mixer(hn, hy_w_in[j], hy_b_in[j], hy_conv[j], hy_conv_b[j],
                            hy_f_w1[j], hy_f_b1[j], hy_f_w2[j], hy_f_b2[j], hy_f_w3[j],
                            hy_f_freq[j], hy_d[j], hy_w_out[j], hy_b_out[j])
        else:
            y = gated_deltanet_mixer(hn, gd_w_in[j], gd_conv[j], gd_a_log[j], gd_dt_bias[j],
                                     gd_norm[j], gd_w_out[j])
        x = x + y
        x = x + 0.5 * swiglu(rms_norm(x, norms[i, 2]), ffn_w_in[i, 1], ffn_w_out[i, 1])
    return rms_norm(x, final_norm)
```

```python
import contextlib
import math
import numpy as np
import concourse.bass as bass
import concourse.mybir as mybir
from concourse.bass_utils import run_bass_kernel_spmd

F32 = mybir.dt.float32
BF16 = mybir.dt.bfloat16
AF = mybir.ActivationFunctionType
ALU = mybir.AluOpType
AX = mybir.AxisListType

D = 1024
KC = 8
FF = 2816
JC = 22
NCORES = 8
BATCH = 2
SEQ = 8192
TPC = BATCH * SEQ // NCORES
EPS = 1e-6

ENGS = ("pe", "act", "dve", "pool", "sp")
NDMA_SEM = 20


class Sched:
    def __init__(self, nc, es):
        self.nc = nc
        self.q = {e: [] for e in ENGS}
        self.cnt = {e: 0 for e in ENGS}
        self.seen = {e: {} for e in ENGS}
        self.buf = {}
        self.sems = {}
        for e in ENGS:
            self.sems[("E", e)] = es.enter_context(nc.semaphore("sem_" + e))
        self.dma_rr = {e: 0 for e in ENGS}
        self.dma_val = {}
        for e in ("sp", "pool", "act"):
            for i in range(NDMA_SEM):
                k = ("D", e, i)
                self.sems[k] = es.enter_context(nc.semaphore("dsem_%s_%d" % (e, i)))
                self.dma_val[k] = 0
        self.out_tokens = []
        self.excl = set()

    def _deps(self, eng, reads, writes):
        deps = {}

        def add(tok):
            if tok is None:
                return
            k, v = tok
            if deps.get(k, 0) < v:
                deps[k] = v

        for k in reads:
            b = self.buf.get(k)
            if b:
                add(b["w"])
                if k in self.excl:
                    for rk, rv in b["r"].items():
                        if rk != ("E", eng):
                            add((rk, rv))
        for k in writes:
            b = self.buf.get(k)
            if b:
                add(b["w"])
                for rk, rv in b["r"].items():
                    add((rk, rv))
        waits = []
        for k, v in deps.items():
            if eng == "pe" and k == ("E", "pe"):
                continue
            if self.seen[eng].get(k, 0) >= v:
                continue
            self.seen[eng][k] = v
            waits.append((k, v))
        return waits

    def _record(self, tok, reads, writes):
        for k in reads:
            b = self.buf.setdefault(k, {"w": None, "r": {}})
            if b["r"].get(tok[0], 0) < tok[1]:
                b["r"][tok[0]] = tok[1]
        for k in writes:
            self.buf[k] = {"w": tok, "r": {}}

    def op(self, eng, name, reads=(), writes=(), **kw):
        fn = (name, kw)
        waits = self._deps(eng, reads, writes)
        self.cnt[eng] += 1
        tok = (("E", eng), self.cnt[eng])
        self.q[eng].append((waits, fn, tok, 1))
        self._record(tok, reads, writes)
        return tok

    def dma(self, eng, reads=(), writes=(), is_out=False, **kw):
        fn = ("dma_start", kw)
        waits = self._deps(eng, reads, writes)
        i = self.dma_rr[eng]
        self.dma_rr[eng] = (i + 1) % NDMA_SEM
        k = ("D", eng, i)
        prev = self.dma_val[k]
        if prev and self.seen[eng].get(k, 0) < prev:
            self.seen[eng][k] = prev
            waits.append((k, prev))
        self.dma_val[k] = prev + 16
        tok = (k, prev + 16)
        self.q[eng].append((waits, fn, tok, 16))
        self._record(tok, reads, writes)
        if is_out:
            self.out_tokens.append(tok)
        return tok

    def barrier(self):
        allv = [(("E", f), self.cnt[f]) for f in ENGS if self.cnt[f]]
        allv += [(k, v) for k, v in self.dma_val.items() if v]
        for e in ENGS:
            waits = []
            for k, v in allv:
                if k == ("E", e) and e == "pe":
                    continue
                if self.seen[e].get(k, 0) >= v:
                    continue
                self.seen[e][k] = v
                waits.append((k, v))
            if waits:
                self.q[e].append((waits, None, None, 0))
        self.buf = {}

    def mm(self, out, lhsT, rhs, start, stop, reads, writes):
        return self.op("pe", "matmul", reads, writes, out=out, lhsT=lhsT, rhs=rhs, start=start, stop=stop)

    def replay(self):
        nc = self.nc
        fin = list(self.out_tokens)
        with nc.Block() as block:
            def run(engname, eng):
                for waits, fn, tok, inc in self.q[engname]:
                    for k, v in waits:
                        eng.wait_ge(self.sems[k], v)
                    if fn is None:
                        continue
                    ins = getattr(eng, fn[0])(**fn[1])
                    ins.then_inc(self.sems[tok[0]], inc)
                if engname == "sp":
                    for k, v in fin:
                        eng.wait_ge(self.sems[k], v)

            @block.tensor
            def _(e):
                run("pe", e)

            @block.scalar
            def _(e):
                run("act", e)

            @block.vector
            def _(e):
                run("dve", e)

            @block.gpsimd
            def _(e):
                run("pool", e)

            @block.sync
            def _(e):
                run("sp", e)


class Ctx:
    def __init__(self, nc, es):
        self.nc = nc
        self.es = es
        self.S = Sched(nc, es)
        self.n = 0
        self.scopes = [es]

    def sb(self, name, shape, dt):
        self.n += 1
        return self.scopes[-1].enter_context(self.nc.sbuf_tensor("%s_%d" % (name, self.n), shape, dt))

    def ps(self, name, shape, dt=F32):
        self.n += 1
        return self.scopes[-1].enter_context(self.nc.psum_tensor("%s_%d" % (name, self.n), shape, dt))

    @contextlib.contextmanager
    def scope(self):
        with contextlib.ExitStack() as s:
            self.scopes.append(s)
            try:
                yield
            finally:
                self.S.barrier()
                self.scopes.pop()


def interleave(gens):
    gens = list(gens)
    while gens:
        for g_ in list(gens):
            try:
                next(g_)
            except StopIteration:
                gens.remove(g_)


def pipeline(gens, width):
    gens = iter(gens)
    active = []
    done = False
    while True:
        while not done and len(active) < width:
            try:
                active.append(next(gens))
            except StopIteration:
                done = True
        if not active:
            return
        for g_ in list(active):
            try:
                next(g_)
            except StopIteration:
                active.remove(g_)


def dram_in(nc, name, shape, dt=F32):
    return nc.dram_tensor(name, list(shape), dt, kind="ExternalInput").ap()


def dram_out(nc, name, shape, dt=F32):
    return nc.dram_tensor(name, list(shape), dt, kind="ExternalOutput").ap()


def emit_consts(C):
    ones = C.sb("ones", [128, 128], F32)
    C.S.op("dve", "memset", writes=["ones"], ap=ones[:], constant=1.0)
    C.ones = ones
    C.rn_sq = [C.sb("rn_sq%d" % i, [128, 512], F32) for i in range(3)]
    C.rn_rs = [C.sb("rn_rs%d" % i, [128, 512], F32) for i in range(2)]
    C.rn_ps = [C.ps("rn_ps%d" % i, [128, 512], F32) for i in range(1)]
    C.rn_i = 0


def load_vec_pk(C, name, vec_dram, nchunk, eng="sp"):
    t = C.sb(name, [128, nchunk], F32)
    C.S.dma(eng, writes=[name], out=t[:], in_=vec_dram.rearrange("(kc p) -> p kc", p=128),
            allow_slow_non_contiguous=True)
    return t


def emit_rmsnorm(C, x, xk, t0, ntok, g_sb, gk, hn, hk, hoff=0):
    S = C.S
    nt = (ntok + 511) // 512
    for tt in range(nt):
        n = min(512, ntok - tt * 512)
        c0 = t0 + tt * 512
        xkeys = [(xk, k, c0 // 512) for k in range(KC)]
        if (c0 % 512) + n > 512:
            xkeys += [(xk, k, c0 // 512 + 1) for k in range(KC)]
        ps = C.rn_ps[0]
        for k in range(KC):
            C.rn_i += 1
            sq = C.rn_sq[C.rn_i % 3]
            sqk = ("rn_sq", C.rn_i % 3)
            S.op("act", "activation", reads=[kk for kk in xkeys if kk[1] == k], writes=[sqk],
                 out=sq[:, :n], in_=x[:, k, c0:c0 + n], func=AF.Square)
            S.mm(ps[:, :n], C.ones[:], sq[:, :n], k == 0, k == KC - 1, ["ones", sqk], ["rn_ps"])
        C.rn_i += 1
        rs = C.rn_rs[C.rn_i % 2]
        rsk = ("rn_rs", C.rn_i % 2)
        S.op("dve", "tensor_scalar", reads=["rn_ps"], writes=[rsk], out=rs[:, :n], in0=ps[:, :n],
             scalar1=1.0 / D, scalar2=EPS, op0=ALU.mult, op1=ALU.add)
        S.op("act", "activation", reads=[rsk], writes=[rsk], out=rs[:, :n], in_=rs[:, :n], func=AF.Sqrt)
        S.op("dve", "reciprocal", reads=[rsk], writes=[rsk], out=rs[:, :n], in_=rs[:, :n])
        for k in range(KC):
            eng = "dve"
            S.op(eng, "scalar_tensor_tensor", reads=[kk for kk in xkeys if kk[1] == k] + [rsk, gk],
                 writes=[(hk, k, tt)], out=hn[:, k, hoff + tt * 512: hoff + tt * 512 + n], in0=x[:, k, c0:c0 + n],
                 scalar=g_sb[:, k:k + 1], in1=rs[:, :n], op0=ALU.mult, op1=ALU.mult)


def emit_ffn(C, x, groups, g_dram, w_in, w_out, pref):
    S = C.S
    TG = 1024
    w_in_v = w_in.rearrange("(kc p) n -> p kc n", p=128)
    w_out_v = w_out.rearrange("(jc p) n -> p jc n", p=128)
    with C.scope():
        g_sb = load_vec_pk(C, pref + "g", g_dram, KC)
        gk = pref + "g"
        hn = C.sb("ffn_hn", [128, KC, TG], BF16)
        act = C.sb("ffn_act", [128, JC, TG], BF16)
        wbuf = [C.sb("ffn_wi%d" % i, [128, KC, 256], BF16) for i in range(3)]
        wobuf = [C.sb("ffn_wo%d" % i, [128, JC, 128], BF16) for i in range(2)]
        sgb = [C.sb("ffn_sg%d" % i, [128, 512], F32) for i in range(2)]
        pg = [C.ps("ffn_pg%d" % i, [128, 512]) for i in range(2)]
        pu = [C.ps("ffn_pu%d" % i, [128, 512]) for i in range(2)]
        po = [C.ps("ffn_po%d" % i, [128, 512]) for i in range(2)]
        it = 0
        io = 0
        for (t0, ntok) in groups:
            ntg = (ntok + 511) // 512
            emit_rmsnorm(C, x, "x", t0, ntok, g_sb, gk, hn, "ffn_hn")
            for j in range(JC):
                wb = wbuf[j % 3]
                wk = ("ffn_wi", j % 3)
                S.dma("pool", writes=[wk + (0,)], out=wb[:, :, 0:128], in_=w_in_v[:, :, j * 128:(j + 1) * 128])
                S.dma("pool", writes=[wk + (1,)], out=wb[:, :, 128:256],
                      in_=w_in_v[:, :, FF + j * 128: FF + (j + 1) * 128])
                for tt in range(ntg):
                    n = min(512, ntok - tt * 512)
                    it += 1
                    b = it % 2
                    sl = slice(tt * 512, tt * 512 + n)
                    for k in range(KC):
                        S.mm(pg[b][:, :n], wb[:, k, 0:128], hn[:, k, sl], k == 0, k == KC - 1,
                             [wk + (0,), ("ffn_hn", k, tt)], [("ffn_pg", b)])
                    for k in range(KC):
                        S.mm(pu[b][:, :n], wb[:, k, 128:256], hn[:, k, sl], k == 0, k == KC - 1,
                             [wk + (1,), ("ffn_hn", k, tt)], [("ffn_pu", b)])
                    S.op("act", "activation", reads=[("ffn_pg", b)], writes=[("ffn_sg", b)],
                         out=sgb[b][:, :n], in_=pg[b][:, :n], func=AF.Silu)
                    S.op("dve", "tensor_tensor", reads=[("ffn_pu", b), ("ffn_sg", b)], writes=[("ffn_act", j, tt)],
                         out=act[:, j, sl], in0=pu[b][:, :n], in1=sgb[b][:, :n], op=ALU.mult)
            for m in range(KC):
                wo = wobuf[m % 2]
                wok = ("ffn_wo", m % 2)
                S.dma("pool", writes=[wok], out=wo[:], in_=w_out_v[:, :, m * 128:(m + 1) * 128])
                for tt in range(ntg):
                    n = min(512, ntok - tt * 512)
                    io += 1
                    b = io % 2
                    sl = slice(tt * 512, tt * 512 + n)
                    gsl = slice(t0 + tt * 512, t0 + tt * 512 + n)
                    for j in range(JC):
                        S.mm(po[b][:, :n], wo[:, j, :], act[:, j, sl], j == 0, j == JC - 1,
                             [wok, ("ffn_act", j, tt)], [("ffn_po", b)])
                    xkey = ("x", m, (t0 + tt * 512) // 512)
                    S.op("dve", "scalar_tensor_tensor", reads=[("ffn_po", b), xkey], writes=[xkey],
                         out=x[:, m, gsl], in0=po[b][:, :n], scalar=0.5, in1=x[:, m, gsl], op0=ALU.mult, op1=ALU.add)


def emit_sc_mixer(C, x, T, g_dram, w_in, conv, w_out):
    S = C.S
    NT = T // 512
    w_in_v = w_in.rearrange("(kc p) n -> p kc n", p=128)
    w_out_v = w_out.rearrange("(kc p) n -> p kc n", p=128)
    with C.scope():
        g_sb = load_vec_pk(C, "sc_g", g_dram, KC)
        cw = C.sb("sc_cw", [128, KC, 3], F32)
        for j in range(3):
            S.dma("sp", writes=[("sc_cw", j)], out=cw[:, :, j], in_=conv[j].rearrange("(i p) -> p i", p=128),
                  allow_slow_non_contiguous=True)
        hn = C.sb("sc_hn", [128, KC, T + 2], BF16)
        ybf = C.sb("sc_y", [128, KC, T], BF16)
        chb = [C.sb("sc_ch%d" % i, [128, T + 2], F32) for i in range(2)]
        bsv = [C.sb("sc_b%d" % i, [128, T], F32) for i in range(2)]
        csb = [C.sb("sc_c%d" % i, [128, 512], F32) for i in range(2)]
        acc = [C.sb("sc_acc%d" % i, [128, 512], F32) for i in range(2)]
        wbuf = [C.sb("sc_wi%d" % i, [128, KC, 384], BF16) for i in range(2)]
        wobuf = [C.sb("sc_wo%d" % i, [128, KC, 128], BF16) for i in range(2)]
        pb = [C.ps("sc_pb%d" % i, [128, 512]) for i in range(2)]
        pc = [C.ps("sc_pc%d" % i, [128, 512]) for i in range(2)]
        ph = [C.ps("sc_ph%d" % i, [128, 512]) for i in range(2)]
        po = [C.ps("sc_po%d" % i, [128, 512]) for i in range(1)]
        emit_rmsnorm(C, x, "x", 0, T + 2, g_sb, "sc_g", hn, "sc_hn")
        it = 0
        for i in range(KC):
            wb = wbuf[i % 2]
            wk = ("sc_wi", i % 2)
            for q in range(3):
                S.dma("pool", writes=[wk + (q,)], out=wb[:, :, q * 128:(q + 1) * 128],
                      in_=w_in_v[:, :, q * D + i * 128: q * D + (i + 1) * 128])
            ch = chb[i % 2]
            bs = bsv[i % 2]
            for tt in range(NT + 1):
                n = 512 if tt < NT else 2
                it += 1
                b = it % 2
                sl = slice(tt * 512, tt * 512 + n)
                hkeys = lambda k: [("sc_hn", k, tt)]
                if tt < NT:
                    for k in range(KC):
                        S.mm(pb[b][:, :n], wb[:, k, 0:128], hn[:, k, sl], k == 0, k == KC - 1,
                             [wk + (0,)] + hkeys(k), [("sc_pb", b)])
                for k in range(KC):
                    S.mm(pc[b][:, :n], wb[:, k, 128:256], hn[:, k, sl], k == 0, k == KC - 1,
                         [wk + (1,)] + hkeys(k), [("sc_pc", b)])
                for k in range(KC):
                    S.mm(ph[b][:, :n], wb[:, k, 256:384], hn[:, k, sl], k == 0, k == KC - 1,
                         [wk + (2,)] + hkeys(k), [("sc_ph", b)])
                S.op("act", "activation", reads=[("sc_pc", b)], writes=[("sc_c", b)],
                     out=csb[b][:, :n], in_=pc[b][:, :n], func=AF.Copy)
                if tt < NT:
                    S.op("dve", "tensor_tensor", reads=[("sc_c", b), ("sc_ph", b)], writes=[("sc_ch", i % 2, tt)],
                         out=ch[:, 1 + tt * 512: 1 + tt * 512 + n], in0=ph[b][:, :n], in1=csb[b][:, :n], op=ALU.mult)
                    S.op("act", "activation", reads=[("sc_pb", b)], writes=[("sc_b", i % 2, tt)],
                         out=bs[:, sl], in_=pb[b][:, :n], func=AF.Copy)
                else:
                    S.op("dve", "tensor_tensor", reads=[("sc_c", b), ("sc_ph", b)], writes=[("sc_ch", i % 2, "hl")],
                         out=ch[:, 0:1], in0=ph[b][:, 0:1], in1=csb[b][:, 0:1], op=ALU.mult)
                    S.op("dve", "tensor_tensor", reads=[("sc_c", b), ("sc_ph", b)], writes=[("sc_ch", i % 2, "hr")],
                         out=ch[:, T + 1:T + 2], in0=ph[b][:, 1:2], in1=csb[b][:, 1:2], op=ALU.mult)
            for tt in range(NT):
                a = acc[tt % 2]
                ak = ("sc_acc", tt % 2)
                rk = [("sc_ch", i % 2, tt)]
                if tt > 0:
                    rk.append(("sc_ch", i % 2, tt - 1))
                else:
                    rk.append(("sc_ch", i % 2, "hl"))
                if tt < NT - 1:
                    rk.append(("sc_ch", i % 2, tt + 1))
                else:
                    rk.append(("sc_ch", i % 2, "hr"))
                o = tt * 512
                S.op("dve", "tensor_scalar", reads=rk + [("sc_cw", 0)], writes=[ak], out=a[:], in0=ch[:, o:o + 512],
                     scalar1=cw[:, i, 0:1], scalar2=None, op0=ALU.mult)
                S.op("dve", "scalar_tensor_tensor", reads=rk + [("sc_cw", 1), ak], writes=[ak], out=a[:],
                     in0=ch[:, o + 1:o + 513], scalar=cw[:, i, 1:2], in1=a[:], op0=ALU.mult, op1=ALU.add)
                S.op("dve", "scalar_tensor_tensor", reads=rk + [("sc_cw", 2), ak], writes=[ak], out=a[:],
                     in0=ch[:, o + 2:o + 514], scalar=cw[:, i, 2:3], in1=a[:], op0=ALU.mult, op1=ALU.add)
                S.op("dve", "tensor_tensor", reads=[ak, ("sc_b", i % 2, tt)], writes=[("sc_y", i, tt)],
                     out=ybf[:, i, o:o + 512], in0=a[:], in1=bs[:, o:o + 512], op=ALU.mult)
        for m in range(KC):
            wo = wobuf[m % 2]
            wok = ("sc_wo", m % 2)
            S.dma("pool", writes=[wok], out=wo[:], in_=w_out_v[:, :, m * 128:(m + 1) * 128])
            for tt in range(NT):
                sl = slice(tt * 512, (tt + 1) * 512)
                for i in range(KC):
                    S.mm(po[0][:], wo[:, i, :], ybf[:, i, sl], i == 0, i == KC - 1, [wok, ("sc_y", i, tt)], ["sc_po"])
                xkey = ("x", m, tt)
                S.op("dve", "tensor_tensor", reads=["sc_po", xkey], writes=[xkey], out=x[:, m, sl], in0=po[0][:],
                     in1=x[:, m, sl], op=ALU.add)


def emit_final_norm(C, x, T, g_dram):
    with C.scope():
        g_sb = load_vec_pk(C, "fin_g", g_dram, KC)
        emit_rmsnorm(C, x, "x", 0, T, g_sb, "fin_g", x, "x")


def emit_load_x(C, x, xT_dram, T):
    v = xT_dram.rearrange("(kc p) t -> p kc t", p=128)
    for k in range(KC):
        for tt in range((T + 511) // 512):
            n = min(512, T - tt * 512)
            C.S.dma("sp", writes=[("x", k, tt)], out=x[:, k, tt * 512:tt * 512 + n],
                    in_=v[:, k, tt * 512:tt * 512 + n])


def emit_store_x(C, x, yT_dram, T):
    v = yT_dram.rearrange("(kc p) t -> p kc t", p=128)
    for k in range(KC):
        for tt in range(T // 512):
            C.S.dma("sp", reads=[("x", k, tt)], is_out=True, out=v[:, k, tt * 512:(tt + 1) * 512],
                    in_=x[:, k, tt * 512:(tt + 1) * 512])


def build_ffn_prog(T=TPC):
    nc = bass.Bass("TRN2", target_bir_lowering=False)
    xT = dram_in(nc, "xT", [D, T])
    g = dram_in(nc, "g", [D])
    w_in = dram_in(nc, "w_in", [D, 2 * FF])
    w_out = dram_in(nc, "w_out", [FF, D])
    yT = dram_out(nc, "yT", [D, T])
    with contextlib.ExitStack() as es:
        C = Ctx(nc, es)
        emit_consts(C)
        x = C.sb("x", [128, KC, T], F32)
        emit_load_x(C, x, xT, T)
        emit_ffn(C, x, [(t, 1024) for t in range(0, T, 1024)], g, w_in, w_out, "f")
        emit_store_x(C, x, yT, T)
        C.S.replay()
    return nc


def build_sc_prog(T=TPC, final=False):
    nc = bass.Bass("TRN2", target_bir_lowering=False)
    xT = dram_in(nc, "xT", [D, T + 2])
    nrm = dram_in(nc, "nrm", [3, D])
    fw_in = dram_in(nc, "fw_in", [2, D, 2 * FF])
    fw_out = dram_in(nc, "fw_out", [2, FF, D])
    w_in = dram_in(nc, "w_in", [D, 3 * D])
    conv = dram_in(nc, "conv", [3, D])
    w_out = dram_in(nc, "w_out", [D, D])
    gfin = dram_in(nc, "gfin", [D])
    yT = dram_out(nc, "yT", [D, T])
    with contextlib.ExitStack() as es:
        C = Ctx(nc, es)
        emit_consts(C)
        x = C.sb("x", [128, KC, T + 2], F32)
        emit_load_x(C, x, xT, T + 2)
        grp = [(t, 1024) for t in range(0, T, 1024)]
        emit_ffn(C, x, grp + [(T, 2)], nrm[0], fw_in[0], fw_out[0], "f1")
        emit_sc_mixer(C, x, T, nrm[1], w_in, conv, w_out)
        emit_ffn(C, x, grp, nrm[2], fw_in[1], fw_out[1], "f2")
        if final:
            emit_final_norm(C, x, T, gfin)
        emit_store_x(C, x, yT, T)
        C.S.replay()
    return nc

NFFT = 2 * SEQ
HY_EMB = 33
HY_ORD = 64
MAGIC = 12582912.0
TWO_PI = 2.0 * math.pi
PI_LO = 3.1415925


def hyena_consts():
    n = np.arange(128, dtype=np.float64)
    ang = 2.0 * np.pi * np.outer(n, n) / 128.0
    fre, fim = np.cos(ang), -np.sin(ang)
    angt = 2.0 * np.pi * np.outer(n, n) / NFFT
    tre, tim = np.cos(angt), -np.sin(angt)
    cst = np.stack([fim, fre, -fim, tre, tim], 1).astype(np.float32)
    L = SEQ
    f32 = np.float32
    t = np.linspace(0.0, 1.0, L, dtype=f32)
    w = (f32(2.0 * math.pi) * np.arange(L, dtype=f32) / f32(L)).astype(f32)
    f = np.linspace(1e-4, 15.0, 16, dtype=f32)
    fw = (f[None, :] * w[:, None]).astype(f32)
    z = np.concatenate([t[:, None], np.cos(fw), -np.sin(fw)], -1).astype(f32)
    idx = np.concatenate([[0], np.arange(L - 1, 0, -1)])
    z2 = z[idx]
    t2 = t[idx].copy()
    t2[0] = 1e30
    zT = np.ascontiguousarray(np.concatenate([z, z2], 0).T)
    trow = np.concatenate([t, t2]).astype(f32)
    dmin = math.log(1e-2) / 1.5
    dmax = math.log(1e-2) / 0.3
    deltas = np.linspace(dmin, dmax, D, dtype=f32)
    nad = (-np.abs(deltas)).astype(f32)
    return cst, zT, trow, nad


def emit_fft_fwd(C, X, xkey, K, nseq, cst, tl, ps, kp=""):
    S = C.S
    fimfre = cst[:K, 0:2, :].rearrange("p a b -> p (a b)")
    for s_ in range(nseq):
        bank = ps["a"][s_ // 2]
        S.mm(bank[:, (s_ % 2) * 256:(s_ % 2) * 256 + 256], X[:K, s_, :], fimfre, True, True,
             [xkey, "cst"], [(kp + "psa", s_ // 2)])
    yield
    tre = cst[:, 3:4, :]
    tim = cst[:, 4:5, :]
    for h in range((nseq + 1) // 2):
        ns = min(2, nseq - 2 * h)
        av = ps["a"][h][:].rearrange("p (s r k) -> p s r k", s=2, r=2)
        aim = av[:, :ns, 0, :]
        are = av[:, :ns, 1, :]
        sl = slice(2 * h, 2 * h + ns)
        bt = lambda t_: t_.to_broadcast([128, ns, 128])
        S.op("dve", "tensor_tensor", reads=[(kp + "psa", h), "cst"], writes=[(kp + "t1", h)], out=tl["t1"][:, sl, :], in0=are,
             in1=bt(tre), op=ALU.mult)
        S.op("dve", "tensor_tensor", reads=[(kp + "psa", h), "cst"], writes=[(kp + "t2", h)], out=tl["t2"][:, sl, :], in0=aim,
             in1=bt(tim), op=ALU.mult)
        S.op("dve", "tensor_tensor", reads=[(kp + "psa", h), "cst"], writes=[(kp + "t3", h)], out=tl["t3"][:, sl, :], in0=are,
             in1=bt(tim), op=ALU.mult)
        S.op("dve", "tensor_tensor", reads=[(kp + "psa", h), "cst"], writes=[(kp + "t4", h)], out=tl["t4"][:, sl, :], in0=aim,
             in1=bt(tre), op=ALU.mult)
    hs = list(range((nseq + 1) // 2))
    S.op("pool", "tensor_tensor", reads=[(kp + "t1", h) for h in hs] + [(kp + "t2", h) for h in hs], writes=[kp + "bre"],
         out=tl["bre"][:, :nseq, :], in0=tl["t1"][:, :nseq, :], in1=tl["t2"][:, :nseq, :], op=ALU.subtract)
    S.op("pool", "tensor_tensor", reads=[(kp + "t3", h) for h in hs] + [(kp + "t4", h) for h in hs], writes=[kp + "bim"],
         out=tl["bim"][:, :nseq, :], in0=tl["t3"][:, :nseq, :], in1=tl["t4"][:, :nseq, :], op=ALU.add)
    yield
    n = nseq * 128
    bre = tl["bre"][:].rearrange("p s k -> p (s k)")[:, :n]
    bim = tl["bim"][:].rearrange("p s k -> p (s k)")[:, :n]
    S.mm(ps["xre"][:, :n], cst[:, 1, :], bre, True, False, ["cst", kp + "bre"], [kp + "psxre"])
    S.mm(ps["xre"][:, :n], cst[:, 2, :], bim, False, True, ["cst", kp + "bim"], [kp + "psxre"])
    S.mm(ps["xim"][:, :n], cst[:, 1, :], bim, True, False, ["cst", kp + "bim"], [kp + "psxim"])
    S.mm(ps["xim"][:, :n], cst[:, 0, :], bre, False, True, ["cst", kp + "bre"], [kp + "psxim"])
    yield


def build_hy_core_prog():
    nc = bass.Bass("TRN2", target_bir_lowering=False)
    L = SEQ
    u0 = dram_in(nc, "u0", [3, 128, BATCH, L])
    convw = dram_in(nc, "convw", [3, 3, 128])
    convb = dram_in(nc, "convb", [3, 128])
    dvec = dram_in(nc, "dvec", [128])
    fw1 = dram_in(nc, "fw1", [HY_EMB, HY_ORD])
    fb1 = dram_in(nc, "fb1", [HY_ORD])
    fw2 = dram_in(nc, "fw2", [HY_ORD, HY_ORD])
    fb2 = dram_in(nc, "fb2", [HY_ORD])
    fw3 = dram_in(nc, "fw3", [HY_ORD, 2, 128])
    freq = dram_in(nc, "freq", [HY_ORD])
    cstd = dram_in(nc, "cst", [128, 5, 128])
    zT = dram_in(nc, "zT", [HY_EMB, NFFT])
    trow = dram_in(nc, "trow", [128, NFFT])
    nad = dram_in(nc, "nad", [128])
    yT = dram_out(nc, "yT", [128, BATCH, L])
    s_h = nc.dram_tensor("s_h", [128, NFFT], F32).ap()
    s_H = nc.dram_tensor("s_H", [2, 128, 128, 128], F32).ap()
    s_vv = nc.dram_tensor("s_vv", [128, BATCH, L], F32).ap()
    s_x0 = nc.dram_tensor("s_x0", [128, BATCH, L], F32).ap()
    s_y = nc.dram_tensor("s_y", [128, BATCH, L], F32).ap()
    with contextlib.ExitStack() as es:
        C = Ctx(nc, es)
        S = C.S
        cst = C.sb("cst", [128, 5, 128], F32)
        S.dma("sp", writes=["cst"], out=cst[:], in_=cstd)

        def col(name, src, n):
            t_ = C.sb(name, [n, 1], F32)
            S.dma("sp", writes=[name], out=t_[:], in_=src.rearrange("(p o) -> p o", o=1))
            return t_

        with C.scope():
            w1 = C.sb("w1", [HY_EMB, HY_ORD], F32)
            S.dma("sp", writes=["w1"], out=w1[:], in_=fw1)
            w2 = C.sb("w2", [HY_ORD, HY_ORD], F32)
            S.dma("sp", writes=["w2"], out=w2[:], in_=fw2)
            w3 = C.sb("w3", [HY_ORD, 2, 128], F32)
            S.dma("sp", writes=["w3"], out=w3[:], in_=fw3)
            fq = col("fq", freq, HY_ORD)
            b1 = col("b1", fb1, HY_ORD)
            b2 = col("b2", fb2, HY_ORD)
            nadc = col("nadc", nad, 128)
            S.op("dve", "tensor_tensor", reads=["fq", "b1"], writes=["b1"], out=b1[:], in0=b1[:], in1=fq[:], op=ALU.mult)
            S.op("dve", "tensor_tensor", reads=["fq", "b2"], writes=["b2"], out=b2[:], in0=b2[:], in1=fq[:], op=ALU.mult)
            zt = [C.sb("zt%d" % i, [HY_EMB, 512], F32) for i in range(2)]
            tr = [C.sb("tr%d" % i, [128, 512], F32) for i in range(2)]
            NS = 2
            av = [[C.sb("av%d%d" % (l_, i), [HY_ORD, 512], F32) for i in range(NS)] for l_ in range(2)]
            qv = [[C.sb("qv%d%d" % (l_, i), [HY_ORD, 512], F32) for i in range(NS)] for l_ in range(2)]
            hv = [[C.sb("hv%d%d" % (l_, i), [HY_ORD, 512], F32) for i in range(NS)] for l_ in range(2)]
            hc = [C.sb("hc%d" % i, [128, 512], F32) for i in range(NS)]
            p1 = [C.ps("p1%d" % i, [HY_ORD, 512]) for i in range(NS)]
            p2 = [C.ps("p2%d" % i, [HY_ORD, 512]) for i in range(NS)]
            p3 = [C.ps("p3%d" % i, [128, 512]) for i in range(NS)]

            def sin_layer(psrc, pkey, bias, sl_, lay):
                a, q, h = av[lay][sl_], qv[lay][sl_], hv[lay][sl_]
                ak, qk, hk = ("av", lay, sl_), ("qv", lay, sl_), ("hv", lay, sl_)
                S.op("dve", "tensor_scalar", reads=[pkey, "fq", "b1", "b2"], writes=[ak], out=a[:], in0=psrc[:],
                     scalar1=fq[:, 0:1], scalar2=bias[:, 0:1], op0=ALU.mult, op1=ALU.add)
                S.op("dve", "tensor_scalar", reads=[ak], writes=[qk], out=q[:], in0=a[:], scalar1=1.0 / TWO_PI,
                     scalar2=MAGIC, op0=ALU.mult, op1=ALU.add)
                S.op("dve", "tensor_scalar", reads=[qk], writes=[qk], out=q[:], in0=q[:], scalar1=-MAGIC,
                     scalar2=-TWO_PI, op0=ALU.add, op1=ALU.mult)
                S.op("dve", "tensor_tensor", reads=[qk, ak], writes=[ak], out=a[:], in0=a[:], in1=q[:], op=ALU.add)
                S.op("dve", "tensor_scalar", reads=[ak], writes=[ak], out=a[:], in0=a[:], scalar1=-PI_LO,
                     scalar2=PI_LO, op0=ALU.max, op1=ALU.min)
                yield
                S.op("act", "activation", reads=[ak], writes=[hk], out=h[:], in_=a[:], func=AF.Sin)
                yield

            def p0_tile(ti):
                c0 = ti * 512
                sl_ = ti % NS
                z_, zk = zt[sl_], ("zt", sl_)
                t_, tk = tr[sl_], ("tr", sl_)
                S.dma("sp", writes=[zk], out=z_[:], in_=zT[:, c0:c0 + 512])
                S.dma("sp", writes=[tk], out=t_[:], in_=trow[:, c0:c0 + 512])
                S.mm(p1[sl_][:], w1[:], z_[:], True, True, ["w1", zk], [("p1", sl_)])
                yield
                for _ in sin_layer(p1[sl_], ("p1", sl_), b1, sl_, 0):
                    yield
                S.mm(p2[sl_][:], w2[:], hv[0][sl_][:], True, True, ["w2", ("hv", 0, sl_)], [("p2", sl_)])
                S.op("act", "activation", reads=[tk, "nadc"], writes=[tk], out=t_[:], in_=t_[:], func=AF.Exp,
                     scale=nadc[:, 0:1])
                yield
                for _ in sin_layer(p2[sl_], ("p2", sl_), b2, sl_, 1):
                    yield
                half = 0 if ti < (L // 512) else 1
                S.mm(p3[sl_][:], w3[:, half, :], hv[1][sl_][:], True, True, ["w3", ("hv", 1, sl_)], [("p3", sl_)])
                yield
                o_, ok = hc[sl_], ("hc", sl_)
                S.op("dve", "tensor_tensor", reads=[("p3", sl_), tk], writes=[ok], out=o_[:], in0=p3[sl_][:], in1=t_[:],
                     op=ALU.mult)
                S.dma("sp", reads=[ok], writes=[("s_h", ti)], out=s_h[:, c0:c0 + 512], in_=o_[:])
                yield

            pipeline((p0_tile(ti) for ti in range(NFFT // 512)), NS)

        with C.scope():
            cw = C.sb("hcw", [128, 3, 3], F32)
            for j in range(3):
                for gi in range(3):
                    S.dma("sp", writes=[("hcw", j, gi)], out=cw[:, gi, j:j + 1],
                          in_=convw[j, gi].rearrange("(p o) -> p o", o=1))
            cb = C.sb("hcb", [128, 3], F32)
            for gi in range(3):
                S.dma("sp", writes=[("hcb", gi)], out=cb[:, gi:gi + 1], in_=convb[gi].rearrange("(p o) -> p o", o=1))
            cwk = [("hcw", j, gi) for j in range(3) for gi in range(3)] + [("hcb", gi) for gi in range(3)]
            ub = [C.sb("hu%d" % i, [128, L + 2], F32) for i in range(2)] * 2
            uc = [C.sb("huc%d" % i, [128, L], F32) for i in range(3)]
            for b in range(BATCH):
                for gi in range(3):
                    S.op("pool", "memset", writes=[("hu", gi % 2, "e")], ap=ub[gi][:, 0:1], constant=0.0)
                    S.op("pool", "memset", writes=[("hu", gi % 2, "e2")], ap=ub[gi][:, L + 1:L + 2], constant=0.0)
                    S.dma("sp", writes=[("hu", gi % 2)], out=ub[gi][:, 1:L + 1], in_=u0[gi, :, b, :])
                    rk = [("hu", gi % 2), ("hu", gi % 2, "e"), ("hu", gi % 2, "e2")] + cwk
                    uk = ("huc", gi)
                    S.op("dve", "tensor_scalar", reads=rk, writes=[uk], out=uc[gi][:], in0=ub[gi][:, 0:L],
                         scalar1=cw[:, gi, 0:1], scalar2=cb[:, gi:gi + 1], op0=ALU.mult, op1=ALU.add)
                    S.op("dve", "scalar_tensor_tensor", reads=rk + [uk], writes=[uk], out=uc[gi][:],
                         in0=ub[gi][:, 1:L + 1], scalar=cw[:, gi, 1:2], in1=uc[gi][:], op0=ALU.mult, op1=ALU.add)
                    S.op("dve", "scalar_tensor_tensor", reads=rk + [uk], writes=[uk], out=uc[gi][:],
                         in0=ub[gi][:, 2:L + 2], scalar=cw[:, gi, 2:3], in1=uc[gi][:], op0=ALU.mult, op1=ALU.add)
                S.op("pool", "tensor_tensor", reads=[("huc", 1), ("huc", 2)], writes=[("huc", 2)], out=uc[2][:],
                     in0=uc[2][:], in1=uc[1][:], op=ALU.mult)
                S.dma("sp", reads=[("huc", 2)], writes=[("s_vv", b)], out=s_vv[:, b, :], in_=uc[2][:])
                S.dma("sp", reads=[("huc", 0)], writes=[("s_x0", b)], out=s_x0[:, b, :], in_=uc[0][:])
        with C.scope():
            tl = {k: C.sb("fft_" + k, [128, 4, 128], F32) for k in
                  ("t1", "t2", "t3", "t4", "u1", "u2", "u3", "u4", "bre", "bim", "dre", "dim")}
            yre = [C.sb("fft_yre%d" % i, [128, 4, 128], F32) for i in range(2)]
            yim = [C.sb("fft_yim%d" % i, [128, 4, 128], F32) for i in range(2)]
            ps = {"a": [C.ps("psa%d" % i, [128, 512]) for i in range(2)], "xre": C.ps("psxre", [128, 512]),
                  "xim": C.ps("psxim", [128, 512]), "c": [C.ps("psc%d" % i, [128, 512]) for i in range(2)],
                  "y": C.ps("psy", [128, 512])}
            Xb = [C.sb("fft_X%d" % i, [128, 4, 128], F32) for i in range(2)]
            Hb = [[C.sb("fft_H%d%d" % (i, r), [128, 2, 128], F32) for r in range(2)] for i in range(2)]
            ev = [[C.sb("fft_ev%d%d" % (i, r), [128, 4, 128], F32) for r in range(2)] for i in range(2)]
            yo = [C.sb("fft_yo%d" % i, [64, 4, 128], F32) for i in range(2)]
            ps8 = C.ps("psx8", [128, 512])
            tlB = {"t1": tl["u1"], "t2": tl["u2"], "t3": tl["u3"], "t4": tl["u4"], "bre": tl["dre"], "bim": tl["dim"]}
            psB = {"a": ps["c"], "xre": ps["y"], "xim": ps8}

            def p1_group(g):
                q = g % 2
                tl_, ps_, kp = (tl, ps, "") if q == 0 else (tlB, psB, "B")
                X, xk = Xb[q], ("X", q)
                S.dma("sp", reads=[("s_h", ti) for ti in range(NFFT // 512)], writes=[xk], out=X[:],
                      in_=s_h[4 * g:4 * g + 4, :].rearrange("c (n1 n2) -> n1 c n2", n2=128))
                for _ in emit_fft_fwd(C, X, xk, 128, 4, cst, tl_, ps_, kp):
                    yield
                for r, nm in ((0, "xre"), (1, "xim")):
                    e_, ek = ev[q][r], ("ev", q, r)
                    S.op("act", "mul", reads=[kp + "ps" + nm], writes=[ek], out=e_[:].rearrange("p s k -> p (s k)"),
                         in_=ps_[nm][:], mul=1.0 / NFFT)
                    S.dma("sp", reads=[ek], writes=[("s_H", g, r)], out=s_H[r, :, 4 * g:4 * g + 4, :], in_=e_[:])
                yield

            pipeline((p1_group(g) for g in range(32)), 2)
            S.barrier()
            tre = cst[:, 3:4, :]
            tim = cst[:, 4:5, :]
            g1 = cst[:, 1:3, :].rearrange("p a b -> p (a b)")
            g2 = cst[:, 0:2, :].rearrange("p a b -> p (a b)")

            def half1(g):
                p = g % 2
                X, xk = Xb[p], ("X", p)
                S.dma("sp", reads=[("s_vv", 0), ("s_vv", 1)], writes=[xk], out=X[:64],
                      in_=s_vv[2 * g:2 * g + 2].rearrange("c b (n1 n2) -> n1 (c b) n2", n2=128))
                H = Hb[p]
                for r in range(2):
                    S.dma("sp", reads=[("s_H", g // 2, r)], writes=[("H", p, r)], out=H[r][:],
                          in_=s_H[r, :, 2 * g:2 * g + 2, :])
                for _ in emit_fft_fwd(C, X, xk, 64, 4, cst, tl, ps):
                    yield
                hk = [("H", p, 0), ("H", p, 1)]
                xre = ps["xre"][:].rearrange("p (c b k) -> p c b k", c=2, b=2)
                xim = ps["xim"][:].rearrange("p (c b k) -> p c b k", c=2, b=2)
                hb = lambda r: H[r][:].unsqueeze(2).to_broadcast([128, 2, 2, 128])
                v4 = lambda t_: t_[:].rearrange("p (c b) k -> p c b k", c=2)
                S.op("dve", "tensor_tensor", reads=["psxre"] + hk, writes=[("t1", 0), ("t1", 1)], out=v4(tl["t1"]),
                     in0=xre, in1=hb(0), op=ALU.mult)
                S.op("dve", "tensor_tensor", reads=["psxim"] + hk, writes=[("t2", 0), ("t2", 1)], out=v4(tl["t2"]),
                     in0=xim, in1=hb(1), op=ALU.mult)
                S.op("dve", "tensor_tensor", reads=["psxre"] + hk, writes=[("t3", 0), ("t3", 1)], out=v4(tl["t3"]),
                     in0=xre, in1=hb(1), op=ALU.mult)
                S.op("dve", "tensor_tensor", reads=["psxim"] + hk, writes=[("t4", 0), ("t4", 1)], out=v4(tl["t4"]),
                     in0=xim, in1=hb(0), op=ALU.mult)
                S.op("pool", "tensor_tensor", reads=[("t1", 0), ("t1", 1), ("t2", 0), ("t2", 1)], writes=[("yre", p)],
                     out=yre[p][:], in0=tl["t1"][:], in1=tl["t2"][:], op=ALU.subtract)
                S.op("pool", "tensor_tensor", reads=[("t3", 0), ("t3", 1), ("t4", 0), ("t4", 1)], writes=[("yim", p)],
                     out=yim[p][:], in0=tl["t3"][:], in1=tl["t4"][:], op=ALU.add)
                yield

            def half2(g):
                p = g % 2
                for s_ in range(4):
                    bank = ps["c"][s_ // 2]
                    o = (s_ % 2) * 256
                    S.mm(bank[:, o:o + 256], yre[p][:, s_, :], g1, True, False, [("yre", p), "cst"], [("psc", s_ // 2)])
                    S.mm(bank[:, o:o + 256], yim[p][:, s_, :], g2, False, True, [("yim", p), "cst"], [("psc", s_ // 2)])
                yield
                for h in range(2):
                    cv = ps["c"][h][:].rearrange("p (s r k) -> p s r k", s=2, r=2)
                    cre = cv[:, :, 0, :]
                    cim = cv[:, :, 1, :]
                    sl = slice(2 * h, 2 * h + 2)
                    bt = lambda t_: t_.to_broadcast([128, 2, 128])
                    S.op("dve", "tensor_tensor", reads=[("psc", h), "cst"], writes=[("u1", h)], out=tl["u1"][:, sl, :],
                         in0=cre, in1=bt(tre), op=ALU.mult)
                    S.op("dve", "tensor_tensor", reads=[("psc", h), "cst"], writes=[("u2", h)], out=tl["u2"][:, sl, :],
                         in0=cim, in1=bt(tim), op=ALU.mult)
                    S.op("dve", "tensor_tensor", reads=[("psc", h), "cst"], writes=[("u3", h)], out=tl["u3"][:, sl, :],
                         in0=cim, in1=bt(tre), op=ALU.mult)
                    S.op("dve", "tensor_tensor", reads=[("psc", h), "cst"], writes=[("u4", h)], out=tl["u4"][:, sl, :],
                         in0=cre, in1=bt(tim), op=ALU.mult)
                S.op("pool", "tensor_tensor", reads=[("u1", 0), ("u1", 1), ("u2", 0), ("u2", 1)], writes=["dre"],
                     out=tl["dre"][:], in0=tl["u1"][:], in1=tl["u2"][:], op=ALU.add)
                S.op("pool", "tensor_tensor", reads=[("u3", 0), ("u3", 1), ("u4", 0), ("u4", 1)], writes=["dim"],
                     out=tl["dim"][:], in0=tl["u3"][:], in1=tl["u4"][:], op=ALU.subtract)
                yield
                S.mm(ps["y"][:64, :], cst[:, 1, 0:64], tl["dre"][:].rearrange("p s k -> p (s k)"), True, False,
                     ["cst", "dre"], ["psy"])
                S.mm(ps["y"][:64, :], cst[:, 0, 0:64], tl["dim"][:].rearrange("p s k -> p (s k)"), False, True,
                     ["cst", "dim"], ["psy"])
                yield
                y_, yk = yo[p], ("yo", p)
                S.op("act", "activation", reads=["psy"], writes=[yk], out=y_[:].rearrange("p s k -> p (s k)"),
                     in_=ps["y"][:64, :], func=AF.Copy)
                S.dma("sp", reads=[yk], writes=[("s_y", g)], out=s_y[2 * g:2 * g + 2].rearrange(
                    "c b (n1 n2) -> n1 (c b) n2", n2=128), in_=y_[:])
                yield

            for _ in half1(0):
                pass
            for g in range(64):
                interleave([half2(g)] + ([half1(g + 1)] if g + 1 < 64 else []))
        with C.scope():
            dcol = col("dcol", dvec, 128)
            ya = C.sb("p4y", [128, L], F32)
            va = C.sb("p4v", [128, L], F32)
            xa = C.sb("p4x", [128, L], F32)
            for b in range(BATCH):
                S.dma("sp", reads=[("s_y", g) for g in range(64)], writes=["p4y"], out=ya[:], in_=s_y[:, b, :])
                S.dma("sp", reads=[("s_vv", b)], writes=["p4v"], out=va[:], in_=s_vv[:, b, :])
                S.dma("sp", reads=[("s_x0", b)], writes=["p4x"], out=xa[:], in_=s_x0[:, b, :])
                S.op("dve", "scalar_tensor_tensor", reads=["p4y", "p4v", "dcol"], writes=["p4y"], out=ya[:], in0=va[:],
                     scalar=dcol[:, 0:1], in1=ya[:], op0=ALU.mult, op1=ALU.add)
                S.op("dve", "tensor_tensor", reads=["p4y", "p4x"], writes=["p4y"], out=ya[:], in0=ya[:], in1=xa[:],
                     op=ALU.mult)
                S.dma("sp", reads=["p4y"], is_out=True, out=yT[:, b, :], in_=ya[:])
        S.replay()
    return nc

def emit_proj(C, x, T, g_dram, w_in, b_in, nout, uT):
    S = C.S
    NT = T // 512
    w_in_v = w_in.rearrange("(kc p) n -> p kc n", p=128)
    with C.scope():
        g_sb = load_vec_pk(C, "pj_g", g_dram, KC)
        bias = load_vec_pk(C, "pj_b", b_in, nout // 128)
        hn = C.sb("pj_hn", [128, KC, T], BF16)
        wbuf = [C.sb("pj_w%d" % i, [128, KC, 128], BF16) for i in range(3)]
        st = [C.sb("pj_st%d" % i, [128, 512], F32) for i in range(4)]
        pp = [C.ps("pj_ps%d" % i, [128, 512]) for i in range(2)]
        emit_rmsnorm(C, x, "x", 0, T, g_sb, "pj_g", hn, "pj_hn")
        it = 0
        for m in range(nout // 128):
            wb, wk = wbuf[m % 3], ("pj_w", m % 3)
            S.dma("pool", writes=[wk], out=wb[:], in_=w_in_v[:, :, m * 128:(m + 1) * 128])
            for tt in range(NT):
                it += 1
                b = it % 2
                sl = slice(tt * 512, (tt + 1) * 512)
                for k in range(KC):
                    S.mm(pp[b][:], wb[:, k, :], hn[:, k, sl], k == 0, k == KC - 1, [wk, ("pj_hn", k, tt)], [("pj_ps", b)])
                o_, ok = st[it % 4], ("pj_st", it % 4)
                S.op("act", "activation", reads=[("pj_ps", b), "pj_b"], writes=[ok], out=o_[:], in_=pp[b][:],
                     func=AF.Identity, bias=bias[:, m:m + 1])
                S.dma("sp", reads=[ok], is_out=True, out=uT[m * 128:(m + 1) * 128, sl], in_=o_[:])


def emit_outproj(C, x, T, yT, w_out, b_out):
    S = C.S
    NT = T // 512
    w_out_v = w_out.rearrange("(kc p) n -> p kc n", p=128)
    y_v = yT.rearrange("(kc p) t -> p kc t", p=128)
    with C.scope():
        bo = load_vec_pk(C, "op_b", b_out, KC)
        ybf = C.sb("op_y", [128, KC, T], BF16)
        for k in range(KC):
            S.dma("pool", writes=[("op_y", k)], out=ybf[:, k, :], in_=y_v[:, k, :])
        wobuf = [C.sb("op_wo%d" % i, [128, KC, 128], BF16) for i in range(2)]
        po = [C.ps("op_po%d" % i, [128, 512]) for i in range(2)]
        it = 0
        for m in range(KC):
            wo, wok = wobuf[m % 2], ("op_wo", m % 2)
            S.dma("pool", writes=[wok], out=wo[:], in_=w_out_v[:, :, m * 128:(m + 1) * 128])
            for tt in range(NT):
                it += 1
                b = it % 2
                sl = slice(tt * 512, (tt + 1) * 512)
                for i in range(KC):
                    S.mm(po[b][:], wo[:, i, :], ybf[:, i, sl], i == 0, i == KC - 1, [wok, ("op_y", i)], [("op_po", b)])
                xkey = ("x", m, tt)
                S.op("dve", "scalar_tensor_tensor", reads=[("op_po", b), xkey, "op_b"], writes=[xkey], out=x[:, m, sl],
                     in0=po[b][:], scalar=bo[:, m:m + 1], in1=x[:, m, sl], op0=ALU.add, op1=ALU.add)


def build_pre_prog(nout, T=TPC):
    nc = bass.Bass("TRN2", target_bir_lowering=False)
    xT = dram_in(nc, "xT", [D, T])
    nrm = dram_in(nc, "nrm", [2, D])
    fw_in = dram_in(nc, "fw_in", [D, 2 * FF])
    fw_out = dram_in(nc, "fw_out", [FF, D])
    w_in = dram_in(nc, "w_in", [D, nout])
    b_in = dram_in(nc, "b_in", [nout])
    xo = dram_out(nc, "xo", [D, T])
    uT = dram_out(nc, "uT", [nout, T])
    with contextlib.ExitStack() as es:
        C = Ctx(nc, es)
        emit_consts(C)
        x = C.sb("x", [128, KC, T], F32)
        emit_load_x(C, x, xT, T)
        emit_ffn(C, x, [(t, 1024) for t in range(0, T, 1024)], nrm[0], fw_in, fw_out, "f1")
        emit_store_x(C, x, xo, T)
        emit_proj(C, x, T, nrm[1], w_in, b_in, nout, uT)
        C.S.replay()
    return nc


def build_post_prog(T=TPC):
    nc = bass.Bass("TRN2", target_bir_lowering=False)
    xT = dram_in(nc, "xT", [D, T])
    yT = dram_in(nc, "yT", [D, T])
    w_out = dram_in(nc, "w_out", [D, D])
    b_out = dram_in(nc, "b_out", [D])
    nrm = dram_in(nc, "nrm", [D])
    fw_in = dram_in(nc, "fw_in", [D, 2 * FF])
    fw_out = dram_in(nc, "fw_out", [FF, D])
    xo = dram_out(nc, "xo", [D, T])
    with contextlib.ExitStack() as es:
        C = Ctx(nc, es)
        emit_consts(C)
        x = C.sb("x", [128, KC, T], F32)
        emit_load_x(C, x, xT, T)
        emit_outproj(C, x, T, yT, w_out, b_out)
        emit_ffn(C, x, [(t, 1024) for t in range(0, T, 1024)], nrm, fw_in, fw_out, "f2")
        emit_store_x(C, x, xo, T)
        C.S.replay()
    return nc


CH = 64
NCH = SEQ // CH
BLK = 8


def gdn_consts():
    i = np.arange(CH)
    mtri = np.stack([(i[:, None] <= i[None, :]), (i[:, None] >= i[None, :])]).astype(np.float32)
    eye = np.eye(CH, dtype=np.float32)
    mstrict = mtri - eye[None]
    ident = np.eye(128, dtype=np.float32)
    return mtri, mstrict, ident


def build_gd_core_prog(dbg_blocks=None):
    nc = bass.Bass("TRN2", target_bir_lowering=False)
    L, N = SEQ, NCH
    qkv0 = dram_in(nc, "qkv0", [3, 128, BATCH, L])
    ztok = dram_in(nc, "ztok", [CH, BATCH, N, 128])
    abt = dram_in(nc, "abt", [4, CH, BATCH, N])
    convw = dram_in(nc, "convw", [3, 3, 128])
    alog = dram_in(nc, "alog", [CH, 2])
    dtb = dram_in(nc, "dtb", [CH, 2])
    gn = dram_in(nc, "gn", [CH, 128])
    mtri_d = dram_in(nc, "mtri", [2, CH, CH])
    mstr_d = dram_in(nc, "mstrict", [2, CH, CH])
    ident_d = dram_in(nc, "ident", [128, 128])
    ytok = dram_out(nc, "ytok", [CH, BATCH, N, 128])
    s_o = nc.dram_tensor("s_o", [2, CH, BATCH, N, 128], F32).ap()
    with contextlib.ExitStack() as es:
        C = Ctx(nc, es)
        S = C.S
        ones = C.sb("ones", [128, 128], F32)
        S.op("dve", "memset", writes=["ones"], ap=ones[:], constant=1.0)
        ident = C.sb("ident", [128, 128], F32)
        S.dma("sp", writes=["ident"], out=ident[:], in_=ident_d)
        mtri = C.sb("mtri", [CH, 2, CH], F32)
        mstr = C.sb("mstr", [CH, 2, CH], F32)
        for d in range(2):
            S.dma("sp", writes=[("mtri", d)], out=mtri[:, d, :], in_=mtri_d[d])
            S.dma("sp", writes=[("mstr", d)], out=mstr[:, d, :], in_=mstr_d[d])
        cw = C.sb("gcw", [128, 3, 3], F32)
        for j in range(3):
            for gi in range(3):
                S.dma("sp", writes=[("gcw", j, gi)], out=cw[:, gi, j:j + 1], in_=convw[j, gi].rearrange("(p o) -> p o", o=1))
        cwk = [("gcw", j, gi) for j in range(3) for gi in range(3)]
        gtf = C.sb("gt", [128, 2, BATCH, N], F32)
        S.op("dve", "memset", writes=[("gt", 0), ("gt", 1)], ap=gtf[:], constant=0.0)
        gt = gtf[:CH]
        bt_ = C.sb("bt", [CH, 2, BATCH, N], F32)
        al = C.sb("al", [CH, 2], F32)
        db = C.sb("db", [CH, 2], F32)
        S.dma("sp", writes=["al"], out=al[:], in_=alog)
        S.dma("sp", writes=["db"], out=db[:], in_=dtb)
        for d in range(2):
            S.dma("sp", writes=[("gt", d)], out=gt[:, d], in_=abt[d])
            S.dma("sp", writes=[("bt", d)], out=bt_[:, d], in_=abt[2 + d])
        S.op("act", "activation", reads=["al"], writes=["al"], out=al[:], in_=al[:], func=AF.Exp)
        S.op("dve", "tensor_scalar", reads=["al"], writes=["al"], out=al[:], in0=al[:], scalar1=-1.0, scalar2=None,
             op0=ALU.mult)
        for d in range(2):
            gv = gt[:, d].rearrange("p b n -> p (b n)")
            bv = bt_[:, d].rearrange("p b n -> p (b n)")
            S.op("act", "activation", reads=[("gt", d), "db"], writes=[("gt", d)], out=gv, in_=gv, func=AF.Exp,
                 bias=db[:, d:d + 1])
            S.op("dve", "tensor_scalar", reads=[("gt", d)], writes=[("gt", d)], out=gv, in0=gv, scalar1=1.0, scalar2=None,
                 op0=ALU.add)
            S.op("act", "activation", reads=[("gt", d)], writes=[("gt", d)], out=gv, in_=gv, func=AF.Ln)
            S.op("dve", "tensor_scalar", reads=[("gt", d), "al"], writes=[("gt", d)], out=gv, in0=gv, scalar1=al[:, d:d + 1],
                 scalar2=None, op0=ALU.mult)
            S.op("act", "activation", reads=[("bt", d)], writes=[("bt", d)], out=bv, in_=bv, func=AF.Sigmoid)
        with C.scope():
            qT = C.sb("qT", [128, L], F32)
            kT = C.sb("kT", [128, L], F32)
            vT = C.sb("vT", [128, L], F32)
            ubs = [C.sb("gub%d" % i, [128, 2048 + 2], F32) for i in range(2)]
            ci = 0
            rsb = [C.sb("grs%d" % i, [128, 512], F32) for i in range(3)]
            sqb = [C.sb("gsq%d" % i, [128, 512], F32) for i in range(3)]
            gc = C.sb("gc", [CH, N], F32)
            kd = C.sb("kd", [CH, N], F32)
            bg = C.sb("bg", [CH, N], F32)
            egt = C.sb("egt", [128, N], F32)
            Sst = [C.sb("Sst%d" % i, [128, 128], F32) for i in range(1)]
            vnb = [C.sb("vn%d" % i, [128, 128], F32) for i in range(2)]
            attnT = [C.sb("g8_attnT%d" % i, [128, BLK, CH], F32) for i in range(2)]
            for i_ in range(2):
                S.op("dve", "memset", writes=[("attnT", i_)], ap=attnT[i_][:], constant=0.0)
            for i_ in range(2):
                S.op("dve", "memset", writes=[("vn", i_)], ap=vnb[i_][:], constant=0.0)
            names = ["dT", "W1", "LT", "Lm", "X8", "Pa", "Pta", "Pb", "Ptb"]
            t8f = {nm: C.sb("g8_" + nm, [128, BLK, CH], F32) for nm in names}
            for nm in names:
                S.op("dve", "memset", writes=[nm], ap=t8f[nm][:], constant=0.0)
            t8 = {nm: t8f[nm][:CH] for nm in names}
            qd = [C.sb("g8_qd%d" % i, [128, BLK * CH], F32) for i in range(2)]
            egr = C.sb("g8_egr", [128, BLK * CH], F32)
            kdt = [C.sb("g8_kdt%d" % i, [128, BLK, 128], F32) for i in range(2)]
            for i_ in range(2):
                S.op("dve", "memset", writes=[("kdt", i_, 0)], ap=kdt[i_][:, 0:4, :], constant=0.0)
                S.op("dve", "memset", writes=[("kdt", i_, 1)], ap=kdt[i_][:, 4:8, :], constant=0.0)
            kbt = C.sb("g8_kbt", [128, BLK, 128], F32)
            S.op("dve", "memset", writes=[("kbt", 0)], ap=kbt[:, 0:4, :], constant=0.0)
            S.op("dve", "memset", writes=[("kbt", 1)], ap=kbt[:, 4:8, :], constant=0.0)
            vbt = C.sb("g8_vbt", [CH, BLK, 128], F32)
            u8 = [C.sb("g8_u8%d" % i, [CH, BLK, 128], F32) for i in range(2)]
            wT8 = [C.sb("g8_wT8%d" % i, [128, BLK, CH], F32) for i in range(2)]
            o8 = [C.sb("g8_o8%d" % i, [CH, BLK, 128], F32) for i in range(2)]
            Bk = [C.ps("gB%d" % i, [128, 512]) for i in range(8)]
            bk = lambda i: ("B", i)
            b7all = [bk(7)]
            S.excl.update(bk(i) for i in range(8))

            for b in range(BATCH):
                for gi, dst, dk_ in ((0, qT, "qT"), (1, kT, "kT"), (2, vT, "vT")):
                    CT = 2048
                    for ct in range(L // CT):
                        ci += 1
                        u_, uk_ = ubs[ci % 2], ("gub", ci % 2)
                        lo = ct * CT - 1
                        hi = ct * CT + CT + 1
                        a_ = max(lo, 0)
                        b_ = min(hi, L)
                        wr = [uk_]
                        if lo < 0:
                            S.op("pool", "memset", writes=[uk_], ap=u_[:, 0:1], constant=0.0)
                        if hi > L:
                            S.op("pool", "memset", writes=[uk_], ap=u_[:, CT + 1:CT + 2], constant=0.0)
                        S.dma("sp", writes=[uk_], out=u_[:, a_ - lo:b_ - lo], in_=qkv0[gi, :, b, a_:b_])
                        rk = wr + cwk
                        dsl = slice(ct * CT, (ct + 1) * CT)
                        dkt = (dk_, ct)
                        S.op("dve", "tensor_scalar", reads=rk, writes=[dkt], out=dst[:, dsl], in0=u_[:, 0:CT],
                             scalar1=cw[:, gi, 0:1], scalar2=None, op0=ALU.mult)
                        S.op("dve", "scalar_tensor_tensor", reads=rk + [dkt], writes=[dkt], out=dst[:, dsl], in0=u_[:, 1:CT + 1],
                             scalar=cw[:, gi, 1:2], in1=dst[:, dsl], op0=ALU.mult, op1=ALU.add)
                        S.op("dve", "scalar_tensor_tensor", reads=rk + [dkt], writes=[dkt], out=dst[:, dsl], in0=u_[:, 2:CT + 2],
                             scalar=cw[:, gi, 2:3], in1=dst[:, dsl], op0=ALU.mult, op1=ALU.add)
                        S.op("act", "activation", reads=[dkt], writes=[dkt], out=dst[:, dsl], in_=dst[:, dsl], func=AF.Silu)
                    S.op("act", "activation", reads=[(dk_, ct) for ct in range(L // CT)], writes=[dk_], out=dst[:, 0:1],
                         in_=dst[:, 0:1], func=AF.Copy)
                    if gi < 2:
                        sc = (128.0 ** -0.5) if gi == 0 else 1.0

                        def l2_tile(tt, dst=dst, dk_=dk_, sc=sc):
                            sl = slice(tt * 512, (tt + 1) * 512)
                            q3 = tt % 3
                            sq, sqk = sqb[q3], ("gsq", q3)
                            rs, rsk = rsb[q3], ("grs", q3)
                            S.op("act", "activation", reads=[dk_], writes=[sqk], out=sq[:], in_=dst[:, sl], func=AF.Square)
                            yield
                            S.mm(Bk[q3][:], ones[:], sq[:], True, True, ["ones", sqk], [bk(q3)])
                            yield
                            S.op("dve", "tensor_scalar", reads=[bk(q3)], writes=[rsk], out=rs[:], in0=Bk[q3][:], scalar1=1e-6,
                                 scalar2=None, op0=ALU.add)
                            yield
                            S.op("act", "activation", reads=[rsk], writes=[rsk], out=rs[:], in_=rs[:], func=AF.Sqrt)
                            yield
                            S.op("dve", "reciprocal", reads=[rsk], writes=[rsk], out=rs[:], in_=rs[:])
                            S.op("dve", "scalar_tensor_tensor", reads=[rsk, dk_], writes=[(dk_, "n", tt)], out=dst[:, sl],
                                 in0=dst[:, sl], scalar=sc, in1=rs[:], op0=ALU.mult, op1=ALU.mult)
                            yield

                        pipeline((l2_tile(tt) for tt in range(L // 512)), 3)
                        S.op("dve", "tensor_copy", reads=[(dk_, "n", tt) for tt in range(L // 512)], writes=[dk_],
                             out=dst[:, 0:1], in_=dst[:, 0:1])
                for d in range(2):
                    Gd = gt[:, d, b, :]
                    Bd = bt_[:, d, b, :]
                    S.mm(Bk[0][:CH, :N], mtri[:, d, :], Gd, True, True, [("mtri", d), ("gt", d)], [bk(0)])
                    S.op("act", "activation", reads=[bk(0)], writes=["gc"], out=gc[:], in_=Bk[0][:CH, :N], func=AF.Copy)
                    S.mm(Bk[1][:, :N], ones[:], gtf[:, d, b, :], True, True, ["ones", ("gt", d)], [bk(1)])
                    S.op("act", "activation", reads=[bk(1)], writes=["egt"], out=egt[:], in_=Bk[1][:, :N], func=AF.Exp)
                    S.op("dve", "tensor_tensor", reads=[bk(1), "gc"], writes=["kd"], out=kd[:], in0=Bk[1][:CH, :N], in1=gc[:],
                         op=ALU.subtract)
                    S.op("act", "activation", reads=["kd"], writes=["kd"], out=kd[:], in_=kd[:], func=AF.Exp)
                    S.op("act", "activation", reads=["gc"], writes=["bg"], out=bg[:], in_=gc[:], func=AF.Exp)
                    S.op("dve", "tensor_tensor", reads=["bg", ("bt", d)], writes=["bg"], out=bg[:], in0=bg[:], in1=Bd, op=ALU.mult)
                    S.op("dve", "memset", writes=[("S", 0)], ap=Sst[0][:], constant=0.0)
                    si = 0
                    vi = 0
                    blocks = list(range(N // BLK))
                    if d == 1:
                        blocks = blocks[::-1]
                    if dbg_blocks is not None:
                        blocks = blocks[:dbg_blocks]
                    flat = lambda t_: t_[:].rearrange("p i c -> p (i c)")
                    mt_b = mtri[:, d:d + 1, :].to_broadcast([CH, BLK, CH])
                    ms_b = mstr[:, d:d + 1, :].to_broadcast([CH, BLK, CH])
                    id_b = ident[:CH, 0:CH].unsqueeze(1).to_broadcast([CH, BLK, CH])
                    k3 = lambda i_: Bk[i_][:CH, :].rearrange("p (i c) -> p i c", c=CH)

                    def prep(nb, pb, d=d, Gd=Gd, Bd=Bd):
                        n0 = nb * BLK
                        tsl = slice(n0 * CH, (n0 + BLK) * CH)
                        gb = lambda t_: t_[:, n0:n0 + BLK].unsqueeze(2).to_broadcast([CH, BLK, CH])
                        qd_, at_, kdt_, u8_, wT8_ = qd[pb], attnT[pb], kdt[pb], u8[pb], wT8[pb]
                        S.op("dve", "tensor_tensor", reads=[("gt", d), ("mtri", d)], writes=["Pa"], out=t8["Pa"][:],
                             in0=mt_b, in1=gb(Gd), op=ALU.mult)
                        S.op("dve", "tensor_tensor", reads=[("bt", d), "ident"], writes=["Pb"], out=t8["Pb"][:], in0=id_b,
                             in1=gb(Bd), op=ALU.mult)
                        yield
                        S.mm(Bk[3][:], ones[:], t8f["Pa"][:].rearrange("p i c -> p (i c)"), True, True, ["ones", "Pa"], [bk(3)])
                        S.mm(Bk[4][:CH, :], ones[:CH, :CH], flat(t8["Pb"]), True, True, ["ones", "Pb"], [bk(4)])
                        for i in range(BLK):
                            csl = slice((n0 + i) * CH, (n0 + i + 1) * CH)
                            S.mm(Bk[5][:CH, i * CH:(i + 1) * CH], kT[:, csl], kT[:, csl], True, True, ["kT"], [bk(5)])
                            S.mm(Bk[6][:CH, i * CH:(i + 1) * CH], kT[:, csl], qT[:, csl], True, True, ["kT", "qT"], [bk(6)])
                        yield
                        S.op("act", "activation", reads=[bk(3)], writes=["egr"], out=egr[:], in_=Bk[3][:], func=AF.Exp)
                        S.op("dve", "tensor_tensor", reads=[bk(3), "gc"], writes=["dT"], out=t8["dT"][:], in0=k3(3), in1=gb(gc),
                             op=ALU.subtract)
                        S.op("dve", "tensor_scalar", reads=["dT"], writes=["dT"], out=t8["dT"][:], in0=t8["dT"][:], scalar1=0.0,
                             scalar2=None, op0=ALU.min)
                        yield
                        S.op("act", "activation", reads=["dT"], writes=["dT"], out=t8["dT"][:], in_=t8["dT"][:], func=AF.Exp)
                        S.op("dve", "tensor_tensor", reads=["egr", "qT"], writes=[("qd", pb)], out=qd_[:], in0=qT[:, tsl],
                             in1=egr[:], op=ALU.mult)
                        yield
                        S.op("dve", "tensor_tensor", reads=["dT", ("mstr", d)], writes=["W1"], out=t8["W1"][:], in0=t8["dT"][:],
                             in1=ms_b, op=ALU.mult)
                        S.op("dve", "tensor_tensor", reads=["W1", bk(4)], writes=["W1"], out=t8["W1"][:], in0=t8["W1"][:],
                             in1=k3(4), op=ALU.mult)
                        S.op("dve", "tensor_tensor", reads=["dT", ("mtri", d)], writes=["dT"], out=t8["dT"][:], in0=t8["dT"][:],
                             in1=mt_b, op=ALU.mult)
                        S.op("dve", "tensor_tensor", reads=[bk(5), "W1"], writes=["LT"], out=t8["LT"][:], in0=k3(5),
                             in1=t8["W1"][:], op=ALU.mult)
                        S.op("dve", "tensor_tensor", reads=[bk(6), "dT"], writes=[("attnT", pb)], out=at_[:CH], in0=k3(6),
                             in1=t8["dT"][:], op=ALU.mult)
                        yield
                        for i in range(BLK):
                            S.mm(Bk[7][:CH, i * CH:(i + 1) * CH], t8["LT"][:, i, :], ident[:CH, :CH], True, True,
                                 ["LT", "ident"], [bk(7)])
                        S.op("dve", "tensor_tensor", reads=["LT", "ident"], writes=["X8"], out=t8["X8"][:], in0=id_b,
                             in1=t8["LT"][:], op=ALU.subtract)
                        yield
                        S.op("act", "activation", reads=[bk(7)], writes=["Lm"], out=flat(t8["Lm"]), in_=Bk[7][:CH, :],
                             func=AF.Copy)
                        yield
                        A_, At_, ak, atk = t8["LT"], t8["Lm"], "LT", "Lm"
                        for lev in range(5):
                            P_, Pt_ = (t8["Pa"], t8["Pta"]) if lev % 2 == 0 else (t8["Pb"], t8["Ptb"])
                            pk, ptk = ("Pa", "Pta") if lev % 2 == 0 else ("Pb", "Ptb")
                            for i in range(BLK):
                                cs = slice(i * CH, (i + 1) * CH)
                                S.mm(Bk[4][:CH, cs], A_[:, i, :], At_[:, i, :], True, True, [ak, atk], [bk(4)])
                                if lev < 4:
                                    S.mm(Bk[3][:CH, cs], At_[:, i, :], A_[:, i, :], True, True, [ak, atk], [bk(3)])
                            yield
                            S.op("act", "activation", reads=[bk(4)], writes=[ptk], out=flat(Pt_), in_=Bk[4][:CH, :], func=AF.Copy)
                            if lev < 4:
                                S.op("act", "activation", reads=[bk(3)], writes=[pk], out=flat(P_), in_=Bk[3][:CH, :], func=AF.Copy)
                            yield
                            for i in range(BLK):
                                cs = slice(i * CH, (i + 1) * CH)
                                S.mm(Bk[5][:CH, cs], Pt_[:, i, :], t8["X8"][:, i, :], True, True, [ptk, "X8"], [bk(5)])
                            yield
                            S.op("dve", "tensor_tensor", reads=[bk(5), "X8"], writes=["X8"], out=flat(t8["X8"]),
                                 in0=Bk[5][:CH, :], in1=flat(t8["X8"]), op=ALU.add)
                            A_, At_, ak, atk = P_, Pt_, pk, ptk
                        yield
                        for i in range(BLK):
                            csl = slice((n0 + i) * CH, (n0 + i + 1) * CH)
                            bkk, bkv = (6, 3) if i < 4 else (7, 4)
                            o_ = (i % 4) * 128
                            S.mm(Bk[bkk][:CH, o_:o_ + 128], kT[:, csl], ident[:], True, True, ["kT", "ident"], [bk(bkk)])
                            S.mm(Bk[bkv][:CH, o_:o_ + 128], vT[:, csl], ident[:], True, True, ["vT", "ident"], [bk(bkv)])
                        yield
                        for hh in range(2):
                            isl = slice(hh * 4, hh * 4 + 4)
                            sc_b = lambda t_: t_[:, n0 + hh * 4:n0 + hh * 4 + 4].unsqueeze(2).to_broadcast([CH, 4, 128])
                            kps = Bk[6 + hh][:CH, :].rearrange("p (i e) -> p i e", e=128)
                            vps = Bk[3 + hh][:CH, :].rearrange("p (i e) -> p i e", e=128)
                            S.op("dve", "tensor_tensor", reads=[bk(6 + hh), "kd"], writes=[("kdt", pb, hh)],
                                 out=kdt_[:CH, isl, :], in0=kps, in1=sc_b(kd), op=ALU.mult)
                            S.op("dve", "tensor_tensor", reads=[bk(6 + hh), "bg"], writes=[("kbt", hh)], out=kbt[:CH, isl, :],
                                 in0=kps, in1=sc_b(bg), op=ALU.mult)
                            S.op("dve", "tensor_tensor", reads=[bk(3 + hh), ("bt", d)], writes=[("vbt", hh)],
                                 out=vbt[:, isl, :], in0=vps, in1=sc_b(Bd), op=ALU.mult)
                        yield
                        for i in range(BLK):
                            hh = i // 4
                            o_ = (i % 4) * 128
                            S.mm(Bk[5 + hh][:CH, o_:o_ + 128], t8["X8"][:, i, :], vbt[:, i, :], True, True,
                                 ["X8", ("vbt", hh)], [bk(5 + hh)])
                            S.mm(Bk[7][:, i * CH:(i + 1) * CH], kbt[:, i, :], t8f["X8"][:, i, :], True, True,
                                 ["X8", ("kbt", hh)], [bk(7)])
                        yield
                        for hh in range(2):
                            S.op("act", "activation", reads=[bk(5 + hh)], writes=[("u8", pb, hh)],
                                 out=u8_[:, hh * 4:hh * 4 + 4, :].rearrange("p i e -> p (i e)"), in_=Bk[5 + hh][:CH, :],
                                 func=AF.Copy)
                        S.op("act", "activation", reads=[bk(7)], writes=[("wT8", pb)], out=wT8_[:].rearrange("p i c -> p (i c)"),
                             in_=Bk[7][:], func=AF.Copy)
                        yield

                    def scan(nb, pb, bi_, d=d, b=b):
                        n0 = nb * BLK
                        qd_, at_, kdt_, u8_, wT8_ = qd[pb], attnT[pb], kdt[pb], u8[pb], wT8[pb]
                        ob, obk = o8[bi_ % 2], ("o8", bi_ % 2)
                        order = list(range(BLK)) if d == 0 else list(range(BLK))[::-1]
                        Sc, sk = Sst[0], ("S", 0)
                        vi = 0
                        for i in order:
                            n = n0 + i
                            hh = i // 4
                            vn, vk = vnb[vi % 2], ("vn", vi % 2)
                            vi += 1
                            S.mm(Bk[0][:CH, 0:128], wT8_[:, i, :], Sc[:], True, True, [("wT8", pb), sk], [bk(0)])
                            S.mm(Bk[1][:CH, 0:128], qd_[:, i * CH:(i + 1) * CH], Sc[:], True, True, [("qd", pb), sk], [bk(1)])
                            yield
                            S.op("dve", "tensor_tensor", reads=[bk(0), ("u8", pb, hh)], writes=[vk], out=vn[:CH, :],
                                 in0=u8_[:, i, :], in1=Bk[0][:CH, 0:128], op=ALU.subtract)
                            S.op("act", "activation", reads=["egt"], writes=[sk], out=Sc[:], in_=Sc[:], func=AF.Copy,
                                 scale=egt[:, n:n + 1])
                            yield
                            S.mm(Bk[2][:, 0:128], kdt_[:, i, :], vn[:], True, True, [("kdt", pb, hh), vk], [bk(2)])
                            S.mm(Bk[1][:CH, 128:256], at_[:, i, :], vn[:], True, True, [("attnT", pb), vk], [bk(1)])
                            yield
                            S.op("dve", "tensor_tensor", reads=[bk(2)], writes=[sk], out=Sc[:], in0=Sc[:], in1=Bk[2][:, 0:128],
                                 op=ALU.add)
                            S.op("act", "activation", reads=[bk(1)], writes=[obk], out=ob[:, i, :], in_=Bk[1][:CH, 0:128],
                                 func=AF.Copy)
                            S.op("dve", "tensor_tensor", reads=[bk(1), obk], writes=[obk], out=ob[:, i, :], in0=ob[:, i, :],
                                 in1=Bk[1][:CH, 128:256], op=ALU.add)
                            yield
                        S.dma("sp", reads=[obk], writes=[("s_o", d, b, nb)], out=s_o[d, :, b, n0:n0 + BLK, :], in_=ob[:])

                    for _ in prep(blocks[0], 0):
                        pass
                    for bi_, nb in enumerate(blocks):
                        gens = [scan(nb, bi_ % 2, bi_)]
                        if bi_ + 1 < len(blocks):
                            gens.append(prep(blocks[bi_ + 1], (bi_ + 1) % 2))
                        interleave(gens)
        with C.scope():
            gnr = C.sb("gnr", [CH, 128], F32)
            S.dma("sp", writes=["gnr"], out=gnr[:], in_=gn)
            NB3 = 3
            of = [C.sb("of%d" % i, [CH, BLK, 128], F32) for i in range(NB3)]
            obb = [C.sb("ob%d" % i, [CH, BLK, 128], F32) for i in range(NB3)]
            zz = [C.sb("zz%d" % i, [CH, BLK, 128], F32) for i in range(NB3)]
            sq8 = [C.sb("sq8%d" % i, [CH, BLK, 128], F32) for i in range(NB3)]
            ss = [C.sb("ss%d" % i, [CH, BLK], F32) for i in range(NB3)]

            def out_block(it, b, nb):
                p = it % NB3
                n0 = nb * BLK
                S.dma("sp", reads=[("s_o", 0, b, nb)], writes=[("of", p)], out=of[p][:], in_=s_o[0, :, b, n0:n0 + BLK, :])
                S.dma("sp", reads=[("s_o", 1, b, nb)], writes=[("ob", p)], out=obb[p][:], in_=s_o[1, :, b, n0:n0 + BLK, :])
                S.dma("sp", writes=[("zz", p)], out=zz[p][:], in_=ztok[:, b, n0:n0 + BLK, :])
                yield
                S.op("dve", "tensor_tensor", reads=[("of", p), ("ob", p)], writes=[("of", p)], out=of[p][:], in0=of[p][:],
                     in1=obb[p][:], op=ALU.add)
                S.op("act", "activation", reads=[("zz", p)], writes=[("zz", p)], out=zz[p][:], in_=zz[p][:], func=AF.Silu)
                yield
                S.op("act", "activation", reads=[("of", p)], writes=[("sq8", p)], out=sq8[p][:], in_=of[p][:],
                     func=AF.Square)
                yield
                S.op("dve", "tensor_reduce", reads=[("sq8", p)], writes=[("ss", p)], out=ss[p][:], in_=sq8[p][:],
                     axis=AX.X, op=ALU.add)
                S.op("dve", "tensor_scalar", reads=[("ss", p)], writes=[("ss", p)], out=ss[p][:], in0=ss[p][:],
                     scalar1=1.0 / 128.0, scalar2=EPS, op0=ALU.mult, op1=ALU.add)
                yield
                S.op("act", "activation", reads=[("ss", p)], writes=[("ss", p)], out=ss[p][:], in_=ss[p][:], func=AF.Sqrt)
                yield
                S.op("dve", "reciprocal", reads=[("ss", p)], writes=[("ss", p)], out=ss[p][:], in_=ss[p][:])
                S.op("dve", "tensor_tensor", reads=[("of", p), ("ss", p)], writes=[("of", p)], out=of[p][:], in0=of[p][:],
                     in1=ss[p][:].unsqueeze(2).to_broadcast([CH, BLK, 128]), op=ALU.mult)
                S.op("dve", "tensor_tensor", reads=[("of", p), "gnr"], writes=[("of", p)], out=of[p][:], in0=of[p][:],
                     in1=gnr[:].unsqueeze(1).to_broadcast([CH, BLK, 128]), op=ALU.mult)
                S.op("dve", "tensor_tensor", reads=[("of", p), ("zz", p)], writes=[("of", p)], out=of[p][:], in0=of[p][:],
                     in1=zz[p][:], op=ALU.mult)
                S.dma("sp", reads=[("of", p)], is_out=True, out=ytok[:, b, n0:n0 + BLK, :], in_=of[p][:])
                yield

            pipeline((out_block(b * (N // BLK) + nb, b, nb) for b in range(BATCH) for nb in range(N // BLK)), NB3)
        S.replay()
    return nc


_PROGS = {}


def _prog(key, fn):
    if key not in _PROGS:
        _PROGS[key] = fn()
    return _PROGS[key]


def _run(nc, in_maps):
    res = run_bass_kernel_spmd(nc, in_maps, core_ids=list(range(NCORES)))
    return res.results


def _c(a):
    return np.ascontiguousarray(a, dtype=np.float32)


def _tok_shards_T(xf):
    return [_c(xf[c * TPC:(c + 1) * TPC].T) for c in range(NCORES)]


def _from_T(outs, name):
    return np.concatenate([r[name].T for r in outs], 0)


def _run_sc_layer(xf, p, li, j, final):
    nc = _prog(("sc", final), lambda: build_sc_prog(TPC, final))
    x3 = xf.reshape(BATCH, SEQ, D)
    zero = np.zeros((1, D), np.float32)
    in_maps = []
    for c in range(NCORES):
        b, s0 = divmod(c * TPC, SEQ)
        left = x3[b, s0 - 1:s0] if s0 > 0 else zero
        right = x3[b, s0 + TPC:s0 + TPC + 1] if s0 + TPC < SEQ else zero
        xs = np.concatenate([x3[b, s0:s0 + TPC], left, right], 0)
        in_maps.append({"xT": _c(xs.T), "nrm": _c(p["norms"][li]), "fw_in": _c(p["ffn_w_in"][li]),
                        "fw_out": _c(p["ffn_w_out"][li]), "w_in": _c(p["sc_w_in"][j]), "conv": _c(p["sc_conv"][j]),
                        "w_out": _c(p["sc_w_out"][j]), "gfin": _c(p["final_norm"])})
    return _from_T(_run(nc, in_maps), "yT")


def _run_pre(xf, p, li, w_in, b_in):
    nout = w_in.shape[1]
    nc = _prog(("pre", nout), lambda: build_pre_prog(nout, TPC))
    xs = _tok_shards_T(xf)
    in_maps = [{"xT": xs[c], "nrm": _c(p["norms"][li, 0:2]), "fw_in": _c(p["ffn_w_in"][li, 0]),
                "fw_out": _c(p["ffn_w_out"][li, 0]), "w_in": _c(w_in), "b_in": _c(b_in)} for c in range(NCORES)]
    outs = _run(nc, in_maps)
    x1 = _from_T(outs, "xo")
    u = np.concatenate([r["uT"] for r in outs], 1)
    return x1, u


def _run_post(xf, yfm, p, li, w_out, b_out):
    nc = _prog(("post",), lambda: build_post_prog(TPC))
    xs = _tok_shards_T(xf)
    in_maps = [{"xT": xs[c], "yT": _c(yfm[:, c * TPC:(c + 1) * TPC]), "w_out": _c(w_out), "b_out": _c(b_out),
                "nrm": _c(p["norms"][li, 2]), "fw_in": _c(p["ffn_w_in"][li, 1]), "fw_out": _c(p["ffn_w_out"][li, 1])}
               for c in range(NCORES)]
    return _from_T(_run(nc, in_maps), "xo")


def _run_hyena_core(u, p, j):
    nc = _prog(("hy",), build_hy_core_prog)
    cst, zT, trow, nad = hyena_consts()
    trow_rep = _c(np.broadcast_to(trow, (128, NFFT)))
    u4 = u.reshape(3, D, BATCH, SEQ)
    hc = p["hy_conv"][j].reshape(3, 3, D)
    cb = p["hy_conv_b"][j].reshape(3, D)
    w3 = p["hy_f_w3"][j].reshape(HY_ORD, 2, D)
    in_maps = []
    for c in range(NCORES):
        sl = slice(c * 128, (c + 1) * 128)
        in_maps.append({"u0": _c(u4[:, sl]), "convw": _c(hc[:, :, sl]), "convb": _c(cb[:, sl]), "dvec": _c(p["hy_d"][j][sl]),
                        "fw1": _c(p["hy_f_w1"][j]), "fb1": _c(p["hy_f_b1"][j]), "fw2": _c(p["hy_f_w2"][j]),
                        "fb2": _c(p["hy_f_b2"][j]), "fw3": _c(w3[:, :, sl]), "freq": _c(p["hy_f_freq"][j]), "cst": cst,
                        "zT": zT, "trow": trow_rep, "nad": _c(nad[sl])})
    outs = _run(nc, in_maps)
    return np.concatenate([r["yT"].reshape(128, BATCH * SEQ) for r in outs], 0)


def _run_gdn_core(u, p, j):
    nc = _prog(("gd",), build_gd_core_prog)
    mtri, mstrict, ident = gdn_consts()
    H = 8
    gcv = p["gd_conv"][j].reshape(3, 3, D)
    in_maps = []
    for h in range(NCORES):
        sl = slice(h * 128, (h + 1) * 128)
        qkv0 = u[0:3 * D].reshape(3, D, BATCH, SEQ)[:, sl]
        zfm = u[3 * D + h * 128: 3 * D + (h + 1) * 128]
        ztok = zfm.T.reshape(BATCH, NCH, CH, 128).transpose(2, 0, 1, 3)
        rows = [4 * D + 0 * H + h, 4 * D + 1 * H + h, 4 * D + 2 * H + 0 * H + h, 4 * D + 2 * H + 1 * H + h]
        abt = u[rows].reshape(4, BATCH, NCH, CH).transpose(0, 3, 1, 2)
        in_maps.append({"qkv0": _c(qkv0), "ztok": _c(ztok), "abt": _c(abt), "convw": _c(gcv[:, :, sl]),
                        "alog": _c(np.broadcast_to(p["gd_a_log"][j][:, h], (CH, 2))),
                        "dtb": _c(np.broadcast_to(p["gd_dt_bias"][j][:, h], (CH, 2))),
                        "gn": _c(np.broadcast_to(p["gd_norm"][j], (CH, 128))), "mtri": mtri, "mstrict": mstrict,
                        "ident": ident})
    outs = _run(nc, in_maps)
    return np.concatenate([r["ytok"].transpose(3, 1, 2, 0).reshape(128, BATCH * SEQ) for r in outs], 0)


def kernel(**inputs):
    p = {k: np.asarray(v, dtype=np.float32) for k, v in inputs.items()}
    xf = p["x"].reshape(BATCH * SEQ, D)
    xf = _run_sc_layer(xf, p, 0, 0, final=False)
    xf, u = _run_pre(xf, p, 1, p["hy_w_in"][0], p["hy_b_in"][0])
    yfm = _run_hyena_core(u, p, 0)
    xf = _run_post(xf, yfm, p, 1, p["hy_w_out"][0], p["hy_b_out"][0])
    nproj = p["gd_w_in"].shape[2]
    npad = ((nproj + 127) // 128) * 128
    w_in = np.zeros((D, npad), np.float32)
    w_in[:, :nproj] = p["gd_w_in"][0]
    xf, u = _run_pre(xf, p, 2, w_in, np.zeros((npad,), np.float32))
    yfm = _run_gdn_core(u, p, 0)
    xf = _run_post(xf, yfm, p, 2, p["gd_w_out"][0], np.zeros((D,), np.float32))
    xf = _run_sc_layer(xf, p, 3, 1, final=True)
    return np.ascontiguousarray(xf.reshape(BATCH, SEQ, D).astype(np.float32))
```

```python
import contextlib
import math
import numpy as np
import concourse.bass as bass
import concourse.mybir as mybir
from concourse.bass_utils import run_bass_kernel_spmd

F32 = mybir.dt.float32
BF16 = mybir.dt.bfloat16
AF = mybir.ActivationFunctionType
ALU = mybir.AluOpType
AX = mybir.AxisListType

D = 1024
KC = 8
FF = 2816
JC = 22
NCORES = 8
BATCH = 2
SEQ = 8192
TPC = BATCH * SEQ // NCORES
EPS = 1e-6

ENGS = ("pe", "act", "dve", "pool", "sp")
NDMA_SEM = 20


class Sched:
    def __init__(self, nc, es):
        self.nc = nc
        self.q = {e: [] for e in ENGS}
        self.cnt = {e: 0 for e in ENGS}
        self.seen = {e: {} for e in ENGS}
        self.buf = {}
        self.sems = {}
        for e in ENGS:
            self.sems[("E", e)] = es.enter_context(nc.semaphore("sem_" + e))
        self.dma_rr = {e: 0 for e in ENGS}
        self.dma_val = {}
        for e in ("sp", "pool", "act"):
            for i in range(NDMA_SEM):
                k = ("D", e, i)
                self.sems[k] = es.enter_context(nc.semaphore("dsem_%s_%d" % (e, i)))
                self.dma_val[k] = 0
        self.out_tokens = []
        self.excl = set()

    def _deps(self, eng, reads, writes):
        deps = {}

        def add(tok):
            if tok is None:
                return
            k, v = tok
            if deps.get(k, 0) < v:
                deps[k] = v

        for k in reads:
            b = self.buf.get(k)
            if b:
                add(b["w"])
                if k in self.excl:
                    for rk, rv in b["r"].items():
                        if rk != ("E", eng):
                            add((rk, rv))
        for k in writes:
            b = self.buf.get(k)
            if b:
                add(b["w"])
                for rk, rv in b["r"].items():
                    add((rk, rv))
        waits = []
        for k, v in deps.items():
            if eng == "pe" and k == ("E", "pe"):
                continue
            if self.seen[eng].get(k, 0) >= v:
                continue
            self.seen[eng][k] = v
            waits.append((k, v))
        return waits

    def _record(self, tok, reads, writes):
        for k in reads:
            b = self.buf.setdefault(k, {"w": None, "r": {}})
            if b["r"].get(tok[0], 0) < tok[1]:
                b["r"][tok[0]] = tok[1]
        for k in writes:
            self.buf[k] = {"w": tok, "r": {}}

    def op(self, eng, name, reads=(), writes=(), **kw):
        fn = (name, kw)
        waits = self._deps(eng, reads, writes)
        self.cnt[eng] += 1
        tok = (("E", eng), self.cnt[eng])
        self.q[eng].append((waits, fn, tok, 1))
        self._record(tok, reads, writes)
        return tok

    def dma(self, eng, reads=(), writes=(), is_out=False, **kw):
        fn = ("dma_start", kw)
        waits = self._deps(eng, reads, writes)
        i = self.dma_rr[eng]
        self.dma_rr[eng] = (i + 1) % NDMA_SEM
        k = ("D", eng, i)
        prev = self.dma_val[k]
        if prev and self.seen[eng].get(k, 0) < prev:
            self.seen[eng][k] = prev
            waits.append((k, prev))
        self.dma_val[k] = prev + 16
        tok = (k, prev + 16)
        self.q[eng].append((waits, fn, tok, 16))
        self._record(tok, reads, writes)
        if is_out:
            self.out_tokens.append(tok)
        return tok

    def barrier(self):
        allv = [(("E", f), self.cnt[f]) for f in ENGS if self.cnt[f]]
        allv += [(k, v) for k, v in self.dma_val.items() if v]
        for e in ENGS:
            waits = []
            for k, v in allv:
                if k == ("E", e) and e == "pe":
                    continue
                if self.seen[e].get(k, 0) >= v:
                    continue
                self.seen[e][k] = v
                waits.append((k, v))
            if waits:
                self.q[e].append((waits, None, None, 0))
        self.buf = {}

    def mm(self, out, lhsT, rhs, start, stop, reads, writes):
        return self.op("pe", "matmul", reads, writes, out=out, lhsT=lhsT, rhs=rhs, start=start, stop=stop)

    def replay(self):
        nc = self.nc
        fin = list(self.out_tokens)
        with nc.Block() as block:
            def run(engname, eng):
                for waits, fn, tok, inc in self.q[engname]:
                    for k, v in waits:
                        eng.wait_ge(self.sems[k], v)
                    if fn is None:
                        continue
                    ins = getattr(eng, fn[0])(**fn[1])
                    ins.then_inc(self.sems[tok[0]], inc)
                if engname == "sp":
                    for k, v in fin:
                        eng.wait_ge(self.sems[k], v)

            @block.tensor
            def _(e):
                run("pe", e)

            @block.scalar
            def _(e):
                run("act", e)

            @block.vector
            def _(e):
                run("dve", e)

            @block.gpsimd
            def _(e):
                run("pool", e)

            @block.sync
            def _(e):
                run("sp", e)


class Ctx:
    def __init__(self, nc, es):
        self.nc = nc
        self.es = es
        self.S = Sched(nc, es)
        self.n = 0
        self.scopes = [es]

    def sb(self, name, shape, dt):
        self.n += 1
        return self.scopes[-1].enter_context(self.nc.sbuf_tensor("%s_%d" % (name, self.n), shape, dt))

    def ps(self, name, shape, dt=F32):
        self.n += 1
        return self.scopes[-1].enter_context(self.nc.psum_tensor("%s_%d" % (name, self.n), shape, dt))

    @contextlib.contextmanager
    def scope(self):
        with contextlib.ExitStack() as s:
            self.scopes.append(s)
            try:
                yield
            finally:
                self.S.barrier()
                self.scopes.pop()


def interleave(gens):
    gens = list(gens)
    while gens:
        for g_ in list(gens):
            try:
                next(g_)
            except StopIteration:
                gens.remove(g_)


def pipeline(gens, width):
    gens = iter(gens)
    active = []
    done = False
    while True:
        while not done and len(active) < width:
            try:
                active.append(next(gens))
            except StopIteration:
                done = True
        if not active:
            return
        for g_ in list(active):
            try:
                next(g_)
            except StopIteration:
                active.remove(g_)


def dram_in(nc, name, shape, dt=F32):
    return nc.dram_tensor(name, list(shape), dt, kind="ExternalInput").ap()


def dram_out(nc, name, shape, dt=F32):
    return nc.dram_tensor(name, list(shape), dt, kind="ExternalOutput").ap()


def emit_consts(C):
    ones = C.sb("ones", [128, 128], F32)
    C.S.op("dve", "memset", writes=["ones"], ap=ones[:], constant=1.0)
    C.ones = ones
    C.rn_sq = [C.sb("rn_sq%d" % i, [128, 512], F32) for i in range(3)]
    C.rn_rs = [C.sb("rn_rs%d" % i, [128, 512], F32) for i in range(2)]
    C.rn_ps = [C.ps("rn_ps%d" % i, [128, 512], F32) for i in range(1)]
    C.rn_i = 0


def load_vec_pk(C, name, vec_dram, nchunk, eng="sp"):
    t = C.sb(name, [128, nchunk], F32)
    C.S.dma(eng, writes=[name], out=t[:], in_=vec_dram.rearrange("(kc p) -> p kc", p=128),
            allow_slow_non_contiguous=True)
    return t


def emit_rmsnorm(C, x, xk, t0, ntok, g_sb, gk, hn, hk, hoff=0):
    S = C.S
    nt = (ntok + 511) // 512
    for tt in range(nt):
        n = min(512, ntok - tt * 512)
        c0 = t0 + tt * 512
        xkeys = [(xk, k, c0 // 512) for k in range(KC)]
        if (c0 % 512) + n > 512:
            xkeys += [(xk, k, c0 // 512 + 1) for k in range(KC)]
        ps = C.rn_ps[0]
        for k in range(KC):
            C.rn_i += 1
            sq = C.rn_sq[C.rn_i % 3]
            sqk = ("rn_sq", C.rn_i % 3)
            S.op("act", "activation", reads=[kk for kk in xkeys if kk[1] == k], writes=[sqk],
                 out=sq[:, :n], in_=x[:, k, c0:c0 + n], func=AF.Square)
            S.mm(ps[:, :n], C.ones[:], sq[:, :n], k == 0, k == KC - 1, ["ones", sqk], ["rn_ps"])
        C.rn_i += 1
        rs = C.rn_rs[C.rn_i % 2]
        rsk = ("rn_rs", C.rn_i % 2)
        S.op("dve", "tensor_scalar", reads=["rn_ps"], writes=[rsk], out=rs[:, :n], in0=ps[:, :n],
             scalar1=1.0 / D, scalar2=EPS, op0=ALU.mult, op1=ALU.add)
        S.op("act", "activation", reads=[rsk], writes=[rsk], out=rs[:, :n], in_=rs[:, :n], func=AF.Sqrt)
        S.op("dve", "reciprocal", reads=[rsk], writes=[rsk], out=rs[:, :n], in_=rs[:, :n])
        for k in range(KC):
            eng = "dve"
            S.op(eng, "scalar_tensor_tensor", reads=[kk for kk in xkeys if kk[1] == k] + [rsk, gk],
                 writes=[(hk, k, tt)], out=hn[:, k, hoff + tt * 512: hoff + tt * 512 + n], in0=x[:, k, c0:c0 + n],
                 scalar=g_sb[:, k:k + 1], in1=rs[:, :n], op0=ALU.mult, op1=ALU.mult)


def emit_ffn(C, x, groups, g_dram, w_in, w_out, pref):
    S = C.S
    TG = 1024
    w_in_v = w_in.rearrange("(kc p) n -> p kc n", p=128)
    w_out_v = w_out.rearrange("(jc p) n -> p jc n", p=128)
    with C.scope():
        g_sb = load_vec_pk(C, pref + "g", g_dram, KC)
        gk = pref + "g"
        hn = C.sb("ffn_hn", [128, KC, TG], BF16)
        act = C.sb("ffn_act", [128, JC, TG], BF16)
        wbuf = [C.sb("ffn_wi%d" % i, [128, KC, 256], BF16) for i in range(3)]
        wobuf = [C.sb("ffn_wo%d" % i, [128, JC, 128], BF16) for i in range(2)]
        sgb = [C.sb("ffn_sg%d" % i, [128, 512], F32) for i in range(2)]
        pg = [C.ps("ffn_pg%d" % i, [128, 512]) for i in range(2)]
        pu = [C.ps("ffn_pu%d" % i, [128, 512]) for i in range(2)]
        po = [C.ps("ffn_po%d" % i, [128, 512]) for i in range(2)]
        it = 0
        io = 0
        for (t0, ntok) in groups:
            ntg = (ntok + 511) // 512
            emit_rmsnorm(C, x, "x", t0, ntok, g_sb, gk, hn, "ffn_hn")
            for j in range(JC):
                wb = wbuf[j % 3]
                wk = ("ffn_wi", j % 3)
                S.dma("pool", writes=[wk + (0,)], out=wb[:, :, 0:128], in_=w_in_v[:, :, j * 128:(j + 1) * 128])
                S.dma("pool", writes=[wk + (1,)], out=wb[:, :, 128:256],
                      in_=w_in_v[:, :, FF + j * 128: FF + (j + 1) * 128])
                for tt in range(ntg):
                    n = min(512, ntok - tt * 512)
                    it += 1
                    b = it % 2
                    sl = slice(tt * 512, tt * 512 + n)
                    for k in range(KC):
                        S.mm(pg[b][:, :n], wb[:, k, 0:128], hn[:, k, sl], k == 0, k == KC - 1,
                             [wk + (0,), ("ffn_hn", k, tt)], [("ffn_pg", b)])
                    for k in range(KC):
                        S.mm(pu[b][:, :n], wb[:, k, 128:256], hn[:, k, sl], k == 0, k == KC - 1,
                             [wk + (1,), ("ffn_hn", k, tt)], [("ffn_pu", b)])
                    S.op("act", "activation", reads=[("ffn_pg", b)], writes=[("ffn_sg", b)],
                         out=sgb[b][:, :n], in_=pg[b][:, :n], func=AF.Silu)
                    S.op("dve", "tensor_tensor", reads=[("ffn_pu", b), ("ffn_sg", b)], writes=[("ffn_act", j, tt)],
                         out=act[:, j, sl], in0=pu[b][:, :n], in1=sgb[b][:, :n], op=ALU.mult)
            for m in range(KC):
                wo = wobuf[m % 2]
                wok = ("ffn_wo", m % 2)
                S.dma("pool", writes=[wok], out=wo[:], in_=w_out_v[:, :, m * 128:(m + 1) * 128])
                for tt in range(ntg):
                    n = min(512, ntok - tt * 512)
                    io += 1
                    b = io % 2
                    sl = slice(tt * 512, tt * 512 + n)
                    gsl = slice(t0 + tt * 512, t0 + tt * 512 + n)
                    for j in range(JC):
                        S.mm(po[b][:, :n], wo[:, j, :], act[:, j, sl], j == 0, j == JC - 1,
                             [wok, ("ffn_act", j, tt)], [("ffn_po", b)])
                    xkey = ("x", m, (t0 + tt * 512) // 512)
                    S.op("dve", "scalar_tensor_tensor", reads=[("ffn_po", b), xkey], writes=[xkey],
                         out=x[:, m, gsl], in0=po[b][:, :n], scalar=0.5, in1=x[:, m, gsl], op0=ALU.mult, op1=ALU.add)


def emit_sc_mixer(C, x, T, g_dram, w_in, conv, w_out):
    S = C.S
    NT = T // 512
    w_in_v = w_in.rearrange("(kc p) n -> p kc n", p=128)
    w_out_v = w_out.rearrange("(kc p) n -> p kc n", p=128)
    with C.scope():
        g_sb = load_vec_pk(C, "sc_g", g_dram, KC)
        cw = C.sb("sc_cw", [128, KC, 3], F32)
        for j in range(3):
            S.dma("sp", writes=[("sc_cw", j)], out=cw[:, :, j], in_=conv[j].rearrange("(i p) -> p i", p=128),
                  allow_slow_non_contiguous=True)
        hn = C.sb("sc_hn", [128, KC, T + 2], BF16)
        ybf = C.sb("sc_y", [128, KC, T], BF16)
        chb = [C.sb("sc_ch%d" % i, [128, T + 2], F32) for i in range(2)]
        bsv = [C.sb("sc_b%d" % i, [128, T], F32) for i in range(2)]
        csb = [C.sb("sc_c%d" % i, [128, 512], F32) for i in range(2)]
        acc = [C.sb("sc_acc%d" % i, [128, 512], F32) for i in range(2)]
        wbuf = [C.sb("sc_wi%d" % i, [128, KC, 384], BF16) for i in range(2)]
        wobuf = [C.sb("sc_wo%d" % i, [128, KC, 128], BF16) for i in range(2)]
        pb = [C.ps("sc_pb%d" % i, [128, 512]) for i in range(2)]
        pc = [C.ps("sc_pc%d" % i, [128, 512]) for i in range(2)]
        ph = [C.ps("sc_ph%d" % i, [128, 512]) for i in range(2)]
        po = [C.ps("sc_po%d" % i, [128, 512]) for i in range(1)]
        emit_rmsnorm(C, x, "x", 0, T + 2, g_sb, "sc_g", hn, "sc_hn")
        it = 0
        for i in range(KC):
            wb = wbuf[i % 2]
            wk = ("sc_wi", i % 2)
            for q in range(3):
                S.dma("pool", writes=[wk + (q,)], out=wb[:, :, q * 128:(q + 1) * 128],
                      in_=w_in_v[:, :, q * D + i * 128: q * D + (i + 1) * 128])
            ch = chb[i % 2]
            bs = bsv[i % 2]
            for tt in range(NT + 1):
                n = 512 if tt < NT else 2
                it += 1
                b = it % 2
                sl = slice(tt * 512, tt * 512 + n)
                hkeys = lambda k: [("sc_hn", k, tt)]
                if tt < NT:
                    for k in range(KC):
                        S.mm(pb[b][:, :n], wb[:, k, 0:128], hn[:, k, sl], k == 0, k == KC - 1,
                             [wk + (0,)] + hkeys(k), [("sc_pb", b)])
                for k in range(KC):
                    S.mm(pc[b][:, :n], wb[:, k, 128:256], hn[:, k, sl], k == 0, k == KC - 1,
                         [wk + (1,)] + hkeys(k), [("sc_pc", b)])
                for k in range(KC):
                    S.mm(ph[b][:, :n], wb[:, k, 256:384], hn[:, k, sl], k == 0, k == KC - 1,
                         [wk + (2,)] + hkeys(k), [("sc_ph", b)])
                S.op("act", "activation", reads=[("sc_pc", b)], writes=[("sc_c", b)],
                     out=csb[b][:, :n], in_=pc[b][:, :n], func=AF.Copy)
                if tt < NT:
                    S.op("dve", "tensor_tensor", reads=[("sc_c", b), ("sc_ph", b)], writes=[("sc_ch", i % 2, tt)],
                         out=ch[:, 1 + tt * 512: 1 + tt * 512 + n], in0=ph[b][:, :n], in1=csb[b][:, :n], op=ALU.mult)
                    S.op("act", "activation", reads=[("sc_pb", b)], writes=[("sc_b", i % 2, tt)],
                         out=bs[:, sl], in_=pb[b][:, :n], func=AF.Copy)
                else:
                    S.op("dve", "tensor_tensor", reads=[("sc_c", b), ("sc_ph", b)], writes=[("sc_ch", i % 2, "hl")],
                         out=ch[:, 0:1], in0=ph[b][:, 0:1], in1=csb[b][:, 0:1], op=ALU.mult)
                    S.op("dve", "tensor_tensor", reads=[("sc_c", b), ("sc_ph", b)], writes=[("sc_ch", i % 2, "hr")],
                         out=ch[:, T + 1:T + 2], in0=ph[b][:, 1:2], in1=csb[b][:, 1:2], op=ALU.mult)
            for tt in range(NT):
                a = acc[tt % 2]
                ak = ("sc_acc", tt % 2)
                rk = [("sc_ch", i % 2, tt)]
                if tt > 0:
                    rk.append(("sc_ch", i % 2, tt - 1))
                else:
                    rk.append(("sc_ch", i % 2, "hl"))
                if tt < NT - 1:
                    rk.append(("sc_ch", i % 2, tt + 1))
                else:
                    rk.append(("sc_ch", i % 2, "hr"))
                o = tt * 512
                S.op("dve", "tensor_scalar", reads=rk + [("sc_cw", 0)], writes=[ak], out=a[:], in0=ch[:, o:o + 512],
                     scalar1=cw[:, i, 0:1], scalar2=None, op0=ALU.mult)
                S.op("dve", "scalar_tensor_tensor", reads=rk + [("sc_cw", 1), ak], writes=[ak], out=a[:],
                     in0=ch[:, o + 1:o + 513], scalar=cw[:, i, 1:2], in1=a[:], op0=ALU.mult, op1=ALU.add)
                S.op("dve", "scalar_tensor_tensor", reads=rk + [("sc_cw", 2), ak], writes=[ak], out=a[:],
                     in0=ch[:, o + 2:o + 514], scalar=cw[:, i, 2:3], in1=a[:], op0=ALU.mult, op1=ALU.add)
                S.op("dve", "tensor_tensor", reads=[ak, ("sc_b", i % 2, tt)], writes=[("sc_y", i, tt)],
                     out=ybf[:, i, o:o + 512], in0=a[:], in1=bs[:, o:o + 512], op=ALU.mult)
        for m in range(KC):
            wo = wobuf[m % 2]
            wok = ("sc_wo", m % 2)
            S.dma("pool", writes=[wok], out=wo[:], in_=w_out_v[:, :, m * 128:(m + 1) * 128])
            for tt in range(NT):
                sl = slice(tt * 512, (tt + 1) * 512)
                for i in range(KC):
                    S.mm(po[0][:], wo[:, i, :], ybf[:, i, sl], i == 0, i == KC - 1, [wok, ("sc_y", i, tt)], ["sc_po"])
                xkey = ("x", m, tt)
                S.op("dve", "tensor_tensor", reads=["sc_po", xkey], writes=[xkey], out=x[:, m, sl], in0=po[0][:],
                     in1=x[:, m, sl], op=ALU.add)


def emit_final_norm(C, x, T, g_dram):
    with C.scope():
        g_sb = load_vec_pk(C, "fin_g", g_dram, KC)
        emit_rmsnorm(C, x, "x", 0, T, g_sb, "fin_g", x, "x")


def emit_load_x(C, x, xT_dram, T):
    v = xT_dram.rearrange("(kc p) t -> p kc t", p=128)
    for k in range(KC):
        for tt in range((T + 511) // 512):
            n = min(512, T - tt * 512)
            C.S.dma("sp", writes=[("x", k, tt)], out=x[:, k, tt * 512:tt * 512 + n],
                    in_=v[:, k, tt * 512:tt * 512 + n])


def emit_store_x(C, x, yT_dram, T):
    v = yT_dram.rearrange("(kc p) t -> p kc t", p=128)
    for k in range(KC):
        for tt in range(T // 512):
            C.S.dma("sp", reads=[("x", k, tt)], is_out=True, out=v[:, k, tt * 512:(tt + 1) * 512],
                    in_=x[:, k, tt * 512:(tt + 1) * 512])


def build_ffn_prog(T=TPC):
    nc = bass.Bass("TRN2", target_bir_lowering=False)
    xT = dram_in(nc, "xT", [D, T])
    g = dram_in(nc, "g", [D])
    w_in = dram_in(nc, "w_in", [D, 2 * FF])
    w_out = dram_in(nc, "w_out", [FF, D])
    yT = dram_out(nc, "yT", [D, T])
    with contextlib.ExitStack() as es:
        C = Ctx(nc, es)
        emit_consts(C)
        x = C.sb("x", [128, KC, T], F32)
        emit_load_x(C, x, xT, T)
        emit_ffn(C, x, [(t, 1024) for t in range(0, T, 1024)], g, w_in, w_out, "f")
        emit_store_x(C, x, yT, T)
        C.S.replay()
    return nc


def build_sc_prog(T=TPC, final=False):
    nc = bass.Bass("TRN2", target_bir_lowering=False)
    xT = dram_in(nc, "xT", [D, T + 2])
    nrm = dram_in(nc, "nrm", [3, D])
    fw_in = dram_in(nc, "fw_in", [2, D, 2 * FF])
    fw_out = dram_in(nc, "fw_out", [2, FF, D])
    w_in = dram_in(nc, "w_in", [D, 3 * D])
    conv = dram_in(nc, "conv", [3, D])
    w_out = dram_in(nc, "w_out", [D, D])
    gfin = dram_in(nc, "gfin", [D])
    yT = dram_out(nc, "yT", [D, T])
    with contextlib.ExitStack() as es:
        C = Ctx(nc, es)
        emit_consts(C)
        x = C.sb("x", [128, KC, T + 2], F32)
        emit_load_x(C, x, xT, T + 2)
        grp = [(t, 1024) for t in range(0, T, 1024)]
        emit_ffn(C, x, grp + [(T, 2)], nrm[0], fw_in[0], fw_out[0], "f1")
        emit_sc_mixer(C, x, T, nrm[1], w_in, conv, w_out)
        emit_ffn(C, x, grp, nrm[2], fw_in[1], fw_out[1], "f2")
        if final:
            emit_final_norm(C, x, T, gfin)
        emit_store_x(C, x, yT, T)
        C.S.replay()
    return nc

NFFT = 2 * SEQ
HY_EMB = 33
HY_ORD = 64
MAGIC = 12582912.0
TWO_PI = 2.0 * math.pi
PI_LO = 3.1415925


def hyena_consts():
    n = np.arange(128, dtype=np.float64)
    ang = 2.0 * np.pi * np.outer(n, n) / 128.0
    fre, fim = np.cos(ang), -np.sin(ang)
    angt = 2.0 * np.pi * np.outer(n, n) / NFFT
    tre, tim = np.cos(angt), -np.sin(angt)
    cst = np.stack([fim, fre, -fim, tre, tim], 1).astype(np.float32)
    L = SEQ
    f32 = np.float32
    t = np.linspace(0.0, 1.0, L, dtype=f32)
    w = (f32(2.0 * math.pi) * np.arange(L, dtype=f32) / f32(L)).astype(f32)
    f = np.linspace(1e-4, 15.0, 16, dtype=f32)
    fw = (f[None, :] * w[:, None]).astype(f32)
    z = np.concatenate([t[:, None], np.cos(fw), -np.sin(fw)], -1).astype(f32)
    idx = np.concatenate([[0], np.arange(L - 1, 0, -1)])
    z2 = z[idx]
    t2 = t[idx].copy()
    t2[0] = 1e30
    zT = np.ascontiguousarray(np.concatenate([z, z2], 0).T)
    trow = np.concatenate([t, t2]).astype(f32)
    dmin = math.log(1e-2) / 1.5
    dmax = math.log(1e-2) / 0.3
    deltas = np.linspace(dmin, dmax, D, dtype=f32)
    nad = (-np.abs(deltas)).astype(f32)
    return cst, zT, trow, nad


def emit_fft_fwd(C, X, xkey, K, nseq, cst, tl, ps, kp=""):
    S = C.S
    fimfre = cst[:K, 0:2, :].rearrange("p a b -> p (a b)")
    for s_ in range(nseq):
        bank = ps["a"][s_ // 2]
        S.mm(bank[:, (s_ % 2) * 256:(s_ % 2) * 256 + 256], X[:K, s_, :], fimfre, True, True,
             [xkey, "cst"], [(kp + "psa", s_ // 2)])
    yield
    tre = cst[:, 3:4, :]
    tim = cst[:, 4:5, :]
    for h in range((nseq + 1) // 2):
        ns = min(2, nseq - 2 * h)
        av = ps["a"][h][:].rearrange("p (s r k) -> p s r k", s=2, r=2)
        aim = av[:, :ns, 0, :]
        are = av[:, :ns, 1, :]
        sl = slice(2 * h, 2 * h + ns)
        bt = lambda t_: t_.to_broadcast([128, ns, 128])
        S.op("dve", "tensor_tensor", reads=[(kp + "psa", h), "cst"], writes=[(kp + "t1", h)], out=tl["t1"][:, sl, :], in0=are,
             in1=bt(tre), op=ALU.mult)
        S.op("dve", "tensor_tensor", reads=[(kp + "psa", h), "cst"], writes=[(kp + "t2", h)], out=tl["t2"][:, sl, :], in0=aim,
             in1=bt(tim), op=ALU.mult)
        S.op("dve", "tensor_tensor", reads=[(kp + "psa", h), "cst"], writes=[(kp + "t3", h)], out=tl["t3"][:, sl, :], in0=are,
             in1=bt(tim), op=ALU.mult)
        S.op("dve", "tensor_tensor", reads=[(kp + "psa", h), "cst"], writes=[(kp + "t4", h)], out=tl["t4"][:, sl, :], in0=aim,
             in1=bt(tre), op=ALU.mult)
    hs = list(range((nseq + 1) // 2))
    S.op("pool", "tensor_tensor", reads=[(kp + "t1", h) for h in hs] + [(kp + "t2", h) for h in hs], writes=[kp + "bre"],
         out=tl["bre"][:, :nseq, :], in0=tl["t1"][:, :nseq, :], in1=tl["t2"][:, :nseq, :], op=ALU.subtract)
    S.op("pool", "tensor_tensor", reads=[(kp + "t3", h) for h in hs] + [(kp + "t4", h) for h in hs], writes=[kp + "bim"],
         out=tl["bim"][:, :nseq, :], in0=tl["t3"][:, :nseq, :], in1=tl["t4"][:, :nseq, :], op=ALU.add)
    yield
    n = nseq * 128
    bre = tl["bre"][:].rearrange("p s k -> p (s k)")[:, :n]
    bim = tl["bim"][:].rearrange("p s k -> p (s k)")[:, :n]
    S.mm(ps["xre"][:, :n], cst[:, 1, :], bre, True, False, ["cst", kp + "bre"], [kp + "psxre"])
    S.mm(ps["xre"][:, :n], cst[:, 2, :], bim, False, True, ["cst", kp + "bim"], [kp + "psxre"])
    S.mm(ps["xim"][:, :n], cst[:, 1, :], bim, True, False, ["cst", kp + "bim"], [kp + "psxim"])
    S.mm(ps["xim"][:, :n], cst[:, 0, :], bre, False, True, ["cst", kp + "bre"], [kp + "psxim"])
    yield


def build_hy_core_prog():
    nc = bass.Bass("TRN2", target_bir_lowering=False)
    L = SEQ
    u0 = dram_in(nc, "u0", [3, 128, BATCH, L])
    convw = dram_in(nc, "convw", [3, 3, 128])
    convb = dram_in(nc, "convb", [3, 128])
    dvec = dram_in(nc, "dvec", [128])
    fw1 = dram_in(nc, "fw1", [HY_EMB, HY_ORD])
    fb1 = dram_in(nc, "fb1", [HY_ORD])
    fw2 = dram_in(nc, "fw2", [HY_ORD, HY_ORD])
    fb2 = dram_in(nc, "fb2", [HY_ORD])
    fw3 = dram_in(nc, "fw3", [HY_ORD, 2, 128])
    freq = dram_in(nc, "freq", [HY_ORD])
    cstd = dram_in(nc, "cst", [128, 5, 128])
    zT = dram_in(nc, "zT", [HY_EMB, NFFT])
    trow = dram_in(nc, "trow", [128, NFFT])
    nad = dram_in(nc, "nad", [128])
    yT = dram_out(nc, "yT", [128, BATCH, L])
    s_h = nc.dram_tensor("s_h", [128, NFFT], F32).ap()
    s_H = nc.dram_tensor("s_H", [2, 128, 128, 128], F32).ap()
    s_vv = nc.dram_tensor("s_vv", [128, BATCH, L], F32).ap()
    s_x0 = nc.dram_tensor("s_x0", [128, BATCH, L], F32).ap()
    s_y = nc.dram_tensor("s_y", [128, BATCH, L], F32).ap()
    with contextlib.ExitStack() as es:
        C = Ctx(nc, es)
        S = C.S
        cst = C.sb("cst", [128, 5, 128], F32)
        S.dma("sp", writes=["cst"], out=cst[:], in_=cstd)

        def col(name, src, n):
            t_ = C.sb(name, [n, 1], F32)
            S.dma("sp", writes=[name], out=t_[:], in_=src.rearrange("(p o) -> p o", o=1))
            return t_

        with C.scope():
            w1 = C.sb("w1", [HY_EMB, HY_ORD], F32)
            S.dma("sp", writes=["w1"], out=w1[:], in_=fw1)
            w2 = C.sb("w2", [HY_ORD, HY_ORD], F32)
            S.dma("sp", writes=["w2"], out=w2[:], in_=fw2)
            w3 = C.sb("w3", [HY_ORD, 2, 128], F32)
            S.dma("sp", writes=["w3"], out=w3[:], in_=fw3)
            fq = col("fq", freq, HY_ORD)
            b1 = col("b1", fb1, HY_ORD)
            b2 = col("b2", fb2, HY_ORD)
            nadc = col("nadc", nad, 128)
            S.op("dve", "tensor_tensor", reads=["fq", "b1"], writes=["b1"], out=b1[:], in0=b1[:], in1=fq[:], op=ALU.mult)
            S.op("dve", "tensor_tensor", reads=["fq", "b2"], writes=["b2"], out=b2[:], in0=b2[:], in1=fq[:], op=ALU.mult)
            zt = [C.sb("zt%d" % i, [HY_EMB, 512], F32) for i in range(2)]
            tr = [C.sb("tr%d" % i, [128, 512], F32) for i in range(2)]
            NS = 2
            av = [[C.sb("av%d%d" % (l_, i), [HY_ORD, 512], F32) for i in range(NS)] for l_ in range(2)]
            qv = [[C.sb("qv%d%d" % (l_, i), [HY_ORD, 512], F32) for i in range(NS)] for l_ in range(2)]
            hv = [[C.sb("hv%d%d" % (l_, i), [HY_ORD, 512], F32) for i in range(NS)] for l_ in range(2)]
            hc = [C.sb("hc%d" % i, [128, 512], F32) for i in range(NS)]
            p1 = [C.ps("p1%d" % i, [HY_ORD, 512]) for i in range(NS)]
            p2 = [C.ps("p2%d" % i, [HY_ORD, 512]) for i in range(NS)]
            p3 = [C.ps("p3%d" % i, [128, 512]) for i in range(NS)]

            def sin_layer(psrc, pkey, bias, sl_, lay):
                a, q, h = av[lay][sl_], qv[lay][sl_], hv[lay][sl_]
                ak, qk, hk = ("av", lay, sl_), ("qv", lay, sl_), ("hv", lay, sl_)
                S.op("dve", "tensor_scalar", reads=[pkey, "fq", "b1", "b2"], writes=[ak], out=a[:], in0=psrc[:],
                     scalar1=fq[:, 0:1], scalar2=bias[:, 0:1], op0=ALU.mult, op1=ALU.add)
                S.op("dve", "tensor_scalar", reads=[ak], writes=[qk], out=q[:], in0=a[:], scalar1=1.0 / TWO_PI,
                     scalar2=MAGIC, op0=ALU.mult, op1=ALU.add)
                S.op("dve", "tensor_scalar", reads=[qk], writes=[qk], out=q[:], in0=q[:], scalar1=-MAGIC,
                     scalar2=-TWO_PI, op0=ALU.add, op1=ALU.mult)
                S.op("dve", "tensor_tensor", reads=[qk, ak], writes=[ak], out=a[:], in0=a[:], in1=q[:], op=ALU.add)
                S.op("dve", "tensor_scalar", reads=[ak], writes=[ak], out=a[:], in0=a[:], scalar1=-PI_LO,
                     scalar2=PI_LO, op0=ALU.max, op1=ALU.min)
                yield
                S.op("act", "activation", reads=[ak], writes=[hk], out=h[:], in_=a[:], func=AF.Sin)
                yield

            def p0_tile(ti):
                c0 = ti * 512
                sl_ = ti % NS
                z_, zk = zt[sl_], ("zt", sl_)
                t_, tk = tr[sl_], ("tr", sl_)
                S.dma("sp", writes=[zk], out=z_[:], in_=zT[:, c0:c0 + 512])
                S.dma("sp", writes=[tk], out=t_[:], in_=trow[:, c0:c0 + 512])
                S.mm(p1[sl_][:], w1[:], z_[:], True, True, ["w1", zk], [("p1", sl_)])
                yield
                for _ in sin_layer(p1[sl_], ("p1", sl_), b1, sl_, 0):
                    yield
                S.mm(p2[sl_][:], w2[:], hv[0][sl_][:], True, True, ["w2", ("hv", 0, sl_)], [("p2", sl_)])
                S.op("act", "activation", reads=[tk, "nadc"], writes=[tk], out=t_[:], in_=t_[:], func=AF.Exp,
                     scale=nadc[:, 0:1])
                yield
                for _ in sin_layer(p2[sl_], ("p2", sl_), b2, sl_, 1):
                    yield
                half = 0 if ti < (L // 512) else 1
                S.mm(p3[sl_][:], w3[:, half, :], hv[1][sl_][:], True, True, ["w3", ("hv", 1, sl_)], [("p3", sl_)])
                yield
                o_, ok = hc[sl_], ("hc", sl_)
                S.op("dve", "tensor_tensor", reads=[("p3", sl_), tk], writes=[ok], out=o_[:], in0=p3[sl_][:], in1=t_[:],
                     op=ALU.mult)
                S.dma("sp", reads=[ok], writes=[("s_h", ti)], out=s_h[:, c0:c0 + 512], in_=o_[:])
                yield

            pipeline((p0_tile(ti) for ti in range(NFFT // 512)), NS)

        with C.scope():
            cw = C.sb("hcw", [128, 3, 3], F32)
            for j in range(3):
                for gi in range(3):
                    S.dma("sp", writes=[("hcw", j, gi)], out=cw[:, gi, j:j + 1],
                          in_=convw[j, gi].rearrange("(p o) -> p o", o=1))
            cb = C.sb("hcb", [128, 3], F32)
            for gi in range(3):
                S.dma("sp", writes=[("hcb", gi)], out=cb[:, gi:gi + 1], in_=convb[gi].rearrange("(p o) -> p o", o=1))
            cwk = [("hcw", j, gi) for j in range(3) for gi in range(3)] + [("hcb", gi) for gi in range(3)]
            ub = [C.sb("hu%d" % i, [128, L + 2], F32) for i in range(2)] * 2
            uc = [C.sb("huc%d" % i, [128, L], F32) for i in range(3)]
            for b in range(BATCH):
                for gi in range(3):
                    S.op("pool", "memset", writes=[("hu", gi % 2, "e")], ap=ub[gi][:, 0:1], constant=0.0)
                    S.op("pool", "memset", writes=[("hu", gi % 2, "e2")], ap=ub[gi][:, L + 1:L + 2], constant=0.0)
                    S.dma("sp", writes=[("hu", gi % 2)], out=ub[gi][:, 1:L + 1], in_=u0[gi, :, b, :])
                    rk = [("hu", gi % 2), ("hu", gi % 2, "e"), ("hu", gi % 2, "e2")] + cwk
                    uk = ("huc", gi)
                    S.op("dve", "tensor_scalar", reads=rk, writes=[uk], out=uc[gi][:], in0=ub[gi][:, 0:L],
                         scalar1=cw[:, gi, 0:1], scalar2=cb[:, gi:gi + 1], op0=ALU.mult, op1=ALU.add)
                    S.op("dve", "scalar_tensor_tensor", reads=rk + [uk], writes=[uk], out=uc[gi][:],
                         in0=ub[gi][:, 1:L + 1], scalar=cw[:, gi, 1:2], in1=uc[gi][:], op0=ALU.mult, op1=ALU.add)
                    S.op("dve", "scalar_tensor_tensor", reads=rk + [uk], writes=[uk], out=uc[gi][:],
                         in0=ub[gi][:, 2:L + 2], scalar=cw[:, gi, 2:3], in1=uc[gi][:], op0=ALU.mult, op1=ALU.add)
                S.op("pool", "tensor_tensor", reads=[("huc", 1), ("huc", 2)], writes=[("huc", 2)], out=uc[2][:],
                     in0=uc[2][:], in1=uc[1][:], op=ALU.mult)
                S.dma("sp", reads=[("huc", 2)], writes=[("s_vv", b)], out=s_vv[:, b, :], in_=uc[2][:])
                S.dma("sp", reads=[("huc", 0)], writes=[("s_x0", b)], out=s_x0[:, b, :], in_=uc[0][:])
        with C.scope():
            tl = {k: C.sb("fft_" + k, [128, 4, 128], F32) for k in
                  ("t1", "t2", "t3", "t4", "u1", "u2", "u3", "u4", "bre", "bim", "dre", "dim")}
            yre = [C.sb("fft_yre%d" % i, [128, 4, 128], F32) for i in range(2)]
            yim = [C.sb("fft_yim%d" % i, [128, 4, 128], F32) for i in range(2)]
            ps = {"a": [C.ps("psa%d" % i, [128, 512]) for i in range(2)], "xre": C.ps("psxre", [128, 512]),
                  "xim": C.ps("psxim", [128, 512]), "c": [C.ps("psc%d" % i, [128, 512]) for i in range(2)],
                  "y": C.ps("psy", [128, 512])}
            Xb = [C.sb("fft_X%d" % i, [128, 4, 128], F32) for i in range(2)]
            Hb = [[C.sb("fft_H%d%d" % (i, r), [128, 2, 128], F32) for r in range(2)] for i in range(2)]
            ev = [[C.sb("fft_ev%d%d" % (i, r), [128, 4, 128], F32) for r in range(2)] for i in range(2)]
            yo = [C.sb("fft_yo%d" % i, [64, 4, 128], F32) for i in range(2)]
            ps8 = C.ps("psx8", [128, 512])
            tlB = {"t1": tl["u1"], "t2": tl["u2"], "t3": tl["u3"], "t4": tl["u4"], "bre": tl["dre"], "bim": tl["dim"]}
            psB = {"a": ps["c"], "xre": ps["y"], "xim": ps8}

            def p1_group(g):
                q = g % 2
                tl_, ps_, kp = (tl, ps, "") if q == 0 else (tlB, psB, "B")
                X, xk = Xb[q], ("X", q)
                S.dma("sp", reads=[("s_h", ti) for ti in range(NFFT // 512)], writes=[xk], out=X[:],
                      in_=s_h[4 * g:4 * g + 4, :].rearrange("c (n1 n2) -> n1 c n2", n2=128))
                for _ in emit_fft_fwd(C, X, xk, 128, 4, cst, tl_, ps_, kp):
                    yield
                for r, nm in ((0, "xre"), (1, "xim")):
                    e_, ek = ev[q][r], ("ev", q, r)
                    S.op("act", "mul", reads=[kp + "ps" + nm], writes=[ek], out=e_[:].rearrange("p s k -> p (s k)"),
                         in_=ps_[nm][:], mul=1.0 / NFFT)
                    S.dma("sp", reads=[ek], writes=[("s_H", g, r)], out=s_H[r, :, 4 * g:4 * g + 4, :], in_=e_[:])
                yield

            pipeline((p1_group(g) for g in range(32)), 2)
            S.barrier()
            tre = cst[:, 3:4, :]
            tim = cst[:, 4:5, :]
            g1 = cst[:, 1:3, :].rearrange("p a b -> p (a b)")
            g2 = cst[:, 0:2, :].rearrange("p a b -> p (a b)")

            def half1(g):
                p = g % 2
                X, xk = Xb[p], ("X", p)
                S.dma("sp", reads=[("s_vv", 0), ("s_vv", 1)], writes=[xk], out=X[:64],
                      in_=s_vv[2 * g:2 * g + 2].rearrange("c b (n1 n2) -> n1 (c b) n2", n2=128))
                H = Hb[p]
                for r in range(2):
                    S.dma("sp", reads=[("s_H", g // 2, r)], writes=[("H", p, r)], out=H[r][:],
                          in_=s_H[r, :, 2 * g:2 * g + 2, :])
                for _ in emit_fft_fwd(C, X, xk, 64, 4, cst, tl, ps):
                    yield
                hk = [("H", p, 0), ("H", p, 1)]
                xre = ps["xre"][:].rearrange("p (c b k) -> p c b k", c=2, b=2)
                xim = ps["xim"][:].rearrange("p (c b k) -> p c b k", c=2, b=2)
                hb = lambda r: H[r][:].unsqueeze(2).to_broadcast([128, 2, 2, 128])
                v4 = lambda t_: t_[:].rearrange("p (c b) k -> p c b k", c=2)
                S.op("dve", "tensor_tensor", reads=["psxre"] + hk, writes=[("t1", 0), ("t1", 1)], out=v4(tl["t1"]),
                     in0=xre, in1=hb(0), op=ALU.mult)
                S.op("dve", "tensor_tensor", reads=["psxim"] + hk, writes=[("t2", 0), ("t2", 1)], out=v4(tl["t2"]),
                     in0=xim, in1=hb(1), op=ALU.mult)
                S.op("dve", "tensor_tensor", reads=["psxre"] + hk, writes=[("t3", 0), ("t3", 1)], out=v4(tl["t3"]),
                     in0=xre, in1=hb(1), op=ALU.mult)
                S.op("dve", "tensor_tensor", reads=["psxim"] + hk, writes=[("t4", 0), ("t4", 1)], out=v4(tl["t4"]),
                     in0=xim, in1=hb(0), op=ALU.mult)
                S.op("pool", "tensor_tensor", reads=[("t1", 0), ("t1", 1), ("t2", 0), ("t2", 1)], writes=[("yre", p)],
                     out=yre[p][:], in0=tl["t1"][:], in1=tl["t2"][:], op=ALU.subtract)
                S.op("pool", "tensor_tensor", reads=[("t3", 0), ("t3", 1), ("t4", 0), ("t4", 1)], writes=[("yim", p)],
                     out=yim[p][:], in0=tl["t3"][:], in1=tl["t4"][:], op=ALU.add)
                yield

            def half2(g):
                p = g % 2
                for s_ in range(4):
                    bank = ps["c"][s_ // 2]
                    o = (s_ % 2) * 256
                    S.mm(bank[:, o:o + 256], yre[p][:, s_, :], g1, True, False, [("yre", p), "cst"], [("psc", s_ // 2)])
                    S.mm(bank[:, o:o + 256], yim[p][:, s_, :], g2, False, True, [("yim", p), "cst"], [("psc", s_ // 2)])
                yield
                for h in range(2):
                    cv = ps["c"][h][:].rearrange("p (s r k) -> p s r k", s=2, r=2)
                    cre = cv[:, :, 0, :]
                    cim = cv[:, :, 1, :]
                    sl = slice(2 * h, 2 * h + 2)
                    bt = lambda t_: t_.to_broadcast([128, 2, 128])
                    S.op("dve", "tensor_tensor", reads=[("psc", h), "cst"], writes=[("u1", h)], out=tl["u1"][:, sl, :],
                         in0=cre, in1=bt(tre), op=ALU.mult)
                    S.op("dve", "tensor_tensor", reads=[("psc", h), "cst"], writes=[("u2", h)], out=tl["u2"][:, sl, :],
                         in0=cim, in1=bt(tim), op=ALU.mult)
                    S.op("dve", "tensor_tensor", reads=[("psc", h), "cst"], writes=[("u3", h)], out=tl["u3"][:, sl, :],
                         in0=cim, in1=bt(tre), op=ALU.mult)
                    S.op("dve", "tensor_tensor", reads=[("psc", h), "cst"], writes=[("u4", h)], out=tl["u4"][:, sl, :],
                         in0=cre, in1=bt(tim), op=ALU.mult)
                S.op("pool", "tensor_tensor", reads=[("u1", 0), ("u1", 1), ("u2", 0), ("u2", 1)], writes=["dre"],
                     out=tl["dre"][:], in0=tl["u1"][:], in1=tl["u2"][:], op=ALU.add)
                S.op("pool", "tensor_tensor", reads=[("u3", 0), ("u3", 1), ("u4", 0), ("u4", 1)], writes=["dim"],
                     out=tl["dim"][:], in0=tl["u3"][:], in1=tl["u4"][:], op=ALU.subtract)
                yield
                S.mm(ps["y"][:64, :], cst[:, 1, 0:64], tl["dre"][:].rearrange("p s k -> p (s k)"), True, False,
                     ["cst", "dre"], ["psy"])
                S.mm(ps["y"][:64, :], cst[:, 0, 0:64], tl["dim"][:].rearrange("p s k -> p (s k)"), False, True,
                     ["cst", "dim"], ["psy"])
                yield
                y_, yk = yo[p], ("yo", p)
                S.op("act", "activation", reads=["psy"], writes=[yk], out=y_[:].rearrange("p s k -> p (s k)"),
                     in_=ps["y"][:64, :], func=AF.Copy)
                S.dma("sp", reads=[yk], writes=[("s_y", g)], out=s_y[2 * g:2 * g + 2].rearrange(
                    "c b (n1 n2) -> n1 (c b) n2", n2=128), in_=y_[:])
                yield

            for _ in half1(0):
                pass
            for g in range(64):
                interleave([half2(g)] + ([half1(g + 1)] if g + 1 < 64 else []))
        with C.scope():
            dcol = col("dcol", dvec, 128)
            ya = C.sb("p4y", [128, L], F32)
            va = C.sb("p4v", [128, L], F32)
            xa = C.sb("p4x", [128, L], F32)
            for b in range(BATCH):
                S.dma("sp", reads=[("s_y", g) for g in range(64)], writes=["p4y"], out=ya[:], in_=s_y[:, b, :])
                S.dma("sp", reads=[("s_vv", b)], writes=["p4v"], out=va[:], in_=s_vv[:, b, :])
                S.dma("sp", reads=[("s_x0", b)], writes=["p4x"], out=xa[:], in_=s_x0[:, b, :])
                S.op("dve", "scalar_tensor_tensor", reads=["p4y", "p4v", "dcol"], writes=["p4y"], out=ya[:], in0=va[:],
                     scalar=dcol[:, 0:1], in1=ya[:], op0=ALU.mult, op1=ALU.add)
                S.op("dve", "tensor_tensor", reads=["p4y", "p4x"], writes=["p4y"], out=ya[:], in0=ya[:], in1=xa[:],
                     op=ALU.mult)
                S.dma("sp", reads=["p4y"], is_out=True, out=yT[:, b, :], in_=ya[:])
        S.replay()
    return nc

def emit_proj(C, x, T, g_dram, w_in, b_in, nout, uT):
    S = C.S
    NT = T // 512
    w_in_v = w_in.rearrange("(kc p) n -> p kc n", p=128)
    with C.scope():
        g_sb = load_vec_pk(C, "pj_g", g_dram, KC)
        bias = load_vec_pk(C, "pj_b", b_in, nout // 128)
        hn = C.sb("pj_hn", [128, KC, T], BF16)
        wbuf = [C.sb("pj_w%d" % i, [128, KC, 128], BF16) for i in range(3)]
        st = [C.sb("pj_st%d" % i, [128, 512], F32) for i in range(4)]
        pp = [C.ps("pj_ps%d" % i, [128, 512]) for i in range(2)]
        emit_rmsnorm(C, x, "x", 0, T, g_sb, "pj_g", hn, "pj_hn")
        it = 0
        for m in range(nout // 128):
            wb, wk = wbuf[m % 3], ("pj_w", m % 3)
            S.dma("pool", writes=[wk], out=wb[:], in_=w_in_v[:, :, m * 128:(m + 1) * 128])
            for tt in range(NT):
                it += 1
                b = it % 2
                sl = slice(tt * 512, (tt + 1) * 512)
                for k in range(KC):
                    S.mm(pp[b][:], wb[:, k, :], hn[:, k, sl], k == 0, k == KC - 1, [wk, ("pj_hn", k, tt)], [("pj_ps", b)])
                o_, ok = st[it % 4], ("pj_st", it % 4)
                S.op("act", "activation", reads=[("pj_ps", b), "pj_b"], writes=[ok], out=o_[:], in_=pp[b][:],
                     func=AF.Identity, bias=bias[:, m:m + 1])
                S.dma("sp", reads=[ok], is_out=True, out=uT[m * 128:(m + 1) * 128, sl], in_=o_[:])


def emit_outproj(C, x, T, yT, w_out, b_out):
    S = C.S
    NT = T // 512
    w_out_v = w_out.rearrange("(kc p) n -> p kc n", p=128)
    y_v = yT.rearrange("(kc p) t -> p kc t", p=128)
    with C.scope():
        bo = load_vec_pk(C, "op_b", b_out, KC)
        ybf = C.sb("op_y", [128, KC, T], BF16)
        for k in range(KC):
            S.dma("pool", writes=[("op_y", k)], out=ybf[:, k, :], in_=y_v[:, k, :])
        wobuf = [C.sb("op_wo%d" % i, [128, KC, 128], BF16) for i in range(2)]
        po = [C.ps("op_po%d" % i, [128, 512]) for i in range(2)]
        it = 0
        for m in range(KC):
            wo, wok = wobuf[m % 2], ("op_wo", m % 2)
            S.dma("pool", writes=[wok], out=wo[:], in_=w_out_v[:, :, m * 128:(m + 1) * 128])
            for tt in range(NT):
                it += 1
                b = it % 2
                sl = slice(tt * 512, (tt + 1) * 512)
                for i in range(KC):
                    S.mm(po[b][:], wo[:, i, :], ybf[:, i, sl], i == 0, i == KC - 1, [wok, ("op_y", i)], [("op_po", b)])
                xkey = ("x", m, tt)
                S.op("dve", "scalar_tensor_tensor", reads=[("op_po", b), xkey, "op_b"], writes=[xkey], out=x[:, m, sl],
                     in0=po[b][:], scalar=bo[:, m:m + 1], in1=x[:, m, sl], op0=ALU.add, op1=ALU.add)


def build_pre_prog(nout, T=TPC):
    nc = bass.Bass("TRN2", target_bir_lowering=False)
    xT = dram_in(nc, "xT", [D, T])
    nrm = dram_in(nc, "nrm", [2, D])
    fw_in = dram_in(nc, "fw_in", [D, 2 * FF])
    fw_out = dram_in(nc, "fw_out", [FF, D])
    w_in = dram_in(nc, "w_in", [D, nout])
    b_in = dram_in(nc, "b_in", [nout])
    xo = dram_out(nc, "xo", [D, T])
    uT = dram_out(nc, "uT", [nout, T])
    with contextlib.ExitStack() as es:
        C = Ctx(nc, es)
        emit_consts(C)
        x = C.sb("x", [128, KC, T], F32)
        emit_load_x(C, x, xT, T)
        emit_ffn(C, x, [(t, 1024) for t in range(0, T, 1024)], nrm[0], fw_in, fw_out, "f1")
        emit_store_x(C, x, xo, T)
        emit_proj(C, x, T, nrm[1], w_in, b_in, nout, uT)
        C.S.replay()
    return nc


def build_post_prog(T=TPC):
    nc = bass.Bass("TRN2", target_bir_lowering=False)
    xT = dram_in(nc, "xT", [D, T])
    yT = dram_in(nc, "yT", [D, T])
    w_out = dram_in(nc, "w_out", [D, D])
    b_out = dram_in(nc, "b_out", [D])
    nrm = dram_in(nc, "nrm", [D])
    fw_in = dram_in(nc, "fw_in", [D, 2 * FF])
    fw_out = dram_in(nc, "fw_out", [FF, D])
    xo = dram_out(nc, "xo", [D, T])
    with contextlib.ExitStack() as es:
        C = Ctx(nc, es)
        emit_consts(C)
        x = C.sb("x", [128, KC, T], F32)
        emit_load_x(C, x, xT, T)
        emit_outproj(C, x, T, yT, w_out, b_out)
        emit_ffn(C, x, [(t, 1024) for t in range(0, T, 1024)], nrm, fw_in, fw_out, "f2")
        emit_store_x(C, x, xo, T)
        C.S.replay()
    return nc


CH = 64
NCH = SEQ // CH
BLK = 8


def gdn_consts():
    i = np.arange(CH)
    mtri = np.stack([(i[:, None] <= i[None, :]), (i[:, None] >= i[None, :])]).astype(np.float32)
    eye = np.eye(CH, dtype=np.float32)
    mstrict = mtri - eye[None]
    ident = np.eye(128, dtype=np.float32)
    return mtri, mstrict, ident


def build_gd_core_prog(dbg_blocks=None):
    nc = bass.Bass("TRN2", target_bir_lowering=False)
    L, N = SEQ, NCH
    qkv0 = dram_in(nc, "qkv0", [3, 128, BATCH, L])
    ztok = dram_in(nc, "ztok", [CH, BATCH, N, 128])
    abt = dram_in(nc, "abt", [4, CH, BATCH, N])
    convw = dram_in(nc, "convw", [3, 3, 128])
    alog = dram_in(nc, "alog", [CH, 2])
    dtb = dram_in(nc, "dtb", [CH, 2])
    gn = dram_in(nc, "gn", [CH, 128])
    mtri_d = dram_in(nc, "mtri", [2, CH, CH])
    mstr_d = dram_in(nc, "mstrict", [2, CH, CH])
    ident_d = dram_in(nc, "ident", [128, 128])
    ytok = dram_out(nc, "ytok", [CH, BATCH, N, 128])
    s_o = nc.dram_tensor("s_o", [2, CH, BATCH, N, 128], F32).ap()
    with contextlib.ExitStack() as es:
        C = Ctx(nc, es)
        S = C.S
        ones = C.sb("ones", [128, 128], F32)
        S.op("dve", "memset", writes=["ones"], ap=ones[:], constant=1.0)
        ident = C.sb("ident", [128, 128], F32)
        S.dma("sp", writes=["ident"], out=ident[:], in_=ident_d)
        mtri = C.sb("mtri", [CH, 2, CH], F32)
        mstr = C.sb("mstr", [CH, 2, CH], F32)
        for d in range(2):
            S.dma("sp", writes=[("mtri", d)], out=mtri[:, d, :], in_=mtri_d[d])
            S.dma("sp", writes=[("mstr", d)], out=mstr[:, d, :], in_=mstr_d[d])
        cw = C.sb("gcw", [128, 3, 3], F32)
        for j in range(3):
            for gi in range(3):
                S.dma("sp", writes=[("gcw", j, gi)], out=cw[:, gi, j:j + 1], in_=convw[j, gi].rearrange("(p o) -> p o", o=1))
        cwk = [("gcw", j, gi) for j in range(3) for gi in range(3)]
        gtf = C.sb("gt", [128, 2, BATCH, N], F32)
        S.op("dve", "memset", writes=[("gt", 0), ("gt", 1)], ap=gtf[:], constant=0.0)
        gt = gtf[:CH]
        bt_ = C.sb("bt", [CH, 2, BATCH, N], F32)
        al = C.sb("al", [CH, 2], F32)
        db = C.sb("db", [CH, 2], F32)
        S.dma("sp", writes=["al"], out=al[:], in_=alog)
        S.dma("sp", writes=["db"], out=db[:], in_=dtb)
        for d in range(2):
            S.dma("sp", writes=[("gt", d)], out=gt[:, d], in_=abt[d])
            S.dma("sp", writes=[("bt", d)], out=bt_[:, d], in_=abt[2 + d])
        S.op("act", "activation", reads=["al"], writes=["al"], out=al[:], in_=al[:], func=AF.Exp)
        S.op("dve", "tensor_scalar", reads=["al"], writes=["al"], out=al[:], in0=al[:], scalar1=-1.0, scalar2=None,
             op0=ALU.mult)
        for d in range(2):
            gv = gt[:, d].rearrange("p b n -> p (b n)")
            bv = bt_[:, d].rearrange("p b n -> p (b n)")
            S.op("act", "activation", reads=[("gt", d), "db"], writes=[("gt", d)], out=gv, in_=gv, func=AF.Exp,
                 bias=db[:, d:d + 1])
            S.op("dve", "tensor_scalar", reads=[("gt", d)], writes=[("gt", d)], out=gv, in0=gv, scalar1=1.0, scalar2=None,
                 op0=ALU.add)
            S.op("act", "activation", reads=[("gt", d)], writes=[("gt", d)], out=gv, in_=gv, func=AF.Ln)
            S.op("dve", "tensor_scalar", reads=[("gt", d), "al"], writes=[("gt", d)], out=gv, in0=gv, scalar1=al[:, d:d + 1],
                 scalar2=None, op0=ALU.mult)
            S.op("act", "activation", reads=[("bt", d)], writes=[("bt", d)], out=bv, in_=bv, func=AF.Sigmoid)
        with C.scope():
            qT = C.sb("qT", [128, L], F32)
            kT = C.sb("kT", [128, L], F32)
            vT = C.sb("vT", [128, L], F32)
            ubs = [C.sb("gub%d" % i, [128, 2048 + 2], F32) for i in range(2)]
            ci = 0
            rsb = [C.sb("grs%d" % i, [128, 512], F32) for i in range(2)]
            sqb = [C.sb("gsq%d" % i, [128, 512], F32) for i in range(2)]
            gc = C.sb("gc", [CH, N], F32)
            kd = C.sb("kd", [CH, N], F32)
            bg = C.sb("bg", [CH, N], F32)
            egt = C.sb("egt", [128, N], F32)
            Sst = [C.sb("Sst%d" % i, [128, 128], F32) for i in range(2)]
            attnT = [C.sb("g8_attnT%d" % i, [128, BLK, CH], F32) for i in range(2)]
            for i_ in range(2):
                S.op("dve", "memset", writes=[("attnT", i_)], ap=attnT[i_][:], constant=0.0)
            names = ["dT", "W1", "LT", "Lm", "X8", "Pa", "Pta", "Pb", "Ptb"]
            t8f = {nm: C.sb("g8_" + nm, [128, BLK, CH], F32) for nm in names}
            for nm in names:
                S.op("dve", "memset", writes=[nm], ap=t8f[nm][:], constant=0.0)
            t8 = {nm: t8f[nm][:CH] for nm in names}
            qd = [C.sb("g8_qd%d" % i, [128, BLK * CH], F32) for i in range(2)]
            egr = C.sb("g8_egr", [128, BLK * CH], F32)
            kdt = [C.sb("g8_kdt%d" % i, [128, BLK, 128], F32) for i in range(2)]
            for i_ in range(2):
                S.op("dve", "memset", writes=[("kdt", i_, 0)], ap=kdt[i_][:, 0:4, :], constant=0.0)
                S.op("dve", "memset", writes=[("kdt", i_, 1)], ap=kdt[i_][:, 4:8, :], constant=0.0)
            kbt = C.sb("g8_kbt", [128, BLK, 128], F32)
            S.op("dve", "memset", writes=[("kbt", 0)], ap=kbt[:, 0:4, :], constant=0.0)
            S.op("dve", "memset", writes=[("kbt", 1)], ap=kbt[:, 4:8, :], constant=0.0)
            vbt = C.sb("g8_vbt", [CH, BLK, 128], F32)
            u8 = [C.sb("g8_u8%d" % i, [128, BLK, 128], F32) for i in range(2)]
            MT8 = [C.sb("g8_MT%d" % i, [128, BLK, 128], F32) for i in range(2)]
            w8 = C.sb("g8_w8", [128, BLK, 128], F32)
            mtmp = C.sb("g8_mtmp", [128, BLK, 128], F32)
            for i_ in range(2):
                for hh_ in range(2):
                    S.op("dve", "memset", writes=[("u8", i_, hh_)], ap=u8[i_][:, hh_ * 4:hh_ * 4 + 4, :], constant=0.0)
            for hh_ in range(2):
                S.op("dve", "memset", writes=[("w8", hh_)], ap=w8[:, hh_ * 4:hh_ * 4 + 4, :], constant=0.0)
            o8 = [C.sb("g8_o8%d" % i, [CH, BLK, 128], F32) for i in range(2)]
            Bk = [C.ps("gB%d" % i, [128, 512]) for i in range(8)]
            bk = lambda i: ("B", i)
            b7all = [bk(7)]
            S.excl.update(bk(i) for i in range(8))

            for b in range(BATCH):
                for gi, dst, dk_ in ((0, qT, "qT"), (1, kT, "kT"), (2, vT, "vT")):
                    CT = 2048
                    for ct in range(L // CT):
                        ci += 1
                        u_, uk_ = ubs[ci % 2], ("gub", ci % 2)
                        lo = ct * CT - 1
                        hi = ct * CT + CT + 1
                        a_ = max(lo, 0)
                        b_ = min(hi, L)
                        wr = [uk_]
                        if lo < 0:
                            S.op("pool", "memset", writes=[uk_], ap=u_[:, 0:1], constant=0.0)
                        if hi > L:
                            S.op("pool", "memset", writes=[uk_], ap=u_[:, CT + 1:CT + 2], constant=0.0)
                        S.dma("sp", writes=[uk_], out=u_[:, a_ - lo:b_ - lo], in_=qkv0[gi, :, b, a_:b_])
                        rk = wr + cwk
                        dsl = slice(ct * CT, (ct + 1) * CT)
                        dkt = (dk_, ct)
                        S.op("dve", "tensor_scalar", reads=rk, writes=[dkt], out=dst[:, dsl], in0=u_[:, 0:CT],
                             scalar1=cw[:, gi, 0:1], scalar2=None, op0=ALU.mult)
                        S.op("dve", "scalar_tensor_tensor", reads=rk + [dkt], writes=[dkt], out=dst[:, dsl], in0=u_[:, 1:CT + 1],
                             scalar=cw[:, gi, 1:2], in1=dst[:, dsl], op0=ALU.mult, op1=ALU.add)
                        S.op("dve", "scalar_tensor_tensor", reads=rk + [dkt], writes=[dkt], out=dst[:, dsl], in0=u_[:, 2:CT + 2],
                             scalar=cw[:, gi, 2:3], in1=dst[:, dsl], op0=ALU.mult, op1=ALU.add)
                        S.op("act", "activation", reads=[dkt], writes=[dkt], out=dst[:, dsl], in_=dst[:, dsl], func=AF.Silu)
                    S.op("act", "activation", reads=[(dk_, ct) for ct in range(L // CT)], writes=[dk_], out=dst[:, 0:1],
                         in_=dst[:, 0:1], func=AF.Copy)
                    if gi < 2:
                        sc = (128.0 ** -0.5) if gi == 0 else 1.0

                        def l2_tile(tt, dst=dst, dk_=dk_, sc=sc):
                            sl = slice(tt * 512, (tt + 1) * 512)
                            q3 = tt % 2
                            sq, sqk = sqb[q3], ("gsq", q3)
                            rs, rsk = rsb[q3], ("grs", q3)
                            S.op("act", "activation", reads=[dk_], writes=[sqk], out=sq[:], in_=dst[:, sl], func=AF.Square)
                            yield
                            S.mm(Bk[q3][:], ones[:], sq[:], True, True, ["ones", sqk], [bk(q3)])
                            yield
                            S.op("dve", "tensor_scalar", reads=[bk(q3)], writes=[rsk], out=rs[:], in0=Bk[q3][:], scalar1=1e-6,
                                 scalar2=None, op0=ALU.add)
                            yield
                            S.op("act", "activation", reads=[rsk], writes=[rsk], out=rs[:], in_=rs[:], func=AF.Sqrt)
                            yield
                            S.op("dve", "reciprocal", reads=[rsk], writes=[rsk], out=rs[:], in_=rs[:])
                            S.op("dve", "scalar_tensor_tensor", reads=[rsk, dk_], writes=[(dk_, "n", tt)], out=dst[:, sl],
                                 in0=dst[:, sl], scalar=sc, in1=rs[:], op0=ALU.mult, op1=ALU.mult)
                            yield

                        pipeline((l2_tile(tt) for tt in range(L // 512)), 2)
                        S.op("dve", "tensor_copy", reads=[(dk_, "n", tt) for tt in range(L // 512)], writes=[dk_],
                             out=dst[:, 0:1], in_=dst[:, 0:1])
                for d in range(2):
                    Gd = gt[:, d, b, :]
                    Bd = bt_[:, d, b, :]
                    S.mm(Bk[0][:CH, :N], mtri[:, d, :], Gd, True, True, [("mtri", d), ("gt", d)], [bk(0)])
                    S.op("act", "activation", reads=[bk(0)], writes=["gc"], out=gc[:], in_=Bk[0][:CH, :N], func=AF.Copy)
                    S.mm(Bk[1][:, :N], ones[:], gtf[:, d, b, :], True, True, ["ones", ("gt", d)], [bk(1)])
                    S.op("act", "activation", reads=[bk(1)], writes=["egt"], out=egt[:], in_=Bk[1][:, :N], func=AF.Exp)
                    S.op("dve", "tensor_tensor", reads=[bk(1), "gc"], writes=["kd"], out=kd[:], in0=Bk[1][:CH, :N], in1=gc[:],
                         op=ALU.subtract)
                    S.op("act", "activation", reads=["kd"], writes=["kd"], out=kd[:], in_=kd[:], func=AF.Exp)
                    S.op("act", "activation", reads=["gc"], writes=["bg"], out=bg[:], in_=gc[:], func=AF.Exp)
                    S.op("dve", "tensor_tensor", reads=["bg", ("bt", d)], writes=["bg"], out=bg[:], in0=bg[:], in1=Bd, op=ALU.mult)
                    S.op("dve", "memset", writes=[("S", 0)], ap=Sst[0][:], constant=0.0)
                    sidx = [0]
                    si = 0
                    vi = 0
                    blocks = list(range(N // BLK))
                    if d == 1:
                        blocks = blocks[::-1]
                    if dbg_blocks is not None:
                        blocks = blocks[:dbg_blocks]
                    flat = lambda t_: t_[:].rearrange("p i c -> p (i c)")
                    mt_b = mtri[:, d:d + 1, :].to_broadcast([CH, BLK, CH])
                    ms_b = mstr[:, d:d + 1, :].to_broadcast([CH, BLK, CH])
                    id_b = ident[:CH, 0:CH].unsqueeze(1).to_broadcast([CH, BLK, CH])
                    k3 = lambda i_: Bk[i_][:CH, :].rearrange("p (i c) -> p i c", c=CH)

                    def prep(nb, pb, d=d, Gd=Gd, Bd=Bd):
                        n0 = nb * BLK
                        tsl = slice(n0 * CH, (n0 + BLK) * CH)
                        gb = lambda t_: t_[:, n0:n0 + BLK].unsqueeze(2).to_broadcast([CH, BLK, CH])
                        qd_, at_, kdt_, u8_ = qd[pb], attnT[pb], kdt[pb], u8[pb]
                        S.op("dve", "tensor_tensor", reads=[("gt", d), ("mtri", d)], writes=["Pa"], out=t8["Pa"][:],
                             in0=mt_b, in1=gb(Gd), op=ALU.mult)
                        S.op("dve", "tensor_tensor", reads=[("bt", d), "ident"], writes=["Pb"], out=t8["Pb"][:], in0=id_b,
                             in1=gb(Bd), op=ALU.mult)
                        yield
                        S.mm(Bk[3][:], ones[:], t8f["Pa"][:].rearrange("p i c -> p (i c)"), True, True, ["ones", "Pa"], [bk(3)])
                        S.mm(Bk[4][:CH, :], ones[:CH, :CH], flat(t8["Pb"]), True, True, ["ones", "Pb"], [bk(4)])
                        for i in range(BLK):
                            csl = slice((n0 + i) * CH, (n0 + i + 1) * CH)
                            S.mm(Bk[5][:CH, i * CH:(i + 1) * CH], kT[:, csl], kT[:, csl], True, True, ["kT"], [bk(5)])
                            S.mm(Bk[6][:CH, i * CH:(i + 1) * CH], kT[:, csl], qT[:, csl], True, True, ["kT", "qT"], [bk(6)])
                        yield
                        S.op("act", "activation", reads=[bk(3)], writes=["egr"], out=egr[:], in_=Bk[3][:], func=AF.Exp)
                        S.op("dve", "tensor_tensor", reads=[bk(3), "gc"], writes=["dT"], out=t8["dT"][:], in0=k3(3), in1=gb(gc),
                             op=ALU.subtract)
                        S.op("dve", "tensor_scalar", reads=["dT"], writes=["dT"], out=t8["dT"][:], in0=t8["dT"][:], scalar1=0.0,
                             scalar2=None, op0=ALU.min)
                        yield
                        S.op("act", "activation", reads=["dT"], writes=["dT"], out=t8["dT"][:], in_=t8["dT"][:], func=AF.Exp)
                        S.op("dve", "tensor_tensor", reads=["egr", "qT"], writes=[("qd", pb)], out=qd_[:], in0=qT[:, tsl],
                             in1=egr[:], op=ALU.mult)
                        yield
                        S.op("dve", "tensor_tensor", reads=["dT", ("mstr", d)], writes=["W1"], out=t8["W1"][:], in0=t8["dT"][:],
                             in1=ms_b, op=ALU.mult)
                        S.op("dve", "tensor_tensor", reads=["W1", bk(4)], writes=["W1"], out=t8["W1"][:], in0=t8["W1"][:],
                             in1=k3(4), op=ALU.mult)
                        S.op("dve", "tensor_tensor", reads=["dT", ("mtri", d)], writes=["dT"], out=t8["dT"][:], in0=t8["dT"][:],
                             in1=mt_b, op=ALU.mult)
                        S.op("dve", "tensor_tensor", reads=[bk(5), "W1"], writes=["LT"], out=t8["LT"][:], in0=k3(5),
                             in1=t8["W1"][:], op=ALU.mult)
                        S.op("dve", "tensor_tensor", reads=[bk(6), "dT"], writes=[("attnT", pb)], out=at_[:CH], in0=k3(6),
                             in1=t8["dT"][:], op=ALU.mult)
                        yield
                        for i in range(BLK):
                            S.mm(Bk[7][:CH, i * CH:(i + 1) * CH], t8["LT"][:, i, :], ident[:CH, :CH], True, True,
                                 ["LT", "ident"], [bk(7)])
                        S.op("dve", "tensor_tensor", reads=["LT", "ident"], writes=["X8"], out=t8["X8"][:], in0=id_b,
                             in1=t8["LT"][:], op=ALU.subtract)
                        yield
                        S.op("act", "activation", reads=[bk(7)], writes=["Lm"], out=flat(t8["Lm"]), in_=Bk[7][:CH, :],
                             func=AF.Copy)
                        yield
                        A_, At_, ak, atk = t8["LT"], t8["Lm"], "LT", "Lm"
                        for lev in range(5):
                            P_, Pt_ = (t8["Pa"], t8["Pta"]) if lev % 2 == 0 else (t8["Pb"], t8["Ptb"])
                            pk, ptk = ("Pa", "Pta") if lev % 2 == 0 else ("Pb", "Ptb")
                            for i in range(BLK):
                                cs = slice(i * CH, (i + 1) * CH)
                                S.mm(Bk[4][:CH, cs], A_[:, i, :], At_[:, i, :], True, True, [ak, atk], [bk(4)])
                                if lev < 4:
                                    S.mm(Bk[3][:CH, cs], At_[:, i, :], A_[:, i, :], True, True, [ak, atk], [bk(3)])
                            yield
                            S.op("act", "activation", reads=[bk(4)], writes=[ptk], out=flat(Pt_), in_=Bk[4][:CH, :], func=AF.Copy)
                            if lev < 4:
                                S.op("act", "activation", reads=[bk(3)], writes=[pk], out=flat(P_), in_=Bk[3][:CH, :], func=AF.Copy)
                            yield
                            for i in range(BLK):
                                cs = slice(i * CH, (i + 1) * CH)
                                S.mm(Bk[5][:CH, cs], Pt_[:, i, :], t8["X8"][:, i, :], True, True, [ptk, "X8"], [bk(5)])
                            yield
                            S.op("dve", "tensor_tensor", reads=[bk(5), "X8"], writes=["X8"], out=flat(t8["X8"]),
                                 in0=Bk[5][:CH, :], in1=flat(t8["X8"]), op=ALU.add)
                            A_, At_, ak, atk = P_, Pt_, pk, ptk
                        yield
                        for i in range(BLK):
                            csl = slice((n0 + i) * CH, (n0 + i + 1) * CH)
                            bkk, bkv = (6, 3) if i < 4 else (7, 4)
                            o_ = (i % 4) * 128
                            S.op("pe", "transpose", reads=["kT", "ident"], writes=[bk(bkk)], out=Bk[bkk][:CH, o_:o_ + 128],
                                 in_=kT[:, csl], identity=ident[:])
                            S.op("pe", "transpose", reads=["vT", "ident"], writes=[bk(bkv)], out=Bk[bkv][:CH, o_:o_ + 128],
                                 in_=vT[:, csl], identity=ident[:])
                        yield
                        for hh in range(2):
                            isl = slice(hh * 4, hh * 4 + 4)
                            sc_b = lambda t_: t_[:, n0 + hh * 4:n0 + hh * 4 + 4].unsqueeze(2).to_broadcast([CH, 4, 128])
                            kps = Bk[6 + hh][:CH, :].rearrange("p (i e) -> p i e", e=128)
                            vps = Bk[3 + hh][:CH, :].rearrange("p (i e) -> p i e", e=128)
                            S.op("dve", "tensor_tensor", reads=[bk(6 + hh), "kd"], writes=[("kdt", pb, hh)],
                                 out=kdt_[:CH, isl, :], in0=kps, in1=sc_b(kd), op=ALU.mult)
                            S.op("dve", "tensor_tensor", reads=[bk(6 + hh), "bg"], writes=[("kbt", hh)], out=kbt[:CH, isl, :],
                                 in0=kps, in1=sc_b(bg), op=ALU.mult)
                            S.op("dve", "tensor_tensor", reads=[bk(3 + hh), ("bt", d)], writes=[("vbt", hh)],
                                 out=vbt[:, isl, :], in0=vps, in1=sc_b(Bd), op=ALU.mult)
                        yield
                        MT_ = MT8[pb]
                        for i in range(BLK):
                            hh = i // 4
                            o_ = (i % 4) * 128
                            S.mm(Bk[5 + hh][:CH, o_:o_ + 128], t8["X8"][:, i, :], vbt[:, i, :], True, True,
                                 ["X8", ("vbt", hh)], [bk(5 + hh)])
                            S.mm(Bk[3 + hh][:CH, o_:o_ + 128], t8["X8"][:, i, :], kbt[:CH, i, :], True, True,
                                 ["X8", ("kbt", hh)], [bk(3 + hh)])
                        yield
                        for hh in range(2):
                            S.op("act", "activation", reads=[bk(5 + hh)], writes=[("u8", pb, hh)],
                                 out=u8_[:CH, hh * 4:hh * 4 + 4, :].rearrange("p i e -> p (i e)"), in_=Bk[5 + hh][:CH, :],
                                 func=AF.Copy)
                            S.op("act", "activation", reads=[bk(3 + hh)], writes=[("w8", hh)],
                                 out=w8[:CH, hh * 4:hh * 4 + 4, :].rearrange("p i e -> p (i e)"), in_=Bk[3 + hh][:CH, :],
                                 func=AF.Copy)
                        yield
                        for i in range(BLK):
                            hh = i // 4
                            o_ = (i % 4) * 128
                            S.mm(Bk[5 + hh][:, o_:o_ + 128], w8[:, i, :], kdt_[:, i, :], True, True,
                                 [("w8", hh), ("kdt", pb, hh)], [bk(5 + hh)])
                            S.mm(Bk[7][:, i * CH:(i + 1) * CH], w8[:, i, :], at_[:, i, :], True, True,
                                 [("w8", hh), ("attnT", pb)], [bk(7)])
                        yield
                        for hh in range(2):
                            S.op("act", "mul", reads=[bk(5 + hh)], writes=[("mtmp", hh)],
                                 out=mtmp[:, hh * 4:hh * 4 + 4, :].rearrange("p i e -> p (i e)"), in_=Bk[5 + hh][:], mul=-1.0)
                        S.op("dve", "tensor_tensor", reads=[bk(7), ("qd", pb)], writes=[("qd", pb)], out=qd_[:], in0=qd_[:],
                             in1=Bk[7][:], op=ALU.subtract)
                        yield
                        for i in range(BLK):
                            n = n0 + i
                            S.op("dve", "scalar_tensor_tensor", reads=[("mtmp", i // 4), "ident", "egt"], writes=[("MT", pb, i)],
                                 out=MT_[:, i, :], in0=ident[:], scalar=egt[:, n:n + 1], in1=mtmp[:, i, :], op0=ALU.mult,
                                 op1=ALU.add)
                            if i % 4 == 3:
                                yield

                    def scan(nb, pb, bi_, d=d, b=b):
                        n0 = nb * BLK
                        qd_, at_, kdt_, u8_, MT_ = qd[pb], attnT[pb], kdt[pb], u8[pb], MT8[pb]
                        ob, obk = o8[bi_ % 2], ("o8", bi_ % 2)
                        order = list(range(BLK)) if d == 0 else list(range(BLK))[::-1]
                        for i in order:
                            hh = i // 4
                            cur = sidx[0]
                            Sc, sk = Sst[cur], ("S", cur)
                            Sn, snk = Sst[1 - cur], ("S", 1 - cur)
                            sidx[0] = 1 - cur
                            S.mm(Bk[0][:, 0:128], MT_[:, i, :], Sc[:], True, False, [("MT", pb, i), sk], [bk(0)])
                            S.mm(Bk[0][:, 0:128], kdt_[:, i, :], u8_[:, i, :], False, True, [("kdt", pb, hh), ("u8", pb, hh)], [bk(0)])
                            S.mm(Bk[1][:CH, 0:128], qd_[:, i * CH:(i + 1) * CH], Sc[:], True, False, [("qd", pb), sk], [bk(1)])
                            S.mm(Bk[1][:CH, 0:128], at_[:, i, :], u8_[:, i, :], False, True, [("attnT", pb), ("u8", pb, hh)], [bk(1)])
                            yield
                            S.op("act", "activation", reads=[bk(0)], writes=[snk], out=Sn[:], in_=Bk[0][:, 0:128], func=AF.Copy)
                            S.op("dve", "tensor_copy", reads=[bk(1)], writes=[obk], out=ob[:, i, :], in_=Bk[1][:CH, 0:128])
                            yield
                        S.dma("sp", reads=[obk], writes=[("s_o", d, b, nb)], out=s_o[d, :, b, n0:n0 + BLK, :], in_=ob[:])

                    for _ in prep(blocks[0], 0):
                        pass
                    for bi_, nb in enumerate(blocks):
                        gens = [scan(nb, bi_ % 2, bi_)]
                        if bi_ + 1 < len(blocks):
                            gens.append(prep(blocks[bi_ + 1], (bi_ + 1) % 2))
                        interleave(gens)
        with C.scope():
            gnr = C.sb("gnr", [CH, 128], F32)
            S.dma("sp", writes=["gnr"], out=gnr[:], in_=gn)
            NB3 = 3
            of = [C.sb("of%d" % i, [CH, BLK, 128], F32) for i in range(NB3)]
            obb = [C.sb("ob%d" % i, [CH, BLK, 128], F32) for i in range(NB3)]
            zz = [C.sb("zz%d" % i, [CH, BLK, 128], F32) for i in range(NB3)]
            sq8 = [C.sb("sq8%d" % i, [CH, BLK, 128], F32) for i in range(NB3)]
            ss = [C.sb("ss%d" % i, [CH, BLK], F32) for i in range(NB3)]

            def out_block(it, b, nb):
                p = it % NB3
                n0 = nb * BLK
                S.dma("sp", reads=[("s_o", 0, b, nb)], writes=[("of", p)], out=of[p][:], in_=s_o[0, :, b, n0:n0 + BLK, :])
                S.dma("sp", reads=[("s_o", 1, b, nb)], writes=[("ob", p)], out=obb[p][:], in_=s_o[1, :, b, n0:n0 + BLK, :])
                S.dma("sp", writes=[("zz", p)], out=zz[p][:], in_=ztok[:, b, n0:n0 + BLK, :])
                yield
                S.op("dve", "tensor_tensor", reads=[("of", p), ("ob", p)], writes=[("of", p)], out=of[p][:], in0=of[p][:],
                     in1=obb[p][:], op=ALU.add)
                S.op("act", "activation", reads=[("zz", p)], writes=[("zz", p)], out=zz[p][:], in_=zz[p][:], func=AF.Silu)
                yield
                S.op("act", "activation", reads=[("of", p)], writes=[("sq8", p)], out=sq8[p][:], in_=of[p][:],
                     func=AF.Square)
                yield
                S.op("dve", "tensor_reduce", reads=[("sq8", p)], writes=[("ss", p)], out=ss[p][:], in_=sq8[p][:],
                     axis=AX.X, op=ALU.add)
                S.op("dve", "tensor_scalar", reads=[("ss", p)], writes=[("ss", p)], out=ss[p][:], in0=ss[p][:],
                     scalar1=1.0 / 128.0, scalar2=EPS, op0=ALU.mult, op1=ALU.add)
                yield
                S.op("act", "activation", reads=[("ss", p)], writes=[("ss", p)], out=ss[p][:], in_=ss[p][:], func=AF.Sqrt)
                yield
                S.op("dve", "reciprocal", reads=[("ss", p)], writes=[("ss", p)], out=ss[p][:], in_=ss[p][:])
                S.op("dve", "tensor_tensor", reads=[("of", p), ("ss", p)], writes=[("of", p)], out=of[p][:], in0=of[p][:],
                     in1=ss[p][:].unsqueeze(2).to_broadcast([CH, BLK, 128]), op=ALU.mult)
                S.op("dve", "tensor_tensor", reads=[("of", p), "gnr"], writes=[("of", p)], out=of[p][:], in0=of[p][:],
                     in1=gnr[:].unsqueeze(1).to_broadcast([CH, BLK, 128]), op=ALU.mult)
                S.op("dve", "tensor_tensor", reads=[("of", p), ("zz", p)], writes=[("of", p)], out=of[p][:], in0=of[p][:],
                     in1=zz[p][:], op=ALU.mult)
                S.dma("sp", reads=[("of", p)], is_out=True, out=ytok[:, b, n0:n0 + BLK, :], in_=of[p][:])
                yield

            pipeline((out_block(b * (N // BLK) + nb, b, nb) for b in range(BATCH) for nb in range(N // BLK)), NB3)
        S.replay()
    return nc


_PROGS = {}


def _prog(key, fn):
    if key not in _PROGS:
        _PROGS[key] = fn()
    return _PROGS[key]


def _run(nc, in_maps):
    res = run_bass_kernel_spmd(nc, in_maps, core_ids=list(range(NCORES)))
    return res.results


def _c(a):
    return np.ascontiguousarray(a, dtype=np.float32)


def _tok_shards_T(xf):
    return [_c(xf[c * TPC:(c + 1) * TPC].T) for c in range(NCORES)]


def _from_T(outs, name):
    return np.concatenate([r[name].T for r in outs], 0)


def _run_sc_layer(xf, p, li, j, final):
    nc = _prog(("sc", final), lambda: build_sc_prog(TPC, final))
    x3 = xf.reshape(BATCH, SEQ, D)
    zero = np.zeros((1, D), np.float32)
    in_maps = []
    for c in range(NCORES):
        b, s0 = divmod(c * TPC, SEQ)
        left = x3[b, s0 - 1:s0] if s0 > 0 else zero
        right = x3[b, s0 + TPC:s0 + TPC + 1] if s0 + TPC < SEQ else zero
        xs = np.concatenate([x3[b, s0:s0 + TPC], left, right], 0)
        in_maps.append({"xT": _c(xs.T), "nrm": _c(p["norms"][li]), "fw_in": _c(p["ffn_w_in"][li]),
                        "fw_out": _c(p["ffn_w_out"][li]), "w_in": _c(p["sc_w_in"][j]), "conv": _c(p["sc_conv"][j]),
                        "w_out": _c(p["sc_w_out"][j]), "gfin": _c(p["final_norm"])})
    return _from_T(_run(nc, in_maps), "yT")


def _run_pre(xf, p, li, w_in, b_in):
    nout = w_in.shape[1]
    nc = _prog(("pre", nout), lambda: build_pre_prog(nout, TPC))
    xs = _tok_shards_T(xf)
    in_maps = [{"xT": xs[c], "nrm": _c(p["norms"][li, 0:2]), "fw_in": _c(p["ffn_w_in"][li, 0]),
                "fw_out": _c(p["ffn_w_out"][li, 0]), "w_in": _c(w_in), "b_in": _c(b_in)} for c in range(NCORES)]
    outs = _run(nc, in_maps)
    x1 = _from_T(outs, "xo")
    u = np.concatenate([r["uT"] for r in outs], 1)
    return x1, u


def _run_post(xf, yfm, p, li, w_out, b_out):
    nc = _prog(("post",), lambda: build_post_prog(TPC))
    xs = _tok_shards_T(xf)
    in_maps = [{"xT": xs[c], "yT": _c(yfm[:, c * TPC:(c + 1) * TPC]), "w_out": _c(w_out), "b_out": _c(b_out),
                "nrm": _c(p["norms"][li, 2]), "fw_in": _c(p["ffn_w_in"][li, 1]), "fw_out": _c(p["ffn_w_out"][li, 1])}
               for c in range(NCORES)]
    return _from_T(_run(nc, in_maps), "xo")


def _run_hyena_core(u, p, j):
    nc = _prog(("hy",), build_hy_core_prog)
    cst, zT, trow, nad = hyena_consts()
    trow_rep = _c(np.broadcast_to(trow, (128, NFFT)))
    u4 = u.reshape(3, D, BATCH, SEQ)
    hc = p["hy_conv"][j].reshape(3, 3, D)
    cb = p["hy_conv_b"][j].reshape(3, D)
    w3 = p["hy_f_w3"][j].reshape(HY_ORD, 2, D)
    in_maps = []
    for c in range(NCORES):
        sl = slice(c * 128, (c + 1) * 128)
        in_maps.append({"u0": _c(u4[:, sl]), "convw": _c(hc[:, :, sl]), "convb": _c(cb[:, sl]), "dvec": _c(p["hy_d"][j][sl]),
                        "fw1": _c(p["hy_f_w1"][j]), "fb1": _c(p["hy_f_b1"][j]), "fw2": _c(p["hy_f_w2"][j]),
                        "fb2": _c(p["hy_f_b2"][j]), "fw3": _c(w3[:, :, sl]), "freq": _c(p["hy_f_freq"][j]), "cst": cst,
                        "zT": zT, "trow": trow_rep, "nad": _c(nad[sl])})
    outs = _run(nc, in_maps)
    return np.concatenate([r["yT"].reshape(128, BATCH * SEQ) for r in outs], 0)


def _run_gdn_core(u, p, j):
    nc = _prog(("gd",), build_gd_core_prog)
    mtri, mstrict, ident = gdn_consts()
    H = 8
    gcv = p["gd_conv"][j].reshape(3, 3, D)
    in_maps = []
    for h in range(NCORES):
        sl = slice(h * 128, (h + 1) * 128)
        qkv0 = u[0:3 * D].reshape(3, D, BATCH, SEQ)[:, sl]
        zfm = u[3 * D + h * 128: 3 * D + (h + 1) * 128]
        ztok = zfm.T.reshape(BATCH, NCH, CH, 128).transpose(2, 0, 1, 3)
        rows = [4 * D + 0 * H + h, 4 * D + 1 * H + h, 4 * D + 2 * H + 0 * H + h, 4 * D + 2 * H + 1 * H + h]
        abt = u[rows].reshape(4, BATCH, NCH, CH).transpose(0, 3, 1, 2)
        in_maps.append({"qkv0": _c(qkv0), "ztok": _c(ztok), "abt": _c(abt), "convw": _c(gcv[:, :, sl]),
                        "alog": _c(np.broadcast_to(p["gd_a_log"][j][:, h], (CH, 2))),
                        "dtb": _c(np.broadcast_to(p["gd_dt_bias"][j][:, h], (CH, 2))),
                        "gn": _c(np.broadcast_to(p["gd_norm"][j], (CH, 128))), "mtri": mtri, "mstrict": mstrict,
                        "ident": ident})
    outs = _run(nc, in_maps)
    return np.concatenate([r["ytok"].transpose(3, 1, 2, 0).reshape(128, BATCH * SEQ) for r in outs], 0)


def kernel(**inputs):
    p = {k: np.asarray(v, dtype=np.float32) for k, v in inputs.items()}
    xf = p["x"].reshape(BATCH * SEQ, D)
    xf = _run_sc_layer(xf, p, 0, 0, final=False)
    xf, u = _run_pre(xf, p, 1, p["hy_w_in"][0], p["hy_b_in"][0])
    yfm = _run_hyena_core(u, p, 0)
    xf = _run_post(xf, yfm, p, 1, p["hy_w_out"][0], p["hy_b_out"][0])
    nproj = p["gd_w_in"].shape[2]
    npad = ((nproj + 127) // 128) * 128
    w_in = np.zeros((D, npad), np.float32)
    w_in[:, :nproj] = p["gd_w_in"][0]
    xf, u = _run_pre(xf, p, 2, w_in, np.zeros((npad,), np.float32))
    yfm = _run_gdn_core(u, p, 0)
    xf = _run_post(xf, yfm, p, 2, p["gd_w_out"][0], np.zeros((D,), np.float32))
    xf = _run_sc_layer(xf, p, 3, 1, final=True)
    return np.ascontiguousarray(xf.reshape(BATCH, SEQ, D).astype(np.float32))
```

```python
import contextlib
import math
import numpy as np
import concourse.bass as bass
import concourse.mybir as mybir
from concourse.bass_utils import run_bass_kernel_spmd

F32 = mybir.dt.float32
BF16 = mybir.dt.bfloat16
AF = mybir.ActivationFunctionType
ALU = mybir.AluOpType
AX = mybir.AxisListType

D = 1024
KC = 8
FF = 2816
JC = 22
NCORES = 8
BATCH = 2
SEQ = 8192
TPC = BATCH * SEQ // NCORES
EPS = 1e-6

ENGS = ("pe", "act", "dve", "pool", "sp")
NDMA_SEM = 20


class Sched:
    def __init__(self, nc, es):
        self.nc = nc
        self.q = {e: [] for e in ENGS}
        self.cnt = {e: 0 for e in ENGS}
        self.seen = {e: {} for e in ENGS}
        self.buf = {}
        self.sems = {}
        for e in ENGS:
            self.sems[("E", e)] = es.enter_context(nc.semaphore("sem_" + e))
        self.dma_rr = {e: 0 for e in ENGS}
        self.dma_val = {}
        for e in ("sp", "pool", "act"):
            for i in range(NDMA_SEM):
                k = ("D", e, i)
                self.sems[k] = es.enter_context(nc.semaphore("dsem_%s_%d" % (e, i)))
                self.dma_val[k] = 0
        self.out_tokens = []
        self.excl = set()

    def _deps(self, eng, reads, writes):
        deps = {}

        def add(tok):
            if tok is None:
                return
            k, v = tok
            if deps.get(k, 0) < v:
                deps[k] = v

        for k in reads:
            b = self.buf.get(k)
            if b:
                add(b["w"])
                if k in self.excl:
                    for rk, rv in b["r"].items():
                        if rk != ("E", eng):
                            add((rk, rv))
        for k in writes:
            b = self.buf.get(k)
            if b:
                add(b["w"])
                for rk, rv in b["r"].items():
                    add((rk, rv))
        waits = []
        for k, v in deps.items():
            if eng == "pe" and k == ("E", "pe"):
                continue
            if self.seen[eng].get(k, 0) >= v:
                continue
            self.seen[eng][k] = v
            waits.append((k, v))
        return waits

    def _record(self, tok, reads, writes):
        for k in reads:
            b = self.buf.setdefault(k, {"w": None, "r": {}})
            if b["r"].get(tok[0], 0) < tok[1]:
                b["r"][tok[0]] = tok[1]
        for k in writes:
            self.buf[k] = {"w": tok, "r": {}}

    def op(self, eng, name, reads=(), writes=(), **kw):
        fn = (name, kw)
        waits = self._deps(eng, reads, writes)
        self.cnt[eng] += 1
        tok = (("E", eng), self.cnt[eng])
        self.q[eng].append((waits, fn, tok, 1))
        self._record(tok, reads, writes)
        return tok

    def dma(self, eng, reads=(), writes=(), is_out=False, **kw):
        fn = ("dma_start", kw)
        waits = self._deps(eng, reads, writes)
        i = self.dma_rr[eng]
        self.dma_rr[eng] = (i + 1) % NDMA_SEM
        k = ("D", eng, i)
        prev = self.dma_val[k]
        if prev and self.seen[eng].get(k, 0) < prev:
            self.seen[eng][k] = prev
            waits.append((k, prev))
        self.dma_val[k] = prev + 16
        tok = (k, prev + 16)
        self.q[eng].append((waits, fn, tok, 16))
        self._record(tok, reads, writes)
        if is_out:
            self.out_tokens.append(tok)
        return tok

    def barrier(self):
        allv = [(("E", f), self.cnt[f]) for f in ENGS if self.cnt[f]]
        allv += [(k, v) for k, v in self.dma_val.items() if v]
        for e in ENGS:
            waits = []
            for k, v in allv:
                if k == ("E", e) and e == "pe":
                    continue
                if self.seen[e].get(k, 0) >= v:
                    continue
                self.seen[e][k] = v
                waits.append((k, v))
            if waits:
                self.q[e].append((waits, None, None, 0))
        self.buf = {}

    def mm(self, out, lhsT, rhs, start, stop, reads, writes):
        return self.op("pe", "matmul", reads, writes, out=out, lhsT=lhsT, rhs=rhs, start=start, stop=stop)

    def replay(self):
        nc = self.nc
        fin = list(self.out_tokens)
        with nc.Block() as block:
            def run(engname, eng):
                for waits, fn, tok, inc in self.q[engname]:
                    for k, v in waits:
                        eng.wait_ge(self.sems[k], v)
                    if fn is None:
                        continue
                    ins = getattr(eng, fn[0])(**fn[1])
                    ins.then_inc(self.sems[tok[0]], inc)
                if engname == "sp":
                    for k, v in fin:
                        eng.wait_ge(self.sems[k], v)

            @block.tensor
            def _(e):
                run("pe", e)

            @block.scalar
            def _(e):
                run("act", e)

            @block.vector
            def _(e):
                run("dve", e)

            @block.gpsimd
            def _(e):
                run("pool", e)

            @block.sync
            def _(e):
                run("sp", e)


class Ctx:
    def __init__(self, nc, es):
        self.nc = nc
        self.es = es
        self.S = Sched(nc, es)
        self.n = 0
        self.scopes = [es]

    def sb(self, name, shape, dt):
        self.n += 1
        return self.scopes[-1].enter_context(self.nc.sbuf_tensor("%s_%d" % (name, self.n), shape, dt))

    def ps(self, name, shape, dt=F32):
        self.n += 1
        return self.scopes[-1].enter_context(self.nc.psum_tensor("%s_%d" % (name, self.n), shape, dt))

    @contextlib.contextmanager
    def scope(self):
        with contextlib.ExitStack() as s:
            self.scopes.append(s)
            try:
                yield
            finally:
                self.S.barrier()
                self.scopes.pop()


def interleave(gens):
    gens = list(gens)
    while gens:
        for g_ in list(gens):
            try:
                next(g_)
            except StopIteration:
                gens.remove(g_)


def pipeline(gens, width):
    gens = iter(gens)
    active = []
    done = False
    while True:
        while not done and len(active) < width:
            try:
                active.append(next(gens))
            except StopIteration:
                done = True
        if not active:
            return
        for g_ in list(active):
            try:
                next(g_)
            except StopIteration:
                active.remove(g_)


def dram_in(nc, name, shape, dt=F32):
    return nc.dram_tensor(name, list(shape), dt, kind="ExternalInput").ap()


def dram_out(nc, name, shape, dt=F32):
    return nc.dram_tensor(name, list(shape), dt, kind="ExternalOutput").ap()


def emit_consts(C):
    ones = C.sb("ones", [128, 128], F32)
    C.S.op("dve", "memset", writes=["ones"], ap=ones[:], constant=1.0)
    C.ones = ones
    C.rn_sq = [C.sb("rn_sq%d" % i, [128, 512], F32) for i in range(3)]
    C.rn_rs = [C.sb("rn_rs%d" % i, [128, 512], F32) for i in range(2)]
    C.rn_ps = [C.ps("rn_ps%d" % i, [128, 512], F32) for i in range(1)]
    C.rn_i = 0


def load_vec_pk(C, name, vec_dram, nchunk, eng="sp"):
    t = C.sb(name, [128, nchunk], F32)
    C.S.dma(eng, writes=[name], out=t[:], in_=vec_dram.rearrange("(kc p) -> p kc", p=128),
            allow_slow_non_contiguous=True)
    return t


def emit_rmsnorm(C, x, xk, t0, ntok, g_sb, gk, hn, hk, hoff=0):
    S = C.S
    nt = (ntok + 511) // 512
    for tt in range(nt):
        n = min(512, ntok - tt * 512)
        c0 = t0 + tt * 512
        xkeys = [(xk, k, c0 // 512) for k in range(KC)]
        if (c0 % 512) + n > 512:
            xkeys += [(xk, k, c0 // 512 + 1) for k in range(KC)]
        ps = C.rn_ps[0]
        for k in range(KC):
            C.rn_i += 1
            sq = C.rn_sq[C.rn_i % 3]
            sqk = ("rn_sq", C.rn_i % 3)
            S.op("act", "activation", reads=[kk for kk in xkeys if kk[1] == k], writes=[sqk],
                 out=sq[:, :n], in_=x[:, k, c0:c0 + n], func=AF.Square)
            S.mm(ps[:, :n], C.ones[:], sq[:, :n], k == 0, k == KC - 1, ["ones", sqk], ["rn_ps"])
        C.rn_i += 1
        rs = C.rn_rs[C.rn_i % 2]
        rsk = ("rn_rs", C.rn_i % 2)
        S.op("dve", "tensor_scalar", reads=["rn_ps"], writes=[rsk], out=rs[:, :n], in0=ps[:, :n],
             scalar1=1.0 / D, scalar2=EPS, op0=ALU.mult, op1=ALU.add)
        S.op("act", "activation", reads=[rsk], writes=[rsk], out=rs[:, :n], in_=rs[:, :n], func=AF.Sqrt)
        S.op("dve", "reciprocal", reads=[rsk], writes=[rsk], out=rs[:, :n], in_=rs[:, :n])
        for k in range(KC):
            eng = "dve"
            S.op(eng, "scalar_tensor_tensor", reads=[kk for kk in xkeys if kk[1] == k] + [rsk, gk],
                 writes=[(hk, k, tt)], out=hn[:, k, hoff + tt * 512: hoff + tt * 512 + n], in0=x[:, k, c0:c0 + n],
                 scalar=g_sb[:, k:k + 1], in1=rs[:, :n], op0=ALU.mult, op1=ALU.mult)


def emit_ffn(C, x, groups, g_dram, w_in, w_out, pref):
    S = C.S
    TG = 1024
    w_in_v = w_in.rearrange("(kc p) n -> p kc n", p=128)
    w_out_v = w_out.rearrange("(jc p) n -> p jc n", p=128)
    with C.scope():
        g_sb = load_vec_pk(C, pref + "g", g_dram, KC)
        gk = pref + "g"
        hns = [C.sb("ffn_hn%d" % i, [128, KC, TG], BF16) for i in range(min(2, len(groups)))]
        act = C.sb("ffn_act", [128, JC, TG], BF16)
        wbuf = [C.sb("ffn_wi%d" % i, [128, KC, 256], BF16) for i in range(3)]
        wobuf = [C.sb("ffn_wo%d" % i, [128, JC, 128], BF16) for i in range(2)]
        sgb = [C.sb("ffn_sg%d" % i, [128, 512], F32) for i in range(2)]
        pg = [C.ps("ffn_pg%d" % i, [128, 512]) for i in range(2)]
        pu = [C.ps("ffn_pu%d" % i, [128, 512]) for i in range(2)]
        po = [C.ps("ffn_po%d" % i, [128, 512]) for i in range(2)]
        it = 0
        io = 0
        for gi_, (t0, ntok) in enumerate(groups):
            ntg = (ntok + 511) // 512
            hn = hns[gi_ % 2]
            hnk = "ffn_hn%d" % (gi_ % 2)
            if gi_ == 0:
                emit_rmsnorm(C, x, "x", t0, ntok, g_sb, gk, hn, hnk)
            for j in range(JC):
                wb = wbuf[j % 3]
                wk = ("ffn_wi", j % 3)
                S.dma("pool", writes=[wk + (0,)], out=wb[:, :, 0:128], in_=w_in_v[:, :, j * 128:(j + 1) * 128])
                S.dma("pool", writes=[wk + (1,)], out=wb[:, :, 128:256],
                      in_=w_in_v[:, :, FF + j * 128: FF + (j + 1) * 128])
                for tt in range(ntg):
                    n = min(512, ntok - tt * 512)
                    it += 1
                    b = it % 2
                    sl = slice(tt * 512, tt * 512 + n)
                    for k in range(KC):
                        S.mm(pg[b][:, :n], wb[:, k, 0:128], hn[:, k, sl], k == 0, k == KC - 1,
                             [wk + (0,), (hnk, k, tt)], [("ffn_pg", b)])
                    for k in range(KC):
                        S.mm(pu[b][:, :n], wb[:, k, 128:256], hn[:, k, sl], k == 0, k == KC - 1,
                             [wk + (1,), (hnk, k, tt)], [("ffn_pu", b)])
                    S.op("act", "activation", reads=[("ffn_pg", b)], writes=[("ffn_sg", b)],
                         out=sgb[b][:, :n], in_=pg[b][:, :n], func=AF.Silu)
                    S.op("dve", "tensor_tensor", reads=[("ffn_pu", b), ("ffn_sg", b)], writes=[("ffn_act", j, tt)],
                         out=act[:, j, sl], in0=pu[b][:, :n], in1=sgb[b][:, :n], op=ALU.mult)
            if gi_ + 1 < len(groups):
                t1, n1 = groups[gi_ + 1]
                emit_rmsnorm(C, x, "x", t1, n1, g_sb, gk, hns[(gi_ + 1) % 2], "ffn_hn%d" % ((gi_ + 1) % 2))
            for m in range(KC):
                wo = wobuf[m % 2]
                wok = ("ffn_wo", m % 2)
                S.dma("pool", writes=[wok], out=wo[:], in_=w_out_v[:, :, m * 128:(m + 1) * 128])
                for tt in range(ntg):
                    n = min(512, ntok - tt * 512)
                    io += 1
                    b = io % 2
                    sl = slice(tt * 512, tt * 512 + n)
                    gsl = slice(t0 + tt * 512, t0 + tt * 512 + n)
                    for j in range(JC):
                        S.mm(po[b][:, :n], wo[:, j, :], act[:, j, sl], j == 0, j == JC - 1,
                             [wok, ("ffn_act", j, tt)], [("ffn_po", b)])
                    xkey = ("x", m, (t0 + tt * 512) // 512)
                    S.op("dve", "scalar_tensor_tensor", reads=[("ffn_po", b), xkey], writes=[xkey],
                         out=x[:, m, gsl], in0=po[b][:, :n], scalar=0.5, in1=x[:, m, gsl], op0=ALU.mult, op1=ALU.add)


def emit_sc_mixer(C, x, T, g_dram, w_in, conv, w_out):
    S = C.S
    NT = T // 512
    w_in_v = w_in.rearrange("(kc p) n -> p kc n", p=128)
    w_out_v = w_out.rearrange("(kc p) n -> p kc n", p=128)
    with C.scope():
        g_sb = load_vec_pk(C, "sc_g", g_dram, KC)
        cw = C.sb("sc_cw", [128, KC, 3], F32)
        for j in range(3):
            S.dma("sp", writes=[("sc_cw", j)], out=cw[:, :, j], in_=conv[j].rearrange("(i p) -> p i", p=128),
                  allow_slow_non_contiguous=True)
        hn = C.sb("sc_hn", [128, KC, T + 2], BF16)
        ybf = C.sb("sc_y", [128, KC, T], BF16)
        chb = [C.sb("sc_ch%d" % i, [128, T + 2], F32) for i in range(2)]
        bsv = [C.sb("sc_b%d" % i, [128, T], F32) for i in range(2)]
        csb = [C.sb("sc_c%d" % i, [128, 512], F32) for i in range(2)]
        acc = [C.sb("sc_acc%d" % i, [128, 512], F32) for i in range(2)]
        wbuf = [C.sb("sc_wi%d" % i, [128, KC, 384], BF16) for i in range(2)]
        wobuf = [C.sb("sc_wo%d" % i, [128, KC, 128], BF16) for i in range(2)]
        pb = [C.ps("sc_pb%d" % i, [128, 512]) for i in range(2)]
        pc = [C.ps("sc_pc%d" % i, [128, 512]) for i in range(2)]
        ph = [C.ps("sc_ph%d" % i, [128, 512]) for i in range(2)]
        po = [C.ps("sc_po%d" % i, [128, 512]) for i in range(1)]
        emit_rmsnorm(C, x, "x", 0, T + 2, g_sb, "sc_g", hn, "sc_hn")
        it = 0
        for i in range(KC):
            wb = wbuf[i % 2]
            wk = ("sc_wi", i % 2)
            for q in range(3):
                S.dma("pool", writes=[wk + (q,)], out=wb[:, :, q * 128:(q + 1) * 128],
                      in_=w_in_v[:, :, q * D + i * 128: q * D + (i + 1) * 128])
            ch = chb[i % 2]
            bs = bsv[i % 2]
            for tt in range(NT + 1):
                n = 512 if tt < NT else 2
                it += 1
                b = it % 2
                sl = slice(tt * 512, tt * 512 + n)
                hkeys = lambda k: [("sc_hn", k, tt)]
                if tt < NT:
                    for k in range(KC):
                        S.mm(pb[b][:, :n], wb[:, k, 0:128], hn[:, k, sl], k == 0, k == KC - 1,
                             [wk + (0,)] + hkeys(k), [("sc_pb", b)])
                for k in range(KC):
                    S.mm(pc[b][:, :n], wb[:, k, 128:256], hn[:, k, sl], k == 0, k == KC - 1,
                         [wk + (1,)] + hkeys(k), [("sc_pc", b)])
                for k in range(KC):
                    S.mm(ph[b][:, :n], wb[:, k, 256:384], hn[:, k, sl], k == 0, k == KC - 1,
                         [wk + (2,)] + hkeys(k), [("sc_ph", b)])
                S.op("act", "activation", reads=[("sc_pc", b)], writes=[("sc_c", b)],
                     out=csb[b][:, :n], in_=pc[b][:, :n], func=AF.Copy)
                if tt < NT:
                    S.op("dve", "tensor_tensor", reads=[("sc_c", b), ("sc_ph", b)], writes=[("sc_ch", i % 2, tt)],
                         out=ch[:, 1 + tt * 512: 1 + tt * 512 + n], in0=ph[b][:, :n], in1=csb[b][:, :n], op=ALU.mult)
                    S.op("act", "activation", reads=[("sc_pb", b)], writes=[("sc_b", i % 2, tt)],
                         out=bs[:, sl], in_=pb[b][:, :n], func=AF.Copy)
                else:
                    S.op("dve", "tensor_tensor", reads=[("sc_c", b), ("sc_ph", b)], writes=[("sc_ch", i % 2, "hl")],
                         out=ch[:, 0:1], in0=ph[b][:, 0:1], in1=csb[b][:, 0:1], op=ALU.mult)
                    S.op("dve", "tensor_tensor", reads=[("sc_c", b), ("sc_ph", b)], writes=[("sc_ch", i % 2, "hr")],
                         out=ch[:, T + 1:T + 2], in0=ph[b][:, 1:2], in1=csb[b][:, 1:2], op=ALU.mult)
            for tt in range(NT):
                a = acc[tt % 2]
                ak = ("sc_acc", tt % 2)
                rk = [("sc_ch", i % 2, tt)]
                if tt > 0:
                    rk.append(("sc_ch", i % 2, tt - 1))
                else:
                    rk.append(("sc_ch", i % 2, "hl"))
                if tt < NT - 1:
                    rk.append(("sc_ch", i % 2, tt + 1))
                else:
                    rk.append(("sc_ch", i % 2, "hr"))
                o = tt * 512
                S.op("dve", "tensor_scalar", reads=rk + [("sc_cw", 0)], writes=[ak], out=a[:], in0=ch[:, o:o + 512],
                     scalar1=cw[:, i, 0:1], scalar2=None, op0=ALU.mult)
                S.op("dve", "scalar_tensor_tensor", reads=rk + [("sc_cw", 1), ak], writes=[ak], out=a[:],
                     in0=ch[:, o + 1:o + 513], scalar=cw[:, i, 1:2], in1=a[:], op0=ALU.mult, op1=ALU.add)
                S.op("dve", "scalar_tensor_tensor", reads=rk + [("sc_cw", 2), ak], writes=[ak], out=a[:],
                     in0=ch[:, o + 2:o + 514], scalar=cw[:, i, 2:3], in1=a[:], op0=ALU.mult, op1=ALU.add)
                S.op("dve", "tensor_tensor", reads=[ak, ("sc_b", i % 2, tt)], writes=[("sc_y", i, tt)],
                     out=ybf[:, i, o:o + 512], in0=a[:], in1=bs[:, o:o + 512], op=ALU.mult)
        for m in range(KC):
            wo = wobuf[m % 2]
            wok = ("sc_wo", m % 2)
            S.dma("pool", writes=[wok], out=wo[:], in_=w_out_v[:, :, m * 128:(m + 1) * 128])
            for tt in range(NT):
                sl = slice(tt * 512, (tt + 1) * 512)
                for i in range(KC):
                    S.mm(po[0][:], wo[:, i, :], ybf[:, i, sl], i == 0, i == KC - 1, [wok, ("sc_y", i, tt)], ["sc_po"])
                xkey = ("x", m, tt)
                S.op("dve", "tensor_tensor", reads=["sc_po", xkey], writes=[xkey], out=x[:, m, sl], in0=po[0][:],
                     in1=x[:, m, sl], op=ALU.add)


def emit_final_norm(C, x, T, g_dram):
    with C.scope():
        g_sb = load_vec_pk(C, "fin_g", g_dram, KC)
        emit_rmsnorm(C, x, "x", 0, T, g_sb, "fin_g", x, "x")


def emit_load_x(C, x, xT_dram, T):
    v = xT_dram.rearrange("(kc p) t -> p kc t", p=128)
    for k in range(KC):
        for tt in range((T + 511) // 512):
            n = min(512, T - tt * 512)
            C.S.dma("sp", writes=[("x", k, tt)], out=x[:, k, tt * 512:tt * 512 + n],
                    in_=v[:, k, tt * 512:tt * 512 + n])


def emit_store_x(C, x, yT_dram, T):
    v = yT_dram.rearrange("(kc p) t -> p kc t", p=128)
    for k in range(KC):
        for tt in range(T // 512):
            C.S.dma("sp", reads=[("x", k, tt)], is_out=True, out=v[:, k, tt * 512:(tt + 1) * 512],
                    in_=x[:, k, tt * 512:(tt + 1) * 512])


def build_ffn_prog(T=TPC):
    nc = bass.Bass("TRN2", target_bir_lowering=False)
    xT = dram_in(nc, "xT", [D, T])
    g = dram_in(nc, "g", [D])
    w_in = dram_in(nc, "w_in", [D, 2 * FF])
    w_out = dram_in(nc, "w_out", [FF, D])
    yT = dram_out(nc, "yT", [D, T])
    with contextlib.ExitStack() as es:
        C = Ctx(nc, es)
        emit_consts(C)
        x = C.sb("x", [128, KC, T], F32)
        emit_load_x(C, x, xT, T)
        emit_ffn(C, x, [(t, 1024) for t in range(0, T, 1024)], g, w_in, w_out, "f")
        emit_store_x(C, x, yT, T)
        C.S.replay()
    return nc


def build_sc_prog(T=TPC, final=False):
    nc = bass.Bass("TRN2", target_bir_lowering=False)
    xT = dram_in(nc, "xT", [D, T + 2])
    nrm = dram_in(nc, "nrm", [3, D])
    fw_in = dram_in(nc, "fw_in", [2, D, 2 * FF])
    fw_out = dram_in(nc, "fw_out", [2, FF, D])
    w_in = dram_in(nc, "w_in", [D, 3 * D])
    conv = dram_in(nc, "conv", [3, D])
    w_out = dram_in(nc, "w_out", [D, D])
    gfin = dram_in(nc, "gfin", [D])
    yT = dram_out(nc, "yT", [D, T])
    with contextlib.ExitStack() as es:
        C = Ctx(nc, es)
        emit_consts(C)
        x = C.sb("x", [128, KC, T + 2], F32)
        emit_load_x(C, x, xT, T + 2)
        grp = [(t, 1024) for t in range(0, T, 1024)]
        emit_ffn(C, x, grp + [(T, 2)], nrm[0], fw_in[0], fw_out[0], "f1")
        emit_sc_mixer(C, x, T, nrm[1], w_in, conv, w_out)
        emit_ffn(C, x, grp, nrm[2], fw_in[1], fw_out[1], "f2")
        if final:
            emit_final_norm(C, x, T, gfin)
        emit_store_x(C, x, yT, T)
        C.S.replay()
    return nc

NFFT = 2 * SEQ
HY_EMB = 33
HY_ORD = 64
MAGIC = 12582912.0
TWO_PI = 2.0 * math.pi
PI_LO = 3.1415925


def hyena_consts():
    n = np.arange(128, dtype=np.float64)
    ang = 2.0 * np.pi * np.outer(n, n) / 128.0
    fre, fim = np.cos(ang), -np.sin(ang)
    angt = 2.0 * np.pi * np.outer(n, n) / NFFT
    tre, tim = np.cos(angt), -np.sin(angt)
    cst = np.stack([fim, fre, -fim, tre, tim], 1).astype(np.float32)
    L = SEQ
    f32 = np.float32
    t = np.linspace(0.0, 1.0, L, dtype=f32)
    w = (f32(2.0 * math.pi) * np.arange(L, dtype=f32) / f32(L)).astype(f32)
    f = np.linspace(1e-4, 15.0, 16, dtype=f32)
    fw = (f[None, :] * w[:, None]).astype(f32)
    z = np.concatenate([t[:, None], np.cos(fw), -np.sin(fw)], -1).astype(f32)
    idx = np.concatenate([[0], np.arange(L - 1, 0, -1)])
    z2 = z[idx]
    t2 = t[idx].copy()
    t2[0] = 1e30
    zT = np.ascontiguousarray(np.concatenate([z, z2], 0).T)
    trow = np.concatenate([t, t2]).astype(f32)
    dmin = math.log(1e-2) / 1.5
    dmax = math.log(1e-2) / 0.3
    deltas = np.linspace(dmin, dmax, D, dtype=f32)
    nad = (-np.abs(deltas)).astype(f32)
    return cst, zT, trow, nad


def emit_fft_fwd(C, X, xkey, K, nseq, cst, tl, ps, kp=""):
    S = C.S
    fimfre = cst[:K, 0:2, :].rearrange("p a b -> p (a b)")
    for s_ in range(nseq):
        bank = ps["a"][s_ // 2]
        S.mm(bank[:, (s_ % 2) * 256:(s_ % 2) * 256 + 256], X[:K, s_, :], fimfre, True, True,
             [xkey, "cst"], [(kp + "psa", s_ // 2)])
    yield
    tre = cst[:, 3:4, :]
    tim = cst[:, 4:5, :]
    for h in range((nseq + 1) // 2):
        ns = min(2, nseq - 2 * h)
        av = ps["a"][h][:].rearrange("p (s r k) -> p s r k", s=2, r=2)
        aim = av[:, :ns, 0, :]
        are = av[:, :ns, 1, :]
        sl = slice(2 * h, 2 * h + ns)
        bt = lambda t_: t_.to_broadcast([128, ns, 128])
        S.op("dve", "tensor_tensor", reads=[(kp + "psa", h), "cst"], writes=[(kp + "t1", h)], out=tl["t1"][:, sl, :], in0=are,
             in1=bt(tre), op=ALU.mult)
        S.op("dve", "tensor_tensor", reads=[(kp + "psa", h), "cst"], writes=[(kp + "t2", h)], out=tl["t2"][:, sl, :], in0=aim,
             in1=bt(tim), op=ALU.mult)
        S.op("dve", "tensor_tensor", reads=[(kp + "psa", h), "cst"], writes=[(kp + "t3", h)], out=tl["t3"][:, sl, :], in0=are,
             in1=bt(tim), op=ALU.mult)
        S.op("dve", "tensor_tensor", reads=[(kp + "psa", h), "cst"], writes=[(kp + "t4", h)], out=tl["t4"][:, sl, :], in0=aim,
             in1=bt(tre), op=ALU.mult)
    hs = list(range((nseq + 1) // 2))
    S.op("pool", "tensor_tensor", reads=[(kp + "t1", h) for h in hs] + [(kp + "t2", h) for h in hs], writes=[kp + "bre"],
         out=tl["bre"][:, :nseq, :], in0=tl["t1"][:, :nseq, :], in1=tl["t2"][:, :nseq, :], op=ALU.subtract)
    S.op("pool", "tensor_tensor", reads=[(kp + "t3", h) for h in hs] + [(kp + "t4", h) for h in hs], writes=[kp + "bim"],
         out=tl["bim"][:, :nseq, :], in0=tl["t3"][:, :nseq, :], in1=tl["t4"][:, :nseq, :], op=ALU.add)
    yield
    n = nseq * 128
    bre = tl["bre"][:].rearrange("p s k -> p (s k)")[:, :n]
    bim = tl["bim"][:].rearrange("p s k -> p (s k)")[:, :n]
    S.mm(ps["xre"][:, :n], cst[:, 1, :], bre, True, False, ["cst", kp + "bre"], [kp + "psxre"])
    S.mm(ps["xre"][:, :n], cst[:, 2, :], bim, False, True, ["cst", kp + "bim"], [kp + "psxre"])
    S.mm(ps["xim"][:, :n], cst[:, 1, :], bim, True, False, ["cst", kp + "bim"], [kp + "psxim"])
    S.mm(ps["xim"][:, :n], cst[:, 0, :], bre, False, True, ["cst", kp + "bre"], [kp + "psxim"])
    yield


def build_hy_core_prog():
    nc = bass.Bass("TRN2", target_bir_lowering=False)
    L = SEQ
    u0 = dram_in(nc, "u0", [3, 128, BATCH, L])
    convw = dram_in(nc, "convw", [3, 3, 128])
    convb = dram_in(nc, "convb", [3, 128])
    dvec = dram_in(nc, "dvec", [128])
    fw1 = dram_in(nc, "fw1", [HY_EMB, HY_ORD])
    fb1 = dram_in(nc, "fb1", [HY_ORD])
    fw2 = dram_in(nc, "fw2", [HY_ORD, HY_ORD])
    fb2 = dram_in(nc, "fb2", [HY_ORD])
    fw3 = dram_in(nc, "fw3", [HY_ORD, 2, 128])
    freq = dram_in(nc, "freq", [HY_ORD])
    cstd = dram_in(nc, "cst", [128, 5, 128])
    zT = dram_in(nc, "zT", [HY_EMB, NFFT])
    trow = dram_in(nc, "trow", [128, NFFT])
    nad = dram_in(nc, "nad", [128])
    yT = dram_out(nc, "yT", [128, BATCH, L])
    s_h = nc.dram_tensor("s_h", [128, NFFT], F32).ap()
    s_H = nc.dram_tensor("s_H", [2, 128, 128, 128], F32).ap()
    s_vv = nc.dram_tensor("s_vv", [128, BATCH, L], F32).ap()
    s_x0 = nc.dram_tensor("s_x0", [128, BATCH, L], F32).ap()
    s_y = nc.dram_tensor("s_y", [128, BATCH, L], F32).ap()
    with contextlib.ExitStack() as es:
        C = Ctx(nc, es)
        S = C.S
        cst = C.sb("cst", [128, 5, 128], F32)
        S.dma("sp", writes=["cst"], out=cst[:], in_=cstd)

        def col(name, src, n):
            t_ = C.sb(name, [n, 1], F32)
            S.dma("sp", writes=[name], out=t_[:], in_=src.rearrange("(p o) -> p o", o=1))
            return t_

        with C.scope():
            w1 = C.sb("w1", [HY_EMB, HY_ORD], F32)
            S.dma("sp", writes=["w1"], out=w1[:], in_=fw1)
            w2 = C.sb("w2", [HY_ORD, HY_ORD], F32)
            S.dma("sp", writes=["w2"], out=w2[:], in_=fw2)
            w3 = C.sb("w3", [HY_ORD, 2, 128], F32)
            S.dma("sp", writes=["w3"], out=w3[:], in_=fw3)
            fq = col("fq", freq, HY_ORD)
            b1 = col("b1", fb1, HY_ORD)
            b2 = col("b2", fb2, HY_ORD)
            nadc = col("nadc", nad, 128)
            S.op("dve", "tensor_tensor", reads=["fq", "b1"], writes=["b1"], out=b1[:], in0=b1[:], in1=fq[:], op=ALU.mult)
            S.op("dve", "tensor_tensor", reads=["fq", "b2"], writes=["b2"], out=b2[:], in0=b2[:], in1=fq[:], op=ALU.mult)
            zt = [C.sb("zt%d" % i, [HY_EMB, 512], F32) for i in range(2)]
            tr = [C.sb("tr%d" % i, [128, 512], F32) for i in range(2)]
            NS = 2
            av = [[C.sb("av%d%d" % (l_, i), [HY_ORD, 512], F32) for i in range(NS)] for l_ in range(2)]
            qv = [[C.sb("qv%d%d" % (l_, i), [HY_ORD, 512], F32) for i in range(NS)] for l_ in range(2)]
            hv = [[C.sb("hv%d%d" % (l_, i), [HY_ORD, 512], F32) for i in range(NS)] for l_ in range(2)]
            hc = [C.sb("hc%d" % i, [128, 512], F32) for i in range(NS)]
            p1 = [C.ps("p1%d" % i, [HY_ORD, 512]) for i in range(NS)]
            p2 = [C.ps("p2%d" % i, [HY_ORD, 512]) for i in range(NS)]
            p3 = [C.ps("p3%d" % i, [128, 512]) for i in range(NS)]

            def sin_layer(psrc, pkey, bias, sl_, lay):
                a, q, h = av[lay][sl_], qv[lay][sl_], hv[lay][sl_]
                ak, qk, hk = ("av", lay, sl_), ("qv", lay, sl_), ("hv", lay, sl_)
                S.op("dve", "tensor_scalar", reads=[pkey, "fq", "b1", "b2"], writes=[ak], out=a[:], in0=psrc[:],
                     scalar1=fq[:, 0:1], scalar2=bias[:, 0:1], op0=ALU.mult, op1=ALU.add)
                S.op("dve", "tensor_scalar", reads=[ak], writes=[qk], out=q[:], in0=a[:], scalar1=1.0 / TWO_PI,
                     scalar2=MAGIC, op0=ALU.mult, op1=ALU.add)
                S.op("dve", "tensor_scalar", reads=[qk], writes=[qk], out=q[:], in0=q[:], scalar1=-MAGIC,
                     scalar2=-TWO_PI, op0=ALU.add, op1=ALU.mult)
                S.op("dve", "tensor_tensor", reads=[qk, ak], writes=[ak], out=a[:], in0=a[:], in1=q[:], op=ALU.add)
                S.op("dve", "tensor_scalar", reads=[ak], writes=[ak], out=a[:], in0=a[:], scalar1=-PI_LO,
                     scalar2=PI_LO, op0=ALU.max, op1=ALU.min)
                yield
                S.op("act", "activation", reads=[ak], writes=[hk], out=h[:], in_=a[:], func=AF.Sin)
                yield

            def p0_tile(ti):
                c0 = ti * 512
                sl_ = ti % NS
                z_, zk = zt[sl_], ("zt", sl_)
                t_, tk = tr[sl_], ("tr", sl_)
                S.dma("sp", writes=[zk], out=z_[:], in_=zT[:, c0:c0 + 512])
                S.dma("sp", writes=[tk], out=t_[:], in_=trow[:, c0:c0 + 512])
                S.mm(p1[sl_][:], w1[:], z_[:], True, True, ["w1", zk], [("p1", sl_)])
                yield
                for _ in sin_layer(p1[sl_], ("p1", sl_), b1, sl_, 0):
                    yield
                S.mm(p2[sl_][:], w2[:], hv[0][sl_][:], True, True, ["w2", ("hv", 0, sl_)], [("p2", sl_)])
                S.op("act", "activation", reads=[tk, "nadc"], writes=[tk], out=t_[:], in_=t_[:], func=AF.Exp,
                     scale=nadc[:, 0:1])
                yield
                for _ in sin_layer(p2[sl_], ("p2", sl_), b2, sl_, 1):
                    yield
                half = 0 if ti < (L // 512) else 1
                S.mm(p3[sl_][:], w3[:, half, :], hv[1][sl_][:], True, True, ["w3", ("hv", 1, sl_)], [("p3", sl_)])
                yield
                o_, ok = hc[sl_], ("hc", sl_)
                S.op("dve", "tensor_tensor", reads=[("p3", sl_), tk], writes=[ok], out=o_[:], in0=p3[sl_][:], in1=t_[:],
                     op=ALU.mult)
                S.dma("sp", reads=[ok], writes=[("s_h", ti)], out=s_h[:, c0:c0 + 512], in_=o_[:])
                yield

            pipeline((p0_tile(ti) for ti in range(NFFT // 512)), NS)

        with C.scope():
            cw = C.sb("hcw", [128, 3, 3], F32)
            for j in range(3):
                for gi in range(3):
                    S.dma("sp", writes=[("hcw", j, gi)], out=cw[:, gi, j:j + 1],
                          in_=convw[j, gi].rearrange("(p o) -> p o", o=1))
            cb = C.sb("hcb", [128, 3], F32)
            for gi in range(3):
                S.dma("sp", writes=[("hcb", gi)], out=cb[:, gi:gi + 1], in_=convb[gi].rearrange("(p o) -> p o", o=1))
            cwk = [("hcw", j, gi) for j in range(3) for gi in range(3)] + [("hcb", gi) for gi in range(3)]
            ub = [C.sb("hu%d" % i, [128, L + 2], F32) for i in range(2)] * 2
            uc = [C.sb("huc%d" % i, [128, L], F32) for i in range(3)]
            for b in range(BATCH):
                for gi in range(3):
                    S.op("pool", "memset", writes=[("hu", gi % 2, "e")], ap=ub[gi][:, 0:1], constant=0.0)
                    S.op("pool", "memset", writes=[("hu", gi % 2, "e2")], ap=ub[gi][:, L + 1:L + 2], constant=0.0)
                    S.dma("sp", writes=[("hu", gi % 2)], out=ub[gi][:, 1:L + 1], in_=u0[gi, :, b, :])
                    rk = [("hu", gi % 2), ("hu", gi % 2, "e"), ("hu", gi % 2, "e2")] + cwk
                    uk = ("huc", gi)
                    S.op("dve", "tensor_scalar", reads=rk, writes=[uk], out=uc[gi][:], in0=ub[gi][:, 0:L],
                         scalar1=cw[:, gi, 0:1], scalar2=cb[:, gi:gi + 1], op0=ALU.mult, op1=ALU.add)
                    S.op("dve", "scalar_tensor_tensor", reads=rk + [uk], writes=[uk], out=uc[gi][:],
                         in0=ub[gi][:, 1:L + 1], scalar=cw[:, gi, 1:2], in1=uc[gi][:], op0=ALU.mult, op1=ALU.add)
                    S.op("dve", "scalar_tensor_tensor", reads=rk + [uk], writes=[uk], out=uc[gi][:],
                         in0=ub[gi][:, 2:L + 2], scalar=cw[:, gi, 2:3], in1=uc[gi][:], op0=ALU.mult, op1=ALU.add)
                S.op("pool", "tensor_tensor", reads=[("huc", 1), ("huc", 2)], writes=[("huc", 2)], out=uc[2][:],
                     in0=uc[2][:], in1=uc[1][:], op=ALU.mult)
                S.dma("sp", reads=[("huc", 2)], writes=[("s_vv", b)], out=s_vv[:, b, :], in_=uc[2][:])
                S.dma("sp", reads=[("huc", 0)], writes=[("s_x0", b)], out=s_x0[:, b, :], in_=uc[0][:])
        with C.scope():
            tl = {k: C.sb("fft_" + k, [128, 4, 128], F32) for k in
                  ("t1", "t2", "t3", "t4", "u1", "u2", "u3", "u4", "bre", "bim", "dre", "dim")}
            yre = [C.sb("fft_yre%d" % i, [128, 4, 128], F32) for i in range(2)]
            yim = [C.sb("fft_yim%d" % i, [128, 4, 128], F32) for i in range(2)]
            ps = {"a": [C.ps("psa%d" % i, [128, 512]) for i in range(2)], "xre": C.ps("psxre", [128, 512]),
                  "xim": C.ps("psxim", [128, 512]), "c": [C.ps("psc%d" % i, [128, 512]) for i in range(2)],
                  "y": C.ps("psy", [128, 512])}
            Xb = [C.sb("fft_X%d" % i, [128, 4, 128], F32) for i in range(2)]
            Hb = [[C.sb("fft_H%d%d" % (i, r), [128, 2, 128], F32) for r in range(2)] for i in range(2)]
            ev = [[C.sb("fft_ev%d%d" % (i, r), [128, 4, 128], F32) for r in range(2)] for i in range(2)]
            yo = [C.sb("fft_yo%d" % i, [64, 4, 128], F32) for i in range(2)]
            ps8 = C.ps("psx8", [128, 512])
            tlB = {"t1": tl["u1"], "t2": tl["u2"], "t3": tl["u3"], "t4": tl["u4"], "bre": tl["dre"], "bim": tl["dim"]}
            psB = {"a": ps["c"], "xre": ps["y"], "xim": ps8}

            def p1_group(g):
                q = g % 2
                tl_, ps_, kp = (tl, ps, "") if q == 0 else (tlB, psB, "B")
                X, xk = Xb[q], ("X", q)
                S.dma("sp", reads=[("s_h", ti) for ti in range(NFFT // 512)], writes=[xk], out=X[:],
                      in_=s_h[4 * g:4 * g + 4, :].rearrange("c (n1 n2) -> n1 c n2", n2=128))
                for _ in emit_fft_fwd(C, X, xk, 128, 4, cst, tl_, ps_, kp):
                    yield
                for r, nm in ((0, "xre"), (1, "xim")):
                    e_, ek = ev[q][r], ("ev", q, r)
                    S.op("act", "mul", reads=[kp + "ps" + nm], writes=[ek], out=e_[:].rearrange("p s k -> p (s k)"),
                         in_=ps_[nm][:], mul=1.0 / NFFT)
                    S.dma("sp", reads=[ek], writes=[("s_H", g, r)], out=s_H[r, :, 4 * g:4 * g + 4, :], in_=e_[:])
                yield

            pipeline((p1_group(g) for g in range(32)), 2)
            S.barrier()
            tre = cst[:, 3:4, :]
            tim = cst[:, 4:5, :]
            g1 = cst[:, 1:3, :].rearrange("p a b -> p (a b)")
            g2 = cst[:, 0:2, :].rearrange("p a b -> p (a b)")

            def half1(g):
                p = g % 2
                X, xk = Xb[p], ("X", p)
                S.dma("sp", reads=[("s_vv", 0), ("s_vv", 1)], writes=[xk], out=X[:64],
                      in_=s_vv[2 * g:2 * g + 2].rearrange("c b (n1 n2) -> n1 (c b) n2", n2=128))
                H = Hb[p]
                for r in range(2):
                    S.dma("sp", reads=[("s_H", g // 2, r)], writes=[("H", p, r)], out=H[r][:],
                          in_=s_H[r, :, 2 * g:2 * g + 2, :])
                for _ in emit_fft_fwd(C, X, xk, 64, 4, cst, tl, ps):
                    yield
                hk = [("H", p, 0), ("H", p, 1)]
                xre = ps["xre"][:].rearrange("p (c b k) -> p c b k", c=2, b=2)
                xim = ps["xim"][:].rearrange("p (c b k) -> p c b k", c=2, b=2)
                hb = lambda r: H[r][:].unsqueeze(2).to_broadcast([128, 2, 2, 128])
                v4 = lambda t_: t_[:].rearrange("p (c b) k -> p c b k", c=2)
                S.op("dve", "tensor_tensor", reads=["psxre"] + hk, writes=[("t1", 0), ("t1", 1)], out=v4(tl["t1"]),
                     in0=xre, in1=hb(0), op=ALU.mult)
                S.op("dve", "tensor_tensor", reads=["psxim"] + hk, writes=[("t2", 0), ("t2", 1)], out=v4(tl["t2"]),
                     in0=xim, in1=hb(1), op=ALU.mult)
                S.op("dve", "tensor_tensor", reads=["psxre"] + hk, writes=[("t3", 0), ("t3", 1)], out=v4(tl["t3"]),
                     in0=xre, in1=hb(1), op=ALU.mult)
                S.op("dve", "tensor_tensor", reads=["psxim"] + hk, writes=[("t4", 0), ("t4", 1)], out=v4(tl["t4"]),
                     in0=xim, in1=hb(0), op=ALU.mult)
                S.op("pool", "tensor_tensor", reads=[("t1", 0), ("t1", 1), ("t2", 0), ("t2", 1)], writes=[("yre", p)],
                     out=yre[p][:], in0=tl["t1"][:], in1=tl["t2"][:], op=ALU.subtract)
                S.op("pool", "tensor_tensor", reads=[("t3", 0), ("t3", 1), ("t4", 0), ("t4", 1)], writes=[("yim", p)],
                     out=yim[p][:], in0=tl["t3"][:], in1=tl["t4"][:], op=ALU.add)
                yield

            def half2(g):
                p = g % 2
                for s_ in range(4):
                    bank = ps["c"][s_ // 2]
                    o = (s_ % 2) * 256
                    S.mm(bank[:, o:o + 256], yre[p][:, s_, :], g1, True, False, [("yre", p), "cst"], [("psc", s_ // 2)])
                    S.mm(bank[:, o:o + 256], yim[p][:, s_, :], g2, False, True, [("yim", p), "cst"], [("psc", s_ // 2)])
                yield
                for h in range(2):
                    cv = ps["c"][h][:].rearrange("p (s r k) -> p s r k", s=2, r=2)
                    cre = cv[:, :, 0, :]
                    cim = cv[:, :, 1, :]
                    sl = slice(2 * h, 2 * h + 2)
                    bt = lambda t_: t_.to_broadcast([128, 2, 128])
                    S.op("dve", "tensor_tensor", reads=[("psc", h), "cst"], writes=[("u1", h)], out=tl["u1"][:, sl, :],
                         in0=cre, in1=bt(tre), op=ALU.mult)
                    S.op("dve", "tensor_tensor", reads=[("psc", h), "cst"], writes=[("u2", h)], out=tl["u2"][:, sl, :],
                         in0=cim, in1=bt(tim), op=ALU.mult)
                    S.op("dve", "tensor_tensor", reads=[("psc", h), "cst"], writes=[("u3", h)], out=tl["u3"][:, sl, :],
                         in0=cim, in1=bt(tre), op=ALU.mult)
                    S.op("dve", "tensor_tensor", reads=[("psc", h), "cst"], writes=[("u4", h)], out=tl["u4"][:, sl, :],
                         in0=cre, in1=bt(tim), op=ALU.mult)
                S.op("pool", "tensor_tensor", reads=[("u1", 0), ("u1", 1), ("u2", 0), ("u2", 1)], writes=["dre"],
                     out=tl["dre"][:], in0=tl["u1"][:], in1=tl["u2"][:], op=ALU.add)
                S.op("pool", "tensor_tensor", reads=[("u3", 0), ("u3", 1), ("u4", 0), ("u4", 1)], writes=["dim"],
                     out=tl["dim"][:], in0=tl["u3"][:], in1=tl["u4"][:], op=ALU.subtract)
                yield
                S.mm(ps["y"][:64, :], cst[:, 1, 0:64], tl["dre"][:].rearrange("p s k -> p (s k)"), True, False,
                     ["cst", "dre"], ["psy"])
                S.mm(ps["y"][:64, :], cst[:, 0, 0:64], tl["dim"][:].rearrange("p s k -> p (s k)"), False, True,
                     ["cst", "dim"], ["psy"])
                yield
                y_, yk = yo[p], ("yo", p)
                S.op("act", "activation", reads=["psy"], writes=[yk], out=y_[:].rearrange("p s k -> p (s k)"),
                     in_=ps["y"][:64, :], func=AF.Copy)
                S.dma("sp", reads=[yk], writes=[("s_y", g)], out=s_y[2 * g:2 * g + 2].rearrange(
                    "c b (n1 n2) -> n1 (c b) n2", n2=128), in_=y_[:])
                yield

            for _ in half1(0):
                pass
            for g in range(64):
                interleave([half2(g)] + ([half1(g + 1)] if g + 1 < 64 else []))
        with C.scope():
            dcol = col("dcol", dvec, 128)
            ya = C.sb("p4y", [128, L], F32)
            va = C.sb("p4v", [128, L], F32)
            xa = C.sb("p4x", [128, L], F32)
            for b in range(BATCH):
                S.dma("sp", reads=[("s_y", g) for g in range(64)], writes=["p4y"], out=ya[:], in_=s_y[:, b, :])
                S.dma("sp", reads=[("s_vv", b)], writes=["p4v"], out=va[:], in_=s_vv[:, b, :])
                S.dma("sp", reads=[("s_x0", b)], writes=["p4x"], out=xa[:], in_=s_x0[:, b, :])
                S.op("dve", "scalar_tensor_tensor", reads=["p4y", "p4v", "dcol"], writes=["p4y"], out=ya[:], in0=va[:],
                     scalar=dcol[:, 0:1], in1=ya[:], op0=ALU.mult, op1=ALU.add)
                S.op("dve", "tensor_tensor", reads=["p4y", "p4x"], writes=["p4y"], out=ya[:], in0=ya[:], in1=xa[:],
                     op=ALU.mult)
                S.dma("sp", reads=["p4y"], is_out=True, out=yT[:, b, :], in_=ya[:])
        S.replay()
    return nc

def emit_proj(C, x, T, g_dram, w_in, b_in, nout, uT):
    S = C.S
    NT = T // 512
    w_in_v = w_in.rearrange("(kc p) n -> p kc n", p=128)
    with C.scope():
        g_sb = load_vec_pk(C, "pj_g", g_dram, KC)
        bias = load_vec_pk(C, "pj_b", b_in, nout // 128)
        hn = C.sb("pj_hn", [128, KC, T], BF16)
        wbuf = [C.sb("pj_w%d" % i, [128, KC, 128], BF16) for i in range(3)]
        st = [C.sb("pj_st%d" % i, [128, 512], F32) for i in range(4)]
        pp = [C.ps("pj_ps%d" % i, [128, 512]) for i in range(2)]
        emit_rmsnorm(C, x, "x", 0, T, g_sb, "pj_g", hn, "pj_hn")
        it = 0
        for m in range(nout // 128):
            wb, wk = wbuf[m % 3], ("pj_w", m % 3)
            S.dma("pool", writes=[wk], out=wb[:], in_=w_in_v[:, :, m * 128:(m + 1) * 128])
            for tt in range(NT):
                it += 1
                b = it % 2
                sl = slice(tt * 512, (tt + 1) * 512)
                for k in range(KC):
                    S.mm(pp[b][:], wb[:, k, :], hn[:, k, sl], k == 0, k == KC - 1, [wk, ("pj_hn", k, tt)], [("pj_ps", b)])
                o_, ok = st[it % 4], ("pj_st", it % 4)
                S.op("act", "activation", reads=[("pj_ps", b), "pj_b"], writes=[ok], out=o_[:], in_=pp[b][:],
                     func=AF.Identity, bias=bias[:, m:m + 1])
                S.dma("sp", reads=[ok], is_out=True, out=uT[m * 128:(m + 1) * 128, sl], in_=o_[:])


def emit_outproj(C, x, T, yT, w_out, b_out):
    S = C.S
    NT = T // 512
    w_out_v = w_out.rearrange("(kc p) n -> p kc n", p=128)
    y_v = yT.rearrange("(kc p) t -> p kc t", p=128)
    with C.scope():
        bo = load_vec_pk(C, "op_b", b_out, KC)
        ybf = C.sb("op_y", [128, KC, T], BF16)
        for k in range(KC):
            S.dma("pool", writes=[("op_y", k)], out=ybf[:, k, :], in_=y_v[:, k, :])
        wobuf = [C.sb("op_wo%d" % i, [128, KC, 128], BF16) for i in range(2)]
        po = [C.ps("op_po%d" % i, [128, 512]) for i in range(2)]
        it = 0
        for m in range(KC):
            wo, wok = wobuf[m % 2], ("op_wo", m % 2)
            S.dma("pool", writes=[wok], out=wo[:], in_=w_out_v[:, :, m * 128:(m + 1) * 128])
            for tt in range(NT):
                it += 1
                b = it % 2
                sl = slice(tt * 512, (tt + 1) * 512)
                for i in range(KC):
                    S.mm(po[b][:], wo[:, i, :], ybf[:, i, sl], i == 0, i == KC - 1, [wok, ("op_y", i)], [("op_po", b)])
                xkey = ("x", m, tt)
                S.op("dve", "scalar_tensor_tensor", reads=[("op_po", b), xkey, "op_b"], writes=[xkey], out=x[:, m, sl],
                     in0=po[b][:], scalar=bo[:, m:m + 1], in1=x[:, m, sl], op0=ALU.add, op1=ALU.add)


def build_pre_prog(nout, T=TPC):
    nc = bass.Bass("TRN2", target_bir_lowering=False)
    xT = dram_in(nc, "xT", [D, T])
    nrm = dram_in(nc, "nrm", [2, D])
    fw_in = dram_in(nc, "fw_in", [D, 2 * FF])
    fw_out = dram_in(nc, "fw_out", [FF, D])
    w_in = dram_in(nc, "w_in", [D, nout])
    b_in = dram_in(nc, "b_in", [nout])
    xo = dram_out(nc, "xo", [D, T])
    uT = dram_out(nc, "uT", [nout, T])
    with contextlib.ExitStack() as es:
        C = Ctx(nc, es)
        emit_consts(C)
        x = C.sb("x", [128, KC, T], F32)
        emit_load_x(C, x, xT, T)
        emit_ffn(C, x, [(t, 1024) for t in range(0, T, 1024)], nrm[0], fw_in, fw_out, "f1")
        emit_store_x(C, x, xo, T)
        emit_proj(C, x, T, nrm[1], w_in, b_in, nout, uT)
        C.S.replay()
    return nc


def build_post_prog(T=TPC):
    nc = bass.Bass("TRN2", target_bir_lowering=False)
    xT = dram_in(nc, "xT", [D, T])
    yT = dram_in(nc, "yT", [D, T])
    w_out = dram_in(nc, "w_out", [D, D])
    b_out = dram_in(nc, "b_out", [D])
    nrm = dram_in(nc, "nrm", [D])
    fw_in = dram_in(nc, "fw_in", [D, 2 * FF])
    fw_out = dram_in(nc, "fw_out", [FF, D])
    xo = dram_out(nc, "xo", [D, T])
    with contextlib.ExitStack() as es:
        C = Ctx(nc, es)
        emit_consts(C)
        x = C.sb("x", [128, KC, T], F32)
        emit_load_x(C, x, xT, T)
        emit_outproj(C, x, T, yT, w_out, b_out)
        emit_ffn(C, x, [(t, 1024) for t in range(0, T, 1024)], nrm, fw_in, fw_out, "f2")
        emit_store_x(C, x, xo, T)
        C.S.replay()
    return nc


CH = 64
NCH = SEQ // CH
BLK = 8


def gdn_consts():
    i = np.arange(CH)
    mtri = np.stack([(i[:, None] <= i[None, :]), (i[:, None] >= i[None, :])]).astype(np.float32)
    eye = np.eye(CH, dtype=np.float32)
    mstrict = mtri - eye[None]
    ident = np.eye(128, dtype=np.float32)
    return mtri, mstrict, ident


def build_gd_core_prog(dbg_blocks=None):
    nc = bass.Bass("TRN2", target_bir_lowering=False)
    L, N = SEQ, NCH
    qkv0 = dram_in(nc, "qkv0", [3, 128, BATCH, L])
    ztok = dram_in(nc, "ztok", [CH, BATCH, N, 128])
    abt = dram_in(nc, "abt", [4, CH, BATCH, N])
    convw = dram_in(nc, "convw", [3, 3, 128])
    alog = dram_in(nc, "alog", [CH, 2])
    dtb = dram_in(nc, "dtb", [CH, 2])
    gn = dram_in(nc, "gn", [CH, 128])
    mtri_d = dram_in(nc, "mtri", [2, CH, CH])
    mstr_d = dram_in(nc, "mstrict", [2, CH, CH])
    ident_d = dram_in(nc, "ident", [128, 128])
    ytok = dram_out(nc, "ytok", [CH, BATCH, N, 128])
    s_o = nc.dram_tensor("s_o", [2, CH, BATCH, N, 128], F32).ap()
    with contextlib.ExitStack() as es:
        C = Ctx(nc, es)
        S = C.S
        ones = C.sb("ones", [128, 128], F32)
        S.op("dve", "memset", writes=["ones"], ap=ones[:], constant=1.0)
        ident = C.sb("ident", [128, 128], F32)
        S.dma("sp", writes=["ident"], out=ident[:], in_=ident_d)
        mtri = C.sb("mtri", [CH, 2, CH], F32)
        mstr = C.sb("mstr", [CH, 2, CH], F32)
        for d in range(2):
            S.dma("sp", writes=[("mtri", d)], out=mtri[:, d, :], in_=mtri_d[d])
            S.dma("sp", writes=[("mstr", d)], out=mstr[:, d, :], in_=mstr_d[d])
        cw = C.sb("gcw", [128, 3, 3], F32)
        for j in range(3):
            for gi in range(3):
                S.dma("sp", writes=[("gcw", j, gi)], out=cw[:, gi, j:j + 1], in_=convw[j, gi].rearrange("(p o) -> p o", o=1))
        cwk = [("gcw", j, gi) for j in range(3) for gi in range(3)]
        gtf = C.sb("gt", [128, 2, BATCH, N], F32)
        S.op("dve", "memset", writes=[("gt", 0), ("gt", 1)], ap=gtf[:], constant=0.0)
        gt = gtf[:CH]
        bt_ = C.sb("bt", [CH, 2, BATCH, N], F32)
        al = C.sb("al", [CH, 2], F32)
        db = C.sb("db", [CH, 2], F32)
        S.dma("sp", writes=["al"], out=al[:], in_=alog)
        S.dma("sp", writes=["db"], out=db[:], in_=dtb)
        for d in range(2):
            S.dma("sp", writes=[("gt", d)], out=gt[:, d], in_=abt[d])
            S.dma("sp", writes=[("bt", d)], out=bt_[:, d], in_=abt[2 + d])
        S.op("act", "activation", reads=["al"], writes=["al"], out=al[:], in_=al[:], func=AF.Exp)
        S.op("dve", "tensor_scalar", reads=["al"], writes=["al"], out=al[:], in0=al[:], scalar1=-1.0, scalar2=None,
             op0=ALU.mult)
        for d in range(2):
            gv = gt[:, d].rearrange("p b n -> p (b n)")
            bv = bt_[:, d].rearrange("p b n -> p (b n)")
            S.op("act", "activation", reads=[("gt", d), "db"], writes=[("gt", d)], out=gv, in_=gv, func=AF.Exp,
                 bias=db[:, d:d + 1])
            S.op("dve", "tensor_scalar", reads=[("gt", d)], writes=[("gt", d)], out=gv, in0=gv, scalar1=1.0, scalar2=None,
                 op0=ALU.add)
            S.op("act", "activation", reads=[("gt", d)], writes=[("gt", d)], out=gv, in_=gv, func=AF.Ln)
            S.op("dve", "tensor_scalar", reads=[("gt", d), "al"], writes=[("gt", d)], out=gv, in0=gv, scalar1=al[:, d:d + 1],
                 scalar2=None, op0=ALU.mult)
            S.op("act", "activation", reads=[("bt", d)], writes=[("bt", d)], out=bv, in_=bv, func=AF.Sigmoid)
        with C.scope():
            qT = C.sb("qT", [128, L], F32)
            kT = C.sb("kT", [128, L], F32)
            vT = C.sb("vT", [128, L], F32)
            ubs = [C.sb("gub%d" % i, [128, 2048 + 2], F32) for i in range(2)]
            ci = 0
            rsb = [C.sb("grs%d" % i, [128, 512], F32) for i in range(2)]
            sqb = [C.sb("gsq%d" % i, [128, 512], F32) for i in range(2)]
            gc = C.sb("gc", [CH, N], F32)
            kd = C.sb("kd", [CH, N], F32)
            bg = C.sb("bg", [CH, N], F32)
            egt = C.sb("egt", [128, N], F32)
            Sst = [C.sb("Sst%d" % i, [128, 128], F32) for i in range(2)]
            attnT = [C.sb("g8_attnT%d" % i, [128, BLK, CH], F32) for i in range(2)]
            for i_ in range(2):
                S.op("dve", "memset", writes=[("attnT", i_)], ap=attnT[i_][:], constant=0.0)
            names = ["dT", "W1", "LT", "Lm", "X8", "Pa", "Pta", "Pb", "Ptb"]
            t8f = {nm: C.sb("g8_" + nm, [128, BLK, CH], F32) for nm in names}
            for nm in names:
                S.op("dve", "memset", writes=[nm], ap=t8f[nm][:], constant=0.0)
            t8 = {nm: t8f[nm][:CH] for nm in names}
            qd = [C.sb("g8_qd%d" % i, [128, BLK * CH], F32) for i in range(2)]
            egr = C.sb("g8_egr", [128, BLK * CH], F32)
            kdt = [C.sb("g8_kdt%d" % i, [128, BLK, 128], F32) for i in range(2)]
            for i_ in range(2):
                S.op("dve", "memset", writes=[("kdt", i_, 0)], ap=kdt[i_][:, 0:4, :], constant=0.0)
                S.op("dve", "memset", writes=[("kdt", i_, 1)], ap=kdt[i_][:, 4:8, :], constant=0.0)
            kbt = C.sb("g8_kbt", [128, BLK, 128], F32)
            S.op("dve", "memset", writes=[("kbt", 0)], ap=kbt[:, 0:4, :], constant=0.0)
            S.op("dve", "memset", writes=[("kbt", 1)], ap=kbt[:, 4:8, :], constant=0.0)
            vbt = C.sb("g8_vbt", [CH, BLK, 128], F32)
            u8 = [C.sb("g8_u8%d" % i, [128, BLK, 128], F32) for i in range(2)]
            MT8 = [C.sb("g8_MT%d" % i, [128, BLK, 128], F32) for i in range(2)]
            w8 = C.sb("g8_w8", [128, BLK, 128], F32)
            mtmp = C.sb("g8_mtmp", [128, BLK, 128], F32)
            for i_ in range(2):
                for hh_ in range(2):
                    S.op("dve", "memset", writes=[("u8", i_, hh_)], ap=u8[i_][:, hh_ * 4:hh_ * 4 + 4, :], constant=0.0)
            for hh_ in range(2):
                S.op("dve", "memset", writes=[("w8", hh_)], ap=w8[:, hh_ * 4:hh_ * 4 + 4, :], constant=0.0)
            o8 = [C.sb("g8_o8%d" % i, [CH, BLK, 128], F32) for i in range(2)]
            Bk = [C.ps("gB%d" % i, [128, 512]) for i in range(8)]
            bk = lambda i: ("B", i)
            b7all = [bk(7)]
            S.excl.update(bk(i) for i in range(8))

            for b in range(BATCH):
                for gi, dst, dk_ in ((0, qT, "qT"), (1, kT, "kT"), (2, vT, "vT")):
                    CT = 2048
                    for ct in range(L // CT):
                        ci += 1
                        u_, uk_ = ubs[ci % 2], ("gub", ci % 2)
                        lo = ct * CT - 1
                        hi = ct * CT + CT + 1
                        a_ = max(lo, 0)
                        b_ = min(hi, L)
                        wr = [uk_]
                        if lo < 0:
                            S.op("pool", "memset", writes=[uk_], ap=u_[:, 0:1], constant=0.0)
                        if hi > L:
                            S.op("pool", "memset", writes=[uk_], ap=u_[:, CT + 1:CT + 2], constant=0.0)
                        S.dma("sp", writes=[uk_], out=u_[:, a_ - lo:b_ - lo], in_=qkv0[gi, :, b, a_:b_])
                        rk = wr + cwk
                        dsl = slice(ct * CT, (ct + 1) * CT)
                        dkt = (dk_, ct)
                        S.op("dve", "tensor_scalar", reads=rk, writes=[dkt], out=dst[:, dsl], in0=u_[:, 0:CT],
                             scalar1=cw[:, gi, 0:1], scalar2=None, op0=ALU.mult)
                        S.op("dve", "scalar_tensor_tensor", reads=rk + [dkt], writes=[dkt], out=dst[:, dsl], in0=u_[:, 1:CT + 1],
                             scalar=cw[:, gi, 1:2], in1=dst[:, dsl], op0=ALU.mult, op1=ALU.add)
                        S.op("dve", "scalar_tensor_tensor", reads=rk + [dkt], writes=[dkt], out=dst[:, dsl], in0=u_[:, 2:CT + 2],
                             scalar=cw[:, gi, 2:3], in1=dst[:, dsl], op0=ALU.mult, op1=ALU.add)
                        S.op("act", "activation", reads=[dkt], writes=[dkt], out=dst[:, dsl], in_=dst[:, dsl], func=AF.Silu)
                    S.op("act", "activation", reads=[(dk_, ct) for ct in range(L // CT)], writes=[dk_], out=dst[:, 0:1],
                         in_=dst[:, 0:1], func=AF.Copy)
                    if gi < 2:
                        sc = (128.0 ** -0.5) if gi == 0 else 1.0

                        def l2_tile(tt, dst=dst, dk_=dk_, sc=sc):
                            sl = slice(tt * 512, (tt + 1) * 512)
                            q3 = tt % 2
                            sq, sqk = sqb[q3], ("gsq", q3)
                            rs, rsk = rsb[q3], ("grs", q3)
                            S.op("act", "activation", reads=[dk_], writes=[sqk], out=sq[:], in_=dst[:, sl], func=AF.Square)
                            yield
                            S.mm(Bk[q3][:], ones[:], sq[:], True, True, ["ones", sqk], [bk(q3)])
                            yield
                            S.op("dve", "tensor_scalar", reads=[bk(q3)], writes=[rsk], out=rs[:], in0=Bk[q3][:], scalar1=1e-6,
                                 scalar2=None, op0=ALU.add)
                            yield
                            S.op("act", "activation", reads=[rsk], writes=[rsk], out=rs[:], in_=rs[:], func=AF.Sqrt)
                            yield
                            S.op("dve", "reciprocal", reads=[rsk], writes=[rsk], out=rs[:], in_=rs[:])
                            S.op("dve", "scalar_tensor_tensor", reads=[rsk, dk_], writes=[(dk_, "n", tt)], out=dst[:, sl],
                                 in0=dst[:, sl], scalar=sc, in1=rs[:], op0=ALU.mult, op1=ALU.mult)
                            yield

                        pipeline((l2_tile(tt) for tt in range(L // 512)), 2)
                        S.op("dve", "tensor_copy", reads=[(dk_, "n", tt) for tt in range(L // 512)], writes=[dk_],
                             out=dst[:, 0:1], in_=dst[:, 0:1])
                for d in range(2):
                    Gd = gt[:, d, b, :]
                    Bd = bt_[:, d, b, :]
                    S.mm(Bk[0][:CH, :N], mtri[:, d, :], Gd, True, True, [("mtri", d), ("gt", d)], [bk(0)])
                    S.op("act", "activation", reads=[bk(0)], writes=["gc"], out=gc[:], in_=Bk[0][:CH, :N], func=AF.Copy)
                    S.mm(Bk[1][:, :N], ones[:], gtf[:, d, b, :], True, True, ["ones", ("gt", d)], [bk(1)])
                    S.op("act", "activation", reads=[bk(1)], writes=["egt"], out=egt[:], in_=Bk[1][:, :N], func=AF.Exp)
                    S.op("dve", "tensor_tensor", reads=[bk(1), "gc"], writes=["kd"], out=kd[:], in0=Bk[1][:CH, :N], in1=gc[:],
                         op=ALU.subtract)
                    S.op("act", "activation", reads=["kd"], writes=["kd"], out=kd[:], in_=kd[:], func=AF.Exp)
                    S.op("act", "activation", reads=["gc"], writes=["bg"], out=bg[:], in_=gc[:], func=AF.Exp)
                    S.op("dve", "tensor_tensor", reads=["bg", ("bt", d)], writes=["bg"], out=bg[:], in0=bg[:], in1=Bd, op=ALU.mult)
                    S.op("dve", "memset", writes=[("S", 0)], ap=Sst[0][:], constant=0.0)
                    sidx = [0]
                    si = 0
                    vi = 0
                    blocks = list(range(N // BLK))
                    if d == 1:
                        blocks = blocks[::-1]
                    if dbg_blocks is not None:
                        blocks = blocks[:dbg_blocks]
                    flat = lambda t_: t_[:].rearrange("p i c -> p (i c)")
                    mt_b = mtri[:, d:d + 1, :].to_broadcast([CH, BLK, CH])
                    ms_b = mstr[:, d:d + 1, :].to_broadcast([CH, BLK, CH])
                    id_b = ident[:CH, 0:CH].unsqueeze(1).to_broadcast([CH, BLK, CH])
                    k3 = lambda i_: Bk[i_][:CH, :].rearrange("p (i c) -> p i c", c=CH)

                    def prep(nb, pb, d=d, Gd=Gd, Bd=Bd):
                        n0 = nb * BLK
                        tsl = slice(n0 * CH, (n0 + BLK) * CH)
                        gb = lambda t_: t_[:, n0:n0 + BLK].unsqueeze(2).to_broadcast([CH, BLK, CH])
                        qd_, at_, kdt_, u8_ = qd[pb], attnT[pb], kdt[pb], u8[pb]
                        S.op("dve", "tensor_tensor", reads=[("gt", d), ("mtri", d)], writes=["Pa"], out=t8["Pa"][:],
                             in0=mt_b, in1=gb(Gd), op=ALU.mult)
                        S.op("dve", "tensor_tensor", reads=[("bt", d), "ident"], writes=["Pb"], out=t8["Pb"][:], in0=id_b,
                             in1=gb(Bd), op=ALU.mult)
                        yield
                        S.mm(Bk[3][:], ones[:], t8f["Pa"][:].rearrange("p i c -> p (i c)"), True, True, ["ones", "Pa"], [bk(3)])
                        S.mm(Bk[4][:CH, :], ones[:CH, :CH], flat(t8["Pb"]), True, True, ["ones", "Pb"], [bk(4)])
                        for i in range(BLK):
                            csl = slice((n0 + i) * CH, (n0 + i + 1) * CH)
                            S.mm(Bk[5][:CH, i * CH:(i + 1) * CH], kT[:, csl], kT[:, csl], True, True, ["kT"], [bk(5)])
                            S.mm(Bk[6][:CH, i * CH:(i + 1) * CH], kT[:, csl], qT[:, csl], True, True, ["kT", "qT"], [bk(6)])
                        yield
                        S.op("act", "activation", reads=[bk(3)], writes=["egr"], out=egr[:], in_=Bk[3][:], func=AF.Exp)
                        S.op("dve", "tensor_tensor", reads=[bk(3), "gc"], writes=["dT"], out=t8["dT"][:], in0=k3(3), in1=gb(gc),
                             op=ALU.subtract)
                        S.op("dve", "tensor_scalar", reads=["dT"], writes=["dT"], out=t8["dT"][:], in0=t8["dT"][:], scalar1=0.0,
                             scalar2=None, op0=ALU.min)
                        yield
                        S.op("act", "activation", reads=["dT"], writes=["dT"], out=t8["dT"][:], in_=t8["dT"][:], func=AF.Exp)
                        S.op("dve", "tensor_tensor", reads=["egr", "qT"], writes=[("qd", pb)], out=qd_[:], in0=qT[:, tsl],
                             in1=egr[:], op=ALU.mult)
                        yield
                        S.op("dve", "tensor_tensor", reads=["dT", ("mstr", d)], writes=["W1"], out=t8["W1"][:], in0=t8["dT"][:],
                             in1=ms_b, op=ALU.mult)
                        S.op("dve", "tensor_tensor", reads=["W1", bk(4)], writes=["W1"], out=t8["W1"][:], in0=t8["W1"][:],
                             in1=k3(4), op=ALU.mult)
                        S.op("dve", "tensor_tensor", reads=["dT", ("mtri", d)], writes=["dT"], out=t8["dT"][:], in0=t8["dT"][:],
                             in1=mt_b, op=ALU.mult)
                        S.op("dve", "tensor_tensor", reads=[bk(5), "W1"], writes=["LT"], out=t8["LT"][:], in0=k3(5),
                             in1=t8["W1"][:], op=ALU.mult)
                        S.op("dve", "tensor_tensor", reads=[bk(6), "dT"], writes=[("attnT", pb)], out=at_[:CH], in0=k3(6),
                             in1=t8["dT"][:], op=ALU.mult)
                        yield
                        for i in range(BLK):
                            S.mm(Bk[7][:CH, i * CH:(i + 1) * CH], t8["LT"][:, i, :], ident[:CH, :CH], True, True,
                                 ["LT", "ident"], [bk(7)])
                        S.op("dve", "tensor_tensor", reads=["LT", "ident"], writes=["X8"], out=t8["X8"][:], in0=id_b,
                             in1=t8["LT"][:], op=ALU.subtract)
                        yield
                        S.op("act", "activation", reads=[bk(7)], writes=["Lm"], out=flat(t8["Lm"]), in_=Bk[7][:CH, :],
                             func=AF.Copy)
                        yield
                        A_, At_, ak, atk = t8["LT"], t8["Lm"], "LT", "Lm"
                        for lev in range(5):
                            P_, Pt_ = (t8["Pa"], t8["Pta"]) if lev % 2 == 0 else (t8["Pb"], t8["Ptb"])
                            pk, ptk = ("Pa", "Pta") if lev % 2 == 0 else ("Pb", "Ptb")
                            for i in range(BLK):
                                cs = slice(i * CH, (i + 1) * CH)
                                S.mm(Bk[4][:CH, cs], A_[:, i, :], At_[:, i, :], True, True, [ak, atk], [bk(4)])
                                if lev < 4:
                                    S.mm(Bk[3][:CH, cs], At_[:, i, :], A_[:, i, :], True, True, [ak, atk], [bk(3)])
                            yield
                            S.op("act", "activation", reads=[bk(4)], writes=[ptk], out=flat(Pt_), in_=Bk[4][:CH, :], func=AF.Copy)
                            if lev < 4:
                                S.op("act", "activation", reads=[bk(3)], writes=[pk], out=flat(P_), in_=Bk[3][:CH, :], func=AF.Copy)
                            yield
                            for i in range(BLK):
                                cs = slice(i * CH, (i + 1) * CH)
                                S.mm(Bk[5][:CH, cs], Pt_[:, i, :], t8["X8"][:, i, :], True, True, [ptk, "X8"], [bk(5)])
                            yield
                            S.op("dve", "tensor_tensor", reads=[bk(5), "X8"], writes=["X8"], out=flat(t8["X8"]),
                                 in0=Bk[5][:CH, :], in1=flat(t8["X8"]), op=ALU.add)
                            A_, At_, ak, atk = P_, Pt_, pk, ptk
                        yield
                        for i in range(BLK):
                            csl = slice((n0 + i) * CH, (n0 + i + 1) * CH)
                            bkk, bkv = (6, 3) if i < 4 else (7, 4)
                            o_ = (i % 4) * 128
                            S.op("pe", "transpose", reads=["kT", "ident"], writes=[bk(bkk)], out=Bk[bkk][:CH, o_:o_ + 128],
                                 in_=kT[:, csl], identity=ident[:])
                            S.op("pe", "transpose", reads=["vT", "ident"], writes=[bk(bkv)], out=Bk[bkv][:CH, o_:o_ + 128],
                                 in_=vT[:, csl], identity=ident[:])
                        yield
                        for hh in range(2):
                            isl = slice(hh * 4, hh * 4 + 4)
                            sc_b = lambda t_: t_[:, n0 + hh * 4:n0 + hh * 4 + 4].unsqueeze(2).to_broadcast([CH, 4, 128])
                            kps = Bk[6 + hh][:CH, :].rearrange("p (i e) -> p i e", e=128)
                            vps = Bk[3 + hh][:CH, :].rearrange("p (i e) -> p i e", e=128)
                            S.op("dve", "tensor_tensor", reads=[bk(6 + hh), "kd"], writes=[("kdt", pb, hh)],
                                 out=kdt_[:CH, isl, :], in0=kps, in1=sc_b(kd), op=ALU.mult)
                            S.op("dve", "tensor_tensor", reads=[bk(6 + hh), "bg"], writes=[("kbt", hh)], out=kbt[:CH, isl, :],
                                 in0=kps, in1=sc_b(bg), op=ALU.mult)
                            S.op("dve", "tensor_tensor", reads=[bk(3 + hh), ("bt", d)], writes=[("vbt", hh)],
                                 out=vbt[:, isl, :], in0=vps, in1=sc_b(Bd), op=ALU.mult)
                        yield
                        MT_ = MT8[pb]
                        for i in range(BLK):
                            hh = i // 4
                            o_ = (i % 4) * 128
                            S.mm(Bk[5 + hh][:CH, o_:o_ + 128], t8["X8"][:, i, :], vbt[:, i, :], True, True,
                                 ["X8", ("vbt", hh)], [bk(5 + hh)])
                            S.mm(Bk[3 + hh][:CH, o_:o_ + 128], t8["X8"][:, i, :], kbt[:CH, i, :], True, True,
                                 ["X8", ("kbt", hh)], [bk(3 + hh)])
                        yield
                        for hh in range(2):
                            S.op("act", "activation", reads=[bk(5 + hh)], writes=[("u8", pb, hh)],
                                 out=u8_[:CH, hh * 4:hh * 4 + 4, :].rearrange("p i e -> p (i e)"), in_=Bk[5 + hh][:CH, :],
                                 func=AF.Copy)
                            S.op("act", "activation", reads=[bk(3 + hh)], writes=[("w8", hh)],
                                 out=w8[:CH, hh * 4:hh * 4 + 4, :].rearrange("p i e -> p (i e)"), in_=Bk[3 + hh][:CH, :],
                                 func=AF.Copy)
                        yield
                        for i in range(BLK):
                            hh = i // 4
                            o_ = (i % 4) * 128
                            S.mm(Bk[5 + hh][:, o_:o_ + 128], w8[:, i, :], kdt_[:, i, :], True, True,
                                 [("w8", hh), ("kdt", pb, hh)], [bk(5 + hh)])
                            S.mm(Bk[7][:, i * CH:(i + 1) * CH], w8[:, i, :], at_[:, i, :], True, True,
                                 [("w8", hh), ("attnT", pb)], [bk(7)])
                        yield
                        for hh in range(2):
                            S.op("act", "mul", reads=[bk(5 + hh)], writes=[("mtmp", hh)],
                                 out=mtmp[:, hh * 4:hh * 4 + 4, :].rearrange("p i e -> p (i e)"), in_=Bk[5 + hh][:], mul=-1.0)
                        S.op("dve", "tensor_tensor", reads=[bk(7), ("qd", pb)], writes=[("qd", pb)], out=qd_[:], in0=qd_[:],
                             in1=Bk[7][:], op=ALU.subtract)
                        yield
                        for i in range(BLK):
                            n = n0 + i
                            S.op("dve", "scalar_tensor_tensor", reads=[("mtmp", i // 4), "ident", "egt"], writes=[("MT", pb, i)],
                                 out=MT_[:, i, :], in0=ident[:], scalar=egt[:, n:n + 1], in1=mtmp[:, i, :], op0=ALU.mult,
                                 op1=ALU.add)
                            if i % 4 == 3:
                                yield

                    def scan(nb, pb, bi_, d=d, b=b):
                        n0 = nb * BLK
                        qd_, at_, kdt_, u8_, MT_ = qd[pb], attnT[pb], kdt[pb], u8[pb], MT8[pb]
                        ob, obk = o8[bi_ % 2], ("o8", bi_ % 2)
                        order = list(range(BLK)) if d == 0 else list(range(BLK))[::-1]
                        for i in order:
                            hh = i // 4
                            cur = sidx[0]
                            Sc, sk = Sst[cur], ("S", cur)
                            Sn, snk = Sst[1 - cur], ("S", 1 - cur)
                            sidx[0] = 1 - cur
                            S.mm(Bk[0][:, 0:128], MT_[:, i, :], Sc[:], True, False, [("MT", pb, i), sk], [bk(0)])
                            S.mm(Bk[0][:, 0:128], kdt_[:, i, :], u8_[:, i, :], False, True, [("kdt", pb, hh), ("u8", pb, hh)], [bk(0)])
                            S.mm(Bk[1][:CH, 0:128], qd_[:, i * CH:(i + 1) * CH], Sc[:], True, False, [("qd", pb), sk], [bk(1)])
                            S.mm(Bk[1][:CH, 0:128], at_[:, i, :], u8_[:, i, :], False, True, [("attnT", pb), ("u8", pb, hh)], [bk(1)])
                            yield
                            S.op("act", "activation", reads=[bk(0)], writes=[snk], out=Sn[:], in_=Bk[0][:, 0:128], func=AF.Copy)
                            S.op("dve", "tensor_copy", reads=[bk(1)], writes=[obk], out=ob[:, i, :], in_=Bk[1][:CH, 0:128])
                            yield
                        S.dma("sp", reads=[obk], writes=[("s_o", d, b, nb)], out=s_o[d, :, b, n0:n0 + BLK, :], in_=ob[:])

                    for _ in prep(blocks[0], 0):
                        pass
                    for bi_, nb in enumerate(blocks):
                        gens = [scan(nb, bi_ % 2, bi_)]
                        if bi_ + 1 < len(blocks):
                            gens.append(prep(blocks[bi_ + 1], (bi_ + 1) % 2))
                        interleave(gens)
        with C.scope():
            gnr = C.sb("gnr", [CH, 128], F32)
            S.dma("sp", writes=["gnr"], out=gnr[:], in_=gn)
            NB3 = 3
            of = [C.sb("of%d" % i, [CH, BLK, 128], F32) for i in range(NB3)]
            obb = [C.sb("ob%d" % i, [CH, BLK, 128], F32) for i in range(NB3)]
            zz = [C.sb("zz%d" % i, [CH, BLK, 128], F32) for i in range(NB3)]
            sq8 = [C.sb("sq8%d" % i, [CH, BLK, 128], F32) for i in range(NB3)]
            ss = [C.sb("ss%d" % i, [CH, BLK], F32) for i in range(NB3)]

            def out_block(it, b, nb):
                p = it % NB3
                n0 = nb * BLK
                S.dma("sp", reads=[("s_o", 0, b, nb)], writes=[("of", p)], out=of[p][:], in_=s_o[0, :, b, n0:n0 + BLK, :])
                S.dma("sp", reads=[("s_o", 1, b, nb)], writes=[("ob", p)], out=obb[p][:], in_=s_o[1, :, b, n0:n0 + BLK, :])
                S.dma("sp", writes=[("zz", p)], out=zz[p][:], in_=ztok[:, b, n0:n0 + BLK, :])
                yield
                S.op("dve", "tensor_tensor", reads=[("of", p), ("ob", p)], writes=[("of", p)], out=of[p][:], in0=of[p][:],
                     in1=obb[p][:], op=ALU.add)
                S.op("act", "activation", reads=[("zz", p)], writes=[("zz", p)], out=zz[p][:], in_=zz[p][:], func=AF.Silu)
                yield
                S.op("act", "activation", reads=[("of", p)], writes=[("sq8", p)], out=sq8[p][:], in_=of[p][:],
                     func=AF.Square)
                yield
                S.op("dve", "tensor_reduce", reads=[("sq8", p)], writes=[("ss", p)], out=ss[p][:], in_=sq8[p][:],
                     axis=AX.X, op=ALU.add)
                S.op("dve", "tensor_scalar", reads=[("ss", p)], writes=[("ss", p)], out=ss[p][:], in0=ss[p][:],
                     scalar1=1.0 / 128.0, scalar2=EPS, op0=ALU.mult, op1=ALU.add)
                yield
                S.op("act", "activation", reads=[("ss", p)], writes=[("ss", p)], out=ss[p][:], in_=ss[p][:], func=AF.Sqrt)
                yield
                S.op("dve", "reciprocal", reads=[("ss", p)], writes=[("ss", p)], out=ss[p][:], in_=ss[p][:])
                S.op("dve", "tensor_tensor", reads=[("of", p), ("ss", p)], writes=[("of", p)], out=of[p][:], in0=of[p][:],
                     in1=ss[p][:].unsqueeze(2).to_broadcast([CH, BLK, 128]), op=ALU.mult)
                S.op("dve", "tensor_tensor", reads=[("of", p), "gnr"], writes=[("of", p)], out=of[p][:], in0=of[p][:],
                     in1=gnr[:].unsqueeze(1).to_broadcast([CH, BLK, 128]), op=ALU.mult)
                S.op("dve", "tensor_tensor", reads=[("of", p), ("zz", p)], writes=[("of", p)], out=of[p][:], in0=of[p][:],
                     in1=zz[p][:], op=ALU.mult)
                S.dma("sp", reads=[("of", p)], is_out=True, out=ytok[:, b, n0:n0 + BLK, :], in_=of[p][:])
                yield

            pipeline((out_block(b * (N // BLK) + nb, b, nb) for b in range(BATCH) for nb in range(N // BLK)), NB3)
        S.replay()
    return nc


_PROGS = {}


def _prog(key, fn):
    if key not in _PROGS:
        _PROGS[key] = fn()
    return _PROGS[key]


def _run(nc, in_maps):
    res = run_bass_kernel_spmd(nc, in_maps, core_ids=list(range(NCORES)))
    return res.results


def _c(a):
    return np.ascontiguousarray(a, dtype=np.float32)


def _tok_shards_T(xf):
    return [_c(xf[c * TPC:(c + 1) * TPC].T) for c in range(NCORES)]


def _from_T(outs, name):
    return np.concatenate([r[name].T for r in outs], 0)


def _run_sc_layer(xf, p, li, j, final):
    nc = _prog(("sc", final), lambda: build_sc_prog(TPC, final))
    x3 = xf.reshape(BATCH, SEQ, D)
    zero = np.zeros((1, D), np.float32)
    in_maps = []
    for c in range(NCORES):
        b, s0 = divmod(c * TPC, SEQ)
        left = x3[b, s0 - 1:s0] if s0 > 0 else zero
        right = x3[b, s0 + TPC:s0 + TPC + 1] if s0 + TPC < SEQ else zero
        xs = np.concatenate([x3[b, s0:s0 + TPC], left, right], 0)
        in_maps.append({"xT": _c(xs.T), "nrm": _c(p["norms"][li]), "fw_in": _c(p["ffn_w_in"][li]),
                        "fw_out": _c(p["ffn_w_out"][li]), "w_in": _c(p["sc_w_in"][j]), "conv": _c(p["sc_conv"][j]),
                        "w_out": _c(p["sc_w_out"][j]), "gfin": _c(p["final_norm"])})
    return _from_T(_run(nc, in_maps), "yT")


def _run_pre(xf, p, li, w_in, b_in):
    nout = w_in.shape[1]
    nc = _prog(("pre", nout), lambda: build_pre_prog(nout, TPC))
    xs = _tok_shards_T(xf)
    in_maps = [{"xT": xs[c], "nrm": _c(p["norms"][li, 0:2]), "fw_in": _c(p["ffn_w_in"][li, 0]),
                "fw_out": _c(p["ffn_w_out"][li, 0]), "w_in": _c(w_in), "b_in": _c(b_in)} for c in range(NCORES)]
    outs = _run(nc, in_maps)
    x1 = _from_T(outs, "xo")
    u = np.concatenate([r["uT"] for r in outs], 1)
    return x1, u


def _run_post(xf, yfm, p, li, w_out, b_out):
    nc = _prog(("post",), lambda: build_post_prog(TPC))
    xs = _tok_shards_T(xf)
    in_maps = [{"xT": xs[c], "yT": _c(yfm[:, c * TPC:(c + 1) * TPC]), "w_out": _c(w_out), "b_out": _c(b_out),
                "nrm": _c(p["norms"][li, 2]), "fw_in": _c(p["ffn_w_in"][li, 1]), "fw_out": _c(p["ffn_w_out"][li, 1])}
               for c in range(NCORES)]
    return _from_T(_run(nc, in_maps), "xo")


def _run_hyena_core(u, p, j):
    nc = _prog(("hy",), build_hy_core_prog)
    cst, zT, trow, nad = hyena_consts()
    trow_rep = _c(np.broadcast_to(trow, (128, NFFT)))
    u4 = u.reshape(3, D, BATCH, SEQ)
    hc = p["hy_conv"][j].reshape(3, 3, D)
    cb = p["hy_conv_b"][j].reshape(3, D)
    w3 = p["hy_f_w3"][j].reshape(HY_ORD, 2, D)
    in_maps = []
    for c in range(NCORES):
        sl = slice(c * 128, (c + 1) * 128)
        in_maps.append({"u0": _c(u4[:, sl]), "convw": _c(hc[:, :, sl]), "convb": _c(cb[:, sl]), "dvec": _c(p["hy_d"][j][sl]),
                        "fw1": _c(p["hy_f_w1"][j]), "fb1": _c(p["hy_f_b1"][j]), "fw2": _c(p["hy_f_w2"][j]),
                        "fb2": _c(p["hy_f_b2"][j]), "fw3": _c(w3[:, :, sl]), "freq": _c(p["hy_f_freq"][j]), "cst": cst,
                        "zT": zT, "trow": trow_rep, "nad": _c(nad[sl])})
    outs = _run(nc, in_maps)
    return np.concatenate([r["yT"].reshape(128, BATCH * SEQ) for r in outs], 0)


def _run_gdn_core(u, p, j):
    nc = _prog(("gd",), build_gd_core_prog)
    mtri, mstrict, ident = gdn_consts()
    H = 8
    gcv = p["gd_conv"][j].reshape(3, 3, D)
    in_maps = []
    for h in range(NCORES):
        sl = slice(h * 128, (h + 1) * 128)
        qkv0 = u[0:3 * D].reshape(3, D, BATCH, SEQ)[:, sl]
        zfm = u[3 * D + h * 128: 3 * D + (h + 1) * 128]
        ztok = zfm.T.reshape(BATCH, NCH, CH, 128).transpose(2, 0, 1, 3)
        rows = [4 * D + 0 * H + h, 4 * D + 1 * H + h, 4 * D + 2 * H + 0 * H + h, 4 * D + 2 * H + 1 * H + h]
        abt = u[rows].reshape(4, BATCH, NCH, CH).transpose(0, 3, 1, 2)
        in_maps.append({"qkv0": _c(qkv0), "ztok": _c(ztok), "abt": _c(abt), "convw": _c(gcv[:, :, sl]),
                        "alog": _c(np.broadcast_to(p["gd_a_log"][j][:, h], (CH, 2))),
                        "dtb": _c(np.broadcast_to(p["gd_dt_bias"][j][:, h], (CH, 2))),
                        "gn": _c(np.broadcast_to(p["gd_norm"][j], (CH, 128))), "mtri": mtri, "mstrict": mstrict,
                        "ident": ident})
    outs = _run(nc, in_maps)
    return np.concatenate([r["ytok"].transpose(3, 1, 2, 0).reshape(128, BATCH * SEQ) for r in outs], 0)


def kernel(**inputs):
    p = {k: np.asarray(v, dtype=np.float32) for k, v in inputs.items()}
    xf = p["x"].reshape(BATCH * SEQ, D)
    xf = _run_sc_layer(xf, p, 0, 0, final=False)
    xf, u = _run_pre(xf, p, 1, p["hy_w_in"][0], p["hy_b_in"][0])
    yfm = _run_hyena_core(u, p, 0)
    xf = _run_post(xf, yfm, p, 1, p["hy_w_out"][0], p["hy_b_out"][0])
    nproj = p["gd_w_in"].shape[2]
    npad = ((nproj + 127) // 128) * 128
    w_in = np.zeros((D, npad), np.float32)
    w_in[:, :nproj] = p["gd_w_in"][0]
    xf, u = _run_pre(xf, p, 2, w_in, np.zeros((npad,), np.float32))
    yfm = _run_gdn_core(u, p, 0)
    xf = _run_post(xf, yfm, p, 2, p["gd_w_out"][0], np.zeros((D,), np.float32))
    xf = _run_sc_layer(xf, p, 3, 1, final=True)
    return np.ascontiguousarray(xf.reshape(BATCH, SEQ, D).astype(np.float32))
```

```python
import contextlib
import math
import numpy as np
import concourse.bass as bass
import concourse.mybir as mybir
from concourse.bass_utils import run_bass_kernel_spmd

F32 = mybir.dt.float32
BF16 = mybir.dt.bfloat16
AF = mybir.ActivationFunctionType
ALU = mybir.AluOpType
AX = mybir.AxisListType

D = 1024
KC = 8
FF = 2816
JC = 22
NCORES = 8
BATCH = 2
SEQ = 8192
TPC = BATCH * SEQ // NCORES
EPS = 1e-6

ENGS = ("pe", "act", "dve", "pool", "sp")
NDMA_SEM = 20


class Sched:
    def __init__(self, nc, es):
        self.nc = nc
        self.q = {e: [] for e in ENGS}
        self.cnt = {e: 0 for e in ENGS}
        self.seen = {e: {} for e in ENGS}
        self.buf = {}
        self.sems = {}
        for e in ENGS:
            self.sems[("E", e)] = es.enter_context(nc.semaphore("sem_" + e))
        self.dma_rr = {e: 0 for e in ENGS}
        self.dma_val = {}
        for e in ("sp", "pool", "act"):
            for i in range(NDMA_SEM):
                k = ("D", e, i)
                self.sems[k] = es.enter_context(nc.semaphore("dsem_%s_%d" % (e, i)))
                self.dma_val[k] = 0
        self.out_tokens = []
        self.excl = set()

    def _deps(self, eng, reads, writes):
        deps = {}

        def add(tok):
            if tok is None:
                return
            k, v = tok
            if deps.get(k, 0) < v:
                deps[k] = v

        for k in reads:
            b = self.buf.get(k)
            if b:
                add(b["w"])
                if k in self.excl:
                    for rk, rv in b["r"].items():
                        if rk != ("E", eng):
                            add((rk, rv))
        for k in writes:
            b = self.buf.get(k)
            if b:
                add(b["w"])
                for rk, rv in b["r"].items():
                    add((rk, rv))
        waits = []
        for k, v in deps.items():
            if eng == "pe" and k == ("E", "pe"):
                continue
            if self.seen[eng].get(k, 0) >= v:
                continue
            self.seen[eng][k] = v
            waits.append((k, v))
        return waits

    def _record(self, tok, reads, writes):
        for k in reads:
            b = self.buf.setdefault(k, {"w": None, "r": {}})
            if b["r"].get(tok[0], 0) < tok[1]:
                b["r"][tok[0]] = tok[1]
        for k in writes:
            self.buf[k] = {"w": tok, "r": {}}

    def op(self, eng, name, reads=(), writes=(), **kw):
        fn = (name, kw)
        waits = self._deps(eng, reads, writes)
        self.cnt[eng] += 1
        tok = (("E", eng), self.cnt[eng])
        self.q[eng].append((waits, fn, tok, 1))
        self._record(tok, reads, writes)
        return tok

    def dma(self, eng, reads=(), writes=(), is_out=False, **kw):
        fn = ("dma_start", kw)
        waits = self._deps(eng, reads, writes)
        i = self.dma_rr[eng]
        self.dma_rr[eng] = (i + 1) % NDMA_SEM
        k = ("D", eng, i)
        prev = self.dma_val[k]
        if prev and self.seen[eng].get(k, 0) < prev:
            self.seen[eng][k] = prev
            waits.append((k, prev))
        self.dma_val[k] = prev + 16
        tok = (k, prev + 16)
        self.q[eng].append((waits, fn, tok, 16))
        self._record(tok, reads, writes)
        if is_out:
            self.out_tokens.append(tok)
        return tok

    def barrier(self):
        allv = [(("E", f), self.cnt[f]) for f in ENGS if self.cnt[f]]
        allv += [(k, v) for k, v in self.dma_val.items() if v]
        for e in ENGS:
            waits = []
            for k, v in allv:
                if k == ("E", e) and e == "pe":
                    continue
                if self.seen[e].get(k, 0) >= v:
                    continue
                self.seen[e][k] = v
                waits.append((k, v))
            if waits:
                self.q[e].append((waits, None, None, 0))
        self.buf = {}

    def mm(self, out, lhsT, rhs, start, stop, reads, writes):
        return self.op("pe", "matmul", reads, writes, out=out, lhsT=lhsT, rhs=rhs, start=start, stop=stop)

    def replay(self):
        nc = self.nc
        fin = list(self.out_tokens)
        with nc.Block() as block:
            def run(engname, eng):
                for waits, fn, tok, inc in self.q[engname]:
                    for k, v in waits:
                        eng.wait_ge(self.sems[k], v)
                    if fn is None:
                        continue
                    ins = getattr(eng, fn[0])(**fn[1])
                    ins.then_inc(self.sems[tok[0]], inc)
                if engname == "sp":
                    for k, v in fin:
                        eng.wait_ge(self.sems[k], v)

            @block.tensor
            def _(e):
                run("pe", e)

            @block.scalar
            def _(e):
                run("act", e)

            @block.vector
            def _(e):
                run("dve", e)

            @block.gpsimd
            def _(e):
                run("pool", e)

            @block.sync
            def _(e):
                run("sp", e)


class Ctx:
    def __init__(self, nc, es):
        self.nc = nc
        self.es = es
        self.S = Sched(nc, es)
        self.n = 0
        self.scopes = [es]

    def sb(self, name, shape, dt):
        self.n += 1
        return self.scopes[-1].enter_context(self.nc.sbuf_tensor("%s_%d" % (name, self.n), shape, dt))

    def ps(self, name, shape, dt=F32):
        self.n += 1
        return self.scopes[-1].enter_context(self.nc.psum_tensor("%s_%d" % (name, self.n), shape, dt))

    @contextlib.contextmanager
    def scope(self):
        with contextlib.ExitStack() as s:
            self.scopes.append(s)
            try:
                yield
            finally:
                self.S.barrier()
                self.scopes.pop()


def interleave(gens):
    gens = list(gens)
    while gens:
        for g_ in list(gens):
            try:
                next(g_)
            except StopIteration:
                gens.remove(g_)


def pipeline(gens, width):
    gens = iter(gens)
    active = []
    done = False
    while True:
        while not done and len(active) < width:
            try:
                active.append(next(gens))
            except StopIteration:
                done = True
        if not active:
            return
        for g_ in list(active):
            try:
                next(g_)
            except StopIteration:
                active.remove(g_)


def dram_in(nc, name, shape, dt=F32):
    return nc.dram_tensor(name, list(shape), dt, kind="ExternalInput").ap()


def dram_out(nc, name, shape, dt=F32):
    return nc.dram_tensor(name, list(shape), dt, kind="ExternalOutput").ap()


def emit_consts(C):
    ones = C.sb("ones", [128, 128], F32)
    C.S.op("dve", "memset", writes=["ones"], ap=ones[:], constant=1.0)
    C.ones = ones
    C.rn_sq = [C.sb("rn_sq%d" % i, [128, 512], F32) for i in range(3)]
    C.rn_rs = [C.sb("rn_rs%d" % i, [128, 512], F32) for i in range(2)]
    C.rn_ps = [C.ps("rn_ps%d" % i, [128, 512], F32) for i in range(1)]
    C.rn_i = 0


def load_vec_pk(C, name, vec_dram, nchunk, eng="sp"):
    t = C.sb(name, [128, nchunk], F32)
    C.S.dma(eng, writes=[name], out=t[:], in_=vec_dram.rearrange("(kc p) -> p kc", p=128),
            allow_slow_non_contiguous=True)
    return t


def emit_rmsnorm(C, x, xk, t0, ntok, g_sb, gk, hn, hk, hoff=0):
    S = C.S
    nt = (ntok + 511) // 512
    for tt in range(nt):
        n = min(512, ntok - tt * 512)
        c0 = t0 + tt * 512
        xkeys = [(xk, k, c0 // 512) for k in range(KC)]
        if (c0 % 512) + n > 512:
            xkeys += [(xk, k, c0 // 512 + 1) for k in range(KC)]
        ps = C.rn_ps[0]
        for k in range(KC):
            C.rn_i += 1
            sq = C.rn_sq[C.rn_i % 3]
            sqk = ("rn_sq", C.rn_i % 3)
            S.op("act", "activation", reads=[kk for kk in xkeys if kk[1] == k], writes=[sqk],
                 out=sq[:, :n], in_=x[:, k, c0:c0 + n], func=AF.Square)
            S.mm(ps[:, :n], C.ones[:], sq[:, :n], k == 0, k == KC - 1, ["ones", sqk], ["rn_ps"])
        C.rn_i += 1
        rs = C.rn_rs[C.rn_i % 2]
        rsk = ("rn_rs", C.rn_i % 2)
        S.op("dve", "tensor_scalar", reads=["rn_ps"], writes=[rsk], out=rs[:, :n], in0=ps[:, :n],
             scalar1=1.0 / D, scalar2=EPS, op0=ALU.mult, op1=ALU.add)
        S.op("act", "activation", reads=[rsk], writes=[rsk], out=rs[:, :n], in_=rs[:, :n], func=AF.Sqrt)
        S.op("dve", "reciprocal", reads=[rsk], writes=[rsk], out=rs[:, :n], in_=rs[:, :n])
        for k in range(KC):
            eng = "dve"
            S.op(eng, "scalar_tensor_tensor", reads=[kk for kk in xkeys if kk[1] == k] + [rsk, gk],
                 writes=[(hk, k, tt)], out=hn[:, k, hoff + tt * 512: hoff + tt * 512 + n], in0=x[:, k, c0:c0 + n],
                 scalar=g_sb[:, k:k + 1], in1=rs[:, :n], op0=ALU.mult, op1=ALU.mult)


def emit_ffn(C, x, groups, g_dram, w_in, w_out, pref):
    S = C.S
    TG = 1024
    w_in_v = w_in.rearrange("(kc p) n -> p kc n", p=128)
    w_out_v = w_out.rearrange("(jc p) n -> p jc n", p=128)
    with C.scope():
        g_sb = load_vec_pk(C, pref + "g", g_dram, KC)
        gk = pref + "g"
        hns = [C.sb("ffn_hn%d" % i, [128, KC, TG], BF16) for i in range(min(2, len(groups)))]
        act = C.sb("ffn_act", [128, JC, TG], BF16)
        wbuf = [C.sb("ffn_wi%d" % i, [128, KC, 256], BF16) for i in range(3)]
        wobuf = [C.sb("ffn_wo%d" % i, [128, JC, 128], BF16) for i in range(2)]
        sgb = [C.sb("ffn_sg%d" % i, [128, 512], F32) for i in range(2)]
        pg = [C.ps("ffn_pg%d" % i, [128, 512]) for i in range(2)]
        pu = [C.ps("ffn_pu%d" % i, [128, 512]) for i in range(2)]
        po = [C.ps("ffn_po%d" % i, [128, 512]) for i in range(2)]
        it = 0
        io = 0
        for gi_, (t0, ntok) in enumerate(groups):
            ntg = (ntok + 511) // 512
            hn = hns[gi_ % 2]
            hnk = "ffn_hn%d" % (gi_ % 2)
            if gi_ == 0:
                emit_rmsnorm(C, x, "x", t0, ntok, g_sb, gk, hn, hnk)
            for j in range(JC):
                wb = wbuf[j % 3]
                wk = ("ffn_wi", j % 3)
                S.dma("pool", writes=[wk + (0,)], out=wb[:, :, 0:128], in_=w_in_v[:, :, j * 128:(j + 1) * 128])
                S.dma("pool", writes=[wk + (1,)], out=wb[:, :, 128:256],
                      in_=w_in_v[:, :, FF + j * 128: FF + (j + 1) * 128])
                for tt in range(ntg):
                    n = min(512, ntok - tt * 512)
                    it += 1
                    b = it % 2
                    sl = slice(tt * 512, tt * 512 + n)
                    for k in range(KC):
                        S.mm(pg[b][:, :n], wb[:, k, 0:128], hn[:, k, sl], k == 0, k == KC - 1,
                             [wk + (0,), (hnk, k, tt)], [("ffn_pg", b)])
                    for k in range(KC):
                        S.mm(pu[b][:, :n], wb[:, k, 128:256], hn[:, k, sl], k == 0, k == KC - 1,
                             [wk + (1,), (hnk, k, tt)], [("ffn_pu", b)])
                    S.op("act", "activation", reads=[("ffn_pg", b)], writes=[("ffn_sg", b)],
                         out=sgb[b][:, :n], in_=pg[b][:, :n], func=AF.Silu)
                    S.op("dve", "tensor_tensor", reads=[("ffn_pu", b), ("ffn_sg", b)], writes=[("ffn_act", j, tt)],
                         out=act[:, j, sl], in0=pu[b][:, :n], in1=sgb[b][:, :n], op=ALU.mult)
            if gi_ + 1 < len(groups):
                t1, n1 = groups[gi_ + 1]
                emit_rmsnorm(C, x, "x", t1, n1, g_sb, gk, hns[(gi_ + 1) % 2], "ffn_hn%d" % ((gi_ + 1) % 2))
            for m in range(KC):
                wo = wobuf[m % 2]
                wok = ("ffn_wo", m % 2)
                S.dma("pool", writes=[wok], out=wo[:], in_=w_out_v[:, :, m * 128:(m + 1) * 128])
                for tt in range(ntg):
                    n = min(512, ntok - tt * 512)
                    io += 1
                    b = io % 2
                    sl = slice(tt * 512, tt * 512 + n)
                    gsl = slice(t0 + tt * 512, t0 + tt * 512 + n)
                    for j in range(JC):
                        S.mm(po[b][:, :n], wo[:, j, :], act[:, j, sl], j == 0, j == JC - 1,
                             [wok, ("ffn_act", j, tt)], [("ffn_po", b)])
                    xkey = ("x", m, (t0 + tt * 512) // 512)
                    S.op("dve", "scalar_tensor_tensor", reads=[("ffn_po", b), xkey], writes=[xkey],
                         out=x[:, m, gsl], in0=po[b][:, :n], scalar=0.5, in1=x[:, m, gsl], op0=ALU.mult, op1=ALU.add)


def emit_sc_mixer(C, x, T, g_dram, w_in, conv, w_out):
    S = C.S
    NT = T // 512
    w_in_v = w_in.rearrange("(kc p) n -> p kc n", p=128)
    w_out_v = w_out.rearrange("(kc p) n -> p kc n", p=128)
    with C.scope():
        g_sb = load_vec_pk(C, "sc_g", g_dram, KC)
        cw = C.sb("sc_cw", [128, KC, 3], F32)
        for j in range(3):
            S.dma("sp", writes=[("sc_cw", j)], out=cw[:, :, j], in_=conv[j].rearrange("(i p) -> p i", p=128),
                  allow_slow_non_contiguous=True)
        hn = C.sb("sc_hn", [128, KC, T + 2], BF16)
        ybf = C.sb("sc_y", [128, KC, T], BF16)
        chb = [C.sb("sc_ch%d" % i, [128, T + 2], F32) for i in range(2)]
        bsv = [C.sb("sc_b%d" % i, [128, T], F32) for i in range(2)]
        csb = [C.sb("sc_c%d" % i, [128, 512], F32) for i in range(2)]
        acc = [C.sb("sc_acc%d" % i, [128, 512], F32) for i in range(2)]
        wbuf = [C.sb("sc_wi%d" % i, [128, KC, 384], BF16) for i in range(2)]
        wobuf = [C.sb("sc_wo%d" % i, [128, KC, 128], BF16) for i in range(2)]
        pb = [C.ps("sc_pb%d" % i, [128, 512]) for i in range(2)]
        pc = [C.ps("sc_pc%d" % i, [128, 512]) for i in range(2)]
        ph = [C.ps("sc_ph%d" % i, [128, 512]) for i in range(2)]
        po = [C.ps("sc_po%d" % i, [128, 512]) for i in range(1)]
        emit_rmsnorm(C, x, "x", 0, T + 2, g_sb, "sc_g", hn, "sc_hn")
        it = 0
        for i in range(KC):
            wb = wbuf[i % 2]
            wk = ("sc_wi", i % 2)
            for q in range(3):
                S.dma("pool", writes=[wk + (q,)], out=wb[:, :, q * 128:(q + 1) * 128],
                      in_=w_in_v[:, :, q * D + i * 128: q * D + (i + 1) * 128])
            ch = chb[i % 2]
            bs = bsv[i % 2]
            for tt in range(NT + 1):
                n = 512 if tt < NT else 2
                it += 1
                b = it % 2
                sl = slice(tt * 512, tt * 512 + n)
                hkeys = lambda k: [("sc_hn", k, tt)]
                if tt < NT:
                    for k in range(KC):
                        S.mm(pb[b][:, :n], wb[:, k, 0:128], hn[:, k, sl], k == 0, k == KC - 1,
                             [wk + (0,)] + hkeys(k), [("sc_pb", b)])
                for k in range(KC):
                    S.mm(pc[b][:, :n], wb[:, k, 128:256], hn[:, k, sl], k == 0, k == KC - 1,
                         [wk + (1,)] + hkeys(k), [("sc_pc", b)])
                for k in range(KC):
                    S.mm(ph[b][:, :n], wb[:, k, 256:384], hn[:, k, sl], k == 0, k == KC - 1,
                         [wk + (2,)] + hkeys(k), [("sc_ph", b)])
                S.op("act", "activation", reads=[("sc_pc", b)], writes=[("sc_c", b)],
                     out=csb[b][:, :n], in_=pc[b][:, :n], func=AF.Copy)
                if tt < NT:
                    S.op("dve", "tensor_tensor", reads=[("sc_c", b), ("sc_ph", b)], writes=[("sc_ch", i % 2, tt)],
                         out=ch[:, 1 + tt * 512: 1 + tt * 512 + n], in0=ph[b][:, :n], in1=csb[b][:, :n], op=ALU.mult)
                    S.op("act", "activation", reads=[("sc_pb", b)], writes=[("sc_b", i % 2, tt)],
                         out=bs[:, sl], in_=pb[b][:, :n], func=AF.Copy)
                else:
                    S.op("dve", "tensor_tensor", reads=[("sc_c", b), ("sc_ph", b)], writes=[("sc_ch", i % 2, "hl")],
                         out=ch[:, 0:1], in0=ph[b][:, 0:1], in1=csb[b][:, 0:1], op=ALU.mult)
                    S.op("dve", "tensor_tensor", reads=[("sc_c", b), ("sc_ph", b)], writes=[("sc_ch", i % 2, "hr")],
                         out=ch[:, T + 1:T + 2], in0=ph[b][:, 1:2], in1=csb[b][:, 1:2], op=ALU.mult)
            for tt in range(NT):
                a = acc[tt % 2]
                ak = ("sc_acc", tt % 2)
                rk = [("sc_ch", i % 2, tt)]
                if tt > 0:
                    rk.append(("sc_ch", i % 2, tt - 1))
                else:
                    rk.append(("sc_ch", i % 2, "hl"))
                if tt < NT - 1:
                    rk.append(("sc_ch", i % 2, tt + 1))
                else:
                    rk.append(("sc_ch", i % 2, "hr"))
                o = tt * 512
                S.op("dve", "tensor_scalar", reads=rk + [("sc_cw", 0)], writes=[ak], out=a[:], in0=ch[:, o:o + 512],
                     scalar1=cw[:, i, 0:1], scalar2=None, op0=ALU.mult)
                S.op("dve", "scalar_tensor_tensor", reads=rk + [("sc_cw", 1), ak], writes=[ak], out=a[:],
                     in0=ch[:, o + 1:o + 513], scalar=cw[:, i, 1:2], in1=a[:], op0=ALU.mult, op1=ALU.add)
                S.op("dve", "scalar_tensor_tensor", reads=rk + [("sc_cw", 2), ak], writes=[ak], out=a[:],
                     in0=ch[:, o + 2:o + 514], scalar=cw[:, i, 2:3], in1=a[:], op0=ALU.mult, op1=ALU.add)
                S.op("dve", "tensor_tensor", reads=[ak, ("sc_b", i % 2, tt)], writes=[("sc_y", i, tt)],
                     out=ybf[:, i, o:o + 512], in0=a[:], in1=bs[:, o:o + 512], op=ALU.mult)
        for m in range(KC):
            wo = wobuf[m % 2]
            wok = ("sc_wo", m % 2)
            S.dma("pool", writes=[wok], out=wo[:], in_=w_out_v[:, :, m * 128:(m + 1) * 128])
            for tt in range(NT):
                sl = slice(tt * 512, (tt + 1) * 512)
                for i in range(KC):
                    S.mm(po[0][:], wo[:, i, :], ybf[:, i, sl], i == 0, i == KC - 1, [wok, ("sc_y", i, tt)], ["sc_po"])
                xkey = ("x", m, tt)
                S.op("dve", "tensor_tensor", reads=["sc_po", xkey], writes=[xkey], out=x[:, m, sl], in0=po[0][:],
                     in1=x[:, m, sl], op=ALU.add)


def emit_final_norm(C, x, T, g_dram):
    with C.scope():
        g_sb = load_vec_pk(C, "fin_g", g_dram, KC)
        emit_rmsnorm(C, x, "x", 0, T, g_sb, "fin_g", x, "x")


def emit_load_x(C, x, xT_dram, T):
    v = xT_dram.rearrange("(kc p) t -> p kc t", p=128)
    for k in range(KC):
        for tt in range((T + 511) // 512):
            n = min(512, T - tt * 512)
            C.S.dma("sp", writes=[("x", k, tt)], out=x[:, k, tt * 512:tt * 512 + n],
                    in_=v[:, k, tt * 512:tt * 512 + n])


def emit_store_x(C, x, yT_dram, T):
    v = yT_dram.rearrange("(kc p) t -> p kc t", p=128)
    for k in range(KC):
        for tt in range(T // 512):
            C.S.dma("sp", reads=[("x", k, tt)], is_out=True, out=v[:, k, tt * 512:(tt + 1) * 512],
                    in_=x[:, k, tt * 512:(tt + 1) * 512])


def build_ffn_prog(T=TPC):
    nc = bass.Bass("TRN2", target_bir_lowering=False)
    xT = dram_in(nc, "xT", [D, T])
    g = dram_in(nc, "g", [D])
    w_in = dram_in(nc, "w_in", [D, 2 * FF])
    w_out = dram_in(nc, "w_out", [FF, D])
    yT = dram_out(nc, "yT", [D, T])
    with contextlib.ExitStack() as es:
        C = Ctx(nc, es)
        emit_consts(C)
        x = C.sb("x", [128, KC, T], F32)
        emit_load_x(C, x, xT, T)
        emit_ffn(C, x, [(t, 1024) for t in range(0, T, 1024)], g, w_in, w_out, "f")
        emit_store_x(C, x, yT, T)
        C.S.replay()
    return nc


def build_sc_prog(T=TPC, final=False):
    nc = bass.Bass("TRN2", target_bir_lowering=False)
    xT = dram_in(nc, "xT", [D, T + 2])
    nrm = dram_in(nc, "nrm", [3, D])
    fw_in = dram_in(nc, "fw_in", [2, D, 2 * FF])
    fw_out = dram_in(nc, "fw_out", [2, FF, D])
    w_in = dram_in(nc, "w_in", [D, 3 * D])
    conv = dram_in(nc, "conv", [3, D])
    w_out = dram_in(nc, "w_out", [D, D])
    gfin = dram_in(nc, "gfin", [D])
    yT = dram_out(nc, "yT", [D, T])
    with contextlib.ExitStack() as es:
        C = Ctx(nc, es)
        emit_consts(C)
        x = C.sb("x", [128, KC, T + 2], F32)
        emit_load_x(C, x, xT, T + 2)
        grp = [(t, 1024) for t in range(0, T, 1024)]
        emit_ffn(C, x, grp + [(T, 2)], nrm[0], fw_in[0], fw_out[0], "f1")
        emit_sc_mixer(C, x, T, nrm[1], w_in, conv, w_out)
        emit_ffn(C, x, grp, nrm[2], fw_in[1], fw_out[1], "f2")
        if final:
            emit_final_norm(C, x, T, gfin)
        emit_store_x(C, x, yT, T)
        C.S.replay()
    return nc

NFFT = 2 * SEQ
HY_EMB = 33
HY_ORD = 64
MAGIC = 12582912.0
TWO_PI = 2.0 * math.pi
PI_LO = 3.1415925


def hyena_consts():
    n = np.arange(128, dtype=np.float64)
    ang = 2.0 * np.pi * np.outer(n, n) / 128.0
    fre, fim = np.cos(ang), -np.sin(ang)
    angt = 2.0 * np.pi * np.outer(n, n) / NFFT
    tre, tim = np.cos(angt), -np.sin(angt)
    cst = np.stack([fim, fre, -fim, tre, tim], 1).astype(np.float32)
    L = SEQ
    f32 = np.float32
    t = np.linspace(0.0, 1.0, L, dtype=f32)
    w = (f32(2.0 * math.pi) * np.arange(L, dtype=f32) / f32(L)).astype(f32)
    f = np.linspace(1e-4, 15.0, 16, dtype=f32)
    fw = (f[None, :] * w[:, None]).astype(f32)
    z = np.concatenate([t[:, None], np.cos(fw), -np.sin(fw)], -1).astype(f32)
    idx = np.concatenate([[0], np.arange(L - 1, 0, -1)])
    z2 = z[idx]
    t2 = t[idx].copy()
    t2[0] = 1e30
    zT = np.ascontiguousarray(np.concatenate([z, z2], 0).T)
    trow = np.concatenate([t, t2]).astype(f32)
    dmin = math.log(1e-2) / 1.5
    dmax = math.log(1e-2) / 0.3
    deltas = np.linspace(dmin, dmax, D, dtype=f32)
    nad = (-np.abs(deltas)).astype(f32)
    return cst, zT, trow, nad


def emit_fft_fwd(C, X, xkey, K, nseq, cst, tl, ps, kp=""):
    S = C.S
    fimfre = cst[:K, 0:2, :].rearrange("p a b -> p (a b)")
    for s_ in range(nseq):
        bank = ps["a"][s_ // 2]
        S.mm(bank[:, (s_ % 2) * 256:(s_ % 2) * 256 + 256], X[:K, s_, :], fimfre, True, True,
             [xkey, "cst"], [(kp + "psa", s_ // 2)])
    yield
    tre = cst[:, 3:4, :]
    tim = cst[:, 4:5, :]
    for h in range((nseq + 1) // 2):
        ns = min(2, nseq - 2 * h)
        av = ps["a"][h][:].rearrange("p (s r k) -> p s r k", s=2, r=2)
        aim = av[:, :ns, 0, :]
        are = av[:, :ns, 1, :]
        sl = slice(2 * h, 2 * h + ns)
        bt = lambda t_: t_.to_broadcast([128, ns, 128])
        S.op("dve", "tensor_tensor", reads=[(kp + "psa", h), "cst"], writes=[(kp + "t1", h)], out=tl["t1"][:, sl, :], in0=are,
             in1=bt(tre), op=ALU.mult)
        S.op("dve", "tensor_tensor", reads=[(kp + "psa", h), "cst"], writes=[(kp + "t2", h)], out=tl["t2"][:, sl, :], in0=aim,
             in1=bt(tim), op=ALU.mult)
        S.op("dve", "tensor_tensor", reads=[(kp + "psa", h), "cst"], writes=[(kp + "t3", h)], out=tl["t3"][:, sl, :], in0=are,
             in1=bt(tim), op=ALU.mult)
        S.op("dve", "tensor_tensor", reads=[(kp + "psa", h), "cst"], writes=[(kp + "t4", h)], out=tl["t4"][:, sl, :], in0=aim,
             in1=bt(tre), op=ALU.mult)
    hs = list(range((nseq + 1) // 2))
    S.op("pool", "tensor_tensor", reads=[(kp + "t1", h) for h in hs] + [(kp + "t2", h) for h in hs], writes=[kp + "bre"],
         out=tl["bre"][:, :nseq, :], in0=tl["t1"][:, :nseq, :], in1=tl["t2"][:, :nseq, :], op=ALU.subtract)
    S.op("pool", "tensor_tensor", reads=[(kp + "t3", h) for h in hs] + [(kp + "t4", h) for h in hs], writes=[kp + "bim"],
         out=tl["bim"][:, :nseq, :], in0=tl["t3"][:, :nseq, :], in1=tl["t4"][:, :nseq, :], op=ALU.add)
    yield
    n = nseq * 128
    bre = tl["bre"][:].rearrange("p s k -> p (s k)")[:, :n]
    bim = tl["bim"][:].rearrange("p s k -> p (s k)")[:, :n]
    S.mm(ps["xre"][:, :n], cst[:, 1, :], bre, True, False, ["cst", kp + "bre"], [kp + "psxre"])
    S.mm(ps["xre"][:, :n], cst[:, 2, :], bim, False, True, ["cst", kp + "bim"], [kp + "psxre"])
    S.mm(ps["xim"][:, :n], cst[:, 1, :], bim, True, False, ["cst", kp + "bim"], [kp + "psxim"])
    S.mm(ps["xim"][:, :n], cst[:, 0, :], bre, False, True, ["cst", kp + "bre"], [kp + "psxim"])
    yield


def build_hy_core_prog():
    nc = bass.Bass("TRN2", target_bir_lowering=False)
    L = SEQ
    u0 = dram_in(nc, "u0", [3, 128, BATCH, L])
    convw = dram_in(nc, "convw", [3, 3, 128])
    convb = dram_in(nc, "convb", [3, 128])
    dvec = dram_in(nc, "dvec", [128])
    fw1 = dram_in(nc, "fw1", [HY_EMB, HY_ORD])
    fb1 = dram_in(nc, "fb1", [HY_ORD])
    fw2 = dram_in(nc, "fw2", [HY_ORD, HY_ORD])
    fb2 = dram_in(nc, "fb2", [HY_ORD])
    fw3 = dram_in(nc, "fw3", [HY_ORD, 2, 128])
    freq = dram_in(nc, "freq", [HY_ORD])
    cstd = dram_in(nc, "cst", [128, 5, 128])
    zT = dram_in(nc, "zT", [HY_EMB, NFFT])
    trow = dram_in(nc, "trow", [128, NFFT])
    nad = dram_in(nc, "nad", [128])
    yT = dram_out(nc, "yT", [128, BATCH, L])
    s_h = nc.dram_tensor("s_h", [128, NFFT], F32).ap()
    s_H = nc.dram_tensor("s_H", [2, 128, 128, 128], F32).ap()
    s_vv = nc.dram_tensor("s_vv", [128, BATCH, L], F32).ap()
    s_x0 = nc.dram_tensor("s_x0", [128, BATCH, L], F32).ap()
    s_y = nc.dram_tensor("s_y", [128, BATCH, L], F32).ap()
    with contextlib.ExitStack() as es:
        C = Ctx(nc, es)
        S = C.S
        cst = C.sb("cst", [128, 5, 128], F32)
        S.dma("sp", writes=["cst"], out=cst[:], in_=cstd)

        def col(name, src, n):
            t_ = C.sb(name, [n, 1], F32)
            S.dma("sp", writes=[name], out=t_[:], in_=src.rearrange("(p o) -> p o", o=1))
            return t_

        with C.scope():
            w1 = C.sb("w1", [HY_EMB, HY_ORD], F32)
            S.dma("sp", writes=["w1"], out=w1[:], in_=fw1)
            w2 = C.sb("w2", [HY_ORD, HY_ORD], F32)
            S.dma("sp", writes=["w2"], out=w2[:], in_=fw2)
            w3 = C.sb("w3", [HY_ORD, 2, 128], F32)
            S.dma("sp", writes=["w3"], out=w3[:], in_=fw3)
            fq = col("fq", freq, HY_ORD)
            b1 = col("b1", fb1, HY_ORD)
            b2 = col("b2", fb2, HY_ORD)
            nadc = col("nadc", nad, 128)
            S.op("dve", "tensor_tensor", reads=["fq", "b1"], writes=["b1"], out=b1[:], in0=b1[:], in1=fq[:], op=ALU.mult)
            S.op("dve", "tensor_tensor", reads=["fq", "b2"], writes=["b2"], out=b2[:], in0=b2[:], in1=fq[:], op=ALU.mult)
            zt = [C.sb("zt%d" % i, [HY_EMB, 512], F32) for i in range(2)]
            tr = [C.sb("tr%d" % i, [128, 512], F32) for i in range(2)]
            NS = 2
            av = [[C.sb("av%d%d" % (l_, i), [HY_ORD, 512], F32) for i in range(NS)] for l_ in range(2)]
            qv = [[C.sb("qv%d%d" % (l_, i), [HY_ORD, 512], F32) for i in range(NS)] for l_ in range(2)]
            hv = [[C.sb("hv%d%d" % (l_, i), [HY_ORD, 512], F32) for i in range(NS)] for l_ in range(2)]
            hc = [C.sb("hc%d" % i, [128, 512], F32) for i in range(NS)]
            p1 = [C.ps("p1%d" % i, [HY_ORD, 512]) for i in range(NS)]
            p2 = [C.ps("p2%d" % i, [HY_ORD, 512]) for i in range(NS)]
            p3 = [C.ps("p3%d" % i, [128, 512]) for i in range(NS)]

            def sin_layer(psrc, pkey, bias, sl_, lay):
                a, q, h = av[lay][sl_], qv[lay][sl_], hv[lay][sl_]
                ak, qk, hk = ("av", lay, sl_), ("qv", lay, sl_), ("hv", lay, sl_)
                S.op("dve", "tensor_scalar", reads=[pkey, "fq", "b1", "b2"], writes=[ak], out=a[:], in0=psrc[:],
                     scalar1=fq[:, 0:1], scalar2=bias[:, 0:1], op0=ALU.mult, op1=ALU.add)
                S.op("dve", "tensor_scalar", reads=[ak], writes=[qk], out=q[:], in0=a[:], scalar1=1.0 / TWO_PI,
                     scalar2=MAGIC, op0=ALU.mult, op1=ALU.add)
                S.op("dve", "tensor_scalar", reads=[qk], writes=[qk], out=q[:], in0=q[:], scalar1=-MAGIC,
                     scalar2=-TWO_PI, op0=ALU.add, op1=ALU.mult)
                S.op("dve", "tensor_tensor", reads=[qk, ak], writes=[ak], out=a[:], in0=a[:], in1=q[:], op=ALU.add)
                S.op("dve", "tensor_scalar", reads=[ak], writes=[ak], out=a[:], in0=a[:], scalar1=-PI_LO,
                     scalar2=PI_LO, op0=ALU.max, op1=ALU.min)
                yield
                S.op("act", "activation", reads=[ak], writes=[hk], out=h[:], in_=a[:], func=AF.Sin)
                yield

            def p0_tile(ti):
                c0 = ti * 512
                sl_ = ti % NS
                z_, zk = zt[sl_], ("zt", sl_)
                t_, tk = tr[sl_], ("tr", sl_)
                S.dma("sp", writes=[zk], out=z_[:], in_=zT[:, c0:c0 + 512])
                S.dma("sp", writes=[tk], out=t_[:], in_=trow[:, c0:c0 + 512])
                S.mm(p1[sl_][:], w1[:], z_[:], True, True, ["w1", zk], [("p1", sl_)])
                yield
                for _ in sin_layer(p1[sl_], ("p1", sl_), b1, sl_, 0):
                    yield
                S.mm(p2[sl_][:], w2[:], hv[0][sl_][:], True, True, ["w2", ("hv", 0, sl_)], [("p2", sl_)])
                S.op("act", "activation", reads=[tk, "nadc"], writes=[tk], out=t_[:], in_=t_[:], func=AF.Exp,
                     scale=nadc[:, 0:1])
                yield
                for _ in sin_layer(p2[sl_], ("p2", sl_), b2, sl_, 1):
                    yield
                half = 0 if ti < (L // 512) else 1
                S.mm(p3[sl_][:], w3[:, half, :], hv[1][sl_][:], True, True, ["w3", ("hv", 1, sl_)], [("p3", sl_)])
                yield
                o_, ok = hc[sl_], ("hc", sl_)
                S.op("dve", "tensor_tensor", reads=[("p3", sl_), tk], writes=[ok], out=o_[:], in0=p3[sl_][:], in1=t_[:],
                     op=ALU.mult)
                S.dma("sp", reads=[ok], writes=[("s_h", ti)], out=s_h[:, c0:c0 + 512], in_=o_[:])
                yield

            pipeline((p0_tile(ti) for ti in range(NFFT // 512)), NS)

        with C.scope():
            cw = C.sb("hcw", [128, 3, 3], F32)
            for j in range(3):
                for gi in range(3):
                    S.dma("sp", writes=[("hcw", j, gi)], out=cw[:, gi, j:j + 1],
                          in_=convw[j, gi].rearrange("(p o) -> p o", o=1))
            cb = C.sb("hcb", [128, 3], F32)
            for gi in range(3):
                S.dma("sp", writes=[("hcb", gi)], out=cb[:, gi:gi + 1], in_=convb[gi].rearrange("(p o) -> p o", o=1))
            cwk = [("hcw", j, gi) for j in range(3) for gi in range(3)] + [("hcb", gi) for gi in range(3)]
            ub = [C.sb("hu%d" % i, [128, L + 2], F32) for i in range(2)] * 2
            uc = [C.sb("huc%d" % i, [128, L], F32) for i in range(3)]
            for b in range(BATCH):
                for gi in range(3):
                    S.op("pool", "memset", writes=[("hu", gi % 2, "e")], ap=ub[gi][:, 0:1], constant=0.0)
                    S.op("pool", "memset", writes=[("hu", gi % 2, "e2")], ap=ub[gi][:, L + 1:L + 2], constant=0.0)
                    S.dma("sp", writes=[("hu", gi % 2)], out=ub[gi][:, 1:L + 1], in_=u0[gi, :, b, :])
                    rk = [("hu", gi % 2), ("hu", gi % 2, "e"), ("hu", gi % 2, "e2")] + cwk
                    uk = ("huc", gi)
                    S.op("dve", "tensor_scalar", reads=rk, writes=[uk], out=uc[gi][:], in0=ub[gi][:, 0:L],
                         scalar1=cw[:, gi, 0:1], scalar2=cb[:, gi:gi + 1], op0=ALU.mult, op1=ALU.add)
                    S.op("dve", "scalar_tensor_tensor", reads=rk + [uk], writes=[uk], out=uc[gi][:],
                         in0=ub[gi][:, 1:L + 1], scalar=cw[:, gi, 1:2], in1=uc[gi][:], op0=ALU.mult, op1=ALU.add)
                    S.op("dve", "scalar_tensor_tensor", reads=rk + [uk], writes=[uk], out=uc[gi][:],
                         in0=ub[gi][:, 2:L + 2], scalar=cw[:, gi, 2:3], in1=uc[gi][:], op0=ALU.mult, op1=ALU.add)
                S.op("pool", "tensor_tensor", reads=[("huc", 1), ("huc", 2)], writes=[("huc", 2)], out=uc[2][:],
                     in0=uc[2][:], in1=uc[1][:], op=ALU.mult)
                S.dma("sp", reads=[("huc", 2)], writes=[("s_vv", b)], out=s_vv[:, b, :], in_=uc[2][:])
                S.dma("sp", reads=[("huc", 0)], writes=[("s_x0", b)], out=s_x0[:, b, :], in_=uc[0][:])
        with C.scope():
            tl = {k: C.sb("fft_" + k, [128, 4, 128], F32) for k in
                  ("t1", "t2", "t3", "t4", "u1", "u2", "u3", "u4", "bre", "bim", "dre", "dim")}
            yre = [C.sb("fft_yre%d" % i, [128, 4, 128], F32) for i in range(2)]
            yim = [C.sb("fft_yim%d" % i, [128, 4, 128], F32) for i in range(2)]
            ps = {"a": [C.ps("psa%d" % i, [128, 512]) for i in range(2)], "xre": C.ps("psxre", [128, 512]),
                  "xim": C.ps("psxim", [128, 512]), "c": [C.ps("psc%d" % i, [128, 512]) for i in range(2)],
                  "y": C.ps("psy", [128, 512])}
            Xb = [C.sb("fft_X%d" % i, [128, 4, 128], F32) for i in range(2)]
            Hb = [[C.sb("fft_H%d%d" % (i, r), [128, 2, 128], F32) for r in range(2)] for i in range(2)]
            ev = [[C.sb("fft_ev%d%d" % (i, r), [128, 4, 128], F32) for r in range(2)] for i in range(2)]
            yo = [C.sb("fft_yo%d" % i, [64, 4, 128], F32) for i in range(2)]
            ps8 = C.ps("psx8", [128, 512])
            tlB = {"t1": tl["u1"], "t2": tl["u2"], "t3": tl["u3"], "t4": tl["u4"], "bre": tl["dre"], "bim": tl["dim"]}
            psB = {"a": ps["c"], "xre": ps["y"], "xim": ps8}

            def p1_group(g):
                q = g % 2
                tl_, ps_, kp = (tl, ps, "") if q == 0 else (tlB, psB, "B")
                X, xk = Xb[q], ("X", q)
                S.dma("sp", reads=[("s_h", ti) for ti in range(NFFT // 512)], writes=[xk], out=X[:],
                      in_=s_h[4 * g:4 * g + 4, :].rearrange("c (n1 n2) -> n1 c n2", n2=128))
                for _ in emit_fft_fwd(C, X, xk, 128, 4, cst, tl_, ps_, kp):
                    yield
                for r, nm in ((0, "xre"), (1, "xim")):
                    e_, ek = ev[q][r], ("ev", q, r)
                    S.op("act", "mul", reads=[kp + "ps" + nm], writes=[ek], out=e_[:].rearrange("p s k -> p (s k)"),
                         in_=ps_[nm][:], mul=1.0 / NFFT)
                    S.dma("sp", reads=[ek], writes=[("s_H", g, r)], out=s_H[r, :, 4 * g:4 * g + 4, :], in_=e_[:])
                yield

            pipeline((p1_group(g) for g in range(32)), 2)
            S.barrier()
            tre = cst[:, 3:4, :]
            tim = cst[:, 4:5, :]
            g1 = cst[:, 1:3, :].rearrange("p a b -> p (a b)")
            g2 = cst[:, 0:2, :].rearrange("p a b -> p (a b)")

            def half1(g):
                p = g % 2
                X, xk = Xb[p], ("X", p)
                S.dma("sp", reads=[("s_vv", 0), ("s_vv", 1)], writes=[xk], out=X[:64],
                      in_=s_vv[2 * g:2 * g + 2].rearrange("c b (n1 n2) -> n1 (c b) n2", n2=128))
                H = Hb[p]
                for r in range(2):
                    S.dma("sp", reads=[("s_H", g // 2, r)], writes=[("H", p, r)], out=H[r][:],
                          in_=s_H[r, :, 2 * g:2 * g + 2, :])
                for _ in emit_fft_fwd(C, X, xk, 64, 4, cst, tl, ps):
                    yield
                hk = [("H", p, 0), ("H", p, 1)]
                xre = ps["xre"][:].rearrange("p (c b k) -> p c b k", c=2, b=2)
                xim = ps["xim"][:].rearrange("p (c b k) -> p c b k", c=2, b=2)
                hb = lambda r: H[r][:].unsqueeze(2).to_broadcast([128, 2, 2, 128])
                v4 = lambda t_: t_[:].rearrange("p (c b) k -> p c b k", c=2)
                S.op("dve", "tensor_tensor", reads=["psxre"] + hk, writes=[("t1", 0), ("t1", 1)], out=v4(tl["t1"]),
                     in0=xre, in1=hb(0), op=ALU.mult)
                S.op("dve", "tensor_tensor", reads=["psxim"] + hk, writes=[("t2", 0), ("t2", 1)], out=v4(tl["t2"]),
                     in0=xim, in1=hb(1), op=ALU.mult)
                S.op("dve", "tensor_tensor", reads=["psxre"] + hk, writes=[("t3", 0), ("t3", 1)], out=v4(tl["t3"]),
                     in0=xre, in1=hb(1), op=ALU.mult)
                S.op("dve", "tensor_tensor", reads=["psxim"] + hk, writes=[("t4", 0), ("t4", 1)], out=v4(tl["t4"]),
                     in0=xim, in1=hb(0), op=ALU.mult)
                S.op("pool", "tensor_tensor", reads=[("t1", 0), ("t1", 1), ("t2", 0), ("t2", 1)], writes=[("yre", p)],
                     out=yre[p][:], in0=tl["t1"][:], in1=tl["t2"][:], op=ALU.subtract)
                S.op("pool", "tensor_tensor", reads=[("t3", 0), ("t3", 1), ("t4", 0), ("t4", 1)], writes=[("yim", p)],
                     out=yim[p][:], in0=tl["t3"][:], in1=tl["t4"][:], op=ALU.add)
                yield

            def half2(g):
                p = g % 2
                for s_ in range(4):
                    bank = ps["c"][s_ // 2]
                    o = (s_ % 2) * 256
                    S.mm(bank[:, o:o + 256], yre[p][:, s_, :], g1, True, False, [("yre", p), "cst"], [("psc", s_ // 2)])
                    S.mm(bank[:, o:o + 256], yim[p][:, s_, :], g2, False, True, [("yim", p), "cst"], [("psc", s_ // 2)])
                yield
                for h in range(2):
                    cv = ps["c"][h][:].rearrange("p (s r k) -> p s r k", s=2, r=2)
                    cre = cv[:, :, 0, :]
                    cim = cv[:, :, 1, :]
                    sl = slice(2 * h, 2 * h + 2)
                    bt = lambda t_: t_.to_broadcast([128, 2, 128])
                    S.op("dve", "tensor_tensor", reads=[("psc", h), "cst"], writes=[("u1", h)], out=tl["u1"][:, sl, :],
                         in0=cre, in1=bt(tre), op=ALU.mult)
                    S.op("dve", "tensor_tensor", reads=[("psc", h), "cst"], writes=[("u2", h)], out=tl["u2"][:, sl, :],
                         in0=cim, in1=bt(tim), op=ALU.mult)
                    S.op("dve", "tensor_tensor", reads=[("psc", h), "cst"], writes=[("u3", h)], out=tl["u3"][:, sl, :],
                         in0=cim, in1=bt(tre), op=ALU.mult)
                    S.op("dve", "tensor_tensor", reads=[("psc", h), "cst"], writes=[("u4", h)], out=tl["u4"][:, sl, :],
                         in0=cre, in1=bt(tim), op=ALU.mult)
                S.op("pool", "tensor_tensor", reads=[("u1", 0), ("u1", 1), ("u2", 0), ("u2", 1)], writes=["dre"],
                     out=tl["dre"][:], in0=tl["u1"][:], in1=tl["u2"][:], op=ALU.add)
                S.op("pool", "tensor_tensor", reads=[("u3", 0), ("u3", 1), ("u4", 0), ("u4", 1)], writes=["dim"],
                     out=tl["dim"][:], in0=tl["u3"][:], in1=tl["u4"][:], op=ALU.subtract)
                yield
                S.mm(ps["y"][:64, :], cst[:, 1, 0:64], tl["dre"][:].rearrange("p s k -> p (s k)"), True, False,
                     ["cst", "dre"], ["psy"])
                S.mm(ps["y"][:64, :], cst[:, 0, 0:64], tl["dim"][:].rearrange("p s k -> p (s k)"), False, True,
                     ["cst", "dim"], ["psy"])
                yield
                y_, yk = yo[p], ("yo", p)
                S.op("act", "activation", reads=["psy"], writes=[yk], out=y_[:].rearrange("p s k -> p (s k)"),
                     in_=ps["y"][:64, :], func=AF.Copy)
                S.dma("sp", reads=[yk], writes=[("s_y", g)], out=s_y[2 * g:2 * g + 2].rearrange(
                    "c b (n1 n2) -> n1 (c b) n2", n2=128), in_=y_[:])
                yield

            for _ in half1(0):
                pass
            for g in range(64):
                interleave([half2(g)] + ([half1(g + 1)] if g + 1 < 64 else []))
        with C.scope():
            dcol = col("dcol", dvec, 128)
            ya = C.sb("p4y", [128, L], F32)
            va = C.sb("p4v", [128, L], F32)
            xa = C.sb("p4x", [128, L], F32)
            for b in range(BATCH):
                S.dma("sp", reads=[("s_y", g) for g in range(64)], writes=["p4y"], out=ya[:], in_=s_y[:, b, :])
                S.dma("sp", reads=[("s_vv", b)], writes=["p4v"], out=va[:], in_=s_vv[:, b, :])
                S.dma("sp", reads=[("s_x0", b)], writes=["p4x"], out=xa[:], in_=s_x0[:, b, :])
                S.op("dve", "scalar_tensor_tensor", reads=["p4y", "p4v", "dcol"], writes=["p4y"], out=ya[:], in0=va[:],
                     scalar=dcol[:, 0:1], in1=ya[:], op0=ALU.mult, op1=ALU.add)
                S.op("dve", "tensor_tensor", reads=["p4y", "p4x"], writes=["p4y"], out=ya[:], in0=ya[:], in1=xa[:],
                     op=ALU.mult)
                S.dma("sp", reads=["p4y"], is_out=True, out=yT[:, b, :], in_=ya[:])
        S.replay()
    return nc

def emit_proj(C, x, T, g_dram, w_in, b_in, nout, uT):
    S = C.S
    NT = T // 512
    w_in_v = w_in.rearrange("(kc p) n -> p kc n", p=128)
    with C.scope():
        g_sb = load_vec_pk(C, "pj_g", g_dram, KC)
        bias = load_vec_pk(C, "pj_b", b_in, nout // 128)
        hn = C.sb("pj_hn", [128, KC, T], BF16)
        wbuf = [C.sb("pj_w%d" % i, [128, KC, 128], BF16) for i in range(3)]
        st = [C.sb("pj_st%d" % i, [128, 512], F32) for i in range(4)]
        pp = [C.ps("pj_ps%d" % i, [128, 512]) for i in range(2)]
        emit_rmsnorm(C, x, "x", 0, T, g_sb, "pj_g", hn, "pj_hn")
        it = 0
        for m in range(nout // 128):
            wb, wk = wbuf[m % 3], ("pj_w", m % 3)
            S.dma("pool", writes=[wk], out=wb[:], in_=w_in_v[:, :, m * 128:(m + 1) * 128])
            for tt in range(NT):
                it += 1
                b = it % 2
                sl = slice(tt * 512, (tt + 1) * 512)
                for k in range(KC):
                    S.mm(pp[b][:], wb[:, k, :], hn[:, k, sl], k == 0, k == KC - 1, [wk, ("pj_hn", k, tt)], [("pj_ps", b)])
                o_, ok = st[it % 4], ("pj_st", it % 4)
                S.op("act", "activation", reads=[("pj_ps", b), "pj_b"], writes=[ok], out=o_[:], in_=pp[b][:],
                     func=AF.Identity, bias=bias[:, m:m + 1])
                S.dma("sp", reads=[ok], is_out=True, out=uT[m * 128:(m + 1) * 128, sl], in_=o_[:])


def emit_outproj(C, x, T, yT, w_out, b_out):
    S = C.S
    NT = T // 512
    w_out_v = w_out.rearrange("(kc p) n -> p kc n", p=128)
    y_v = yT.rearrange("(kc p) t -> p kc t", p=128)
    with C.scope():
        bo = load_vec_pk(C, "op_b", b_out, KC)
        ybf = C.sb("op_y", [128, KC, T], BF16)
        for k in range(KC):
            S.dma("pool", writes=[("op_y", k)], out=ybf[:, k, :], in_=y_v[:, k, :])
        wobuf = [C.sb("op_wo%d" % i, [128, KC, 128], BF16) for i in range(2)]
        po = [C.ps("op_po%d" % i, [128, 512]) for i in range(2)]
        it = 0
        for m in range(KC):
            wo, wok = wobuf[m % 2], ("op_wo", m % 2)
            S.dma("pool", writes=[wok], out=wo[:], in_=w_out_v[:, :, m * 128:(m + 1) * 128])
            for tt in range(NT):
                it += 1
                b = it % 2
                sl = slice(tt * 512, (tt + 1) * 512)
                for i in range(KC):
                    S.mm(po[b][:], wo[:, i, :], ybf[:, i, sl], i == 0, i == KC - 1, [wok, ("op_y", i)], [("op_po", b)])
                xkey = ("x", m, tt)
                S.op("dve", "scalar_tensor_tensor", reads=[("op_po", b), xkey, "op_b"], writes=[xkey], out=x[:, m, sl],
                     in0=po[b][:], scalar=bo[:, m:m + 1], in1=x[:, m, sl], op0=ALU.add, op1=ALU.add)


def build_pre_prog(nout, T=TPC):
    nc = bass.Bass("TRN2", target_bir_lowering=False)
    xT = dram_in(nc, "xT", [D, T])
    nrm = dram_in(nc, "nrm", [2, D])
    fw_in = dram_in(nc, "fw_in", [D, 2 * FF])
    fw_out = dram_in(nc, "fw_out", [FF, D])
    w_in = dram_in(nc, "w_in", [D, nout])
    b_in = dram_in(nc, "b_in", [nout])
    xo = dram_out(nc, "xo", [D, T])
    uT = dram_out(nc, "uT", [nout, T])
    with contextlib.ExitStack() as es:
        C = Ctx(nc, es)
        emit_consts(C)
        x = C.sb("x", [128, KC, T], F32)
        emit_load_x(C, x, xT, T)
        emit_ffn(C, x, [(t, 1024) for t in range(0, T, 1024)], nrm[0], fw_in, fw_out, "f1")
        emit_store_x(C, x, xo, T)
        emit_proj(C, x, T, nrm[1], w_in, b_in, nout, uT)
        C.S.replay()
    return nc


def build_post_prog(T=TPC):
    nc = bass.Bass("TRN2", target_bir_lowering=False)
    xT = dram_in(nc, "xT", [D, T])
    yT = dram_in(nc, "yT", [D, T])
    w_out = dram_in(nc, "w_out", [D, D])
    b_out = dram_in(nc, "b_out", [D])
    nrm = dram_in(nc, "nrm", [D])
    fw_in = dram_in(nc, "fw_in", [D, 2 * FF])
    fw_out = dram_in(nc, "fw_out", [FF, D])
    xo = dram_out(nc, "xo", [D, T])
    with contextlib.ExitStack() as es:
        C = Ctx(nc, es)
        emit_consts(C)
        x = C.sb("x", [128, KC, T], F32)
        emit_load_x(C, x, xT, T)
        emit_outproj(C, x, T, yT, w_out, b_out)
        emit_ffn(C, x, [(t, 1024) for t in range(0, T, 1024)], nrm, fw_in, fw_out, "f2")
        emit_store_x(C, x, xo, T)
        C.S.replay()
    return nc


CH = 64
NCH = SEQ // CH
BLK = 8
SCAN_GAP = 3


def gdn_consts():
    i = np.arange(CH)
    mtri = np.stack([(i[:, None] <= i[None, :]), (i[:, None] >= i[None, :])]).astype(np.float32)
    eye = np.eye(CH, dtype=np.float32)
    mstrict = mtri - eye[None]
    ident = np.eye(128, dtype=np.float32)
    return mtri, mstrict, ident


def build_gd_core_prog(dbg_blocks=None):
    nc = bass.Bass("TRN2", target_bir_lowering=False)
    L, N = SEQ, NCH
    qkv0 = dram_in(nc, "qkv0", [3, 128, BATCH, L])
    ztok = dram_in(nc, "ztok", [CH, BATCH, N, 128])
    abt = dram_in(nc, "abt", [4, CH, BATCH, N])
    convw = dram_in(nc, "convw", [3, 3, 128])
    alog = dram_in(nc, "alog", [CH, 2])
    dtb = dram_in(nc, "dtb", [CH, 2])
    gn = dram_in(nc, "gn", [CH, 128])
    mtri_d = dram_in(nc, "mtri", [2, CH, CH])
    mstr_d = dram_in(nc, "mstrict", [2, CH, CH])
    ident_d = dram_in(nc, "ident", [128, 128])
    ytok = dram_out(nc, "ytok", [CH, BATCH, N, 128])
    s_o = nc.dram_tensor("s_o", [2, CH, BATCH, N, 128], F32).ap()
    with contextlib.ExitStack() as es:
        C = Ctx(nc, es)
        S = C.S
        ones = C.sb("ones", [128, 128], F32)
        S.op("dve", "memset", writes=["ones"], ap=ones[:], constant=1.0)
        ident = C.sb("ident", [128, 128], F32)
        S.dma("sp", writes=["ident"], out=ident[:], in_=ident_d)
        mtri = C.sb("mtri", [CH, 2, CH], F32)
        mstr = C.sb("mstr", [CH, 2, CH], F32)
        for d in range(2):
            S.dma("sp", writes=[("mtri", d)], out=mtri[:, d, :], in_=mtri_d[d])
            S.dma("sp", writes=[("mstr", d)], out=mstr[:, d, :], in_=mstr_d[d])
        cw = C.sb("gcw", [128, 3, 3], F32)
        for j in range(3):
            for gi in range(3):
                S.dma("sp", writes=[("gcw", j, gi)], out=cw[:, gi, j:j + 1], in_=convw[j, gi].rearrange("(p o) -> p o", o=1))
        cwk = [("gcw", j, gi) for j in range(3) for gi in range(3)]
        gtf = C.sb("gt", [128, 2, BATCH, N], F32)
        S.op("dve", "memset", writes=[("gt", 0), ("gt", 1)], ap=gtf[:], constant=0.0)
        gt = gtf[:CH]
        bt_ = C.sb("bt", [CH, 2, BATCH, N], F32)
        al = C.sb("al", [CH, 2], F32)
        db = C.sb("db", [CH, 2], F32)
        S.dma("sp", writes=["al"], out=al[:], in_=alog)
        S.dma("sp", writes=["db"], out=db[:], in_=dtb)
        for d in range(2):
            S.dma("sp", writes=[("gt", d)], out=gt[:, d], in_=abt[d])
            S.dma("sp", writes=[("bt", d)], out=bt_[:, d], in_=abt[2 + d])
        S.op("act", "activation", reads=["al"], writes=["al"], out=al[:], in_=al[:], func=AF.Exp)
        S.op("dve", "tensor_scalar", reads=["al"], writes=["al"], out=al[:], in0=al[:], scalar1=-1.0, scalar2=None,
             op0=ALU.mult)
        for d in range(2):
            gv = gt[:, d].rearrange("p b n -> p (b n)")
            bv = bt_[:, d].rearrange("p b n -> p (b n)")
            S.op("act", "activation", reads=[("gt", d), "db"], writes=[("gt", d)], out=gv, in_=gv, func=AF.Exp,
                 bias=db[:, d:d + 1])
            S.op("dve", "tensor_scalar", reads=[("gt", d)], writes=[("gt", d)], out=gv, in0=gv, scalar1=1.0, scalar2=None,
                 op0=ALU.add)
            S.op("act", "activation", reads=[("gt", d)], writes=[("gt", d)], out=gv, in_=gv, func=AF.Ln)
            S.op("dve", "tensor_scalar", reads=[("gt", d), "al"], writes=[("gt", d)], out=gv, in0=gv, scalar1=al[:, d:d + 1],
                 scalar2=None, op0=ALU.mult)
            S.op("act", "activation", reads=[("bt", d)], writes=[("bt", d)], out=bv, in_=bv, func=AF.Sigmoid)
        with C.scope():
            qT = C.sb("qT", [128, L], F32)
            kT = C.sb("kT", [128, L], F32)
            vT = C.sb("vT", [128, L], F32)
            ubs = [C.sb("gub%d" % i, [128, 2048 + 2], F32) for i in range(2)]
            ci = 0
            rsb = [C.sb("grs%d" % i, [128, 512], F32) for i in range(2)]
            sqb = [C.sb("gsq%d" % i, [128, 512], F32) for i in range(2)]
            gc = C.sb("gc", [CH, N], F32)
            kd = C.sb("kd", [CH, N], F32)
            bg = C.sb("bg", [CH, N], F32)
            egt = C.sb("egt", [128, N], F32)
            Sst = [C.sb("Sst%d" % i, [128, 128], F32) for i in range(2)]
            attnT = [C.sb("g8_attnT%d" % i, [128, BLK, CH], F32) for i in range(2)]
            for i_ in range(2):
                S.op("dve", "memset", writes=[("attnT", i_)], ap=attnT[i_][:], constant=0.0)
            names = ["dT", "W1", "LT", "Lm", "X8", "Pa", "Pta", "Pb", "Ptb"]
            t8f = {nm: C.sb("g8_" + nm, [128, BLK, CH], F32) for nm in names}
            for nm in names:
                S.op("dve", "memset", writes=[nm], ap=t8f[nm][:], constant=0.0)
            t8 = {nm: t8f[nm][:CH] for nm in names}
            qd = [C.sb("g8_qd%d" % i, [128, BLK * CH], F32) for i in range(2)]
            egr = C.sb("g8_egr", [128, BLK * CH], F32)
            kdt = [C.sb("g8_kdt%d" % i, [128, BLK, 128], F32) for i in range(2)]
            for i_ in range(2):
                S.op("dve", "memset", writes=[("kdt", i_, 0)], ap=kdt[i_][:, 0:4, :], constant=0.0)
                S.op("dve", "memset", writes=[("kdt", i_, 1)], ap=kdt[i_][:, 4:8, :], constant=0.0)
            kbt = C.sb("g8_kbt", [128, BLK, 128], F32)
            S.op("dve", "memset", writes=[("kbt", 0)], ap=kbt[:, 0:4, :], constant=0.0)
            S.op("dve", "memset", writes=[("kbt", 1)], ap=kbt[:, 4:8, :], constant=0.0)
            vbt = C.sb("g8_vbt", [CH, BLK, 128], F32)
            u8 = [C.sb("g8_u8%d" % i, [128, BLK, 128], F32) for i in range(2)]
            MT8 = [C.sb("g8_MT%d" % i, [128, BLK, 128], F32) for i in range(2)]
            w8 = C.sb("g8_w8", [128, BLK, 128], F32)
            mtmp = C.sb("g8_mtmp", [128, BLK, 128], F32)
            for i_ in range(2):
                for hh_ in range(2):
                    S.op("dve", "memset", writes=[("u8", i_, hh_)], ap=u8[i_][:, hh_ * 4:hh_ * 4 + 4, :], constant=0.0)
            for hh_ in range(2):
                S.op("dve", "memset", writes=[("w8", hh_)], ap=w8[:, hh_ * 4:hh_ * 4 + 4, :], constant=0.0)
            o8 = [C.sb("g8_o8%d" % i, [CH, BLK, 128], F32) for i in range(2)]
            Bk = [C.ps("gB%d" % i, [128, 512]) for i in range(8)]
            bk = lambda i: ("B", i)
            b7all = [bk(7)]
            S.excl.update(bk(i) for i in range(8))

            for b in range(BATCH):
                for gi, dst, dk_ in ((0, qT, "qT"), (1, kT, "kT"), (2, vT, "vT")):
                    CT = 2048
                    for ct in range(L // CT):
                        ci += 1
                        u_, uk_ = ubs[ci % 2], ("gub", ci % 2)
                        lo = ct * CT - 1
                        hi = ct * CT + CT + 1
                        a_ = max(lo, 0)
                        b_ = min(hi, L)
                        wr = [uk_]
                        if lo < 0:
                            S.op("pool", "memset", writes=[uk_], ap=u_[:, 0:1], constant=0.0)
                        if hi > L:
                            S.op("pool", "memset", writes=[uk_], ap=u_[:, CT + 1:CT + 2], constant=0.0)
                        S.dma("sp", writes=[uk_], out=u_[:, a_ - lo:b_ - lo], in_=qkv0[gi, :, b, a_:b_])
                        rk = wr + cwk
                        dsl = slice(ct * CT, (ct + 1) * CT)
                        dkt = (dk_, ct)
                        S.op("dve", "tensor_scalar", reads=rk, writes=[dkt], out=dst[:, dsl], in0=u_[:, 0:CT],
                             scalar1=cw[:, gi, 0:1], scalar2=None, op0=ALU.mult)
                        S.op("dve", "scalar_tensor_tensor", reads=rk + [dkt], writes=[dkt], out=dst[:, dsl], in0=u_[:, 1:CT + 1],
                             scalar=cw[:, gi, 1:2], in1=dst[:, dsl], op0=ALU.mult, op1=ALU.add)
                        S.op("dve", "scalar_tensor_tensor", reads=rk + [dkt], writes=[dkt], out=dst[:, dsl], in0=u_[:, 2:CT + 2],
                             scalar=cw[:, gi, 2:3], in1=dst[:, dsl], op0=ALU.mult, op1=ALU.add)
                        S.op("act", "activation", reads=[dkt], writes=[dkt], out=dst[:, dsl], in_=dst[:, dsl], func=AF.Silu)
                    S.op("act", "activation", reads=[(dk_, ct) for ct in range(L // CT)], writes=[dk_], out=dst[:, 0:1],
                         in_=dst[:, 0:1], func=AF.Copy)
                    if gi < 2:
                        sc = (128.0 ** -0.5) if gi == 0 else 1.0

                        def l2_tile(tt, dst=dst, dk_=dk_, sc=sc):
                            sl = slice(tt * 512, (tt + 1) * 512)
                            q3 = tt % 2
                            sq, sqk = sqb[q3], ("gsq", q3)
                            rs, rsk = rsb[q3], ("grs", q3)
                            S.op("act", "activation", reads=[dk_], writes=[sqk], out=sq[:], in_=dst[:, sl], func=AF.Square)
                            yield
                            S.mm(Bk[q3][:], ones[:], sq[:], True, True, ["ones", sqk], [bk(q3)])
                            yield
                            S.op("dve", "tensor_scalar", reads=[bk(q3)], writes=[rsk], out=rs[:], in0=Bk[q3][:], scalar1=1e-6,
                                 scalar2=None, op0=ALU.add)
                            yield
                            S.op("act", "activation", reads=[rsk], writes=[rsk], out=rs[:], in_=rs[:], func=AF.Sqrt)
                            yield
                            S.op("dve", "reciprocal", reads=[rsk], writes=[rsk], out=rs[:], in_=rs[:])
                            S.op("dve", "scalar_tensor_tensor", reads=[rsk, dk_], writes=[(dk_, "n", tt)], out=dst[:, sl],
                                 in0=dst[:, sl], scalar=sc, in1=rs[:], op0=ALU.mult, op1=ALU.mult)
                            yield

                        pipeline((l2_tile(tt) for tt in range(L // 512)), 2)
                        S.op("dve", "tensor_copy", reads=[(dk_, "n", tt) for tt in range(L // 512)], writes=[dk_],
                             out=dst[:, 0:1], in_=dst[:, 0:1])
                for d in range(2):
                    Gd = gt[:, d, b, :]
                    Bd = bt_[:, d, b, :]
                    S.mm(Bk[0][:CH, :N], mtri[:, d, :], Gd, True, True, [("mtri", d), ("gt", d)], [bk(0)])
                    S.op("act", "activation", reads=[bk(0)], writes=["gc"], out=gc[:], in_=Bk[0][:CH, :N], func=AF.Copy)
                    S.mm(Bk[1][:, :N], ones[:], gtf[:, d, b, :], True, True, ["ones", ("gt", d)], [bk(1)])
                    S.op("act", "activation", reads=[bk(1)], writes=["egt"], out=egt[:], in_=Bk[1][:, :N], func=AF.Exp)
                    S.op("dve", "tensor_tensor", reads=[bk(1), "gc"], writes=["kd"], out=kd[:], in0=Bk[1][:CH, :N], in1=gc[:],
                         op=ALU.subtract)
                    S.op("act", "activation", reads=["kd"], writes=["kd"], out=kd[:], in_=kd[:], func=AF.Exp)
                    S.op("act", "activation", reads=["gc"], writes=["bg"], out=bg[:], in_=gc[:], func=AF.Exp)
                    S.op("dve", "tensor_tensor", reads=["bg", ("bt", d)], writes=["bg"], out=bg[:], in0=bg[:], in1=Bd, op=ALU.mult)
                    S.op("dve", "memset", writes=[("S", 0)], ap=Sst[0][:], constant=0.0)
                    sidx = [0]
                    si = 0
                    vi = 0
                    blocks = list(range(N // BLK))
                    if d == 1:
                        blocks = blocks[::-1]
                    if dbg_blocks is not None:
                        blocks = blocks[:dbg_blocks]
                    flat = lambda t_: t_[:].rearrange("p i c -> p (i c)")
                    mt_b = mtri[:, d:d + 1, :].to_broadcast([CH, BLK, CH])
                    ms_b = mstr[:, d:d + 1, :].to_broadcast([CH, BLK, CH])
                    id_b = ident[:CH, 0:CH].unsqueeze(1).to_broadcast([CH, BLK, CH])
                    k3 = lambda i_: Bk[i_][:CH, :].rearrange("p (i c) -> p i c", c=CH)

                    def prep(nb, pb, d=d, Gd=Gd, Bd=Bd):
                        n0 = nb * BLK
                        tsl = slice(n0 * CH, (n0 + BLK) * CH)
                        gb = lambda t_: t_[:, n0:n0 + BLK].unsqueeze(2).to_broadcast([CH, BLK, CH])
                        qd_, at_, kdt_, u8_ = qd[pb], attnT[pb], kdt[pb], u8[pb]
                        S.op("dve", "tensor_tensor", reads=[("gt", d), ("mtri", d)], writes=["Pa"], out=t8["Pa"][:],
                             in0=mt_b, in1=gb(Gd), op=ALU.mult)
                        S.op("dve", "tensor_tensor", reads=[("bt", d), "ident"], writes=["Pb"], out=t8["Pb"][:], in0=id_b,
                             in1=gb(Bd), op=ALU.mult)
                        yield
                        S.mm(Bk[3][:], ones[:], t8f["Pa"][:].rearrange("p i c -> p (i c)"), True, True, ["ones", "Pa"], [bk(3)])
                        S.mm(Bk[4][:CH, :], ones[:CH, :CH], flat(t8["Pb"]), True, True, ["ones", "Pb"], [bk(4)])
                        for i in range(BLK):
                            csl = slice((n0 + i) * CH, (n0 + i + 1) * CH)
                            S.mm(Bk[5][:CH, i * CH:(i + 1) * CH], kT[:, csl], kT[:, csl], True, True, ["kT"], [bk(5)])
                            S.mm(Bk[6][:CH, i * CH:(i + 1) * CH], kT[:, csl], qT[:, csl], True, True, ["kT", "qT"], [bk(6)])
                        yield
                        S.op("act", "activation", reads=[bk(3)], writes=["egr"], out=egr[:], in_=Bk[3][:], func=AF.Exp)
                        S.op("dve", "tensor_tensor", reads=[bk(3), "gc"], writes=["dT"], out=t8["dT"][:], in0=k3(3), in1=gb(gc),
                             op=ALU.subtract)
                        S.op("dve", "tensor_scalar", reads=["dT"], writes=["dT"], out=t8["dT"][:], in0=t8["dT"][:], scalar1=0.0,
                             scalar2=None, op0=ALU.min)
                        yield
                        S.op("act", "activation", reads=["dT"], writes=["dT"], out=t8["dT"][:], in_=t8["dT"][:], func=AF.Exp)
                        S.op("dve", "tensor_tensor", reads=["egr", "qT"], writes=[("qd", pb)], out=qd_[:], in0=qT[:, tsl],
                             in1=egr[:], op=ALU.mult)
                        yield
                        S.op("dve", "tensor_tensor", reads=["dT", ("mstr", d)], writes=["W1"], out=t8["W1"][:], in0=t8["dT"][:],
                             in1=ms_b, op=ALU.mult)
                        S.op("dve", "tensor_tensor", reads=["W1", bk(4)], writes=["W1"], out=t8["W1"][:], in0=t8["W1"][:],
                             in1=k3(4), op=ALU.mult)
                        S.op("dve", "tensor_tensor", reads=["dT", ("mtri", d)], writes=["dT"], out=t8["dT"][:], in0=t8["dT"][:],
                             in1=mt_b, op=ALU.mult)
                        S.op("dve", "tensor_tensor", reads=[bk(5), "W1"], writes=["LT"], out=t8["LT"][:], in0=k3(5),
                             in1=t8["W1"][:], op=ALU.mult)
                        S.op("dve", "tensor_tensor", reads=[bk(6), "dT"], writes=[("attnT", pb)], out=at_[:CH], in0=k3(6),
                             in1=t8["dT"][:], op=ALU.mult)
                        yield
                        for i in range(BLK):
                            S.mm(Bk[7][:CH, i * CH:(i + 1) * CH], t8["LT"][:, i, :], ident[:CH, :CH], True, True,
                                 ["LT", "ident"], [bk(7)])
                        S.op("dve", "tensor_tensor", reads=["LT", "ident"], writes=["X8"], out=t8["X8"][:], in0=id_b,
                             in1=t8["LT"][:], op=ALU.subtract)
                        yield
                        S.op("act", "activation", reads=[bk(7)], writes=["Lm"], out=flat(t8["Lm"]), in_=Bk[7][:CH, :],
                             func=AF.Copy)
                        yield
                        A_, At_, ak, atk = t8["LT"], t8["Lm"], "LT", "Lm"
                        for lev in range(5):
                            P_, Pt_ = (t8["Pa"], t8["Pta"]) if lev % 2 == 0 else (t8["Pb"], t8["Ptb"])
                            pk, ptk = ("Pa", "Pta") if lev % 2 == 0 else ("Pb", "Ptb")
                            for i in range(BLK):
                                cs = slice(i * CH, (i + 1) * CH)
                                S.mm(Bk[4][:CH, cs], A_[:, i, :], At_[:, i, :], True, True, [ak, atk], [bk(4)])
                                if lev < 4:
                                    S.mm(Bk[3][:CH, cs], At_[:, i, :], A_[:, i, :], True, True, [ak, atk], [bk(3)])
                            yield
                            S.op("act", "activation", reads=[bk(4)], writes=[ptk], out=flat(Pt_), in_=Bk[4][:CH, :], func=AF.Copy)
                            if lev < 4:
                                S.op("act", "activation", reads=[bk(3)], writes=[pk], out=flat(P_), in_=Bk[3][:CH, :], func=AF.Copy)
                            yield
                            for i in range(BLK):
                                cs = slice(i * CH, (i + 1) * CH)
                                S.mm(Bk[5][:CH, cs], Pt_[:, i, :], t8["X8"][:, i, :], True, True, [ptk, "X8"], [bk(5)])
                            yield
                            S.op("dve", "tensor_tensor", reads=[bk(5), "X8"], writes=["X8"], out=flat(t8["X8"]),
                                 in0=Bk[5][:CH, :], in1=flat(t8["X8"]), op=ALU.add)
                            A_, At_, ak, atk = P_, Pt_, pk, ptk
                        yield
                        for i in range(BLK):
                            csl = slice((n0 + i) * CH, (n0 + i + 1) * CH)
                            bkk, bkv = (6, 3) if i < 4 else (7, 4)
                            o_ = (i % 4) * 128
                            S.op("pe", "transpose", reads=["kT", "ident"], writes=[bk(bkk)], out=Bk[bkk][:CH, o_:o_ + 128],
                                 in_=kT[:, csl], identity=ident[:])
                            S.op("pe", "transpose", reads=["vT", "ident"], writes=[bk(bkv)], out=Bk[bkv][:CH, o_:o_ + 128],
                                 in_=vT[:, csl], identity=ident[:])
                        yield
                        for hh in range(2):
                            isl = slice(hh * 4, hh * 4 + 4)
                            sc_b = lambda t_: t_[:, n0 + hh * 4:n0 + hh * 4 + 4].unsqueeze(2).to_broadcast([CH, 4, 128])
                            kps = Bk[6 + hh][:CH, :].rearrange("p (i e) -> p i e", e=128)
                            vps = Bk[3 + hh][:CH, :].rearrange("p (i e) -> p i e", e=128)
                            S.op("dve", "tensor_tensor", reads=[bk(6 + hh), "kd"], writes=[("kdt", pb, hh)],
                                 out=kdt_[:CH, isl, :], in0=kps, in1=sc_b(kd), op=ALU.mult)
                            S.op("dve", "tensor_tensor", reads=[bk(6 + hh), "bg"], writes=[("kbt", hh)], out=kbt[:CH, isl, :],
                                 in0=kps, in1=sc_b(bg), op=ALU.mult)
                            S.op("dve", "tensor_tensor", reads=[bk(3 + hh), ("bt", d)], writes=[("vbt", hh)],
                                 out=vbt[:, isl, :], in0=vps, in1=sc_b(Bd), op=ALU.mult)
                        yield
                        MT_ = MT8[pb]
                        for i in range(BLK):
                            hh = i // 4
                            o_ = (i % 4) * 128
                            S.mm(Bk[5 + hh][:CH, o_:o_ + 128], t8["X8"][:, i, :], vbt[:, i, :], True, True,
                                 ["X8", ("vbt", hh)], [bk(5 + hh)])
                            S.mm(Bk[3 + hh][:CH, o_:o_ + 128], t8["X8"][:, i, :], kbt[:CH, i, :], True, True,
                                 ["X8", ("kbt", hh)], [bk(3 + hh)])
                        yield
                        for hh in range(2):
                            S.op("act", "activation", reads=[bk(5 + hh)], writes=[("u8", pb, hh)],
                                 out=u8_[:CH, hh * 4:hh * 4 + 4, :].rearrange("p i e -> p (i e)"), in_=Bk[5 + hh][:CH, :],
                                 func=AF.Copy)
                            S.op("act", "activation", reads=[bk(3 + hh)], writes=[("w8", hh)],
                                 out=w8[:CH, hh * 4:hh * 4 + 4, :].rearrange("p i e -> p (i e)"), in_=Bk[3 + hh][:CH, :],
                                 func=AF.Copy)
                        yield
                        for i in range(BLK):
                            hh = i // 4
                            o_ = (i % 4) * 128
                            S.mm(Bk[5 + hh][:, o_:o_ + 128], w8[:, i, :], kdt_[:, i, :], True, True,
                                 [("w8", hh), ("kdt", pb, hh)], [bk(5 + hh)])
                            S.mm(Bk[7][:, i * CH:(i + 1) * CH], w8[:, i, :], at_[:, i, :], True, True,
                                 [("w8", hh), ("attnT", pb)], [bk(7)])
                        yield
                        for hh in range(2):
                            S.op("act", "mul", reads=[bk(5 + hh)], writes=[("mtmp", hh)],
                                 out=mtmp[:, hh * 4:hh * 4 + 4, :].rearrange("p i e -> p (i e)"), in_=Bk[5 + hh][:], mul=-1.0)
                        S.op("dve", "tensor_tensor", reads=[bk(7), ("qd", pb)], writes=[("qd", pb)], out=qd_[:], in0=qd_[:],
                             in1=Bk[7][:], op=ALU.subtract)
                        yield
                        for i in range(BLK):
                            n = n0 + i
                            S.op("dve", "scalar_tensor_tensor", reads=[("mtmp", i // 4), "ident", "egt"], writes=[("MT", pb, i)],
                                 out=MT_[:, i, :], in0=ident[:], scalar=egt[:, n:n + 1], in1=mtmp[:, i, :], op0=ALU.mult,
                                 op1=ALU.add)
                            if i % 4 == 3:
                                yield

                    def scan(nb, pb, bi_, d=d, b=b):
                        n0 = nb * BLK
                        qd_, at_, kdt_, u8_, MT_ = qd[pb], attnT[pb], kdt[pb], u8[pb], MT8[pb]
                        ob, obk = o8[bi_ % 2], ("o8", bi_ % 2)
                        order = list(range(BLK)) if d == 0 else list(range(BLK))[::-1]
                        for i in order:
                            hh = i // 4
                            cur = sidx[0]
                            Sc, sk = Sst[cur], ("S", cur)
                            Sn, snk = Sst[1 - cur], ("S", 1 - cur)
                            sidx[0] = 1 - cur
                            S.mm(Bk[0][:, 0:128], MT_[:, i, :], Sc[:], True, False, [("MT", pb, i), sk], [bk(0)])
                            S.mm(Bk[0][:, 0:128], kdt_[:, i, :], u8_[:, i, :], False, True, [("kdt", pb, hh), ("u8", pb, hh)], [bk(0)])
                            S.mm(Bk[1][:CH, 0:128], qd_[:, i * CH:(i + 1) * CH], Sc[:], True, False, [("qd", pb), sk], [bk(1)])
                            S.mm(Bk[1][:CH, 0:128], at_[:, i, :], u8_[:, i, :], False, True, [("attnT", pb), ("u8", pb, hh)], [bk(1)])
                            yield
                            S.op("act", "activation", reads=[bk(0)], writes=[snk], out=Sn[:], in_=Bk[0][:, 0:128], func=AF.Copy)
                            S.op("dve", "tensor_copy", reads=[bk(1)], writes=[obk], out=ob[:, i, :], in_=Bk[1][:CH, 0:128])
                            for _ in range(SCAN_GAP):
                                yield
                        S.dma("sp", reads=[obk], writes=[("s_o", d, b, nb)], out=s_o[d, :, b, n0:n0 + BLK, :], in_=ob[:])

                    for _ in prep(blocks[0], 0):
                        pass
                    for bi_, nb in enumerate(blocks):
                        gens = [scan(nb, bi_ % 2, bi_)]
                        if bi_ + 1 < len(blocks):
                            gens.append(prep(blocks[bi_ + 1], (bi_ + 1) % 2))
                        interleave(gens)
        with C.scope():
            gnr = C.sb("gnr", [CH, 128], F32)
            S.dma("sp", writes=["gnr"], out=gnr[:], in_=gn)
            NB3 = 3
            of = [C.sb("of%d" % i, [CH, BLK, 128], F32) for i in range(NB3)]
            obb = [C.sb("ob%d" % i, [CH, BLK, 128], F32) for i in range(NB3)]
            zz = [C.sb("zz%d" % i, [CH, BLK, 128], F32) for i in range(NB3)]
            sq8 = [C.sb("sq8%d" % i, [CH, BLK, 128], F32) for i in range(NB3)]
            ss = [C.sb("ss%d" % i, [CH, BLK], F32) for i in range(NB3)]

            def out_block(it, b, nb):
                p = it % NB3
                n0 = nb * BLK
                S.dma("sp", reads=[("s_o", 0, b, nb)], writes=[("of", p)], out=of[p][:], in_=s_o[0, :, b, n0:n0 + BLK, :])
                S.dma("sp", reads=[("s_o", 1, b, nb)], writes=[("ob", p)], out=obb[p][:], in_=s_o[1, :, b, n0:n0 + BLK, :])
                S.dma("sp", writes=[("zz", p)], out=zz[p][:], in_=ztok[:, b, n0:n0 + BLK, :])
                yield
                S.op("dve", "tensor_tensor", reads=[("of", p), ("ob", p)], writes=[("of", p)], out=of[p][:], in0=of[p][:],
                     in1=obb[p][:], op=ALU.add)
                S.op("act", "activation", reads=[("zz", p)], writes=[("zz", p)], out=zz[p][:], in_=zz[p][:], func=AF.Silu)
                yield
                S.op("act", "activation", reads=[("of", p)], writes=[("sq8", p)], out=sq8[p][:], in_=of[p][:],
                     func=AF.Square)
                yield
                S.op("dve", "tensor_reduce", reads=[("sq8", p)], writes=[("ss", p)], out=ss[p][:], in_=sq8[p][:],
                     axis=AX.X, op=ALU.add)
                S.op("dve", "tensor_scalar", reads=[("ss", p)], writes=[("ss", p)], out=ss[p][:], in0=ss[p][:],
                     scalar1=1.0 / 128.0, scalar2=EPS, op0=ALU.mult, op1=ALU.add)
                yield
                S.op("act", "activation", reads=[("ss", p)], writes=[("ss", p)], out=ss[p][:], in_=ss[p][:], func=AF.Sqrt)
                yield
                S.op("dve", "reciprocal", reads=[("ss", p)], writes=[("ss", p)], out=ss[p][:], in_=ss[p][:])
                S.op("dve", "tensor_tensor", reads=[("of", p), ("ss", p)], writes=[("of", p)], out=of[p][:], in0=of[p][:],
                     in1=ss[p][:].unsqueeze(2).to_broadcast([CH, BLK, 128]), op=ALU.mult)
                S.op("dve", "tensor_tensor", reads=[("of", p), "gnr"], writes=[("of", p)], out=of[p][:], in0=of[p][:],
                     in1=gnr[:].unsqueeze(1).to_broadcast([CH, BLK, 128]), op=ALU.mult)
                S.op("dve", "tensor_tensor", reads=[("of", p), ("zz", p)], writes=[("of", p)], out=of[p][:], in0=of[p][:],
                     in1=zz[p][:], op=ALU.mult)
                S.dma("sp", reads=[("of", p)], is_out=True, out=ytok[:, b, n0:n0 + BLK, :], in_=of[p][:])
                yield

            pipeline((out_block(b * (N // BLK) + nb, b, nb) for b in range(BATCH) for nb in range(N // BLK)), NB3)
        S.replay()
    return nc


_PROGS = {}


def _prog(key, fn):
    if key not in _PROGS:
        _PROGS[key] = fn()
    return _PROGS[key]


def _run(nc, in_maps):
    res = run_bass_kernel_spmd(nc, in_maps, core_ids=list(range(NCORES)))
    return res.results


def _c(a):
    return np.ascontiguousarray(a, dtype=np.float32)


def _tok_shards_T(xf):
    return [_c(xf[c * TPC:(c + 1) * TPC].T) for c in range(NCORES)]


def _from_T(outs, name):
    return np.concatenate([r[name].T for r in outs], 0)


def _run_sc_layer(xf, p, li, j, final):
    nc = _prog(("sc", final), lambda: build_sc_prog(TPC, final))
    x3 = xf.reshape(BATCH, SEQ, D)
    zero = np.zeros((1, D), np.float32)
    in_maps = []
    for c in range(NCORES):
        b, s0 = divmod(c * TPC, SEQ)
        left = x3[b, s0 - 1:s0] if s0 > 0 else zero
        right = x3[b, s0 + TPC:s0 + TPC + 1] if s0 + TPC < SEQ else zero
        xs = np.concatenate([x3[b, s0:s0 + TPC], left, right], 0)
        in_maps.append({"xT": _c(xs.T), "nrm": _c(p["norms"][li]), "fw_in": _c(p["ffn_w_in"][li]),
                        "fw_out": _c(p["ffn_w_out"][li]), "w_in": _c(p["sc_w_in"][j]), "conv": _c(p["sc_conv"][j]),
                        "w_out": _c(p["sc_w_out"][j]), "gfin": _c(p["final_norm"])})
    return _from_T(_run(nc, in_maps), "yT")


def _run_pre(xf, p, li, w_in, b_in):
    nout = w_in.shape[1]
    nc = _prog(("pre", nout), lambda: build_pre_prog(nout, TPC))
    xs = _tok_shards_T(xf)
    in_maps = [{"xT": xs[c], "nrm": _c(p["norms"][li, 0:2]), "fw_in": _c(p["ffn_w_in"][li, 0]),
                "fw_out": _c(p["ffn_w_out"][li, 0]), "w_in": _c(w_in), "b_in": _c(b_in)} for c in range(NCORES)]
    outs = _run(nc, in_maps)
    x1 = _from_T(outs, "xo")
    u = np.concatenate([r["uT"] for r in outs], 1)
    return x1, u


def _run_post(xf, yfm, p, li, w_out, b_out):
    nc = _prog(("post",), lambda: build_post_prog(TPC))
    xs = _tok_shards_T(xf)
    in_maps = [{"xT": xs[c], "yT": _c(yfm[:, c * TPC:(c + 1) * TPC]), "w_out": _c(w_out), "b_out": _c(b_out),
                "nrm": _c(p["norms"][li, 2]), "fw_in": _c(p["ffn_w_in"][li, 1]), "fw_out": _c(p["ffn_w_out"][li, 1])}
               for c in range(NCORES)]
    return _from_T(_run(nc, in_maps), "xo")


def _run_hyena_core(u, p, j):
    nc = _prog(("hy",), build_hy_core_prog)
    cst, zT, trow, nad = hyena_consts()
    trow_rep = _c(np.broadcast_to(trow, (128, NFFT)))
    u4 = u.reshape(3, D, BATCH, SEQ)
    hc = p["hy_conv"][j].reshape(3, 3, D)
    cb = p["hy_conv_b"][j].reshape(3, D)
    w3 = p["hy_f_w3"][j].reshape(HY_ORD, 2, D)
    in_maps = []
    for c in range(NCORES):
        sl = slice(c * 128, (c + 1) * 128)
        in_maps.append({"u0": _c(u4[:, sl]), "convw": _c(hc[:, :, sl]), "convb": _c(cb[:, sl]), "dvec": _c(p["hy_d"][j][sl]),
                        "fw1": _c(p["hy_f_w1"][j]), "fb1": _c(p["hy_f_b1"][j]), "fw2": _c(p["hy_f_w2"][j]),
                        "fb2": _c(p["hy_f_b2"][j]), "fw3": _c(w3[:, :, sl]), "freq": _c(p["hy_f_freq"][j]), "cst": cst,
                        "zT": zT, "trow": trow_rep, "nad": _c(nad[sl])})
    outs = _run(nc, in_maps)
    return np.concatenate([r["yT"].reshape(128, BATCH * SEQ) for r in outs], 0)


def _run_gdn_core(u, p, j):
    nc = _prog(("gd",), build_gd_core_prog)
    mtri, mstrict, ident = gdn_consts()
    H = 8
    gcv = p["gd_conv"][j].reshape(3, 3, D)
    in_maps = []
    for h in range(NCORES):
        sl = slice(h * 128, (h + 1) * 128)
        qkv0 = u[0:3 * D].reshape(3, D, BATCH, SEQ)[:, sl]
        zfm = u[3 * D + h * 128: 3 * D + (h + 1) * 128]
        ztok = zfm.T.reshape(BATCH, NCH, CH, 128).transpose(2, 0, 1, 3)
        rows = [4 * D + 0 * H + h, 4 * D + 1 * H + h, 4 * D + 2 * H + 0 * H + h, 4 * D + 2 * H + 1 * H + h]
        abt = u[rows].reshape(4, BATCH, NCH, CH).transpose(0, 3, 1, 2)
        in_maps.append({"qkv0": _c(qkv0), "ztok": _c(ztok), "abt": _c(abt), "convw": _c(gcv[:, :, sl]),
                        "alog": _c(np.broadcast_to(p["gd_a_log"][j][:, h], (CH, 2))),
                        "dtb": _c(np.broadcast_to(p["gd_dt_bias"][j][:, h], (CH, 2))),
                        "gn": _c(np.broadcast_to(p["gd_norm"][j], (CH, 128))), "mtri": mtri, "mstrict": mstrict,
                        "ident": ident})
    outs = _run(nc, in_maps)
    return np.concatenate([r["ytok"].transpose(3, 1, 2, 0).reshape(128, BATCH * SEQ) for r in outs], 0)


def kernel(**inputs):
    p = {k: np.asarray(v, dtype=np.float32) for k, v in inputs.items()}
    xf = p["x"].reshape(BATCH * SEQ, D)
    xf = _run_sc_layer(xf, p, 0, 0, final=False)
    xf, u = _run_pre(xf, p, 1, p["hy_w_in"][0], p["hy_b_in"][0])
    yfm = _run_hyena_core(u, p, 0)
    xf = _run_post(xf, yfm, p, 1, p["hy_w_out"][0], p["hy_b_out"][0])
    nproj = p["gd_w_in"].shape[2]
    npad = ((nproj + 127) // 128) * 128
    w_in = np.zeros((D, npad), np.float32)
    w_in[:, :nproj] = p["gd_w_in"][0]
    xf, u = _run_pre(xf, p, 2, w_in, np.zeros((npad,), np.float32))
    yfm = _run_gdn_core(u, p, 0)
    xf = _run_post(xf, yfm, p, 2, p["gd_w_out"][0], np.zeros((D,), np.float32))
    xf = _run_sc_layer(xf, p, 3, 1, final=True)
    return np.ascontiguousarray(xf.reshape(BATCH, SEQ, D).astype(np.float32))
```

```python
import contextlib
import math
import numpy as np
import concourse.bass as bass
import concourse.mybir as mybir
from concourse.bass_utils import run_bass_kernel_spmd

F32 = mybir.dt.float32
BF16 = mybir.dt.bfloat16
AF = mybir.ActivationFunctionType
ALU = mybir.AluOpType
AX = mybir.AxisListType

D = 1024
KC = 8
FF = 2816
JC = 22
NCORES = 8
BATCH = 2
SEQ = 8192
TPC = BATCH * SEQ // NCORES
EPS = 1e-6

ENGS = ("pe", "act", "dve", "pool", "sp")
NDMA_SEM = 20


class Sched:
    def __init__(self, nc, es):
        self.nc = nc
        self.q = {e: [] for e in ENGS}
        self.cnt = {e: 0 for e in ENGS}
        self.seen = {e: {} for e in ENGS}
        self.buf = {}
        self.sems = {}
        for e in ENGS:
            self.sems[("E", e)] = es.enter_context(nc.semaphore("sem_" + e))
        self.dma_rr = {e: 0 for e in ENGS}
        self.dma_val = {}
        for e in ("sp", "pool", "act"):
            for i in range(NDMA_SEM):
                k = ("D", e, i)
                self.sems[k] = es.enter_context(nc.semaphore("dsem_%s_%d" % (e, i)))
                self.dma_val[k] = 0
        self.out_tokens = []
        self.excl = set()

    def _deps(self, eng, reads, writes):
        deps = {}

        def add(tok):
            if tok is None:
                return
            k, v = tok
            if deps.get(k, 0) < v:
                deps[k] = v

        for k in reads:
            b = self.buf.get(k)
            if b:
                add(b["w"])
                if k in self.excl:
                    for rk, rv in b["r"].items():
                        if rk != ("E", eng):
                            add((rk, rv))
        for k in writes:
            b = self.buf.get(k)
            if b:
                add(b["w"])
                for rk, rv in b["r"].items():
                    add((rk, rv))
        waits = []
        for k, v in deps.items():
            if eng == "pe" and k == ("E", "pe"):
                continue
            if self.seen[eng].get(k, 0) >= v:
                continue
            self.seen[eng][k] = v
            waits.append((k, v))
        return waits

    def _record(self, tok, reads, writes):
        for k in reads:
            b = self.buf.setdefault(k, {"w": None, "r": {}})
            if b["r"].get(tok[0], 0) < tok[1]:
                b["r"][tok[0]] = tok[1]
        for k in writes:
            self.buf[k] = {"w": tok, "r": {}}

    def op(self, eng, name, reads=(), writes=(), **kw):
        fn = (name, kw)
        waits = self._deps(eng, reads, writes)
        self.cnt[eng] += 1
        tok = (("E", eng), self.cnt[eng])
        self.q[eng].append((waits, fn, tok, 1))
        self._record(tok, reads, writes)
        return tok

    def dma(self, eng, reads=(), writes=(), is_out=False, **kw):
        fn = ("dma_start", kw)
        waits = self._deps(eng, reads, writes)
        i = self.dma_rr[eng]
        self.dma_rr[eng] = (i + 1) % NDMA_SEM
        k = ("D", eng, i)
        prev = self.dma_val[k]
        if prev and self.seen[eng].get(k, 0) < prev:
            self.seen[eng][k] = prev
            waits.append((k, prev))
        self.dma_val[k] = prev + 16
        tok = (k, prev + 16)
        self.q[eng].append((waits, fn, tok, 16))
        self._record(tok, reads, writes)
        if is_out:
            self.out_tokens.append(tok)
        return tok

    def barrier(self):
        allv = [(("E", f), self.cnt[f]) for f in ENGS if self.cnt[f]]
        allv += [(k, v) for k, v in self.dma_val.items() if v]
        for e in ENGS:
            waits = []
            for k, v in allv:
                if k == ("E", e) and e == "pe":
                    continue
                if self.seen[e].get(k, 0) >= v:
                    continue
                self.seen[e][k] = v
                waits.append((k, v))
            if waits:
                self.q[e].append((waits, None, None, 0))
        self.buf = {}

    def mm(self, out, lhsT, rhs, start, stop, reads, writes):
        return self.op("pe", "matmul", reads, writes, out=out, lhsT=lhsT, rhs=rhs, start=start, stop=stop)

    def replay(self):
        nc = self.nc
        fin = list(self.out_tokens)
        with nc.Block() as block:
            def run(engname, eng):
                for waits, fn, tok, inc in self.q[engname]:
                    for k, v in waits:
                        eng.wait_ge(self.sems[k], v)
                    if fn is None:
                        continue
                    ins = getattr(eng, fn[0])(**fn[1])
                    ins.then_inc(self.sems[tok[0]], inc)
                if engname == "sp":
                    for k, v in fin:
                        eng.wait_ge(self.sems[k], v)

            @block.tensor
            def _(e):
                run("pe", e)

            @block.scalar
            def _(e):
                run("act", e)

            @block.vector
            def _(e):
                run("dve", e)

            @block.gpsimd
            def _(e):
                run("pool", e)

            @block.sync
            def _(e):
                run("sp", e)


class Ctx:
    def __init__(self, nc, es):
        self.nc = nc
        self.es = es
        self.S = Sched(nc, es)
        self.n = 0
        self.scopes = [es]

    def sb(self, name, shape, dt):
        self.n += 1
        return self.scopes[-1].enter_context(self.nc.sbuf_tensor("%s_%d" % (name, self.n), shape, dt))

    def ps(self, name, shape, dt=F32):
        self.n += 1
        return self.scopes[-1].enter_context(self.nc.psum_tensor("%s_%d" % (name, self.n), shape, dt))

    @contextlib.contextmanager
    def scope(self):
        with contextlib.ExitStack() as s:
            self.scopes.append(s)
            try:
                yield
            finally:
                self.S.barrier()
                self.scopes.pop()


def interleave(gens):
    gens = list(gens)
    while gens:
        for g_ in list(gens):
            try:
                next(g_)
            except StopIteration:
                gens.remove(g_)


def pipeline(gens, width):
    gens = iter(gens)
    active = []
    done = False
    while True:
        while not done and len(active) < width:
            try:
                active.append(next(gens))
            except StopIteration:
                done = True
        if not active:
            return
        for g_ in list(active):
            try:
                next(g_)
            except StopIteration:
                active.remove(g_)


def dram_in(nc, name, shape, dt=F32):
    return nc.dram_tensor(name, list(shape), dt, kind="ExternalInput").ap()


def dram_out(nc, name, shape, dt=F32):
    return nc.dram_tensor(name, list(shape), dt, kind="ExternalOutput").ap()


def emit_consts(C):
    ones = C.sb("ones", [128, 128], F32)
    C.S.op("dve", "memset", writes=["ones"], ap=ones[:], constant=1.0)
    C.ones = ones
    C.rn_sq = [C.sb("rn_sq%d" % i, [128, 512], F32) for i in range(3)]
    C.rn_rs = [C.sb("rn_rs%d" % i, [128, 512], F32) for i in range(2)]
    C.rn_ps = [C.ps("rn_ps%d" % i, [128, 512], F32) for i in range(1)]
    C.rn_i = 0


def load_vec_pk(C, name, vec_dram, nchunk, eng="sp"):
    t = C.sb(name, [128, nchunk], F32)
    C.S.dma(eng, writes=[name], out=t[:], in_=vec_dram.rearrange("(kc p) -> p kc", p=128),
            allow_slow_non_contiguous=True)
    return t


def emit_rmsnorm(C, x, xk, t0, ntok, g_sb, gk, hn, hk, hoff=0):
    S = C.S
    nt = (ntok + 511) // 512
    for tt in range(nt):
        n = min(512, ntok - tt * 512)
        c0 = t0 + tt * 512
        xkeys = [(xk, k, c0 // 512) for k in range(KC)]
        if (c0 % 512) + n > 512:
            xkeys += [(xk, k, c0 // 512 + 1) for k in range(KC)]
        ps = C.rn_ps[0]
        for k in range(KC):
            C.rn_i += 1
            sq = C.rn_sq[C.rn_i % 3]
            sqk = ("rn_sq", C.rn_i % 3)
            S.op("act", "activation", reads=[kk for kk in xkeys if kk[1] == k], writes=[sqk],
                 out=sq[:, :n], in_=x[:, k, c0:c0 + n], func=AF.Square)
            S.mm(ps[:, :n], C.ones[:], sq[:, :n], k == 0, k == KC - 1, ["ones", sqk], ["rn_ps"])
        C.rn_i += 1
        rs = C.rn_rs[C.rn_i % 2]
        rsk = ("rn_rs", C.rn_i % 2)
        S.op("dve", "tensor_scalar", reads=["rn_ps"], writes=[rsk], out=rs[:, :n], in0=ps[:, :n],
             scalar1=1.0 / D, scalar2=EPS, op0=ALU.mult, op1=ALU.add)
        S.op("act", "activation", reads=[rsk], writes=[rsk], out=rs[:, :n], in_=rs[:, :n], func=AF.Sqrt)
        S.op("dve", "reciprocal", reads=[rsk], writes=[rsk], out=rs[:, :n], in_=rs[:, :n])
        for k in range(KC):
            eng = "dve"
            S.op(eng, "scalar_tensor_tensor", reads=[kk for kk in xkeys if kk[1] == k] + [rsk, gk],
                 writes=[(hk, k, tt)], out=hn[:, k, hoff + tt * 512: hoff + tt * 512 + n], in0=x[:, k, c0:c0 + n],
                 scalar=g_sb[:, k:k + 1], in1=rs[:, :n], op0=ALU.mult, op1=ALU.mult)


def emit_ffn(C, x, groups, g_dram, w_in, w_out, pref):
    S = C.S
    TG = 1024
    w_in_v = w_in.rearrange("(kc p) n -> p kc n", p=128)
    w_out_v = w_out.rearrange("(jc p) n -> p jc n", p=128)
    with C.scope():
        g_sb = load_vec_pk(C, pref + "g", g_dram, KC)
        gk = pref + "g"
        hns = [C.sb("ffn_hn%d" % i, [128, KC, TG], BF16) for i in range(min(2, len(groups)))]
        act = C.sb("ffn_act", [128, JC, TG], BF16)
        wbuf = [C.sb("ffn_wi%d" % i, [128, KC, 256], BF16) for i in range(3)]
        wobuf = [C.sb("ffn_wo%d" % i, [128, JC, 128], BF16) for i in range(2)]
        sgb = [C.sb("ffn_sg%d" % i, [128, 512], F32) for i in range(2)]
        pg = [C.ps("ffn_pg%d" % i, [128, 512]) for i in range(2)]
        pu = [C.ps("ffn_pu%d" % i, [128, 512]) for i in range(2)]
        po = [C.ps("ffn_po%d" % i, [128, 512]) for i in range(2)]
        it = 0
        io = 0
        for gi_, (t0, ntok) in enumerate(groups):
            ntg = (ntok + 511) // 512
            hn = hns[gi_ % 2]
            hnk = "ffn_hn%d" % (gi_ % 2)
            if gi_ == 0:
                emit_rmsnorm(C, x, "x", t0, ntok, g_sb, gk, hn, hnk)
            for j in range(JC):
                wb = wbuf[j % 3]
                wk = ("ffn_wi", j % 3)
                S.dma("pool", writes=[wk + (0,)], out=wb[:, :, 0:128], in_=w_in_v[:, :, j * 128:(j + 1) * 128])
                S.dma("pool", writes=[wk + (1,)], out=wb[:, :, 128:256],
                      in_=w_in_v[:, :, FF + j * 128: FF + (j + 1) * 128])
                for tt in range(ntg):
                    n = min(512, ntok - tt * 512)
                    it += 1
                    b = it % 2
                    sl = slice(tt * 512, tt * 512 + n)
                    for k in range(KC):
                        S.mm(pg[b][:, :n], wb[:, k, 0:128], hn[:, k, sl], k == 0, k == KC - 1,
                             [wk + (0,), (hnk, k, tt)], [("ffn_pg", b)])
                    for k in range(KC):
                        S.mm(pu[b][:, :n], wb[:, k, 128:256], hn[:, k, sl], k == 0, k == KC - 1,
                             [wk + (1,), (hnk, k, tt)], [("ffn_pu", b)])
                    S.op("act", "activation", reads=[("ffn_pg", b)], writes=[("ffn_sg", b)],
                         out=sgb[b][:, :n], in_=pg[b][:, :n], func=AF.Silu)
                    S.op("dve", "tensor_tensor", reads=[("ffn_pu", b), ("ffn_sg", b)], writes=[("ffn_act", j, tt)],
                         out=act[:, j, sl], in0=pu[b][:, :n], in1=sgb[b][:, :n], op=ALU.mult)
            if gi_ + 1 < len(groups):
                t1, n1 = groups[gi_ + 1]
                emit_rmsnorm(C, x, "x", t1, n1, g_sb, gk, hns[(gi_ + 1) % 2], "ffn_hn%d" % ((gi_ + 1) % 2))
            for m in range(KC):
                wo = wobuf[m % 2]
                wok = ("ffn_wo", m % 2)
                S.dma("pool", writes=[wok], out=wo[:], in_=w_out_v[:, :, m * 128:(m + 1) * 128])
                for tt in range(ntg):
                    n = min(512, ntok - tt * 512)
                    io += 1
                    b = io % 2
                    sl = slice(tt * 512, tt * 512 + n)
                    gsl = slice(t0 + tt * 512, t0 + tt * 512 + n)
                    for j in range(JC):
                        S.mm(po[b][:, :n], wo[:, j, :], act[:, j, sl], j == 0, j == JC - 1,
                             [wok, ("ffn_act", j, tt)], [("ffn_po", b)])
                    xkey = ("x", m, (t0 + tt * 512) // 512)
                    S.op("dve", "scalar_tensor_tensor", reads=[("ffn_po", b), xkey], writes=[xkey],
                         out=x[:, m, gsl], in0=po[b][:, :n], scalar=0.5, in1=x[:, m, gsl], op0=ALU.mult, op1=ALU.add)


def emit_sc_mixer(C, x, T, g_dram, w_in, conv, w_out):
    S = C.S
    NT = T // 512
    w_in_v = w_in.rearrange("(kc p) n -> p kc n", p=128)
    w_out_v = w_out.rearrange("(kc p) n -> p kc n", p=128)
    with C.scope():
        g_sb = load_vec_pk(C, "sc_g", g_dram, KC)
        cw = C.sb("sc_cw", [128, KC, 3], F32)
        for j in range(3):
            S.dma("sp", writes=[("sc_cw", j)], out=cw[:, :, j], in_=conv[j].rearrange("(i p) -> p i", p=128),
                  allow_slow_non_contiguous=True)
        hn = C.sb("sc_hn", [128, KC, T + 2], BF16)
        ybf = C.sb("sc_y", [128, KC, T], BF16)
        chb = [C.sb("sc_ch%d" % i, [128, T + 2], F32) for i in range(2)]
        bsv = [C.sb("sc_b%d" % i, [128, T], F32) for i in range(2)]
        csb = [C.sb("sc_c%d" % i, [128, 512], F32) for i in range(2)]
        acc = [C.sb("sc_acc%d" % i, [128, 512], F32) for i in range(2)]
        wbuf = [C.sb("sc_wi%d" % i, [128, KC, 384], BF16) for i in range(2)]
        wobuf = [C.sb("sc_wo%d" % i, [128, KC, 128], BF16) for i in range(2)]
        pb = [C.ps("sc_pb%d" % i, [128, 512]) for i in range(2)]
        pc = [C.ps("sc_pc%d" % i, [128, 512]) for i in range(2)]
        ph = [C.ps("sc_ph%d" % i, [128, 512]) for i in range(2)]
        po = [C.ps("sc_po%d" % i, [128, 512]) for i in range(1)]
        emit_rmsnorm(C, x, "x", 0, T + 2, g_sb, "sc_g", hn, "sc_hn")
        it = 0
        for i in range(KC):
            wb = wbuf[i % 2]
            wk = ("sc_wi", i % 2)
            for q in range(3):
                S.dma("pool", writes=[wk + (q,)], out=wb[:, :, q * 128:(q + 1) * 128],
                      in_=w_in_v[:, :, q * D + i * 128: q * D + (i + 1) * 128])
            ch = chb[i % 2]
            bs = bsv[i % 2]
            for tt in range(NT + 1):
                n = 512 if tt < NT else 2
                it += 1
                b = it % 2
                sl = slice(tt * 512, tt * 512 + n)
                hkeys = lambda k: [("sc_hn", k, tt)]
                if tt < NT:
                    for k in range(KC):
                        S.mm(pb[b][:, :n], wb[:, k, 0:128], hn[:, k, sl], k == 0, k == KC - 1,
                             [wk + (0,)] + hkeys(k), [("sc_pb", b)])
                for k in range(KC):
                    S.mm(pc[b][:, :n], wb[:, k, 128:256], hn[:, k, sl], k == 0, k == KC - 1,
                         [wk + (1,)] + hkeys(k), [("sc_pc", b)])
                for k in range(KC):
                    S.mm(ph[b][:, :n], wb[:, k, 256:384], hn[:, k, sl], k == 0, k == KC - 1,
                         [wk + (2,)] + hkeys(k), [("sc_ph", b)])
                S.op("act", "activation", reads=[("sc_pc", b)], writes=[("sc_c", b)],
                     out=csb[b][:, :n], in_=pc[b][:, :n], func=AF.Copy)
                if tt < NT:
                    S.op("dve", "tensor_tensor", reads=[("sc_c", b), ("sc_ph", b)], writes=[("sc_ch", i % 2, tt)],
                         out=ch[:, 1 + tt * 512: 1 + tt * 512 + n], in0=ph[b][:, :n], in1=csb[b][:, :n], op=ALU.mult)
                    S.op("act", "activation", reads=[("sc_pb", b)], writes=[("sc_b", i % 2, tt)],
                         out=bs[:, sl], in_=pb[b][:, :n], func=AF.Copy)
                else:
                    S.op("dve", "tensor_tensor", reads=[("sc_c", b), ("sc_ph", b)], writes=[("sc_ch", i % 2, "hl")],
                         out=ch[:, 0:1], in0=ph[b][:, 0:1], in1=csb[b][:, 0:1], op=ALU.mult)
                    S.op("dve", "tensor_tensor", reads=[("sc_c", b), ("sc_ph", b)], writes=[("sc_ch", i % 2, "hr")],
                         out=ch[:, T + 1:T + 2], in0=ph[b][:, 1:2], in1=csb[b][:, 1:2], op=ALU.mult)
            for tt in range(NT):
                a = acc[tt % 2]
                ak = ("sc_acc", tt % 2)
                rk = [("sc_ch", i % 2, tt)]
                if tt > 0:
                    rk.append(("sc_ch", i % 2, tt - 1))
                else:
                    rk.append(("sc_ch", i % 2, "hl"))
                if tt < NT - 1:
                    rk.append(("sc_ch", i % 2, tt + 1))
                else:
                    rk.append(("sc_ch", i % 2, "hr"))
                o = tt * 512
                S.op("dve", "tensor_scalar", reads=rk + [("sc_cw", 0)], writes=[ak], out=a[:], in0=ch[:, o:o + 512],
                     scalar1=cw[:, i, 0:1], scalar2=None, op0=ALU.mult)
                S.op("dve", "scalar_tensor_tensor", reads=rk + [("sc_cw", 1), ak], writes=[ak], out=a[:],
                     in0=ch[:, o + 1:o + 513], scalar=cw[:, i, 1:2], in1=a[:], op0=ALU.mult, op1=ALU.add)
                S.op("dve", "scalar_tensor_tensor", reads=rk + [("sc_cw", 2), ak], writes=[ak], out=a[:],
                     in0=ch[:, o + 2:o + 514], scalar=cw[:, i, 2:3], in1=a[:], op0=ALU.mult, op1=ALU.add)
                S.op("dve", "tensor_tensor", reads=[ak, ("sc_b", i % 2, tt)], writes=[("sc_y", i, tt)],
                     out=ybf[:, i, o:o + 512], in0=a[:], in1=bs[:, o:o + 512], op=ALU.mult)
        for m in range(KC):
            wo = wobuf[m % 2]
            wok = ("sc_wo", m % 2)
            S.dma("pool", writes=[wok], out=wo[:], in_=w_out_v[:, :, m * 128:(m + 1) * 128])
            for tt in range(NT):
                sl = slice(tt * 512, (tt + 1) * 512)
                for i in range(KC):
                    S.mm(po[0][:], wo[:, i, :], ybf[:, i, sl], i == 0, i == KC - 1, [wok, ("sc_y", i, tt)], ["sc_po"])
                xkey = ("x", m, tt)
                S.op("dve", "tensor_tensor", reads=["sc_po", xkey], writes=[xkey], out=x[:, m, sl], in0=po[0][:],
                     in1=x[:, m, sl], op=ALU.add)


def emit_final_norm(C, x, T, g_dram):
    with C.scope():
        g_sb = load_vec_pk(C, "fin_g", g_dram, KC)
        emit_rmsnorm(C, x, "x", 0, T, g_sb, "fin_g", x, "x")


def emit_load_x(C, x, xT_dram, T):
    v = xT_dram.rearrange("(kc p) t -> p kc t", p=128)
    for k in range(KC):
        for tt in range((T + 511) // 512):
            n = min(512, T - tt * 512)
            C.S.dma("sp", writes=[("x", k, tt)], out=x[:, k, tt * 512:tt * 512 + n],
                    in_=v[:, k, tt * 512:tt * 512 + n])


def emit_store_x(C, x, yT_dram, T):
    v = yT_dram.rearrange("(kc p) t -> p kc t", p=128)
    for k in range(KC):
        for tt in range(T // 512):
            C.S.dma("sp", reads=[("x", k, tt)], is_out=True, out=v[:, k, tt * 512:(tt + 1) * 512],
                    in_=x[:, k, tt * 512:(tt + 1) * 512])


def build_ffn_prog(T=TPC):
    nc = bass.Bass("TRN2", target_bir_lowering=False)
    xT = dram_in(nc, "xT", [D, T])
    g = dram_in(nc, "g", [D])
    w_in = dram_in(nc, "w_in", [D, 2 * FF])
    w_out = dram_in(nc, "w_out", [FF, D])
    yT = dram_out(nc, "yT", [D, T])
    with contextlib.ExitStack() as es:
        C = Ctx(nc, es)
        emit_consts(C)
        x = C.sb("x", [128, KC, T], F32)
        emit_load_x(C, x, xT, T)
        emit_ffn(C, x, [(t, 1024) for t in range(0, T, 1024)], g, w_in, w_out, "f")
        emit_store_x(C, x, yT, T)
        C.S.replay()
    return nc


def build_sc_prog(T=TPC, final=False):
    nc = bass.Bass("TRN2", target_bir_lowering=False)
    xT = dram_in(nc, "xT", [D, T + 2])
    nrm = dram_in(nc, "nrm", [3, D])
    fw_in = dram_in(nc, "fw_in", [2, D, 2 * FF])
    fw_out = dram_in(nc, "fw_out", [2, FF, D])
    w_in = dram_in(nc, "w_in", [D, 3 * D])
    conv = dram_in(nc, "conv", [3, D])
    w_out = dram_in(nc, "w_out", [D, D])
    gfin = dram_in(nc, "gfin", [D])
    yT = dram_out(nc, "yT", [D, T])
    with contextlib.ExitStack() as es:
        C = Ctx(nc, es)
        emit_consts(C)
        x = C.sb("x", [128, KC, T + 2], F32)
        emit_load_x(C, x, xT, T + 2)
        grp = [(t, 1024) for t in range(0, T, 1024)]
        emit_ffn(C, x, grp + [(T, 2)], nrm[0], fw_in[0], fw_out[0], "f1")
        emit_sc_mixer(C, x, T, nrm[1], w_in, conv, w_out)
        emit_ffn(C, x, grp, nrm[2], fw_in[1], fw_out[1], "f2")
        if final:
            emit_final_norm(C, x, T, gfin)
        emit_store_x(C, x, yT, T)
        C.S.replay()
    return nc

NFFT = 2 * SEQ
HY_EMB = 33
HY_ORD = 64
MAGIC = 12582912.0
TWO_PI = 2.0 * math.pi
PI_LO = 3.1415925


def hyena_consts():
    n = np.arange(128, dtype=np.float64)
    ang = 2.0 * np.pi * np.outer(n, n) / 128.0
    fre, fim = np.cos(ang), -np.sin(ang)
    angt = 2.0 * np.pi * np.outer(n, n) / NFFT
    tre, tim = np.cos(angt), -np.sin(angt)
    cst = np.stack([fim, fre, -fim, tre, tim], 1).astype(np.float32)
    L = SEQ
    f32 = np.float32
    t = np.linspace(0.0, 1.0, L, dtype=f32)
    w = (f32(2.0 * math.pi) * np.arange(L, dtype=f32) / f32(L)).astype(f32)
    f = np.linspace(1e-4, 15.0, 16, dtype=f32)
    fw = (f[None, :] * w[:, None]).astype(f32)
    z = np.concatenate([t[:, None], np.cos(fw), -np.sin(fw)], -1).astype(f32)
    idx = np.concatenate([[0], np.arange(L - 1, 0, -1)])
    z2 = z[idx]
    t2 = t[idx].copy()
    t2[0] = 1e30
    zT = np.ascontiguousarray(np.concatenate([z, z2], 0).T)
    trow = np.concatenate([t, t2]).astype(f32)
    dmin = math.log(1e-2) / 1.5
    dmax = math.log(1e-2) / 0.3
    deltas = np.linspace(dmin, dmax, D, dtype=f32)
    nad = (-np.abs(deltas)).astype(f32)
    return cst, zT, trow, nad


def emit_fft_fwd(C, X, xkey, K, nseq, cst, tl, ps, kp=""):
    S = C.S
    fimfre = cst[:K, 0:2, :].rearrange("p a b -> p (a b)")
    for s_ in range(nseq):
        bank = ps["a"][s_ // 2]
        S.mm(bank[:, (s_ % 2) * 256:(s_ % 2) * 256 + 256], X[:K, s_, :], fimfre, True, True,
             [xkey, "cst"], [(kp + "psa", s_ // 2)])
    yield
    tre = cst[:, 3:4, :]
    tim = cst[:, 4:5, :]
    for h in range((nseq + 1) // 2):
        ns = min(2, nseq - 2 * h)
        av = ps["a"][h][:].rearrange("p (s r k) -> p s r k", s=2, r=2)
        aim = av[:, :ns, 0, :]
        are = av[:, :ns, 1, :]
        sl = slice(2 * h, 2 * h + ns)
        bt = lambda t_: t_.to_broadcast([128, ns, 128])
        S.op("dve", "tensor_tensor", reads=[(kp + "psa", h), "cst"], writes=[(kp + "t1", h)], out=tl["t1"][:, sl, :], in0=are,
             in1=bt(tre), op=ALU.mult)
        S.op("dve", "tensor_tensor", reads=[(kp + "psa", h), "cst"], writes=[(kp + "t2", h)], out=tl["t2"][:, sl, :], in0=aim,
             in1=bt(tim), op=ALU.mult)
        S.op("dve", "tensor_tensor", reads=[(kp + "psa", h), "cst"], writes=[(kp + "t3", h)], out=tl["t3"][:, sl, :], in0=are,
             in1=bt(tim), op=ALU.mult)
        S.op("dve", "tensor_tensor", reads=[(kp + "psa", h), "cst"], writes=[(kp + "t4", h)], out=tl["t4"][:, sl, :], in0=aim,
             in1=bt(tre), op=ALU.mult)
    hs = list(range((nseq + 1) // 2))
    S.op("pool", "tensor_tensor", reads=[(kp + "t1", h) for h in hs] + [(kp + "t2", h) for h in hs], writes=[kp + "bre"],
         out=tl["bre"][:, :nseq, :], in0=tl["t1"][:, :nseq, :], in1=tl["t2"][:, :nseq, :], op=ALU.subtract)
    S.op("pool", "tensor_tensor", reads=[(kp + "t3", h) for h in hs] + [(kp + "t4", h) for h in hs], writes=[kp + "bim"],
         out=tl["bim"][:, :nseq, :], in0=tl["t3"][:, :nseq, :], in1=tl["t4"][:, :nseq, :], op=ALU.add)
    yield
    n = nseq * 128
    bre = tl["bre"][:].rearrange("p s k -> p (s k)")[:, :n]
    bim = tl["bim"][:].rearrange("p s k -> p (s k)")[:, :n]
    S.mm(ps["xre"][:, :n], cst[:, 1, :], bre, True, False, ["cst", kp + "bre"], [kp + "psxre"])
    S.mm(ps["xre"][:, :n], cst[:, 2, :], bim, False, True, ["cst", kp + "bim"], [kp + "psxre"])
    S.mm(ps["xim"][:, :n], cst[:, 1, :], bim, True, False, ["cst", kp + "bim"], [kp + "psxim"])
    S.mm(ps["xim"][:, :n], cst[:, 0, :], bre, False, True, ["cst", kp + "bre"], [kp + "psxim"])
    yield


def build_hy_core_prog():
    nc = bass.Bass("TRN2", target_bir_lowering=False)
    L = SEQ
    u0 = dram_in(nc, "u0", [3, 128, BATCH, L])
    convw = dram_in(nc, "convw", [3, 3, 128])
    convb = dram_in(nc, "convb", [3, 128])
    dvec = dram_in(nc, "dvec", [128])
    fw1 = dram_in(nc, "fw1", [HY_EMB, HY_ORD])
    fb1 = dram_in(nc, "fb1", [HY_ORD])
    fw2 = dram_in(nc, "fw2", [HY_ORD, HY_ORD])
    fb2 = dram_in(nc, "fb2", [HY_ORD])
    fw3 = dram_in(nc, "fw3", [HY_ORD, 2, 128])
    freq = dram_in(nc, "freq", [HY_ORD])
    cstd = dram_in(nc, "cst", [128, 5, 128])
    zT = dram_in(nc, "zT", [HY_EMB, NFFT])
    trow = dram_in(nc, "trow", [128, NFFT])
    nad = dram_in(nc, "nad", [128])
    yT = dram_out(nc, "yT", [128, BATCH, L])
    s_h = nc.dram_tensor("s_h", [128, NFFT], F32).ap()
    s_H = nc.dram_tensor("s_H", [2, 128, 128, 128], F32).ap()
    s_vv = nc.dram_tensor("s_vv", [128, BATCH, L], F32).ap()
    s_x0 = nc.dram_tensor("s_x0", [128, BATCH, L], F32).ap()
    s_y = nc.dram_tensor("s_y", [128, BATCH, L], F32).ap()
    with contextlib.ExitStack() as es:
        C = Ctx(nc, es)
        S = C.S
        cst = C.sb("cst", [128, 5, 128], F32)
        S.dma("sp", writes=["cst"], out=cst[:], in_=cstd)

        def col(name, src, n):
            t_ = C.sb(name, [n, 1], F32)
            S.dma("sp", writes=[name], out=t_[:], in_=src.rearrange("(p o) -> p o", o=1))
            return t_

        with C.scope():
            w1 = C.sb("w1", [HY_EMB, HY_ORD], F32)
            S.dma("sp", writes=["w1"], out=w1[:], in_=fw1)
            w2 = C.sb("w2", [HY_ORD, HY_ORD], F32)
            S.dma("sp", writes=["w2"], out=w2[:], in_=fw2)
            w3 = C.sb("w3", [HY_ORD, 2, 128], F32)
            S.dma("sp", writes=["w3"], out=w3[:], in_=fw3)
            fq = col("fq", freq, HY_ORD)
            b1 = col("b1", fb1, HY_ORD)
            b2 = col("b2", fb2, HY_ORD)
            nadc = col("nadc", nad, 128)
            S.op("dve", "tensor_tensor", reads=["fq", "b1"], writes=["b1"], out=b1[:], in0=b1[:], in1=fq[:], op=ALU.mult)
            S.op("dve", "tensor_tensor", reads=["fq", "b2"], writes=["b2"], out=b2[:], in0=b2[:], in1=fq[:], op=ALU.mult)
            zt = [C.sb("zt%d" % i, [HY_EMB, 512], F32) for i in range(2)]
            tr = [C.sb("tr%d" % i, [128, 512], F32) for i in range(2)]
            NS = 2
            av = [[C.sb("av%d%d" % (l_, i), [HY_ORD, 512], F32) for i in range(NS)] for l_ in range(2)]
            qv = [[C.sb("qv%d%d" % (l_, i), [HY_ORD, 512], F32) for i in range(NS)] for l_ in range(2)]
            hv = [[C.sb("hv%d%d" % (l_, i), [HY_ORD, 512], F32) for i in range(NS)] for l_ in range(2)]
            hc = [C.sb("hc%d" % i, [128, 512], F32) for i in range(NS)]
            p1 = [C.ps("p1%d" % i, [HY_ORD, 512]) for i in range(NS)]
            p2 = [C.ps("p2%d" % i, [HY_ORD, 512]) for i in range(NS)]
            p3 = [C.ps("p3%d" % i, [128, 512]) for i in range(NS)]

            def sin_layer(psrc, pkey, bias, sl_, lay):
                a, q, h = av[lay][sl_], qv[lay][sl_], hv[lay][sl_]
                ak, qk, hk = ("av", lay, sl_), ("qv", lay, sl_), ("hv", lay, sl_)
                S.op("dve", "tensor_scalar", reads=[pkey, "fq", "b1", "b2"], writes=[ak], out=a[:], in0=psrc[:],
                     scalar1=fq[:, 0:1], scalar2=bias[:, 0:1], op0=ALU.mult, op1=ALU.add)
                S.op("dve", "tensor_scalar", reads=[ak], writes=[qk], out=q[:], in0=a[:], scalar1=1.0 / TWO_PI,
                     scalar2=MAGIC, op0=ALU.mult, op1=ALU.add)
                S.op("dve", "tensor_scalar", reads=[qk], writes=[qk], out=q[:], in0=q[:], scalar1=-MAGIC,
                     scalar2=-TWO_PI, op0=ALU.add, op1=ALU.mult)
                S.op("dve", "tensor_tensor", reads=[qk, ak], writes=[ak], out=a[:], in0=a[:], in1=q[:], op=ALU.add)
                S.op("dve", "tensor_scalar", reads=[ak], writes=[ak], out=a[:], in0=a[:], scalar1=-PI_LO,
                     scalar2=PI_LO, op0=ALU.max, op1=ALU.min)
                yield
                S.op("act", "activation", reads=[ak], writes=[hk], out=h[:], in_=a[:], func=AF.Sin)
                yield

            def p0_tile(ti):
                c0 = ti * 512
                sl_ = ti % NS
                z_, zk = zt[sl_], ("zt", sl_)
                t_, tk = tr[sl_], ("tr", sl_)
                S.dma("sp", writes=[zk], out=z_[:], in_=zT[:, c0:c0 + 512])
                S.dma("sp", writes=[tk], out=t_[:], in_=trow[:, c0:c0 + 512])
                S.mm(p1[sl_][:], w1[:], z_[:], True, True, ["w1", zk], [("p1", sl_)])
                yield
                for _ in sin_layer(p1[sl_], ("p1", sl_), b1, sl_, 0):
                    yield
                S.mm(p2[sl_][:], w2[:], hv[0][sl_][:], True, True, ["w2", ("hv", 0, sl_)], [("p2", sl_)])
                S.op("act", "activation", reads=[tk, "nadc"], writes=[tk], out=t_[:], in_=t_[:], func=AF.Exp,
                     scale=nadc[:, 0:1])
                yield
                for _ in sin_layer(p2[sl_], ("p2", sl_), b2, sl_, 1):
                    yield
                half = 0 if ti < (L // 512) else 1
                S.mm(p3[sl_][:], w3[:, half, :], hv[1][sl_][:], True, True, ["w3", ("hv", 1, sl_)], [("p3", sl_)])
                yield
                o_, ok = hc[sl_], ("hc", sl_)
                S.op("dve", "tensor_tensor", reads=[("p3", sl_), tk], writes=[ok], out=o_[:], in0=p3[sl_][:], in1=t_[:],
                     op=ALU.mult)
                S.dma("sp", reads=[ok], writes=[("s_h", ti)], out=s_h[:, c0:c0 + 512], in_=o_[:])
                yield

            pipeline((p0_tile(ti) for ti in range(NFFT // 512)), NS)

        with C.scope():
            cw = C.sb("hcw", [128, 3, 3], F32)
            for j in range(3):
                for gi in range(3):
                    S.dma("sp", writes=[("hcw", j, gi)], out=cw[:, gi, j:j + 1],
                          in_=convw[j, gi].rearrange("(p o) -> p o", o=1))
            cb = C.sb("hcb", [128, 3], F32)
            for gi in range(3):
                S.dma("sp", writes=[("hcb", gi)], out=cb[:, gi:gi + 1], in_=convb[gi].rearrange("(p o) -> p o", o=1))
            cwk = [("hcw", j, gi) for j in range(3) for gi in range(3)] + [("hcb", gi) for gi in range(3)]
            ub = [C.sb("hu%d" % i, [128, L + 2], F32) for i in range(2)] * 2
            uc = [C.sb("huc%d" % i, [128, L], F32) for i in range(3)]
            for b in range(BATCH):
                for gi in range(3):
                    S.op("pool", "memset", writes=[("hu", gi % 2, "e")], ap=ub[gi][:, 0:1], constant=0.0)
                    S.op("pool", "memset", writes=[("hu", gi % 2, "e2")], ap=ub[gi][:, L + 1:L + 2], constant=0.0)
                    S.dma("sp", writes=[("hu", gi % 2)], out=ub[gi][:, 1:L + 1], in_=u0[gi, :, b, :])
                    rk = [("hu", gi % 2), ("hu", gi % 2, "e"), ("hu", gi % 2, "e2")] + cwk
                    uk = ("huc", gi)
                    S.op("dve", "tensor_scalar", reads=rk, writes=[uk], out=uc[gi][:], in0=ub[gi][:, 0:L],
                         scalar1=cw[:, gi, 0:1], scalar2=cb[:, gi:gi + 1], op0=ALU.mult, op1=ALU.add)
                    S.op("dve", "scalar_tensor_tensor", reads=rk + [uk], writes=[uk], out=uc[gi][:],
                         in0=ub[gi][:, 1:L + 1], scalar=cw[:, gi, 1:2], in1=uc[gi][:], op0=ALU.mult, op1=ALU.add)
                    S.op("dve", "scalar_tensor_tensor", reads=rk + [uk], writes=[uk], out=uc[gi][:],
                         in0=ub[gi][:, 2:L + 2], scalar=cw[:, gi, 2:3], in1=uc[gi][:], op0=ALU.mult, op1=ALU.add)
                S.op("pool", "tensor_tensor", reads=[("huc", 1), ("huc", 2)], writes=[("huc", 2)], out=uc[2][:],
                     in0=uc[2][:], in1=uc[1][:], op=ALU.mult)
                S.dma("sp", reads=[("huc", 2)], writes=[("s_vv", b)], out=s_vv[:, b, :], in_=uc[2][:])
                S.dma("sp", reads=[("huc", 0)], writes=[("s_x0", b)], out=s_x0[:, b, :], in_=uc[0][:])
        with C.scope():
            tl = {k: C.sb("fft_" + k, [128, 4, 128], F32) for k in
                  ("t1", "t2", "t3", "t4", "u1", "u2", "u3", "u4", "bre", "bim", "dre", "dim")}
            yre = [C.sb("fft_yre%d" % i, [128, 4, 128], F32) for i in range(2)]
            yim = [C.sb("fft_yim%d" % i, [128, 4, 128], F32) for i in range(2)]
            ps = {"a": [C.ps("psa%d" % i, [128, 512]) for i in range(2)], "xre": C.ps("psxre", [128, 512]),
                  "xim": C.ps("psxim", [128, 512]), "c": [C.ps("psc%d" % i, [128, 512]) for i in range(2)],
                  "y": C.ps("psy", [128, 512])}
            Xb = [C.sb("fft_X%d" % i, [128, 4, 128], F32) for i in range(2)]
            Hb = [[C.sb("fft_H%d%d" % (i, r), [128, 2, 128], F32) for r in range(2)] for i in range(2)]
            ev = [[C.sb("fft_ev%d%d" % (i, r), [128, 4, 128], F32) for r in range(2)] for i in range(2)]
            yo = [C.sb("fft_yo%d" % i, [64, 4, 128], F32) for i in range(2)]
            ps8 = C.ps("psx8", [128, 512])
            tlB = {"t1": tl["u1"], "t2": tl["u2"], "t3": tl["u3"], "t4": tl["u4"], "bre": tl["dre"], "bim": tl["dim"]}
            psB = {"a": ps["c"], "xre": ps["y"], "xim": ps8}

            def p1_group(g):
                q = g % 2
                tl_, ps_, kp = (tl, ps, "") if q == 0 else (tlB, psB, "B")
                X, xk = Xb[q], ("X", q)
                S.dma("sp", reads=[("s_h", ti) for ti in range(NFFT // 512)], writes=[xk], out=X[:],
                      in_=s_h[4 * g:4 * g + 4, :].rearrange("c (n1 n2) -> n1 c n2", n2=128))
                for _ in emit_fft_fwd(C, X, xk, 128, 4, cst, tl_, ps_, kp):
                    yield
                for r, nm in ((0, "xre"), (1, "xim")):
                    e_, ek = ev[q][r], ("ev", q, r)
                    S.op("act", "mul", reads=[kp + "ps" + nm], writes=[ek], out=e_[:].rearrange("p s k -> p (s k)"),
                         in_=ps_[nm][:], mul=1.0 / NFFT)
                    S.dma("sp", reads=[ek], writes=[("s_H", g, r)], out=s_H[r, :, 4 * g:4 * g + 4, :], in_=e_[:])
                yield

            pipeline((p1_group(g) for g in range(32)), 2)
            S.barrier()
            tre = cst[:, 3:4, :]
            tim = cst[:, 4:5, :]
            g1 = cst[:, 1:3, :].rearrange("p a b -> p (a b)")
            g2 = cst[:, 0:2, :].rearrange("p a b -> p (a b)")

            def half1(g):
                p = g % 2
                X, xk = Xb[p], ("X", p)
                S.dma("sp", reads=[("s_vv", 0), ("s_vv", 1)], writes=[xk], out=X[:64],
                      in_=s_vv[2 * g:2 * g + 2].rearrange("c b (n1 n2) -> n1 (c b) n2", n2=128))
                H = Hb[p]
                for r in range(2):
                    S.dma("sp", reads=[("s_H", g // 2, r)], writes=[("H", p, r)], out=H[r][:],
                          in_=s_H[r, :, 2 * g:2 * g + 2, :])
                for _ in emit_fft_fwd(C, X, xk, 64, 4, cst, tl, ps):
                    yield
                hk = [("H", p, 0), ("H", p, 1)]
                xre = ps["xre"][:].rearrange("p (c b k) -> p c b k", c=2, b=2)
                xim = ps["xim"][:].rearrange("p (c b k) -> p c b k", c=2, b=2)
                hb = lambda r: H[r][:].unsqueeze(2).to_broadcast([128, 2, 2, 128])
                v4 = lambda t_: t_[:].rearrange("p (c b) k -> p c b k", c=2)
                S.op("dve", "tensor_tensor", reads=["psxre"] + hk, writes=[("t1", 0), ("t1", 1)], out=v4(tl["t1"]),
                     in0=xre, in1=hb(0), op=ALU.mult)
                S.op("dve", "tensor_tensor", reads=["psxim"] + hk, writes=[("t2", 0), ("t2", 1)], out=v4(tl["t2"]),
                     in0=xim, in1=hb(1), op=ALU.mult)
                S.op("dve", "tensor_tensor", reads=["psxre"] + hk, writes=[("t3", 0), ("t3", 1)], out=v4(tl["t3"]),
                     in0=xre, in1=hb(1), op=ALU.mult)
                S.op("dve", "tensor_tensor", reads=["psxim"] + hk, writes=[("t4", 0), ("t4", 1)], out=v4(tl["t4"]),
                     in0=xim, in1=hb(0), op=ALU.mult)
                S.op("pool", "tensor_tensor", reads=[("t1", 0), ("t1", 1), ("t2", 0), ("t2", 1)], writes=[("yre", p)],
                     out=yre[p][:], in0=tl["t1"][:], in1=tl["t2"][:], op=ALU.subtract)
                S.op("pool", "tensor_tensor", reads=[("t3", 0), ("t3", 1), ("t4", 0), ("t4", 1)], writes=[("yim", p)],
                     out=yim[p][:], in0=tl["t3"][:], in1=tl["t4"][:], op=ALU.add)
                yield

            def half2(g):
                p = g % 2
                for s_ in range(4):
                    bank = ps["c"][s_ // 2]
                    o = (s_ % 2) * 256
                    S.mm(bank[:, o:o + 256], yre[p][:, s_, :], g1, True, False, [("yre", p), "cst"], [("psc", s_ // 2)])
                    S.mm(bank[:, o:o + 256], yim[p][:, s_, :], g2, False, True, [("yim", p), "cst"], [("psc", s_ // 2)])
                yield
                for h in range(2):
                    cv = ps["c"][h][:].rearrange("p (s r k) -> p s r k", s=2, r=2)
                    cre = cv[:, :, 0, :]
                    cim = cv[:, :, 1, :]
                    sl = slice(2 * h, 2 * h + 2)
                    bt = lambda t_: t_.to_broadcast([128, 2, 128])
                    S.op("dve", "tensor_tensor", reads=[("psc", h), "cst"], writes=[("u1", h)], out=tl["u1"][:, sl, :],
                         in0=cre, in1=bt(tre), op=ALU.mult)
                    S.op("dve", "tensor_tensor", reads=[("psc", h), "cst"], writes=[("u2", h)], out=tl["u2"][:, sl, :],
                         in0=cim, in1=bt(tim), op=ALU.mult)
                    S.op("dve", "tensor_tensor", reads=[("psc", h), "cst"], writes=[("u3", h)], out=tl["u3"][:, sl, :],
                         in0=cim, in1=bt(tre), op=ALU.mult)
                    S.op("dve", "tensor_tensor", reads=[("psc", h), "cst"], writes=[("u4", h)], out=tl["u4"][:, sl, :],
                         in0=cre, in1=bt(tim), op=ALU.mult)
                S.op("pool", "tensor_tensor", reads=[("u1", 0), ("u1", 1), ("u2", 0), ("u2", 1)], writes=["dre"],
                     out=tl["dre"][:], in0=tl["u1"][:], in1=tl["u2"][:], op=ALU.add)
                S.op("pool", "tensor_tensor", reads=[("u3", 0), ("u3", 1), ("u4", 0), ("u4", 1)], writes=["dim"],
                     out=tl["dim"][:], in0=tl["u3"][:], in1=tl["u4"][:], op=ALU.subtract)
                yield
                S.mm(ps["y"][:64, :], cst[:, 1, 0:64], tl["dre"][:].rearrange("p s k -> p (s k)"), True, False,
                     ["cst", "dre"], ["psy"])
                S.mm(ps["y"][:64, :], cst[:, 0, 0:64], tl["dim"][:].rearrange("p s k -> p (s k)"), False, True,
                     ["cst", "dim"], ["psy"])
                yield
                y_, yk = yo[p], ("yo", p)
                S.op("act", "activation", reads=["psy"], writes=[yk], out=y_[:].rearrange("p s k -> p (s k)"),
                     in_=ps["y"][:64, :], func=AF.Copy)
                S.dma("sp", reads=[yk], writes=[("s_y", g)], out=s_y[2 * g:2 * g + 2].rearrange(
                    "c b (n1 n2) -> n1 (c b) n2", n2=128), in_=y_[:])
                yield

            for _ in half1(0):
                pass
            for g in range(64):
                interleave([half2(g)] + ([half1(g + 1)] if g + 1 < 64 else []))
        with C.scope():
            dcol = col("dcol", dvec, 128)
            ya = C.sb("p4y", [128, L], F32)
            va = C.sb("p4v", [128, L], F32)
            xa = C.sb("p4x", [128, L], F32)
            for b in range(BATCH):
                S.dma("sp", reads=[("s_y", g) for g in range(64)], writes=["p4y"], out=ya[:], in_=s_y[:, b, :])
                S.dma("sp", reads=[("s_vv", b)], writes=["p4v"], out=va[:], in_=s_vv[:, b, :])
                S.dma("sp", reads=[("s_x0", b)], writes=["p4x"], out=xa[:], in_=s_x0[:, b, :])
                S.op("dve", "scalar_tensor_tensor", reads=["p4y", "p4v", "dcol"], writes=["p4y"], out=ya[:], in0=va[:],
                     scalar=dcol[:, 0:1], in1=ya[:], op0=ALU.mult, op1=ALU.add)
                S.op("dve", "tensor_tensor", reads=["p4y", "p4x"], writes=["p4y"], out=ya[:], in0=ya[:], in1=xa[:],
                     op=ALU.mult)
                S.dma("sp", reads=["p4y"], is_out=True, out=yT[:, b, :], in_=ya[:])
        S.replay()
    return nc

def emit_proj(C, x, T, g_dram, w_in, b_in, nout, uT):
    S = C.S
    NT = T // 512
    w_in_v = w_in.rearrange("(kc p) n -> p kc n", p=128)
    with C.scope():
        g_sb = load_vec_pk(C, "pj_g", g_dram, KC)
        bias = load_vec_pk(C, "pj_b", b_in, nout // 128)
        hn = C.sb("pj_hn", [128, KC, T], BF16)
        wbuf = [C.sb("pj_w%d" % i, [128, KC, 128], BF16) for i in range(3)]
        st = [C.sb("pj_st%d" % i, [128, 512], F32) for i in range(4)]
        pp = [C.ps("pj_ps%d" % i, [128, 512]) for i in range(2)]
        emit_rmsnorm(C, x, "x", 0, T, g_sb, "pj_g", hn, "pj_hn")
        it = 0
        for m in range(nout // 128):
            wb, wk = wbuf[m % 3], ("pj_w", m % 3)
            S.dma("pool", writes=[wk], out=wb[:], in_=w_in_v[:, :, m * 128:(m + 1) * 128])
            for tt in range(NT):
                it += 1
                b = it % 2
                sl = slice(tt * 512, (tt + 1) * 512)
                for k in range(KC):
                    S.mm(pp[b][:], wb[:, k, :], hn[:, k, sl], k == 0, k == KC - 1, [wk, ("pj_hn", k, tt)], [("pj_ps", b)])
                o_, ok = st[it % 4], ("pj_st", it % 4)
                S.op("act", "activation", reads=[("pj_ps", b), "pj_b"], writes=[ok], out=o_[:], in_=pp[b][:],
                     func=AF.Identity, bias=bias[:, m:m + 1])
                S.dma("sp", reads=[ok], is_out=True, out=uT[m * 128:(m + 1) * 128, sl], in_=o_[:])


def emit_outproj(C, x, T, yT, w_out, b_out):
    S = C.S
    NT = T // 512
    w_out_v = w_out.rearrange("(kc p) n -> p kc n", p=128)
    y_v = yT.rearrange("(kc p) t -> p kc t", p=128)
    with C.scope():
        bo = load_vec_pk(C, "op_b", b_out, KC)
        ybf = C.sb("op_y", [128, KC, T], BF16)
        for k in range(KC):
            S.dma("pool", writes=[("op_y", k)], out=ybf[:, k, :], in_=y_v[:, k, :])
        wobuf = [C.sb("op_wo%d" % i, [128, KC, 128], BF16) for i in range(2)]
        po = [C.ps("op_po%d" % i, [128, 512]) for i in range(2)]
        it = 0
        for m in range(KC):
            wo, wok = wobuf[m % 2], ("op_wo", m % 2)
            S.dma("pool", writes=[wok], out=wo[:], in_=w_out_v[:, :, m * 128:(m + 1) * 128])
            for tt in range(NT):
                it += 1
                b = it % 2
                sl = slice(tt * 512, (tt + 1) * 512)
                for i in range(KC):
                    S.mm(po[b][:], wo[:, i, :], ybf[:, i, sl], i == 0, i == KC - 1, [wok, ("op_y", i)], [("op_po", b)])
                xkey = ("x", m, tt)
                S.op("dve", "scalar_tensor_tensor", reads=[("op_po", b), xkey, "op_b"], writes=[xkey], out=x[:, m, sl],
                     in0=po[b][:], scalar=bo[:, m:m + 1], in1=x[:, m, sl], op0=ALU.add, op1=ALU.add)


def build_pre_prog(nout, T=TPC):
    nc = bass.Bass("TRN2", target_bir_lowering=False)
    xT = dram_in(nc, "xT", [D, T])
    nrm = dram_in(nc, "nrm", [2, D])
    fw_in = dram_in(nc, "fw_in", [D, 2 * FF])
    fw_out = dram_in(nc, "fw_out", [FF, D])
    w_in = dram_in(nc, "w_in", [D, nout])
    b_in = dram_in(nc, "b_in", [nout])
    xo = dram_out(nc, "xo", [D, T])
    uT = dram_out(nc, "uT", [nout, T])
    with contextlib.ExitStack() as es:
        C = Ctx(nc, es)
        emit_consts(C)
        x = C.sb("x", [128, KC, T], F32)
        emit_load_x(C, x, xT, T)
        emit_ffn(C, x, [(t, 1024) for t in range(0, T, 1024)], nrm[0], fw_in, fw_out, "f1")
        emit_store_x(C, x, xo, T)
        emit_proj(C, x, T, nrm[1], w_in, b_in, nout, uT)
        C.S.replay()
    return nc


def build_post_prog(T=TPC):
    nc = bass.Bass("TRN2", target_bir_lowering=False)
    xT = dram_in(nc, "xT", [D, T])
    yT = dram_in(nc, "yT", [D, T])
    w_out = dram_in(nc, "w_out", [D, D])
    b_out = dram_in(nc, "b_out", [D])
    nrm = dram_in(nc, "nrm", [D])
    fw_in = dram_in(nc, "fw_in", [D, 2 * FF])
    fw_out = dram_in(nc, "fw_out", [FF, D])
    xo = dram_out(nc, "xo", [D, T])
    with contextlib.ExitStack() as es:
        C = Ctx(nc, es)
        emit_consts(C)
        x = C.sb("x", [128, KC, T], F32)
        emit_load_x(C, x, xT, T)
        emit_outproj(C, x, T, yT, w_out, b_out)
        emit_ffn(C, x, [(t, 1024) for t in range(0, T, 1024)], nrm, fw_in, fw_out, "f2")
        emit_store_x(C, x, xo, T)
        C.S.replay()
    return nc


CH = 64
NCH = SEQ // CH
BLK = 8
SCAN_GAP = 3


def gdn_consts():
    i = np.arange(CH)
    mtri = np.stack([(i[:, None] <= i[None, :]), (i[:, None] >= i[None, :])]).astype(np.float32)
    eye = np.eye(CH, dtype=np.float32)
    mstrict = mtri - eye[None]
    ident = np.eye(128, dtype=np.float32)
    return mtri, mstrict, ident


def build_gd_core_prog(dbg_blocks=None):
    nc = bass.Bass("TRN2", target_bir_lowering=False)
    L, N = SEQ, NCH
    qkv0 = dram_in(nc, "qkv0", [3, 128, BATCH, L])
    ztok = dram_in(nc, "ztok", [CH, BATCH, N, 128])
    abt = dram_in(nc, "abt", [4, CH, BATCH, N])
    convw = dram_in(nc, "convw", [3, 3, 128])
    alog = dram_in(nc, "alog", [CH, 2])
    dtb = dram_in(nc, "dtb", [CH, 2])
    gn = dram_in(nc, "gn", [CH, 128])
    mtri_d = dram_in(nc, "mtri", [2, CH, CH])
    mstr_d = dram_in(nc, "mstrict", [2, CH, CH])
    ident_d = dram_in(nc, "ident", [128, 128])
    ytok = dram_out(nc, "ytok", [CH, BATCH, N, 128])
    s_o = nc.dram_tensor("s_o", [2, CH, BATCH, N, 128], F32).ap()
    with contextlib.ExitStack() as es:
        C = Ctx(nc, es)
        S = C.S
        ones = C.sb("ones", [128, 128], F32)
        S.op("dve", "memset", writes=["ones"], ap=ones[:], constant=1.0)
        ident = C.sb("ident", [128, 128], F32)
        S.dma("sp", writes=["ident"], out=ident[:], in_=ident_d)
        mtri = C.sb("mtri", [CH, 2, CH], F32)
        mstr = C.sb("mstr", [CH, 2, CH], F32)
        for d in range(2):
            S.dma("sp", writes=[("mtri", d)], out=mtri[:, d, :], in_=mtri_d[d])
            S.dma("sp", writes=[("mstr", d)], out=mstr[:, d, :], in_=mstr_d[d])
        cw = C.sb("gcw", [128, 3, 3], F32)
        for j in range(3):
            for gi in range(3):
                S.dma("sp", writes=[("gcw", j, gi)], out=cw[:, gi, j:j + 1], in_=convw[j, gi].rearrange("(p o) -> p o", o=1))
        cwk = [("gcw", j, gi) for j in range(3) for gi in range(3)]
        gtf = C.sb("gt", [128, 2, BATCH, N], F32)
        S.op("dve", "memset", writes=[("gt", 0), ("gt", 1)], ap=gtf[:], constant=0.0)
        gt = gtf[:CH]
        bt_ = C.sb("bt", [CH, 2, BATCH, N], F32)
        al = C.sb("al", [CH, 2], F32)
        db = C.sb("db", [CH, 2], F32)
        S.dma("sp", writes=["al"], out=al[:], in_=alog)
        S.dma("sp", writes=["db"], out=db[:], in_=dtb)
        for d in range(2):
            S.dma("sp", writes=[("gt", d)], out=gt[:, d], in_=abt[d])
            S.dma("sp", writes=[("bt", d)], out=bt_[:, d], in_=abt[2 + d])
        S.op("act", "activation", reads=["al"], writes=["al"], out=al[:], in_=al[:], func=AF.Exp)
        S.op("dve", "tensor_scalar", reads=["al"], writes=["al"], out=al[:], in0=al[:], scalar1=-1.0, scalar2=None,
             op0=ALU.mult)
        for d in range(2):
            gv = gt[:, d].rearrange("p b n -> p (b n)")
            bv = bt_[:, d].rearrange("p b n -> p (b n)")
            S.op("act", "activation", reads=[("gt", d), "db"], writes=[("gt", d)], out=gv, in_=gv, func=AF.Exp,
                 bias=db[:, d:d + 1])
            S.op("dve", "tensor_scalar", reads=[("gt", d)], writes=[("gt", d)], out=gv, in0=gv, scalar1=1.0, scalar2=None,
                 op0=ALU.add)
            S.op("act", "activation", reads=[("gt", d)], writes=[("gt", d)], out=gv, in_=gv, func=AF.Ln)
            S.op("dve", "tensor_scalar", reads=[("gt", d), "al"], writes=[("gt", d)], out=gv, in0=gv, scalar1=al[:, d:d + 1],
                 scalar2=None, op0=ALU.mult)
            S.op("act", "activation", reads=[("bt", d)], writes=[("bt", d)], out=bv, in_=bv, func=AF.Sigmoid)
        with C.scope():
            qT = C.sb("qT", [128, L], F32)
            kT = C.sb("kT", [128, L], F32)
            vT = C.sb("vT", [128, L], F32)
            ubs = [C.sb("gub%d" % i, [128, 2048 + 2], F32) for i in range(2)]
            ci = 0
            rsb = [C.sb("grs%d" % i, [128, 512], F32) for i in range(2)]
            sqb = [C.sb("gsq%d" % i, [128, 512], F32) for i in range(2)]
            gc = C.sb("gc", [CH, N], F32)
            kd = C.sb("kd", [CH, N], F32)
            bg = C.sb("bg", [CH, N], F32)
            egt = C.sb("egt", [128, N], F32)
            Sst = [C.sb("Sst%d" % i, [128, 128], F32) for i in range(2)]
            attnT = [C.sb("g8_attnT%d" % i, [128, BLK, CH], F32) for i in range(2)]
            for i_ in range(2):
                S.op("dve", "memset", writes=[("attnT", i_)], ap=attnT[i_][:], constant=0.0)
            names = ["dT", "W1", "LT", "Lm", "X8", "Pa", "Pta", "Pb", "Ptb"]
            t8f = {nm: C.sb("g8_" + nm, [128, BLK, CH], F32) for nm in names}
            for nm in names:
                S.op("dve", "memset", writes=[nm], ap=t8f[nm][:], constant=0.0)
            t8 = {nm: t8f[nm][:CH] for nm in names}
            qd = [C.sb("g8_qd%d" % i, [128, BLK * CH], F32) for i in range(2)]
            egr = C.sb("g8_egr", [128, BLK * CH], F32)
            kdt = [C.sb("g8_kdt%d" % i, [128, BLK, 128], F32) for i in range(2)]
            for i_ in range(2):
                S.op("dve", "memset", writes=[("kdt", i_, 0)], ap=kdt[i_][:, 0:4, :], constant=0.0)
                S.op("dve", "memset", writes=[("kdt", i_, 1)], ap=kdt[i_][:, 4:8, :], constant=0.0)
            kbt = C.sb("g8_kbt", [128, BLK, 128], F32)
            S.op("dve", "memset", writes=[("kbt", 0)], ap=kbt[:, 0:4, :], constant=0.0)
            S.op("dve", "memset", writes=[("kbt", 1)], ap=kbt[:, 4:8, :], constant=0.0)
            vbt = C.sb("g8_vbt", [CH, BLK, 128], F32)
            u8 = [C.sb("g8_u8%d" % i, [128, BLK, 128], F32) for i in range(2)]
            MT8 = [C.sb("g8_MT%d" % i, [128, BLK, 128], F32) for i in range(2)]
            w8 = C.sb("g8_w8", [128, BLK, 128], F32)
            mtmp = C.sb("g8_mtmp", [128, BLK, 128], F32)
            for i_ in range(2):
                for hh_ in range(2):
                    S.op("dve", "memset", writes=[("u8", i_, hh_)], ap=u8[i_][:, hh_ * 4:hh_ * 4 + 4, :], constant=0.0)
            for hh_ in range(2):
                S.op("dve", "memset", writes=[("w8", hh_)], ap=w8[:, hh_ * 4:hh_ * 4 + 4, :], constant=0.0)
            o8 = [C.sb("g8_o8%d" % i, [CH, BLK, 128], F32) for i in range(2)]
            Bk = [C.ps("gB%d" % i, [128, 512]) for i in range(8)]
            bk = lambda i: ("B", i)
            b7all = [bk(7)]
            S.excl.update(bk(i) for i in range(8))

            for b in range(BATCH):
                for gi, dst, dk_ in ((0, qT, "qT"), (1, kT, "kT"), (2, vT, "vT")):
                    CT = 2048
                    for ct in range(L // CT):
                        ci += 1
                        u_, uk_ = ubs[ci % 2], ("gub", ci % 2)
                        lo = ct * CT - 1
                        hi = ct * CT + CT + 1
                        a_ = max(lo, 0)
                        b_ = min(hi, L)
                        wr = [uk_]
                        if lo < 0:
                            S.op("pool", "memset", writes=[uk_], ap=u_[:, 0:1], constant=0.0)
                        if hi > L:
                            S.op("pool", "memset", writes=[uk_], ap=u_[:, CT + 1:CT + 2], constant=0.0)
                        S.dma("sp", writes=[uk_], out=u_[:, a_ - lo:b_ - lo], in_=qkv0[gi, :, b, a_:b_])
                        rk = wr + cwk
                        dsl = slice(ct * CT, (ct + 1) * CT)
                        dkt = (dk_, ct)
                        S.op("dve", "tensor_scalar", reads=rk, writes=[dkt], out=dst[:, dsl], in0=u_[:, 0:CT],
                             scalar1=cw[:, gi, 0:1], scalar2=None, op0=ALU.mult)
                        S.op("dve", "scalar_tensor_tensor", reads=rk + [dkt], writes=[dkt], out=dst[:, dsl], in0=u_[:, 1:CT + 1],
                             scalar=cw[:, gi, 1:2], in1=dst[:, dsl], op0=ALU.mult, op1=ALU.add)
                        S.op("dve", "scalar_tensor_tensor", reads=rk + [dkt], writes=[dkt], out=dst[:, dsl], in0=u_[:, 2:CT + 2],
                             scalar=cw[:, gi, 2:3], in1=dst[:, dsl], op0=ALU.mult, op1=ALU.add)
                        S.op("act", "activation", reads=[dkt], writes=[dkt], out=dst[:, dsl], in_=dst[:, dsl], func=AF.Silu)
                    S.op("act", "activation", reads=[(dk_, ct) for ct in range(L // CT)], writes=[dk_], out=dst[:, 0:1],
                         in_=dst[:, 0:1], func=AF.Copy)
                    if gi < 2:
                        sc = (128.0 ** -0.5) if gi == 0 else 1.0

                        def l2_tile(tt, dst=dst, dk_=dk_, sc=sc):
                            sl = slice(tt * 512, (tt + 1) * 512)
                            q3 = tt % 2
                            sq, sqk = sqb[q3], ("gsq", q3)
                            rs, rsk = rsb[q3], ("grs", q3)
                            S.op("act", "activation", reads=[dk_], writes=[sqk], out=sq[:], in_=dst[:, sl], func=AF.Square)
                            yield
                            S.mm(Bk[q3][:], ones[:], sq[:], True, True, ["ones", sqk], [bk(q3)])
                            yield
                            S.op("dve", "tensor_scalar", reads=[bk(q3)], writes=[rsk], out=rs[:], in0=Bk[q3][:], scalar1=1e-6,
                                 scalar2=None, op0=ALU.add)
                            yield
                            S.op("act", "activation", reads=[rsk], writes=[rsk], out=rs[:], in_=rs[:], func=AF.Sqrt)
                            yield
                            S.op("dve", "reciprocal", reads=[rsk], writes=[rsk], out=rs[:], in_=rs[:])
                            S.op("dve", "scalar_tensor_tensor", reads=[rsk, dk_], writes=[(dk_, "n", tt)], out=dst[:, sl],
                                 in0=dst[:, sl], scalar=sc, in1=rs[:], op0=ALU.mult, op1=ALU.mult)
                            yield

                        pipeline((l2_tile(tt) for tt in range(L // 512)), 2)
                        S.op("dve", "tensor_copy", reads=[(dk_, "n", tt) for tt in range(L // 512)], writes=[dk_],
                             out=dst[:, 0:1], in_=dst[:, 0:1])
                for d in range(2):
                    Gd = gt[:, d, b, :]
                    Bd = bt_[:, d, b, :]
                    S.mm(Bk[0][:CH, :N], mtri[:, d, :], Gd, True, True, [("mtri", d), ("gt", d)], [bk(0)])
                    S.op("act", "activation", reads=[bk(0)], writes=["gc"], out=gc[:], in_=Bk[0][:CH, :N], func=AF.Copy)
                    S.mm(Bk[1][:, :N], ones[:], gtf[:, d, b, :], True, True, ["ones", ("gt", d)], [bk(1)])
                    S.op("act", "activation", reads=[bk(1)], writes=["egt"], out=egt[:], in_=Bk[1][:, :N], func=AF.Exp)
                    S.op("dve", "tensor_tensor", reads=[bk(1), "gc"], writes=["kd"], out=kd[:], in0=Bk[1][:CH, :N], in1=gc[:],
                         op=ALU.subtract)
                    S.op("act", "activation", reads=["kd"], writes=["kd"], out=kd[:], in_=kd[:], func=AF.Exp)
                    S.op("act", "activation", reads=["gc"], writes=["bg"], out=bg[:], in_=gc[:], func=AF.Exp)
                    S.op("dve", "tensor_tensor", reads=["bg", ("bt", d)], writes=["bg"], out=bg[:], in0=bg[:], in1=Bd, op=ALU.mult)
                    S.op("dve", "memset", writes=[("S", 0)], ap=Sst[0][:], constant=0.0)
                    sidx = [0]
                    si = 0
                    vi = 0
                    blocks = list(range(N // BLK))
                    if d == 1:
                        blocks = blocks[::-1]
                    if dbg_blocks is not None:
                        blocks = blocks[:dbg_blocks]
                    flat = lambda t_: t_[:].rearrange("p i c -> p (i c)")
                    mt_b = mtri[:, d:d + 1, :].to_broadcast([CH, BLK, CH])
                    ms_b = mstr[:, d:d + 1, :].to_broadcast([CH, BLK, CH])
                    id_b = ident[:CH, 0:CH].unsqueeze(1).to_broadcast([CH, BLK, CH])
                    k3 = lambda i_: Bk[i_][:CH, :].rearrange("p (i c) -> p i c", c=CH)

                    def prep(nb, pb, d=d, Gd=Gd, Bd=Bd):
                        n0 = nb * BLK
                        tsl = slice(n0 * CH, (n0 + BLK) * CH)
                        gb = lambda t_: t_[:, n0:n0 + BLK].unsqueeze(2).to_broadcast([CH, BLK, CH])
                        qd_, at_, kdt_, u8_ = qd[pb], attnT[pb], kdt[pb], u8[pb]
                        S.op("dve", "tensor_tensor", reads=[("gt", d), ("mtri", d)], writes=["Pa"], out=t8["Pa"][:],
                             in0=mt_b, in1=gb(Gd), op=ALU.mult)
                        S.op("dve", "tensor_tensor", reads=[("bt", d), "ident"], writes=["Pb"], out=t8["Pb"][:], in0=id_b,
                             in1=gb(Bd), op=ALU.mult)
                        yield
                        S.mm(Bk[3][:], ones[:], t8f["Pa"][:].rearrange("p i c -> p (i c)"), True, True, ["ones", "Pa"], [bk(3)])
                        S.mm(Bk[4][:CH, :], ones[:CH, :CH], flat(t8["Pb"]), True, True, ["ones", "Pb"], [bk(4)])
                        for i in range(BLK):
                            csl = slice((n0 + i) * CH, (n0 + i + 1) * CH)
                            S.mm(Bk[5][:CH, i * CH:(i + 1) * CH], kT[:, csl], kT[:, csl], True, True, ["kT"], [bk(5)])
                            S.mm(Bk[6][:CH, i * CH:(i + 1) * CH], kT[:, csl], qT[:, csl], True, True, ["kT", "qT"], [bk(6)])
                        yield
                        S.op("act", "activation", reads=[bk(3)], writes=["egr"], out=egr[:], in_=Bk[3][:], func=AF.Exp)
                        S.op("dve", "tensor_tensor", reads=[bk(3), "gc"], writes=["dT"], out=t8["dT"][:], in0=k3(3), in1=gb(gc),
                             op=ALU.subtract)
                        S.op("dve", "tensor_scalar", reads=["dT"], writes=["dT"], out=t8["dT"][:], in0=t8["dT"][:], scalar1=0.0,
                             scalar2=None, op0=ALU.min)
                        yield
                        S.op("act", "activation", reads=["dT"], writes=["dT"], out=t8["dT"][:], in_=t8["dT"][:], func=AF.Exp)
                        S.op("dve", "tensor_tensor", reads=["egr", "qT"], writes=[("qd", pb)], out=qd_[:], in0=qT[:, tsl],
                             in1=egr[:], op=ALU.mult)
                        yield
                        S.op("dve", "tensor_tensor", reads=["dT", ("mstr", d)], writes=["W1"], out=t8["W1"][:], in0=t8["dT"][:],
                             in1=ms_b, op=ALU.mult)
                        S.op("dve", "tensor_tensor", reads=["W1", bk(4)], writes=["W1"], out=t8["W1"][:], in0=t8["W1"][:],
                             in1=k3(4), op=ALU.mult)
                        S.op("dve", "tensor_tensor", reads=["dT", ("mtri", d)], writes=["dT"], out=t8["dT"][:], in0=t8["dT"][:],
                             in1=mt_b, op=ALU.mult)
                        S.op("dve", "tensor_tensor", reads=[bk(5), "W1"], writes=["LT"], out=t8["LT"][:], in0=k3(5),
                             in1=t8["W1"][:], op=ALU.mult)
                        S.op("dve", "tensor_tensor", reads=[bk(6), "dT"], writes=[("attnT", pb)], out=at_[:CH], in0=k3(6),
                             in1=t8["dT"][:], op=ALU.mult)
                        yield
                        for i in range(BLK):
                            S.op("pe", "transpose", reads=["LT", "ident"], writes=[bk(7)], out=Bk[7][:CH, i * CH:(i + 1) * CH],
                                 in_=t8["LT"][:, i, :], identity=ident[:CH, :CH])
                        S.op("dve", "tensor_tensor", reads=["LT", "ident"], writes=["X8"], out=t8["X8"][:], in0=id_b,
                             in1=t8["LT"][:], op=ALU.subtract)
                        yield
                        S.op("act", "activation", reads=[bk(7)], writes=["Lm"], out=flat(t8["Lm"]), in_=Bk[7][:CH, :],
                             func=AF.Copy)
                        yield
                        A_, At_, ak, atk = t8["LT"], t8["Lm"], "LT", "Lm"
                        for lev in range(5):
                            P_, Pt_ = (t8["Pa"], t8["Pta"]) if lev % 2 == 0 else (t8["Pb"], t8["Ptb"])
                            pk, ptk = ("Pa", "Pta") if lev % 2 == 0 else ("Pb", "Ptb")
                            for i in range(BLK):
                                cs = slice(i * CH, (i + 1) * CH)
                                S.mm(Bk[4][:CH, cs], A_[:, i, :], At_[:, i, :], True, True, [ak, atk], [bk(4)])
                                if lev < 4:
                                    S.mm(Bk[3][:CH, cs], At_[:, i, :], A_[:, i, :], True, True, [ak, atk], [bk(3)])
                            yield
                            S.op("act", "activation", reads=[bk(4)], writes=[ptk], out=flat(Pt_), in_=Bk[4][:CH, :], func=AF.Copy)
                            if lev < 4:
                                S.op("act", "activation", reads=[bk(3)], writes=[pk], out=flat(P_), in_=Bk[3][:CH, :], func=AF.Copy)
                            yield
                            for i in range(BLK):
                                cs = slice(i * CH, (i + 1) * CH)
                                S.mm(Bk[5][:CH, cs], Pt_[:, i, :], t8["X8"][:, i, :], True, True, [ptk, "X8"], [bk(5)])
                            yield
                            S.op("dve", "tensor_tensor", reads=[bk(5), "X8"], writes=["X8"], out=flat(t8["X8"]),
                                 in0=Bk[5][:CH, :], in1=flat(t8["X8"]), op=ALU.add)
                            A_, At_, ak, atk = P_, Pt_, pk, ptk
                        yield
                        for i in range(BLK):
                            csl = slice((n0 + i) * CH, (n0 + i + 1) * CH)
                            bkk, bkv = (6, 3) if i < 4 else (7, 4)
                            o_ = (i % 4) * 128
                            S.op("pe", "transpose", reads=["kT", "ident"], writes=[bk(bkk)], out=Bk[bkk][:CH, o_:o_ + 128],
                                 in_=kT[:, csl], identity=ident[:])
                            S.op("pe", "transpose", reads=["vT", "ident"], writes=[bk(bkv)], out=Bk[bkv][:CH, o_:o_ + 128],
                                 in_=vT[:, csl], identity=ident[:])
                        yield
                        for hh in range(2):
                            isl = slice(hh * 4, hh * 4 + 4)
                            sc_b = lambda t_: t_[:, n0 + hh * 4:n0 + hh * 4 + 4].unsqueeze(2).to_broadcast([CH, 4, 128])
                            kps = Bk[6 + hh][:CH, :].rearrange("p (i e) -> p i e", e=128)
                            vps = Bk[3 + hh][:CH, :].rearrange("p (i e) -> p i e", e=128)
                            S.op("dve", "tensor_tensor", reads=[bk(6 + hh), "kd"], writes=[("kdt", pb, hh)],
                                 out=kdt_[:CH, isl, :], in0=kps, in1=sc_b(kd), op=ALU.mult)
                            S.op("dve", "tensor_tensor", reads=[bk(6 + hh), "bg"], writes=[("kbt", hh)], out=kbt[:CH, isl, :],
                                 in0=kps, in1=sc_b(bg), op=ALU.mult)
                            S.op("dve", "tensor_tensor", reads=[bk(3 + hh), ("bt", d)], writes=[("vbt", hh)],
                                 out=vbt[:, isl, :], in0=vps, in1=sc_b(Bd), op=ALU.mult)
                        yield
                        MT_ = MT8[pb]
                        for i in range(BLK):
                            hh = i // 4
                            o_ = (i % 4) * 128
                            S.mm(Bk[5 + hh][:CH, o_:o_ + 128], t8["X8"][:, i, :], vbt[:, i, :], True, True,
                                 ["X8", ("vbt", hh)], [bk(5 + hh)])
                            S.mm(Bk[3 + hh][:CH, o_:o_ + 128], t8["X8"][:, i, :], kbt[:CH, i, :], True, True,
                                 ["X8", ("kbt", hh)], [bk(3 + hh)])
                        yield
                        for hh in range(2):
                            S.op("act", "activation", reads=[bk(5 + hh)], writes=[("u8", pb, hh)],
                                 out=u8_[:CH, hh * 4:hh * 4 + 4, :].rearrange("p i e -> p (i e)"), in_=Bk[5 + hh][:CH, :],
                                 func=AF.Copy)
                            S.op("act", "activation", reads=[bk(3 + hh)], writes=[("w8", hh)],
                                 out=w8[:CH, hh * 4:hh * 4 + 4, :].rearrange("p i e -> p (i e)"), in_=Bk[3 + hh][:CH, :],
                                 func=AF.Copy)
                        yield
                        for i in range(BLK):
                            hh = i // 4
                            o_ = (i % 4) * 128
                            S.mm(Bk[5 + hh][:, o_:o_ + 128], w8[:, i, :], kdt_[:, i, :], True, True,
                                 [("w8", hh), ("kdt", pb, hh)], [bk(5 + hh)])
                            S.mm(Bk[7][:, i * CH:(i + 1) * CH], w8[:, i, :], at_[:, i, :], True, True,
                                 [("w8", hh), ("attnT", pb)], [bk(7)])
                        yield
                        for hh in range(2):
                            S.op("act", "mul", reads=[bk(5 + hh)], writes=[("mtmp", hh)],
                                 out=mtmp[:, hh * 4:hh * 4 + 4, :].rearrange("p i e -> p (i e)"), in_=Bk[5 + hh][:], mul=-1.0)
                        S.op("dve", "tensor_tensor", reads=[bk(7), ("qd", pb)], writes=[("qd", pb)], out=qd_[:], in0=qd_[:],
                             in1=Bk[7][:], op=ALU.subtract)
                        yield
                        for i in range(BLK):
                            n = n0 + i
                            S.op("dve", "scalar_tensor_tensor", reads=[("mtmp", i // 4), "ident", "egt"], writes=[("MT", pb, i)],
                                 out=MT_[:, i, :], in0=ident[:], scalar=egt[:, n:n + 1], in1=mtmp[:, i, :], op0=ALU.mult,
                                 op1=ALU.add)
                            if i % 4 == 3:
                                yield

                    def scan(nb, pb, bi_, d=d, b=b):
                        n0 = nb * BLK
                        qd_, at_, kdt_, u8_, MT_ = qd[pb], attnT[pb], kdt[pb], u8[pb], MT8[pb]
                        ob, obk = o8[bi_ % 2], ("o8", bi_ % 2)
                        order = list(range(BLK)) if d == 0 else list(range(BLK))[::-1]
                        for i in order:
                            hh = i // 4
                            cur = sidx[0]
                            Sc, sk = Sst[cur], ("S", cur)
                            Sn, snk = Sst[1 - cur], ("S", 1 - cur)
                            sidx[0] = 1 - cur
                            S.mm(Bk[0][:, 0:128], MT_[:, i, :], Sc[:], True, False, [("MT", pb, i), sk], [bk(0)])
                            S.mm(Bk[0][:, 0:128], kdt_[:, i, :], u8_[:, i, :], False, True, [("kdt", pb, hh), ("u8", pb, hh)], [bk(0)])
                            S.mm(Bk[1][:CH, 0:128], qd_[:, i * CH:(i + 1) * CH], Sc[:], True, False, [("qd", pb), sk], [bk(1)])
                            S.mm(Bk[1][:CH, 0:128], at_[:, i, :], u8_[:, i, :], False, True, [("attnT", pb), ("u8", pb, hh)], [bk(1)])
                            yield
                            S.op("act", "activation", reads=[bk(0)], writes=[snk], out=Sn[:], in_=Bk[0][:, 0:128], func=AF.Copy)
                            S.op("dve", "tensor_copy", reads=[bk(1)], writes=[obk], out=ob[:, i, :], in_=Bk[1][:CH, 0:128])
                            for _ in range(SCAN_GAP):
                                yield
                        S.dma("sp", reads=[obk], writes=[("s_o", d, b, nb)], out=s_o[d, :, b, n0:n0 + BLK, :], in_=ob[:])

                    for _ in prep(blocks[0], 0):
                        pass
                    for bi_, nb in enumerate(blocks):
                        gens = [scan(nb, bi_ % 2, bi_)]
                        if bi_ + 1 < len(blocks):
                            gens.append(prep(blocks[bi_ + 1], (bi_ + 1) % 2))
                        interleave(gens)
        with C.scope():
            gnr = C.sb("gnr", [CH, 128], F32)
            S.dma("sp", writes=["gnr"], out=gnr[:], in_=gn)
            NB3 = 3
            of = [C.sb("of%d" % i, [CH, BLK, 128], F32) for i in range(NB3)]
            obb = [C.sb("ob%d" % i, [CH, BLK, 128], F32) for i in range(NB3)]
            zz = [C.sb("zz%d" % i, [CH, BLK, 128], F32) for i in range(NB3)]
            sq8 = [C.sb("sq8%d" % i, [CH, BLK, 128], F32) for i in range(NB3)]
            ss = [C.sb("ss%d" % i, [CH, BLK], F32) for i in range(NB3)]

            def out_block(it, b, nb):
                p = it % NB3
                n0 = nb * BLK
                S.dma("sp", reads=[("s_o", 0, b, nb)], writes=[("of", p)], out=of[p][:], in_=s_o[0, :, b, n0:n0 + BLK, :])
                S.dma("sp", reads=[("s_o", 1, b, nb)], writes=[("ob", p)], out=obb[p][:], in_=s_o[1, :, b, n0:n0 + BLK, :])
                S.dma("sp", writes=[("zz", p)], out=zz[p][:], in_=ztok[:, b, n0:n0 + BLK, :])
                yield
                S.op("dve", "tensor_tensor", reads=[("of", p), ("ob", p)], writes=[("of", p)], out=of[p][:], in0=of[p][:],
                     in1=obb[p][:], op=ALU.add)
                S.op("act", "activation", reads=[("zz", p)], writes=[("zz", p)], out=zz[p][:], in_=zz[p][:], func=AF.Silu)
                yield
                S.op("act", "activation", reads=[("of", p)], writes=[("sq8", p)], out=sq8[p][:], in_=of[p][:],
                     func=AF.Square)
                yield
                S.op("dve", "tensor_reduce", reads=[("sq8", p)], writes=[("ss", p)], out=ss[p][:], in_=sq8[p][:],
                     axis=AX.X, op=ALU.add)
                S.op("dve", "tensor_scalar", reads=[("ss", p)], writes=[("ss", p)], out=ss[p][:], in0=ss[p][:],
                     scalar1=1.0 / 128.0, scalar2=EPS, op0=ALU.mult, op1=ALU.add)
                yield
                S.op("act", "activation", reads=[("ss", p)], writes=[("ss", p)], out=ss[p][:], in_=ss[p][:], func=AF.Sqrt)
                yield
                S.op("dve", "reciprocal", reads=[("ss", p)], writes=[("ss", p)], out=ss[p][:], in_=ss[p][:])
                S.op("dve", "tensor_tensor", reads=[("of", p), ("ss", p)], writes=[("of", p)], out=of[p][:], in0=of[p][:],
                     in1=ss[p][:].unsqueeze(2).to_broadcast([CH, BLK, 128]), op=ALU.mult)
                S.op("dve", "tensor_tensor", reads=[("of", p), "gnr"], writes=[("of", p)], out=of[p][:], in0=of[p][:],
                     in1=gnr[:].unsqueeze(1).to_broadcast([CH, BLK, 128]), op=ALU.mult)
                S.op("dve", "tensor_tensor", reads=[("of", p), ("zz", p)], writes=[("of", p)], out=of[p][:], in0=of[p][:],
                     in1=zz[p][:], op=ALU.mult)
                S.dma("sp", reads=[("of", p)], is_out=True, out=ytok[:, b, n0:n0 + BLK, :], in_=of[p][:])
                yield

            pipeline((out_block(b * (N // BLK) + nb, b, nb) for b in range(BATCH) for nb in range(N // BLK)), NB3)
        S.replay()
    return nc


_PROGS = {}


def _prog(key, fn):
    if key not in _PROGS:
        _PROGS[key] = fn()
    return _PROGS[key]


def _run(nc, in_maps):
    res = run_bass_kernel_spmd(nc, in_maps, core_ids=list(range(NCORES)))
    return res.results


def _c(a):
    return np.ascontiguousarray(a, dtype=np.float32)


def _tok_shards_T(xf):
    return [_c(xf[c * TPC:(c + 1) * TPC].T) for c in range(NCORES)]


def _from_T(outs, name):
    return np.concatenate([r[name].T for r in outs], 0)


def _run_sc_layer(xf, p, li, j, final):
    nc = _prog(("sc", final), lambda: build_sc_prog(TPC, final))
    x3 = xf.reshape(BATCH, SEQ, D)
    zero = np.zeros((1, D), np.float32)
    in_maps = []
    for c in range(NCORES):
        b, s0 = divmod(c * TPC, SEQ)
        left = x3[b, s0 - 1:s0] if s0 > 0 else zero
        right = x3[b, s0 + TPC:s0 + TPC + 1] if s0 + TPC < SEQ else zero
        xs = np.concatenate([x3[b, s0:s0 + TPC], left, right], 0)
        in_maps.append({"xT": _c(xs.T), "nrm": _c(p["norms"][li]), "fw_in": _c(p["ffn_w_in"][li]),
                        "fw_out": _c(p["ffn_w_out"][li]), "w_in": _c(p["sc_w_in"][j]), "conv": _c(p["sc_conv"][j]),
                        "w_out": _c(p["sc_w_out"][j]), "gfin": _c(p["final_norm"])})
    return _from_T(_run(nc, in_maps), "yT")


def _run_pre(xf, p, li, w_in, b_in):
    nout = w_in.shape[1]
    nc = _prog(("pre", nout), lambda: build_pre_prog(nout, TPC))
    xs = _tok_shards_T(xf)
    in_maps = [{"xT": xs[c], "nrm": _c(p["norms"][li, 0:2]), "fw_in": _c(p["ffn_w_in"][li, 0]),
                "fw_out": _c(p["ffn_w_out"][li, 0]), "w_in": _c(w_in), "b_in": _c(b_in)} for c in range(NCORES)]
    outs = _run(nc, in_maps)
    x1 = _from_T(outs, "xo")
    u = np.concatenate([r["uT"] for r in outs], 1)
    return x1, u


def _run_post(xf, yfm, p, li, w_out, b_out):
    nc = _prog(("post",), lambda: build_post_prog(TPC))
    xs = _tok_shards_T(xf)
    in_maps = [{"xT": xs[c], "yT": _c(yfm[:, c * TPC:(c + 1) * TPC]), "w_out": _c(w_out), "b_out": _c(b_out),
                "nrm": _c(p["norms"][li, 2]), "fw_in": _c(p["ffn_w_in"][li, 1]), "fw_out": _c(p["ffn_w_out"][li, 1])}
               for c in range(NCORES)]
    return _from_T(_run(nc, in_maps), "xo")


def _run_hyena_core(u, p, j):
    nc = _prog(("hy",), build_hy_core_prog)
    cst, zT, trow, nad = hyena_consts()
    trow_rep = _c(np.broadcast_to(trow, (128, NFFT)))
    u4 = u.reshape(3, D, BATCH, SEQ)
    hc = p["hy_conv"][j].reshape(3, 3, D)
    cb = p["hy_conv_b"][j].reshape(3, D)
    w3 = p["hy_f_w3"][j].reshape(HY_ORD, 2, D)
    in_maps = []
    for c in range(NCORES):
        sl = slice(c * 128, (c + 1) * 128)
        in_maps.append({"u0": _c(u4[:, sl]), "convw": _c(hc[:, :, sl]), "convb": _c(cb[:, sl]), "dvec": _c(p["hy_d"][j][sl]),
                        "fw1": _c(p["hy_f_w1"][j]), "fb1": _c(p["hy_f_b1"][j]), "fw2": _c(p["hy_f_w2"][j]),
                        "fb2": _c(p["hy_f_b2"][j]), "fw3": _c(w3[:, :, sl]), "freq": _c(p["hy_f_freq"][j]), "cst": cst,
                        "zT": zT, "trow": trow_rep, "nad": _c(nad[sl])})
    outs = _run(nc, in_maps)
    return np.concatenate([r["yT"].reshape(128, BATCH * SEQ) for r in outs], 0)


def _run_gdn_core(u, p, j):
    nc = _prog(("gd",), build_gd_core_prog)
    mtri, mstrict, ident = gdn_consts()
    H = 8
    gcv = p["gd_conv"][j].reshape(3, 3, D)
    in_maps = []
    for h in range(NCORES):
        sl = slice(h * 128, (h + 1) * 128)
        qkv0 = u[0:3 * D].reshape(3, D, BATCH, SEQ)[:, sl]
        zfm = u[3 * D + h * 128: 3 * D + (h + 1) * 128]
        ztok = zfm.T.reshape(BATCH, NCH, CH, 128).transpose(2, 0, 1, 3)
        rows = [4 * D + 0 * H + h, 4 * D + 1 * H + h, 4 * D + 2 * H + 0 * H + h, 4 * D + 2 * H + 1 * H + h]
        abt = u[rows].reshape(4, BATCH, NCH, CH).transpose(0, 3, 1, 2)
        in_maps.append({"qkv0": _c(qkv0), "ztok": _c(ztok), "abt": _c(abt), "convw": _c(gcv[:, :, sl]),
                        "alog": _c(np.broadcast_to(p["gd_a_log"][j][:, h], (CH, 2))),
                        "dtb": _c(np.broadcast_to(p["gd_dt_bias"][j][:, h], (CH, 2))),
                        "gn": _c(np.broadcast_to(p["gd_norm"][j], (CH, 128))), "mtri": mtri, "mstrict": mstrict,
                        "ident": ident})
    outs = _run(nc, in_maps)
    return np.concatenate([r["ytok"].transpose(3, 1, 2, 0).reshape(128, BATCH * SEQ) for r in outs], 0)


def kernel(**inputs):
    p = {k: np.asarray(v, dtype=np.float32) for k, v in inputs.items()}
    xf = p["x"].reshape(BATCH * SEQ, D)
    xf = _run_sc_layer(xf, p, 0, 0, final=False)
    xf, u = _run_pre(xf, p, 1, p["hy_w_in"][0], p["hy_b_in"][0])
    yfm = _run_hyena_core(u, p, 0)
    xf = _run_post(xf, yfm, p, 1, p["hy_w_out"][0], p["hy_b_out"][0])
    nproj = p["gd_w_in"].shape[2]
    npad = ((nproj + 127) // 128) * 128
    w_in = np.zeros((D, npad), np.float32)
    w_in[:, :nproj] = p["gd_w_in"][0]
    xf, u = _run_pre(xf, p, 2, w_in, np.zeros((npad,), np.float32))
    yfm = _run_gdn_core(u, p, 0)
    xf = _run_post(xf, yfm, p, 2, p["gd_w_out"][0], np.zeros((D,), np.float32))
    xf = _run_sc_layer(xf, p, 3, 1, final=True)
    return np.ascontiguousarray(xf.reshape(BATCH, SEQ, D).astype(np.float32))
```

```python
import contextlib
import math
import numpy as np
import concourse.bass as bass
import concourse.mybir as mybir
from concourse.bass_utils import run_bass_kernel_spmd

F32 = mybir.dt.float32
BF16 = mybir.dt.bfloat16
AF = mybir.ActivationFunctionType
ALU = mybir.AluOpType
AX = mybir.AxisListType

D = 1024
KC = 8
FF = 2816
JC = 22
NCORES = 8
BATCH = 2
SEQ = 8192
TPC = BATCH * SEQ // NCORES
EPS = 1e-6

ENGS = ("pe", "act", "dve", "pool", "sp")
NDMA_SEM = 20


class Sched:
    def __init__(self, nc, es):
        self.nc = nc
        self.q = {e: [] for e in ENGS}
        self.cnt = {e: 0 for e in ENGS}
        self.seen = {e: {} for e in ENGS}
        self.buf = {}
        self.sems = {}
        for e in ENGS:
            self.sems[("E", e)] = es.enter_context(nc.semaphore("sem_" + e))
        self.dma_rr = {e: 0 for e in ENGS}
        self.dma_val = {}
        for e in ("sp", "pool", "act"):
            for i in range(NDMA_SEM):
                k = ("D", e, i)
                self.sems[k] = es.enter_context(nc.semaphore("dsem_%s_%d" % (e, i)))
                self.dma_val[k] = 0
        self.out_tokens = []
        self.excl = set()

    def _deps(self, eng, reads, writes):
        deps = {}

        def add(tok):
            if tok is None:
                return
            k, v = tok
            if deps.get(k, 0) < v:
                deps[k] = v

        for k in reads:
            b = self.buf.get(k)
            if b:
                add(b["w"])
                if k in self.excl:
                    for rk, rv in b["r"].items():
                        if rk != ("E", eng):
                            add((rk, rv))
        for k in writes:
            b = self.buf.get(k)
            if b:
                add(b["w"])
                for rk, rv in b["r"].items():
                    add((rk, rv))
        waits = []
        for k, v in deps.items():
            if eng == "pe" and k == ("E", "pe"):
                continue
            if self.seen[eng].get(k, 0) >= v:
                continue
            self.seen[eng][k] = v
            waits.append((k, v))
        return waits

    def _record(self, tok, reads, writes):
        for k in reads:
            b = self.buf.setdefault(k, {"w": None, "r": {}})
            if b["r"].get(tok[0], 0) < tok[1]:
                b["r"][tok[0]] = tok[1]
        for k in writes:
            self.buf[k] = {"w": tok, "r": {}}

    def op(self, eng, name, reads=(), writes=(), **kw):
        fn = (name, kw)
        waits = self._deps(eng, reads, writes)
        self.cnt[eng] += 1
        tok = (("E", eng), self.cnt[eng])
        self.q[eng].append((waits, fn, tok, 1))
        self._record(tok, reads, writes)
        return tok

    def dma(self, eng, reads=(), writes=(), is_out=False, **kw):
        fn = ("dma_start", kw)
        waits = self._deps(eng, reads, writes)
        i = self.dma_rr[eng]
        self.dma_rr[eng] = (i + 1) % NDMA_SEM
        k = ("D", eng, i)
        prev = self.dma_val[k]
        if prev and self.seen[eng].get(k, 0) < prev:
            self.seen[eng][k] = prev
            waits.append((k, prev))
        self.dma_val[k] = prev + 16
        tok = (k, prev + 16)
        self.q[eng].append((waits, fn, tok, 16))
        self._record(tok, reads, writes)
        if is_out:
            self.out_tokens.append(tok)
        return tok

    def barrier(self):
        allv = [(("E", f), self.cnt[f]) for f in ENGS if self.cnt[f]]
        allv += [(k, v) for k, v in self.dma_val.items() if v]
        for e in ENGS:
            waits = []
            for k, v in allv:
                if k == ("E", e) and e == "pe":
                    continue
                if self.seen[e].get(k, 0) >= v:
                    continue
                self.seen[e][k] = v
                waits.append((k, v))
            if waits:
                self.q[e].append((waits, None, None, 0))
        self.buf = {}

    def mm(self, out, lhsT, rhs, start, stop, reads, writes):
        return self.op("pe", "matmul", reads, writes, out=out, lhsT=lhsT, rhs=rhs, start=start, stop=stop)

    def replay(self):
        nc = self.nc
        fin = list(self.out_tokens)
        with nc.Block() as block:
            def run(engname, eng):
                for waits, fn, tok, inc in self.q[engname]:
                    for k, v in waits:
                        eng.wait_ge(self.sems[k], v)
                    if fn is None:
                        continue
                    ins = getattr(eng, fn[0])(**fn[1])
                    ins.then_inc(self.sems[tok[0]], inc)
                if engname == "sp":
                    for k, v in fin:
                        eng.wait_ge(self.sems[k], v)

            @block.tensor
            def _(e):
                run("pe", e)

            @block.scalar
            def _(e):
                run("act", e)

            @block.vector
            def _(e):
                run("dve", e)

            @block.gpsimd
            def _(e):
                run("pool", e)

            @block.sync
            def _(e):
                run("sp", e)


class Ctx:
    def __init__(self, nc, es):
        self.nc = nc
        self.es = es
        self.S = Sched(nc, es)
        self.n = 0
        self.scopes = [es]

    def sb(self, name, shape, dt):
        self.n += 1
        return self.scopes[-1].enter_context(self.nc.sbuf_tensor("%s_%d" % (name, self.n), shape, dt))

    def ps(self, name, shape, dt=F32):
        self.n += 1
        return self.scopes[-1].enter_context(self.nc.psum_tensor("%s_%d" % (name, self.n), shape, dt))

    @contextlib.contextmanager
    def scope(self):
        with contextlib.ExitStack() as s:
            self.scopes.append(s)
            try:
                yield
            finally:
                self.S.barrier()
                self.scopes.pop()


def interleave(gens):
    gens = list(gens)
    while gens:
        for g_ in list(gens):
            try:
                next(g_)
            except StopIteration:
                gens.remove(g_)


def pipeline(gens, width):
    gens = iter(gens)
    active = []
    done = False
    while True:
        while not done and len(active) < width:
            try:
                active.append(next(gens))
            except StopIteration:
                done = True
        if not active:
            return
        for g_ in list(active):
            try:
                next(g_)
            except StopIteration:
                active.remove(g_)


def dram_in(nc, name, shape, dt=F32):
    return nc.dram_tensor(name, list(shape), dt, kind="ExternalInput").ap()


def dram_out(nc, name, shape, dt=F32):
    return nc.dram_tensor(name, list(shape), dt, kind="ExternalOutput").ap()


def emit_consts(C):
    ones = C.sb("ones", [128, 128], F32)
    C.S.op("dve", "memset", writes=["ones"], ap=ones[:], constant=1.0)
    C.ones = ones
    C.rn_sq = [C.sb("rn_sq%d" % i, [128, 512], F32) for i in range(3)]
    C.rn_rs = [C.sb("rn_rs%d" % i, [128, 512], F32) for i in range(2)]
    C.rn_ps = [C.ps("rn_ps%d" % i, [128, 512], F32) for i in range(1)]
    C.rn_i = 0


def load_vec_pk(C, name, vec_dram, nchunk, eng="sp"):
    t = C.sb(name, [128, nchunk], F32)
    C.S.dma(eng, writes=[name], out=t[:], in_=vec_dram.rearrange("(kc p) -> p kc", p=128),
            allow_slow_non_contiguous=True)
    return t


def emit_rmsnorm(C, x, xk, t0, ntok, g_sb, gk, hn, hk, hoff=0):
    S = C.S
    nt = (ntok + 511) // 512
    for tt in range(nt):
        n = min(512, ntok - tt * 512)
        c0 = t0 + tt * 512
        xkeys = [(xk, k, c0 // 512) for k in range(KC)]
        if (c0 % 512) + n > 512:
            xkeys += [(xk, k, c0 // 512 + 1) for k in range(KC)]
        ps = C.rn_ps[0]
        for k in range(KC):
            C.rn_i += 1
            sq = C.rn_sq[C.rn_i % 3]
            sqk = ("rn_sq", C.rn_i % 3)
            S.op("act", "activation", reads=[kk for kk in xkeys if kk[1] == k], writes=[sqk],
                 out=sq[:, :n], in_=x[:, k, c0:c0 + n], func=AF.Square)
            S.mm(ps[:, :n], C.ones[:], sq[:, :n], k == 0, k == KC - 1, ["ones", sqk], ["rn_ps"])
        C.rn_i += 1
        rs = C.rn_rs[C.rn_i % 2]
        rsk = ("rn_rs", C.rn_i % 2)
        S.op("dve", "tensor_scalar", reads=["rn_ps"], writes=[rsk], out=rs[:, :n], in0=ps[:, :n],
             scalar1=1.0 / D, scalar2=EPS, op0=ALU.mult, op1=ALU.add)
        S.op("act", "activation", reads=[rsk], writes=[rsk], out=rs[:, :n], in_=rs[:, :n], func=AF.Sqrt)
        S.op("dve", "reciprocal", reads=[rsk], writes=[rsk], out=rs[:, :n], in_=rs[:, :n])
        for k in range(KC):
            eng = "dve"
            S.op(eng, "scalar_tensor_tensor", reads=[kk for kk in xkeys if kk[1] == k] + [rsk, gk],
                 writes=[(hk, k, tt)], out=hn[:, k, hoff + tt * 512: hoff + tt * 512 + n], in0=x[:, k, c0:c0 + n],
                 scalar=g_sb[:, k:k + 1], in1=rs[:, :n], op0=ALU.mult, op1=ALU.mult)


def emit_ffn(C, x, groups, g_dram, w_in, w_out, pref):
    S = C.S
    TG = 1024
    w_in_v = w_in.rearrange("(kc p) n -> p kc n", p=128)
    w_out_v = w_out.rearrange("(jc p) n -> p jc n", p=128)
    with C.scope():
        g_sb = load_vec_pk(C, pref + "g", g_dram, KC)
        gk = pref + "g"
        hns = [C.sb("ffn_hn%d" % i, [128, KC, TG], BF16) for i in range(min(2, len(groups)))]
        act = C.sb("ffn_act", [128, JC, TG], BF16)
        wbuf = [C.sb("ffn_wi%d" % i, [128, KC, 256], BF16) for i in range(3)]
        wobuf = [C.sb("ffn_wo%d" % i, [128, JC, 128], BF16) for i in range(2)]
        sgb = [C.sb("ffn_sg%d" % i, [128, 512], F32) for i in range(2)]
        pg = [C.ps("ffn_pg%d" % i, [128, 512]) for i in range(2)]
        pu = [C.ps("ffn_pu%d" % i, [128, 512]) for i in range(2)]
        po = [C.ps("ffn_po%d" % i, [128, 512]) for i in range(2)]
        it = 0
        io = 0
        for gi_, (t0, ntok) in enumerate(groups):
            ntg = (ntok + 511) // 512
            hn = hns[gi_ % 2]
            hnk = "ffn_hn%d" % (gi_ % 2)
            if gi_ == 0:
                emit_rmsnorm(C, x, "x", t0, ntok, g_sb, gk, hn, hnk)
            for j in range(JC):
                wb = wbuf[j % 3]
                wk = ("ffn_wi", j % 3)
                S.dma("pool", writes=[wk + (0,)], out=wb[:, :, 0:128], in_=w_in_v[:, :, j * 128:(j + 1) * 128])
                S.dma("pool", writes=[wk + (1,)], out=wb[:, :, 128:256],
                      in_=w_in_v[:, :, FF + j * 128: FF + (j + 1) * 128])
                for tt in range(ntg):
                    n = min(512, ntok - tt * 512)
                    it += 1
                    b = it % 2
                    sl = slice(tt * 512, tt * 512 + n)
                    for k in range(KC):
                        S.mm(pg[b][:, :n], wb[:, k, 0:128], hn[:, k, sl], k == 0, k == KC - 1,
                             [wk + (0,), (hnk, k, tt)], [("ffn_pg", b)])
                    for k in range(KC):
                        S.mm(pu[b][:, :n], wb[:, k, 128:256], hn[:, k, sl], k == 0, k == KC - 1,
                             [wk + (1,), (hnk, k, tt)], [("ffn_pu", b)])
                    S.op("act", "activation", reads=[("ffn_pg", b)], writes=[("ffn_sg", b)],
                         out=sgb[b][:, :n], in_=pg[b][:, :n], func=AF.Silu)
                    S.op("dve", "tensor_tensor", reads=[("ffn_pu", b), ("ffn_sg", b)], writes=[("ffn_act", j, tt)],
                         out=act[:, j, sl], in0=pu[b][:, :n], in1=sgb[b][:, :n], op=ALU.mult)
            if gi_ + 1 < len(groups):
                t1, n1 = groups[gi_ + 1]
                emit_rmsnorm(C, x, "x", t1, n1, g_sb, gk, hns[(gi_ + 1) % 2], "ffn_hn%d" % ((gi_ + 1) % 2))
            for m in range(KC):
                wo = wobuf[m % 2]
                wok = ("ffn_wo", m % 2)
                S.dma("pool", writes=[wok], out=wo[:], in_=w_out_v[:, :, m * 128:(m + 1) * 128])
                for tt in range(ntg):
                    n = min(512, ntok - tt * 512)
                    io += 1
                    b = io % 2
                    sl = slice(tt * 512, tt * 512 + n)
                    gsl = slice(t0 + tt * 512, t0 + tt * 512 + n)
                    for j in range(JC):
                        S.mm(po[b][:, :n], wo[:, j, :], act[:, j, sl], j == 0, j == JC - 1,
                             [wok, ("ffn_act", j, tt)], [("ffn_po", b)])
                    xkey = ("x", m, (t0 + tt * 512) // 512)
                    S.op("dve", "scalar_tensor_tensor", reads=[("ffn_po", b), xkey], writes=[xkey],
                         out=x[:, m, gsl], in0=po[b][:, :n], scalar=0.5, in1=x[:, m, gsl], op0=ALU.mult, op1=ALU.add)


def emit_sc_mixer(C, x, T, g_dram, w_in, conv, w_out):
    S = C.S
    NT = T // 512
    w_in_v = w_in.rearrange("(kc p) n -> p kc n", p=128)
    w_out_v = w_out.rearrange("(kc p) n -> p kc n", p=128)
    with C.scope():
        g_sb = load_vec_pk(C, "sc_g", g_dram, KC)
        cw = C.sb("sc_cw", [128, KC, 3], F32)
        for j in range(3):
            S.dma("sp", writes=[("sc_cw", j)], out=cw[:, :, j], in_=conv[j].rearrange("(i p) -> p i", p=128),
                  allow_slow_non_contiguous=True)
        hn = C.sb("sc_hn", [128, KC, T + 2], BF16)
        ybf = C.sb("sc_y", [128, KC, T], BF16)
        chb = [C.sb("sc_ch%d" % i, [128, T + 2], F32) for i in range(2)]
        bsv = [C.sb("sc_b%d" % i, [128, T], F32) for i in range(2)]
        csb = [C.sb("sc_c%d" % i, [128, 512], F32) for i in range(2)]
        acc = [C.sb("sc_acc%d" % i, [128, 512], F32) for i in range(2)]
        wbuf = [C.sb("sc_wi%d" % i, [128, KC, 384], BF16) for i in range(2)]
        wobuf = [C.sb("sc_wo%d" % i, [128, KC, 128], BF16) for i in range(2)]
        pb = [C.ps("sc_pb%d" % i, [128, 512]) for i in range(2)]
        pc = [C.ps("sc_pc%d" % i, [128, 512]) for i in range(2)]
        ph = [C.ps("sc_ph%d" % i, [128, 512]) for i in range(2)]
        po = [C.ps("sc_po%d" % i, [128, 512]) for i in range(1)]
        emit_rmsnorm(C, x, "x", 0, T + 2, g_sb, "sc_g", hn, "sc_hn")
        it = 0
        for i in range(KC):
            wb = wbuf[i % 2]
            wk = ("sc_wi", i % 2)
            for q in range(3):
                S.dma("pool", writes=[wk + (q,)], out=wb[:, :, q * 128:(q + 1) * 128],
                      in_=w_in_v[:, :, q * D + i * 128: q * D + (i + 1) * 128])
            ch = chb[i % 2]
            bs = bsv[i % 2]
            for tt in range(NT + 1):
                n = 512 if tt < NT else 2
                it += 1
                b = it % 2
                sl = slice(tt * 512, tt * 512 + n)
                hkeys = lambda k: [("sc_hn", k, tt)]
                if tt < NT:
                    for k in range(KC):
                        S.mm(pb[b][:, :n], wb[:, k, 0:128], hn[:, k, sl], k == 0, k == KC - 1,
                             [wk + (0,)] + hkeys(k), [("sc_pb", b)])
                for k in range(KC):
                    S.mm(pc[b][:, :n], wb[:, k, 128:256], hn[:, k, sl], k == 0, k == KC - 1,
                         [wk + (1,)] + hkeys(k), [("sc_pc", b)])
                for k in range(KC):
                    S.mm(ph[b][:, :n], wb[:, k, 256:384], hn[:, k, sl], k == 0, k == KC - 1,
                         [wk + (2,)] + hkeys(k), [("sc_ph", b)])
                S.op("act", "activation", reads=[("sc_pc", b)], writes=[("sc_c", b)],
                     out=csb[b][:, :n], in_=pc[b][:, :n], func=AF.Copy)
                if tt < NT:
                    S.op("dve", "tensor_tensor", reads=[("sc_c", b), ("sc_ph", b)], writes=[("sc_ch", i % 2, tt)],
                         out=ch[:, 1 + tt * 512: 1 + tt * 512 + n], in0=ph[b][:, :n], in1=csb[b][:, :n], op=ALU.mult)
                    S.op("act", "activation", reads=[("sc_pb", b)], writes=[("sc_b", i % 2, tt)],
                         out=bs[:, sl], in_=pb[b][:, :n], func=AF.Copy)
                else:
                    S.op("dve", "tensor_tensor", reads=[("sc_c", b), ("sc_ph", b)], writes=[("sc_ch", i % 2, "hl")],
                         out=ch[:, 0:1], in0=ph[b][:, 0:1], in1=csb[b][:, 0:1], op=ALU.mult)
                    S.op("dve", "tensor_tensor", reads=[("sc_c", b), ("sc_ph", b)], writes=[("sc_ch", i % 2, "hr")],
                         out=ch[:, T + 1:T + 2], in0=ph[b][:, 1:2], in1=csb[b][:, 1:2], op=ALU.mult)
            for tt in range(NT):
                a = acc[tt % 2]
                ak = ("sc_acc", tt % 2)
                rk = [("sc_ch", i % 2, tt)]
                if tt > 0:
                    rk.append(("sc_ch", i % 2, tt - 1))
                else:
                    rk.append(("sc_ch", i % 2, "hl"))
                if tt < NT - 1:
                    rk.append(("sc_ch", i % 2, tt + 1))
                else:
                    rk.append(("sc_ch", i % 2, "hr"))
                o = tt * 512
                S.op("dve", "tensor_scalar", reads=rk + [("sc_cw", 0)], writes=[ak], out=a[:], in0=ch[:, o:o + 512],
                     scalar1=cw[:, i, 0:1], scalar2=None, op0=ALU.mult)
                S.op("dve", "scalar_tensor_tensor", reads=rk + [("sc_cw", 1), ak], writes=[ak], out=a[:],
                     in0=ch[:, o + 1:o + 513], scalar=cw[:, i, 1:2], in1=a[:], op0=ALU.mult, op1=ALU.add)
                S.op("dve", "scalar_tensor_tensor", reads=rk + [("sc_cw", 2), ak], writes=[ak], out=a[:],
                     in0=ch[:, o + 2:o + 514], scalar=cw[:, i, 2:3], in1=a[:], op0=ALU.mult, op1=ALU.add)
                S.op("dve", "tensor_tensor", reads=[ak, ("sc_b", i % 2, tt)], writes=[("sc_y", i, tt)],
                     out=ybf[:, i, o:o + 512], in0=a[:], in1=bs[:, o:o + 512], op=ALU.mult)
        for m in range(KC):
            wo = wobuf[m % 2]
            wok = ("sc_wo", m % 2)
            S.dma("pool", writes=[wok], out=wo[:], in_=w_out_v[:, :, m * 128:(m + 1) * 128])
            for tt in range(NT):
                sl = slice(tt * 512, (tt + 1) * 512)
                for i in range(KC):
                    S.mm(po[0][:], wo[:, i, :], ybf[:, i, sl], i == 0, i == KC - 1, [wok, ("sc_y", i, tt)], ["sc_po"])
                xkey = ("x", m, tt)
                S.op("dve", "tensor_tensor", reads=["sc_po", xkey], writes=[xkey], out=x[:, m, sl], in0=po[0][:],
                     in1=x[:, m, sl], op=ALU.add)


def emit_final_norm(C, x, T, g_dram):
    with C.scope():
        g_sb = load_vec_pk(C, "fin_g", g_dram, KC)
        emit_rmsnorm(C, x, "x", 0, T, g_sb, "fin_g", x, "x")


def emit_load_x(C, x, xT_dram, T):
    v = xT_dram.rearrange("(kc p) t -> p kc t", p=128)
    for k in range(KC):
        for tt in range((T + 511) // 512):
            n = min(512, T - tt * 512)
            C.S.dma("sp", writes=[("x", k, tt)], out=x[:, k, tt * 512:tt * 512 + n],
                    in_=v[:, k, tt * 512:tt * 512 + n])


def emit_store_x(C, x, yT_dram, T):
    v = yT_dram.rearrange("(kc p) t -> p kc t", p=128)
    for k in range(KC):
        for tt in range(T // 512):
            C.S.dma("sp", reads=[("x", k, tt)], is_out=True, out=v[:, k, tt * 512:(tt + 1) * 512],
                    in_=x[:, k, tt * 512:(tt + 1) * 512])


def build_ffn_prog(T=TPC):
    nc = bass.Bass("TRN2", target_bir_lowering=False)
    xT = dram_in(nc, "xT", [D, T])
    g = dram_in(nc, "g", [D])
    w_in = dram_in(nc, "w_in", [D, 2 * FF])
    w_out = dram_in(nc, "w_out", [FF, D])
    yT = dram_out(nc, "yT", [D, T])
    with contextlib.ExitStack() as es:
        C = Ctx(nc, es)
        emit_consts(C)
        x = C.sb("x", [128, KC, T], F32)
        emit_load_x(C, x, xT, T)
        emit_ffn(C, x, [(t, 1024) for t in range(0, T, 1024)], g, w_in, w_out, "f")
        emit_store_x(C, x, yT, T)
        C.S.replay()
    return nc


def build_sc_prog(T=TPC, final=False):
    nc = bass.Bass("TRN2", target_bir_lowering=False)
    xT = dram_in(nc, "xT", [D, T + 2])
    nrm = dram_in(nc, "nrm", [3, D])
    fw_in = dram_in(nc, "fw_in", [2, D, 2 * FF])
    fw_out = dram_in(nc, "fw_out", [2, FF, D])
    w_in = dram_in(nc, "w_in", [D, 3 * D])
    conv = dram_in(nc, "conv", [3, D])
    w_out = dram_in(nc, "w_out", [D, D])
    gfin = dram_in(nc, "gfin", [D])
    yT = dram_out(nc, "yT", [D, T])
    with contextlib.ExitStack() as es:
        C = Ctx(nc, es)
        emit_consts(C)
        x = C.sb("x", [128, KC, T + 2], F32)
        emit_load_x(C, x, xT, T + 2)
        grp = [(t, 1024) for t in range(0, T, 1024)]
        emit_ffn(C, x, grp + [(T, 2)], nrm[0], fw_in[0], fw_out[0], "f1")
        emit_sc_mixer(C, x, T, nrm[1], w_in, conv, w_out)
        emit_ffn(C, x, grp, nrm[2], fw_in[1], fw_out[1], "f2")
        if final:
            emit_final_norm(C, x, T, gfin)
        emit_store_x(C, x, yT, T)
        C.S.replay()
    return nc

NFFT = 2 * SEQ
HY_EMB = 33
HY_ORD = 64
MAGIC = 12582912.0
TWO_PI = 2.0 * math.pi
PI_LO = 3.1415925


def hyena_consts():
    n = np.arange(128, dtype=np.float64)
    ang = 2.0 * np.pi * np.outer(n, n) / 128.0
    fre, fim = np.cos(ang), -np.sin(ang)
    angt = 2.0 * np.pi * np.outer(n, n) / NFFT
    tre, tim = np.cos(angt), -np.sin(angt)
    cst = np.stack([fim, fre, -fim, tre, tim], 1).astype(np.float32)
    L = SEQ
    f32 = np.float32
    t = np.linspace(0.0, 1.0, L, dtype=f32)
    w = (f32(2.0 * math.pi) * np.arange(L, dtype=f32) / f32(L)).astype(f32)
    f = np.linspace(1e-4, 15.0, 16, dtype=f32)
    fw = (f[None, :] * w[:, None]).astype(f32)
    z = np.concatenate([t[:, None], np.cos(fw), -np.sin(fw)], -1).astype(f32)
    idx = np.concatenate([[0], np.arange(L - 1, 0, -1)])
    z2 = z[idx]
    t2 = t[idx].copy()
    t2[0] = 1e30
    zT = np.ascontiguousarray(np.concatenate([z, z2], 0).T)
    trow = np.concatenate([t, t2]).astype(f32)
    dmin = math.log(1e-2) / 1.5
    dmax = math.log(1e-2) / 0.3
    deltas = np.linspace(dmin, dmax, D, dtype=f32)
    nad = (-np.abs(deltas)).astype(f32)
    return cst, zT, trow, nad


def emit_fft_fwd(C, X, xkey, K, nseq, cst, tl, ps, kp=""):
    S = C.S
    fimfre = cst[:K, 0:2, :].rearrange("p a b -> p (a b)")
    for s_ in range(nseq):
        bank = ps["a"][s_ // 2]
        S.mm(bank[:, (s_ % 2) * 256:(s_ % 2) * 256 + 256], X[:K, s_, :], fimfre, True, True,
             [xkey, "cst"], [(kp + "psa", s_ // 2)])
    yield
    tre = cst[:, 3:4, :]
    tim = cst[:, 4:5, :]
    for h in range((nseq + 1) // 2):
        ns = min(2, nseq - 2 * h)
        av = ps["a"][h][:].rearrange("p (s r k) -> p s r k", s=2, r=2)
        aim = av[:, :ns, 0, :]
        are = av[:, :ns, 1, :]
        sl = slice(2 * h, 2 * h + ns)
        bt = lambda t_: t_.to_broadcast([128, ns, 128])
        S.op("dve", "tensor_tensor", reads=[(kp + "psa", h), "cst"], writes=[(kp + "t1", h)], out=tl["t1"][:, sl, :], in0=are,
             in1=bt(tre), op=ALU.mult)
        S.op("dve", "tensor_tensor", reads=[(kp + "psa", h), "cst"], writes=[(kp + "t2", h)], out=tl["t2"][:, sl, :], in0=aim,
             in1=bt(tim), op=ALU.mult)
        S.op("dve", "tensor_tensor", reads=[(kp + "psa", h), "cst"], writes=[(kp + "t3", h)], out=tl["t3"][:, sl, :], in0=are,
             in1=bt(tim), op=ALU.mult)
        S.op("dve", "tensor_tensor", reads=[(kp + "psa", h), "cst"], writes=[(kp + "t4", h)], out=tl["t4"][:, sl, :], in0=aim,
             in1=bt(tre), op=ALU.mult)
    hs = list(range((nseq + 1) // 2))
    S.op("pool", "tensor_tensor", reads=[(kp + "t1", h) for h in hs] + [(kp + "t2", h) for h in hs], writes=[kp + "bre"],
         out=tl["bre"][:, :nseq, :], in0=tl["t1"][:, :nseq, :], in1=tl["t2"][:, :nseq, :], op=ALU.subtract)
    S.op("pool", "tensor_tensor", reads=[(kp + "t3", h) for h in hs] + [(kp + "t4", h) for h in hs], writes=[kp + "bim"],
         out=tl["bim"][:, :nseq, :], in0=tl["t3"][:, :nseq, :], in1=tl["t4"][:, :nseq, :], op=ALU.add)
    yield
    n = nseq * 128
    bre = tl["bre"][:].rearrange("p s k -> p (s k)")[:, :n]
    bim = tl["bim"][:].rearrange("p s k -> p (s k)")[:, :n]
    S.mm(ps["xre"][:, :n], cst[:, 1, :], bre, True, False, ["cst", kp + "bre"], [kp + "psxre"])
    S.mm(ps["xre"][:, :n], cst[:, 2, :], bim, False, True, ["cst", kp + "bim"], [kp + "psxre"])
    S.mm(ps["xim"][:, :n], cst[:, 1, :], bim, True, False, ["cst", kp + "bim"], [kp + "psxim"])
    S.mm(ps["xim"][:, :n], cst[:, 0, :], bre, False, True, ["cst", kp + "bre"], [kp + "psxim"])
    yield


def build_hy_core_prog():
    nc = bass.Bass("TRN2", target_bir_lowering=False)
    L = SEQ
    u0 = dram_in(nc, "u0", [3, 128, BATCH, L])
    convw = dram_in(nc, "convw", [3, 3, 128])
    convb = dram_in(nc, "convb", [3, 128])
    dvec = dram_in(nc, "dvec", [128])
    fw1 = dram_in(nc, "fw1", [HY_EMB, HY_ORD])
    fb1 = dram_in(nc, "fb1", [HY_ORD])
    fw2 = dram_in(nc, "fw2", [HY_ORD, HY_ORD])
    fb2 = dram_in(nc, "fb2", [HY_ORD])
    fw3 = dram_in(nc, "fw3", [HY_ORD, 2, 128])
    freq = dram_in(nc, "freq", [HY_ORD])
    cstd = dram_in(nc, "cst", [128, 5, 128])
    zT = dram_in(nc, "zT", [HY_EMB, NFFT])
    trow = dram_in(nc, "trow", [128, NFFT])
    nad = dram_in(nc, "nad", [128])
    yT = dram_out(nc, "yT", [128, BATCH, L])
    s_h = nc.dram_tensor("s_h", [128, NFFT], F32).ap()
    s_H = nc.dram_tensor("s_H", [2, 128, 128, 128], F32).ap()
    s_vv = nc.dram_tensor("s_vv", [128, BATCH, L], F32).ap()
    s_x0 = nc.dram_tensor("s_x0", [128, BATCH, L], F32).ap()
    s_y = nc.dram_tensor("s_y", [128, BATCH, L], F32).ap()
    with contextlib.ExitStack() as es:
        C = Ctx(nc, es)
        S = C.S
        cst = C.sb("cst", [128, 5, 128], F32)
        S.dma("sp", writes=["cst"], out=cst[:], in_=cstd)

        def col(name, src, n):
            t_ = C.sb(name, [n, 1], F32)
            S.dma("sp", writes=[name], out=t_[:], in_=src.rearrange("(p o) -> p o", o=1))
            return t_

        with C.scope():
            w1 = C.sb("w1", [HY_EMB, HY_ORD], F32)
            S.dma("sp", writes=["w1"], out=w1[:], in_=fw1)
            w2 = C.sb("w2", [HY_ORD, HY_ORD], F32)
            S.dma("sp", writes=["w2"], out=w2[:], in_=fw2)
            w3 = C.sb("w3", [HY_ORD, 2, 128], F32)
            S.dma("sp", writes=["w3"], out=w3[:], in_=fw3)
            fq = col("fq", freq, HY_ORD)
            b1 = col("b1", fb1, HY_ORD)
            b2 = col("b2", fb2, HY_ORD)
            nadc = col("nadc", nad, 128)
            S.op("dve", "tensor_tensor", reads=["fq", "b1"], writes=["b1"], out=b1[:], in0=b1[:], in1=fq[:], op=ALU.mult)
            S.op("dve", "tensor_tensor", reads=["fq", "b2"], writes=["b2"], out=b2[:], in0=b2[:], in1=fq[:], op=ALU.mult)
            zt = [C.sb("zt%d" % i, [HY_EMB, 512], F32) for i in range(2)]
            tr = [C.sb("tr%d" % i, [128, 512], F32) for i in range(2)]
            NS = 2
            av = [[C.sb("av%d%d" % (l_, i), [HY_ORD, 512], F32) for i in range(NS)] for l_ in range(2)]
            qv = [[C.sb("qv%d%d" % (l_, i), [HY_ORD, 512], F32) for i in range(NS)] for l_ in range(2)]
            hv = [[C.sb("hv%d%d" % (l_, i), [HY_ORD, 512], F32) for i in range(NS)] for l_ in range(2)]
            hc = [C.sb("hc%d" % i, [128, 512], F32) for i in range(NS)]
            p1 = [C.ps("p1%d" % i, [HY_ORD, 512]) for i in range(NS)]
            p2 = [C.ps("p2%d" % i, [HY_ORD, 512]) for i in range(NS)]
            p3 = [C.ps("p3%d" % i, [128, 512]) for i in range(NS)]

            def sin_layer(psrc, pkey, bias, sl_, lay):
                a, q, h = av[lay][sl_], qv[lay][sl_], hv[lay][sl_]
                ak, qk, hk = ("av", lay, sl_), ("qv", lay, sl_), ("hv", lay, sl_)
                S.op("dve", "tensor_scalar", reads=[pkey, "fq", "b1", "b2"], writes=[ak], out=a[:], in0=psrc[:],
                     scalar1=fq[:, 0:1], scalar2=bias[:, 0:1], op0=ALU.mult, op1=ALU.add)
                S.op("dve", "tensor_scalar", reads=[ak], writes=[qk], out=q[:], in0=a[:], scalar1=1.0 / TWO_PI,
                     scalar2=MAGIC, op0=ALU.mult, op1=ALU.add)
                S.op("dve", "tensor_scalar", reads=[qk], writes=[qk], out=q[:], in0=q[:], scalar1=-MAGIC,
                     scalar2=-TWO_PI, op0=ALU.add, op1=ALU.mult)
                S.op("dve", "tensor_tensor", reads=[qk, ak], writes=[ak], out=a[:], in0=a[:], in1=q[:], op=ALU.add)
                S.op("dve", "tensor_scalar", reads=[ak], writes=[ak], out=a[:], in0=a[:], scalar1=-PI_LO,
                     scalar2=PI_LO, op0=ALU.max, op1=ALU.min)
                yield
                S.op("act", "activation", reads=[ak], writes=[hk], out=h[:], in_=a[:], func=AF.Sin)
                yield

            def p0_tile(ti):
                c0 = ti * 512
                sl_ = ti % NS
                z_, zk = zt[sl_], ("zt", sl_)
                t_, tk = tr[sl_], ("tr", sl_)
                S.dma("sp", writes=[zk], out=z_[:], in_=zT[:, c0:c0 + 512])
                S.dma("sp", writes=[tk], out=t_[:], in_=trow[:, c0:c0 + 512])
                S.mm(p1[sl_][:], w1[:], z_[:], True, True, ["w1", zk], [("p1", sl_)])
                yield
                for _ in sin_layer(p1[sl_], ("p1", sl_), b1, sl_, 0):
                    yield
                S.mm(p2[sl_][:], w2[:], hv[0][sl_][:], True, True, ["w2", ("hv", 0, sl_)], [("p2", sl_)])
                S.op("act", "activation", reads=[tk, "nadc"], writes=[tk], out=t_[:], in_=t_[:], func=AF.Exp,
                     scale=nadc[:, 0:1])
                yield
                for _ in sin_layer(p2[sl_], ("p2", sl_), b2, sl_, 1):
                    yield
                half = 0 if ti < (L // 512) else 1
                S.mm(p3[sl_][:], w3[:, half, :], hv[1][sl_][:], True, True, ["w3", ("hv", 1, sl_)], [("p3", sl_)])
                yield
                o_, ok = hc[sl_], ("hc", sl_)
                S.op("dve", "tensor_tensor", reads=[("p3", sl_), tk], writes=[ok], out=o_[:], in0=p3[sl_][:], in1=t_[:],
                     op=ALU.mult)
                S.dma("sp", reads=[ok], writes=[("s_h", ti)], out=s_h[:, c0:c0 + 512], in_=o_[:])
                yield

            pipeline((p0_tile(ti) for ti in range(NFFT // 512)), NS)

        with C.scope():
            cw = C.sb("hcw", [128, 3, 3], F32)
            for j in range(3):
                for gi in range(3):
                    S.dma("sp", writes=[("hcw", j, gi)], out=cw[:, gi, j:j + 1],
                          in_=convw[j, gi].rearrange("(p o) -> p o", o=1))
            cb = C.sb("hcb", [128, 3], F32)
            for gi in range(3):
                S.dma("sp", writes=[("hcb", gi)], out=cb[:, gi:gi + 1], in_=convb[gi].rearrange("(p o) -> p o", o=1))
            cwk = [("hcw", j, gi) for j in range(3) for gi in range(3)] + [("hcb", gi) for gi in range(3)]
            ub = [C.sb("hu%d" % i, [128, L + 2], F32) for i in range(2)] * 2
            uc = [C.sb("huc%d" % i, [128, L], F32) for i in range(3)]
            for b in range(BATCH):
                for gi in range(3):
                    S.op("pool", "memset", writes=[("hu", gi % 2, "e")], ap=ub[gi][:, 0:1], constant=0.0)
                    S.op("pool", "memset", writes=[("hu", gi % 2, "e2")], ap=ub[gi][:, L + 1:L + 2], constant=0.0)
                    S.dma("sp", writes=[("hu", gi % 2)], out=ub[gi][:, 1:L + 1], in_=u0[gi, :, b, :])
                    rk = [("hu", gi % 2), ("hu", gi % 2, "e"), ("hu", gi % 2, "e2")] + cwk
                    uk = ("huc", gi)
                    S.op("dve", "tensor_scalar", reads=rk, writes=[uk], out=uc[gi][:], in0=ub[gi][:, 0:L],
                         scalar1=cw[:, gi, 0:1], scalar2=cb[:, gi:gi + 1], op0=ALU.mult, op1=ALU.add)
                    S.op("dve", "scalar_tensor_tensor", reads=rk + [uk], writes=[uk], out=uc[gi][:],
                         in0=ub[gi][:, 1:L + 1], scalar=cw[:, gi, 1:2], in1=uc[gi][:], op0=ALU.mult, op1=ALU.add)
                    S.op("dve", "scalar_tensor_tensor", reads=rk + [uk], writes=[uk], out=uc[gi][:],
                         in0=ub[gi][:, 2:L + 2], scalar=cw[:, gi, 2:3], in1=uc[gi][:], op0=ALU.mult, op1=ALU.add)
                S.op("pool", "tensor_tensor", reads=[("huc", 1), ("huc", 2)], writes=[("huc", 2)], out=uc[2][:],
                     in0=uc[2][:], in1=uc[1][:], op=ALU.mult)
                S.dma("sp", reads=[("huc", 2)], writes=[("s_vv", b)], out=s_vv[:, b, :], in_=uc[2][:])
                S.dma("sp", reads=[("huc", 0)], writes=[("s_x0", b)], out=s_x0[:, b, :], in_=uc[0][:])
        with C.scope():
            tl = {k: C.sb("fft_" + k, [128, 4, 128], F32) for k in
                  ("t1", "t2", "t3", "t4", "u1", "u2", "u3", "u4", "bre", "bim", "dre", "dim")}
            yre = [C.sb("fft_yre%d" % i, [128, 4, 128], F32) for i in range(2)]
            yim = [C.sb("fft_yim%d" % i, [128, 4, 128], F32) for i in range(2)]
            ps = {"a": [C.ps("psa%d" % i, [128, 512]) for i in range(2)], "xre": C.ps("psxre", [128, 512]),
                  "xim": C.ps("psxim", [128, 512]), "c": [C.ps("psc%d" % i, [128, 512]) for i in range(2)],
                  "y": C.ps("psy", [128, 512])}
            Xb = [C.sb("fft_X%d" % i, [128, 4, 128], F32) for i in range(2)]
            Hb = [[C.sb("fft_H%d%d" % (i, r), [128, 2, 128], F32) for r in range(2)] for i in range(2)]
            ev = [[C.sb("fft_ev%d%d" % (i, r), [128, 4, 128], F32) for r in range(2)] for i in range(2)]
            yo = [C.sb("fft_yo%d" % i, [64, 4, 128], F32) for i in range(2)]
            ps8 = C.ps("psx8", [128, 512])
            tlB = {"t1": tl["u1"], "t2": tl["u2"], "t3": tl["u3"], "t4": tl["u4"], "bre": tl["dre"], "bim": tl["dim"]}
            psB = {"a": ps["c"], "xre": ps["y"], "xim": ps8}

            def p1_group(g):
                q = g % 2
                tl_, ps_, kp = (tl, ps, "") if q == 0 else (tlB, psB, "B")
                X, xk = Xb[q], ("X", q)
                S.dma("sp", reads=[("s_h", ti) for ti in range(NFFT // 512)], writes=[xk], out=X[:],
                      in_=s_h[4 * g:4 * g + 4, :].rearrange("c (n1 n2) -> n1 c n2", n2=128))
                for _ in emit_fft_fwd(C, X, xk, 128, 4, cst, tl_, ps_, kp):
                    yield
                for r, nm in ((0, "xre"), (1, "xim")):
                    e_, ek = ev[q][r], ("ev", q, r)
                    S.op("act", "mul", reads=[kp + "ps" + nm], writes=[ek], out=e_[:].rearrange("p s k -> p (s k)"),
                         in_=ps_[nm][:], mul=1.0 / NFFT)
                    S.dma("sp", reads=[ek], writes=[("s_H", g, r)], out=s_H[r, :, 4 * g:4 * g + 4, :], in_=e_[:])
                yield

            pipeline((p1_group(g) for g in range(32)), 2)
            S.barrier()
            tre = cst[:, 3:4, :]
            tim = cst[:, 4:5, :]
            g1 = cst[:, 1:3, :].rearrange("p a b -> p (a b)")
            g2 = cst[:, 0:2, :].rearrange("p a b -> p (a b)")

            def half1(g):
                p = g % 2
                X, xk = Xb[p], ("X", p)
                S.dma("sp", reads=[("s_vv", 0), ("s_vv", 1)], writes=[xk], out=X[:64],
                      in_=s_vv[2 * g:2 * g + 2].rearrange("c b (n1 n2) -> n1 (c b) n2", n2=128))
                H = Hb[p]
                for r in range(2):
                    S.dma("sp", reads=[("s_H", g // 2, r)], writes=[("H", p, r)], out=H[r][:],
                          in_=s_H[r, :, 2 * g:2 * g + 2, :])
                for _ in emit_fft_fwd(C, X, xk, 64, 4, cst, tl, ps):
                    yield
                hk = [("H", p, 0), ("H", p, 1)]
                xre = ps["xre"][:].rearrange("p (c b k) -> p c b k", c=2, b=2)
                xim = ps["xim"][:].rearrange("p (c b k) -> p c b k", c=2, b=2)
                hb = lambda r: H[r][:].unsqueeze(2).to_broadcast([128, 2, 2, 128])
                v4 = lambda t_: t_[:].rearrange("p (c b) k -> p c b k", c=2)
                S.op("dve", "tensor_tensor", reads=["psxre"] + hk, writes=[("t1", 0), ("t1", 1)], out=v4(tl["t1"]),
                     in0=xre, in1=hb(0), op=ALU.mult)
                S.op("dve", "tensor_tensor", reads=["psxim"] + hk, writes=[("t2", 0), ("t2", 1)], out=v4(tl["t2"]),
                     in0=xim, in1=hb(1), op=ALU.mult)
                S.op("dve", "tensor_tensor", reads=["psxre"] + hk, writes=[("t3", 0), ("t3", 1)], out=v4(tl["t3"]),
                     in0=xre, in1=hb(1), op=ALU.mult)
                S.op("dve", "tensor_tensor", reads=["psxim"] + hk, writes=[("t4", 0), ("t4", 1)], out=v4(tl["t4"]),
                     in0=xim, in1=hb(0), op=ALU.mult)
                S.op("pool", "tensor_tensor", reads=[("t1", 0), ("t1", 1), ("t2", 0), ("t2", 1)], writes=[("yre", p)],
                     out=yre[p][:], in0=tl["t1"][:], in1=tl["t2"][:], op=ALU.subtract)
                S.op("pool", "tensor_tensor", reads=[("t3", 0), ("t3", 1), ("t4", 0), ("t4", 1)], writes=[("yim", p)],
                     out=yim[p][:], in0=tl["t3"][:], in1=tl["t4"][:], op=ALU.add)
                yield

            def half2(g):
                p = g % 2
                for s_ in range(4):
                    bank = ps["c"][s_ // 2]
                    o = (s_ % 2) * 256
                    S.mm(bank[:, o:o + 256], yre[p][:, s_, :], g1, True, False, [("yre", p), "cst"], [("psc", s_ // 2)])
                    S.mm(bank[:, o:o + 256], yim[p][:, s_, :], g2, False, True, [("yim", p), "cst"], [("psc", s_ // 2)])
                yield
                for h in range(2):
                    cv = ps["c"][h][:].rearrange("p (s r k) -> p s r k", s=2, r=2)
                    cre = cv[:, :, 0, :]
                    cim = cv[:, :, 1, :]
                    sl = slice(2 * h, 2 * h + 2)
                    bt = lambda t_: t_.to_broadcast([128, 2, 128])
                    S.op("dve", "tensor_tensor", reads=[("psc", h), "cst"], writes=[("u1", h)], out=tl["u1"][:, sl, :],
                         in0=cre, in1=bt(tre), op=ALU.mult)
                    S.op("dve", "tensor_tensor", reads=[("psc", h), "cst"], writes=[("u2", h)], out=tl["u2"][:, sl, :],
                         in0=cim, in1=bt(tim), op=ALU.mult)
                    S.op("dve", "tensor_tensor", reads=[("psc", h), "cst"], writes=[("u3", h)], out=tl["u3"][:, sl, :],
                         in0=cim, in1=bt(tre), op=ALU.mult)
                    S.op("dve", "tensor_tensor", reads=[("psc", h), "cst"], writes=[("u4", h)], out=tl["u4"][:, sl, :],
                         in0=cre, in1=bt(tim), op=ALU.mult)
                S.op("pool", "tensor_tensor", reads=[("u1", 0), ("u1", 1), ("u2", 0), ("u2", 1)], writes=["dre"],
                     out=tl["dre"][:], in0=tl["u1"][:], in1=tl["u2"][:], op=ALU.add)
                S.op("pool", "tensor_tensor", reads=[("u3", 0), ("u3", 1), ("u4", 0), ("u4", 1)], writes=["dim"],
                     out=tl["dim"][:], in0=tl["u3"][:], in1=tl["u4"][:], op=ALU.subtract)
                yield
                S.mm(ps["y"][:64, :], cst[:, 1, 0:64], tl["dre"][:].rearrange("p s k -> p (s k)"), True, False,
                     ["cst", "dre"], ["psy"])
                S.mm(ps["y"][:64, :], cst[:, 0, 0:64], tl["dim"][:].rearrange("p s k -> p (s k)"), False, True,
                     ["cst", "dim"], ["psy"])
                yield
                y_, yk = yo[p], ("yo", p)
                S.op("act", "activation", reads=["psy"], writes=[yk], out=y_[:].rearrange("p s k -> p (s k)"),
                     in_=ps["y"][:64, :], func=AF.Copy)
                S.dma("sp", reads=[yk], writes=[("s_y", g)], out=s_y[2 * g:2 * g + 2].rearrange(
                    "c b (n1 n2) -> n1 (c b) n2", n2=128), in_=y_[:])
                yield

            for _ in half1(0):
                pass
            for g in range(64):
                interleave([half2(g)] + ([half1(g + 1)] if g + 1 < 64 else []))
        with C.scope():
            dcol = col("dcol", dvec, 128)
            ya = C.sb("p4y", [128, L], F32)
            va = C.sb("p4v", [128, L], F32)
            xa = C.sb("p4x", [128, L], F32)
            for b in range(BATCH):
                S.dma("sp", reads=[("s_y", g) for g in range(64)], writes=["p4y"], out=ya[:], in_=s_y[:, b, :])
                S.dma("sp", reads=[("s_vv", b)], writes=["p4v"], out=va[:], in_=s_vv[:, b, :])
                S.dma("sp", reads=[("s_x0", b)], writes=["p4x"], out=xa[:], in_=s_x0[:, b, :])
                S.op("dve", "scalar_tensor_tensor", reads=["p4y", "p4v", "dcol"], writes=["p4y"], out=ya[:], in0=va[:],
                     scalar=dcol[:, 0:1], in1=ya[:], op0=ALU.mult, op1=ALU.add)
                S.op("dve", "tensor_tensor", reads=["p4y", "p4x"], writes=["p4y"], out=ya[:], in0=ya[:], in1=xa[:],
                     op=ALU.mult)
                S.dma("sp", reads=["p4y"], is_out=True, out=yT[:, b, :], in_=ya[:])
        S.replay()
    return nc

def emit_proj(C, x, T, g_dram, w_in, b_in, nout, uT):
    S = C.S
    NT = T // 512
    w_in_v = w_in.rearrange("(kc p) n -> p kc n", p=128)
    with C.scope():
        g_sb = load_vec_pk(C, "pj_g", g_dram, KC)
        bias = load_vec_pk(C, "pj_b", b_in, nout // 128)
        hn = C.sb("pj_hn", [128, KC, T], BF16)
        wbuf = [C.sb("pj_w%d" % i, [128, KC, 128], BF16) for i in range(3)]
        st = [C.sb("pj_st%d" % i, [128, 512], F32) for i in range(4)]
        pp = [C.ps("pj_ps%d" % i, [128, 512]) for i in range(2)]
        emit_rmsnorm(C, x, "x", 0, T, g_sb, "pj_g", hn, "pj_hn")
        it = 0
        for m in range(nout // 128):
            wb, wk = wbuf[m % 3], ("pj_w", m % 3)
            S.dma("pool", writes=[wk], out=wb[:], in_=w_in_v[:, :, m * 128:(m + 1) * 128])
            for tt in range(NT):
                it += 1
                b = it % 2
                sl = slice(tt * 512, (tt + 1) * 512)
                for k in range(KC):
                    S.mm(pp[b][:], wb[:, k, :], hn[:, k, sl], k == 0, k == KC - 1, [wk, ("pj_hn", k, tt)], [("pj_ps", b)])
                o_, ok = st[it % 4], ("pj_st", it % 4)
                S.op("act", "activation", reads=[("pj_ps", b), "pj_b"], writes=[ok], out=o_[:], in_=pp[b][:],
                     func=AF.Identity, bias=bias[:, m:m + 1])
                S.dma("sp", reads=[ok], is_out=True, out=uT[m * 128:(m + 1) * 128, sl], in_=o_[:])


def emit_outproj(C, x, T, yT, w_out, b_out):
    S = C.S
    NT = T // 512
    w_out_v = w_out.rearrange("(kc p) n -> p kc n", p=128)
    y_v = yT.rearrange("(kc p) t -> p kc t", p=128)
    with C.scope():
        bo = load_vec_pk(C, "op_b", b_out, KC)
        ybf = C.sb("op_y", [128, KC, T], BF16)
        for k in range(KC):
            S.dma("pool", writes=[("op_y", k)], out=ybf[:, k, :], in_=y_v[:, k, :])
        wobuf = [C.sb("op_wo%d" % i, [128, KC, 128], BF16) for i in range(2)]
        po = [C.ps("op_po%d" % i, [128, 512]) for i in range(2)]
        it = 0
        for m in range(KC):
            wo, wok = wobuf[m % 2], ("op_wo", m % 2)
            S.dma("pool", writes=[wok], out=wo[:], in_=w_out_v[:, :, m * 128:(m + 1) * 128])
            for tt in range(NT):
                it += 1
                b = it % 2
                sl = slice(tt * 512, (tt + 1) * 512)
                for i in range(KC):
                    S.mm(po[b][:], wo[:, i, :], ybf[:, i, sl], i == 0, i == KC - 1, [wok, ("op_y", i)], [("op_po", b)])
                xkey = ("x", m, tt)
                S.op("dve", "scalar_tensor_tensor", reads=[("op_po", b), xkey, "op_b"], writes=[xkey], out=x[:, m, sl],
                     in0=po[b][:], scalar=bo[:, m:m + 1], in1=x[:, m, sl], op0=ALU.add, op1=ALU.add)


def build_pre_prog(nout, T=TPC):
    nc = bass.Bass("TRN2", target_bir_lowering=False)
    xT = dram_in(nc, "xT", [D, T])
    nrm = dram_in(nc, "nrm", [2, D])
    fw_in = dram_in(nc, "fw_in", [D, 2 * FF])
    fw_out = dram_in(nc, "fw_out", [FF, D])
    w_in = dram_in(nc, "w_in", [D, nout])
    b_in = dram_in(nc, "b_in", [nout])
    xo = dram_out(nc, "xo", [D, T])
    uT = dram_out(nc, "uT", [nout, T])
    with contextlib.ExitStack() as es:
        C = Ctx(nc, es)
        emit_consts(C)
        x = C.sb("x", [128, KC, T], F32)
        emit_load_x(C, x, xT, T)
        emit_ffn(C, x, [(t, 1024) for t in range(0, T, 1024)], nrm[0], fw_in, fw_out, "f1")
        emit_store_x(C, x, xo, T)
        emit_proj(C, x, T, nrm[1], w_in, b_in, nout, uT)
        C.S.replay()
    return nc


def build_post_prog(T=TPC):
    nc = bass.Bass("TRN2", target_bir_lowering=False)
    xT = dram_in(nc, "xT", [D, T])
    yT = dram_in(nc, "yT", [D, T])
    w_out = dram_in(nc, "w_out", [D, D])
    b_out = dram_in(nc, "b_out", [D])
    nrm = dram_in(nc, "nrm", [D])
    fw_in = dram_in(nc, "fw_in", [D, 2 * FF])
    fw_out = dram_in(nc, "fw_out", [FF, D])
    xo = dram_out(nc, "xo", [D, T])
    with contextlib.ExitStack() as es:
        C = Ctx(nc, es)
        emit_consts(C)
        x = C.sb("x", [128, KC, T], F32)
        emit_load_x(C, x, xT, T)
        emit_outproj(C, x, T, yT, w_out, b_out)
        emit_ffn(C, x, [(t, 1024) for t in range(0, T, 1024)], nrm, fw_in, fw_out, "f2")
        emit_store_x(C, x, xo, T)
        C.S.replay()
    return nc


CH = 64
NCH = SEQ // CH
BLK = 8
SCAN_GAP = 3


def gdn_consts():
    i = np.arange(CH)
    mtri = np.stack([(i[:, None] <= i[None, :]), (i[:, None] >= i[None, :])]).astype(np.float32)
    eye = np.eye(CH, dtype=np.float32)
    mstrict = mtri - eye[None]
    ident = np.eye(128, dtype=np.float32)
    return mtri, mstrict, ident


def build_gd_core_prog(dbg_blocks=None):
    nc = bass.Bass("TRN2", target_bir_lowering=False)
    L, N = SEQ, NCH
    qkv0 = dram_in(nc, "qkv0", [3, 128, BATCH, L])
    ztok = dram_in(nc, "ztok", [CH, BATCH, N, 128])
    abt = dram_in(nc, "abt", [4, CH, BATCH, N])
    convw = dram_in(nc, "convw", [3, 3, 128])
    alog = dram_in(nc, "alog", [CH, 2])
    dtb = dram_in(nc, "dtb", [CH, 2])
    gn = dram_in(nc, "gn", [CH, 128])
    mtri_d = dram_in(nc, "mtri", [2, CH, CH])
    mstr_d = dram_in(nc, "mstrict", [2, CH, CH])
    ident_d = dram_in(nc, "ident", [128, 128])
    ytok = dram_out(nc, "ytok", [CH, BATCH, N, 128])
    s_o = nc.dram_tensor("s_o", [2, CH, BATCH, N, 128], F32).ap()
    with contextlib.ExitStack() as es:
        C = Ctx(nc, es)
        S = C.S
        ones = C.sb("ones", [128, 128], F32)
        S.op("dve", "memset", writes=["ones"], ap=ones[:], constant=1.0)
        ident = C.sb("ident", [128, 128], F32)
        S.dma("sp", writes=["ident"], out=ident[:], in_=ident_d)
        mtri = C.sb("mtri", [CH, 2, CH], F32)
        mstr = C.sb("mstr", [CH, 2, CH], F32)
        for d in range(2):
            S.dma("sp", writes=[("mtri", d)], out=mtri[:, d, :], in_=mtri_d[d])
            S.dma("sp", writes=[("mstr", d)], out=mstr[:, d, :], in_=mstr_d[d])
        cw = C.sb("gcw", [128, 3, 3], F32)
        for j in range(3):
            for gi in range(3):
                S.dma("sp", writes=[("gcw", j, gi)], out=cw[:, gi, j:j + 1], in_=convw[j, gi].rearrange("(p o) -> p o", o=1))
        cwk = [("gcw", j, gi) for j in range(3) for gi in range(3)]
        gtf = C.sb("gt", [128, 2, BATCH, N], F32)
        S.op("dve", "memset", writes=[("gt", 0), ("gt", 1)], ap=gtf[:], constant=0.0)
        gt = gtf[:CH]
        bt_ = C.sb("bt", [CH, 2, BATCH, N], F32)
        al = C.sb("al", [CH, 2], F32)
        db = C.sb("db", [CH, 2], F32)
        S.dma("sp", writes=["al"], out=al[:], in_=alog)
        S.dma("sp", writes=["db"], out=db[:], in_=dtb)
        for d in range(2):
            S.dma("sp", writes=[("gt", d)], out=gt[:, d], in_=abt[d])
            S.dma("sp", writes=[("bt", d)], out=bt_[:, d], in_=abt[2 + d])
        S.op("act", "activation", reads=["al"], writes=["al"], out=al[:], in_=al[:], func=AF.Exp)
        S.op("dve", "tensor_scalar", reads=["al"], writes=["al"], out=al[:], in0=al[:], scalar1=-1.0, scalar2=None,
             op0=ALU.mult)
        for d in range(2):
            gv = gt[:, d].rearrange("p b n -> p (b n)")
            bv = bt_[:, d].rearrange("p b n -> p (b n)")
            S.op("act", "activation", reads=[("gt", d), "db"], writes=[("gt", d)], out=gv, in_=gv, func=AF.Exp,
                 bias=db[:, d:d + 1])
            S.op("dve", "tensor_scalar", reads=[("gt", d)], writes=[("gt", d)], out=gv, in0=gv, scalar1=1.0, scalar2=None,
                 op0=ALU.add)
            S.op("act", "activation", reads=[("gt", d)], writes=[("gt", d)], out=gv, in_=gv, func=AF.Ln)
            S.op("dve", "tensor_scalar", reads=[("gt", d), "al"], writes=[("gt", d)], out=gv, in0=gv, scalar1=al[:, d:d + 1],
                 scalar2=None, op0=ALU.mult)
            S.op("act", "activation", reads=[("bt", d)], writes=[("bt", d)], out=bv, in_=bv, func=AF.Sigmoid)
        with C.scope():
            qT = C.sb("qT", [128, L], F32)
            kT = C.sb("kT", [128, L], F32)
            vT = C.sb("vT", [128, L], F32)
            ubs = [C.sb("gub%d" % i, [128, 2048 + 2], F32) for i in range(2)]
            ci = 0
            rsb = [C.sb("grs%d" % i, [128, 512], F32) for i in range(2)]
            sqb = [C.sb("gsq%d" % i, [128, 512], F32) for i in range(2)]
            gc = C.sb("gc", [CH, N], F32)
            kd = C.sb("kd", [CH, N], F32)
            bg = C.sb("bg", [CH, N], F32)
            egt = C.sb("egt", [128, N], F32)
            Sst = [C.sb("Sst%d" % i, [128, 128], F32) for i in range(2)]
            attnT = [C.sb("g8_attnT%d" % i, [128, BLK, CH], F32) for i in range(2)]
            for i_ in range(2):
                S.op("dve", "memset", writes=[("attnT", i_)], ap=attnT[i_][:], constant=0.0)
            names = ["dT", "W1", "LT", "Lm", "X8", "Pa", "Pta", "Pb", "Ptb"]
            t8f = {nm: C.sb("g8_" + nm, [128, BLK, CH], F32) for nm in names}
            for nm in names:
                S.op("dve", "memset", writes=[nm], ap=t8f[nm][:], constant=0.0)
            t8 = {nm: t8f[nm][:CH] for nm in names}
            qd = [C.sb("g8_qd%d" % i, [128, BLK * CH], F32) for i in range(2)]
            egr = C.sb("g8_egr", [128, BLK * CH], F32)
            kdt = [C.sb("g8_kdt%d" % i, [128, BLK, 128], F32) for i in range(2)]
            for i_ in range(2):
                S.op("dve", "memset", writes=[("kdt", i_, 0)], ap=kdt[i_][:, 0:4, :], constant=0.0)
                S.op("dve", "memset", writes=[("kdt", i_, 1)], ap=kdt[i_][:, 4:8, :], constant=0.0)
            kbt = C.sb("g8_kbt", [128, BLK, 128], F32)
            S.op("dve", "memset", writes=[("kbt", 0)], ap=kbt[:, 0:4, :], constant=0.0)
            S.op("dve", "memset", writes=[("kbt", 1)], ap=kbt[:, 4:8, :], constant=0.0)
            vbt = C.sb("g8_vbt", [CH, BLK, 128], F32)
            u8 = [C.sb("g8_u8%d" % i, [128, BLK, 128], F32) for i in range(2)]
            MT8 = [C.sb("g8_MT%d" % i, [128, BLK, 128], F32) for i in range(2)]
            w8 = C.sb("g8_w8", [128, BLK, 128], F32)
            mtmp = C.sb("g8_mtmp", [128, BLK, 128], F32)
            for i_ in range(2):
                for hh_ in range(2):
                    S.op("dve", "memset", writes=[("u8", i_, hh_)], ap=u8[i_][:, hh_ * 4:hh_ * 4 + 4, :], constant=0.0)
            for hh_ in range(2):
                S.op("dve", "memset", writes=[("w8", hh_)], ap=w8[:, hh_ * 4:hh_ * 4 + 4, :], constant=0.0)
            o8 = [C.sb("g8_o8%d" % i, [CH, BLK, 128], F32) for i in range(2)]
            Bk = [C.ps("gB%d" % i, [128, 512]) for i in range(8)]
            bk = lambda i: ("B", i)
            b7all = [bk(7)]
            S.excl.update(bk(i) for i in range(8))

            for b in range(BATCH):
                for gi, dst, dk_ in ((0, qT, "qT"), (1, kT, "kT"), (2, vT, "vT")):
                    CT = 2048
                    for ct in range(L // CT):
                        ci += 1
                        u_, uk_ = ubs[ci % 2], ("gub", ci % 2)
                        lo = ct * CT - 1
                        hi = ct * CT + CT + 1
                        a_ = max(lo, 0)
                        b_ = min(hi, L)
                        wr = [uk_]
                        if lo < 0:
                            S.op("pool", "memset", writes=[uk_], ap=u_[:, 0:1], constant=0.0)
                        if hi > L:
                            S.op("pool", "memset", writes=[uk_], ap=u_[:, CT + 1:CT + 2], constant=0.0)
                        S.dma("sp", writes=[uk_], out=u_[:, a_ - lo:b_ - lo], in_=qkv0[gi, :, b, a_:b_])
                        rk = wr + cwk
                        dsl = slice(ct * CT, (ct + 1) * CT)
                        dkt = (dk_, ct)
                        S.op("dve", "tensor_scalar", reads=rk, writes=[dkt], out=dst[:, dsl], in0=u_[:, 0:CT],
                             scalar1=cw[:, gi, 0:1], scalar2=None, op0=ALU.mult)
                        S.op("dve", "scalar_tensor_tensor", reads=rk + [dkt], writes=[dkt], out=dst[:, dsl], in0=u_[:, 1:CT + 1],
                             scalar=cw[:, gi, 1:2], in1=dst[:, dsl], op0=ALU.mult, op1=ALU.add)
                        S.op("dve", "scalar_tensor_tensor", reads=rk + [dkt], writes=[dkt], out=dst[:, dsl], in0=u_[:, 2:CT + 2],
                             scalar=cw[:, gi, 2:3], in1=dst[:, dsl], op0=ALU.mult, op1=ALU.add)
                        S.op("act", "activation", reads=[dkt], writes=[dkt], out=dst[:, dsl], in_=dst[:, dsl], func=AF.Silu)
                    S.op("act", "activation", reads=[(dk_, ct) for ct in range(L // CT)], writes=[dk_], out=dst[:, 0:1],
                         in_=dst[:, 0:1], func=AF.Copy)
                    if gi < 2:
                        sc = (128.0 ** -0.5) if gi == 0 else 1.0

                        def l2_tile(tt, dst=dst, dk_=dk_, sc=sc):
                            sl = slice(tt * 512, (tt + 1) * 512)
                            q3 = tt % 2
                            sq, sqk = sqb[q3], ("gsq", q3)
                            rs, rsk = rsb[q3], ("grs", q3)
                            S.op("act", "activation", reads=[dk_], writes=[sqk], out=sq[:], in_=dst[:, sl], func=AF.Square)
                            yield
                            S.mm(Bk[q3][:], ones[:], sq[:], True, True, ["ones", sqk], [bk(q3)])
                            yield
                            S.op("dve", "tensor_scalar", reads=[bk(q3)], writes=[rsk], out=rs[:], in0=Bk[q3][:], scalar1=1e-6,
                                 scalar2=None, op0=ALU.add)
                            yield
                            S.op("act", "activation", reads=[rsk], writes=[rsk], out=rs[:], in_=rs[:], func=AF.Sqrt)
                            yield
                            S.op("dve", "reciprocal", reads=[rsk], writes=[rsk], out=rs[:], in_=rs[:])
                            S.op("dve", "scalar_tensor_tensor", reads=[rsk, dk_], writes=[(dk_, "n", tt)], out=dst[:, sl],
                                 in0=dst[:, sl], scalar=sc, in1=rs[:], op0=ALU.mult, op1=ALU.mult)
                            yield

                        pipeline((l2_tile(tt) for tt in range(L // 512)), 2)
                        S.op("dve", "tensor_copy", reads=[(dk_, "n", tt) for tt in range(L // 512)], writes=[dk_],
                             out=dst[:, 0:1], in_=dst[:, 0:1])
                for d in range(2):
                    Gd = gt[:, d, b, :]
                    Bd = bt_[:, d, b, :]
                    S.mm(Bk[0][:CH, :N], mtri[:, d, :], Gd, True, True, [("mtri", d), ("gt", d)], [bk(0)])
                    S.op("act", "activation", reads=[bk(0)], writes=["gc"], out=gc[:], in_=Bk[0][:CH, :N], func=AF.Copy)
                    S.mm(Bk[1][:, :N], ones[:], gtf[:, d, b, :], True, True, ["ones", ("gt", d)], [bk(1)])
                    S.op("act", "activation", reads=[bk(1)], writes=["egt"], out=egt[:], in_=Bk[1][:, :N], func=AF.Exp)
                    S.op("dve", "tensor_tensor", reads=[bk(1), "gc"], writes=["kd"], out=kd[:], in0=Bk[1][:CH, :N], in1=gc[:],
                         op=ALU.subtract)
                    S.op("act", "activation", reads=["kd"], writes=["kd"], out=kd[:], in_=kd[:], func=AF.Exp)
                    S.op("act", "activation", reads=["gc"], writes=["bg"], out=bg[:], in_=gc[:], func=AF.Exp)
                    S.op("dve", "tensor_tensor", reads=["bg", ("bt", d)], writes=["bg"], out=bg[:], in0=bg[:], in1=Bd, op=ALU.mult)
                    S.op("dve", "memset", writes=[("S", 0)], ap=Sst[0][:], constant=0.0)
                    sidx = [0]
                    si = 0
                    vi = 0
                    blocks = list(range(N // BLK))
                    if d == 1:
                        blocks = blocks[::-1]
                    if dbg_blocks is not None:
                        blocks = blocks[:dbg_blocks]
                    flat = lambda t_: t_[:].rearrange("p i c -> p (i c)")
                    mt_b = mtri[:, d:d + 1, :].to_broadcast([CH, BLK, CH])
                    ms_b = mstr[:, d:d + 1, :].to_broadcast([CH, BLK, CH])
                    id_b = ident[:CH, 0:CH].unsqueeze(1).to_broadcast([CH, BLK, CH])
                    k3 = lambda i_: Bk[i_][:CH, :].rearrange("p (i c) -> p i c", c=CH)

                    def prep(nb, pb, d=d, Gd=Gd, Bd=Bd):
                        n0 = nb * BLK
                        tsl = slice(n0 * CH, (n0 + BLK) * CH)
                        gb = lambda t_: t_[:, n0:n0 + BLK].unsqueeze(2).to_broadcast([CH, BLK, CH])
                        qd_, at_, kdt_, u8_ = qd[pb], attnT[pb], kdt[pb], u8[pb]
                        S.op("dve", "tensor_tensor", reads=[("gt", d), ("mtri", d)], writes=["Pa"], out=t8["Pa"][:],
                             in0=mt_b, in1=gb(Gd), op=ALU.mult)
                        S.op("dve", "tensor_tensor", reads=[("bt", d), "ident"], writes=["Pb"], out=t8["Pb"][:], in0=id_b,
                             in1=gb(Bd), op=ALU.mult)
                        yield
                        S.mm(Bk[3][:], ones[:], t8f["Pa"][:].rearrange("p i c -> p (i c)"), True, True, ["ones", "Pa"], [bk(3)])
                        S.mm(Bk[4][:CH, :], ones[:CH, :CH], flat(t8["Pb"]), True, True, ["ones", "Pb"], [bk(4)])
                        for i in range(BLK):
                            csl = slice((n0 + i) * CH, (n0 + i + 1) * CH)
                            S.mm(Bk[5][:CH, i * CH:(i + 1) * CH], kT[:, csl], kT[:, csl], True, True, ["kT"], [bk(5)])
                            S.mm(Bk[6][:CH, i * CH:(i + 1) * CH], kT[:, csl], qT[:, csl], True, True, ["kT", "qT"], [bk(6)])
                        yield
                        S.op("act", "activation", reads=[bk(3)], writes=["egr"], out=egr[:], in_=Bk[3][:], func=AF.Exp)
                        S.op("dve", "tensor_tensor", reads=[bk(3), "gc"], writes=["dT"], out=t8["dT"][:], in0=k3(3), in1=gb(gc),
                             op=ALU.subtract)
                        S.op("dve", "tensor_scalar", reads=["dT"], writes=["dT"], out=t8["dT"][:], in0=t8["dT"][:], scalar1=0.0,
                             scalar2=None, op0=ALU.min)
                        yield
                        S.op("act", "activation", reads=["dT"], writes=["dT"], out=t8["dT"][:], in_=t8["dT"][:], func=AF.Exp)
                        S.op("dve", "tensor_tensor", reads=["egr", "qT"], writes=[("qd", pb)], out=qd_[:], in0=qT[:, tsl],
                             in1=egr[:], op=ALU.mult)
                        yield
                        S.op("dve", "tensor_tensor", reads=["dT", ("mstr", d)], writes=["W1"], out=t8["W1"][:], in0=t8["dT"][:],
                             in1=ms_b, op=ALU.mult)
                        S.op("dve", "tensor_tensor", reads=["W1", bk(4)], writes=["W1"], out=t8["W1"][:], in0=t8["W1"][:],
                             in1=k3(4), op=ALU.mult)
                        S.op("dve", "tensor_tensor", reads=["dT", ("mtri", d)], writes=["dT"], out=t8["dT"][:], in0=t8["dT"][:],
                             in1=mt_b, op=ALU.mult)
                        S.op("dve", "tensor_tensor", reads=[bk(5), "W1"], writes=["LT"], out=t8["LT"][:], in0=k3(5),
                             in1=t8["W1"][:], op=ALU.mult)
                        S.op("dve", "tensor_tensor", reads=[bk(6), "dT"], writes=[("attnT", pb)], out=at_[:CH], in0=k3(6),
                             in1=t8["dT"][:], op=ALU.mult)
                        yield
                        for i in range(BLK):
                            S.op("pe", "transpose", reads=["LT", "ident"], writes=[bk(7)], out=Bk[7][:CH, i * CH:(i + 1) * CH],
                                 in_=t8["LT"][:, i, :], identity=ident[:CH, :CH])
                        S.op("dve", "tensor_tensor", reads=["LT", "ident"], writes=["X8"], out=t8["X8"][:], in0=id_b,
                             in1=t8["LT"][:], op=ALU.subtract)
                        yield
                        S.op("act", "activation", reads=[bk(7)], writes=["Lm"], out=flat(t8["Lm"]), in_=Bk[7][:CH, :],
                             func=AF.Copy)
                        yield
                        A_, At_, ak, atk = t8["LT"], t8["Lm"], "LT", "Lm"
                        for lev in range(5):
                            P_, Pt_ = (t8["Pa"], t8["Pta"]) if lev % 2 == 0 else (t8["Pb"], t8["Ptb"])
                            pk, ptk = ("Pa", "Pta") if lev % 2 == 0 else ("Pb", "Ptb")
                            for i in range(BLK):
                                cs = slice(i * CH, (i + 1) * CH)
                                S.mm(Bk[4][:CH, cs], A_[:, i, :], At_[:, i, :], True, True, [ak, atk], [bk(4)])
                                if lev < 4:
                                    S.mm(Bk[3][:CH, cs], At_[:, i, :], A_[:, i, :], True, True, [ak, atk], [bk(3)])
                            yield
                            S.op("act", "activation", reads=[bk(4)], writes=[ptk], out=flat(Pt_), in_=Bk[4][:CH, :], func=AF.Copy)
                            if lev < 4:
                                S.op("act", "activation", reads=[bk(3)], writes=[pk], out=flat(P_), in_=Bk[3][:CH, :], func=AF.Copy)
                            yield
                            for i in range(BLK):
                                cs = slice(i * CH, (i + 1) * CH)
                                S.mm(Bk[5][:CH, cs], Pt_[:, i, :], t8["X8"][:, i, :], True, True, [ptk, "X8"], [bk(5)])
                            yield
                            S.op("dve", "tensor_tensor", reads=[bk(5), "X8"], writes=["X8"], out=flat(t8["X8"]),
                                 in0=Bk[5][:CH, :], in1=flat(t8["X8"]), op=ALU.add)
                            A_, At_, ak, atk = P_, Pt_, pk, ptk
                        yield
                        for i in range(BLK):
                            csl = slice((n0 + i) * CH, (n0 + i + 1) * CH)
                            bkk, bkv = (6, 3) if i < 4 else (7, 4)
                            o_ = (i % 4) * 128
                            S.op("pe", "transpose", reads=["kT", "ident"], writes=[bk(bkk)], out=Bk[bkk][:CH, o_:o_ + 128],
                                 in_=kT[:, csl], identity=ident[:])
                            S.op("pe", "transpose", reads=["vT", "ident"], writes=[bk(bkv)], out=Bk[bkv][:CH, o_:o_ + 128],
                                 in_=vT[:, csl], identity=ident[:])
                        yield
                        for hh in range(2):
                            isl = slice(hh * 4, hh * 4 + 4)
                            sc_b = lambda t_: t_[:, n0 + hh * 4:n0 + hh * 4 + 4].unsqueeze(2).to_broadcast([CH, 4, 128])
                            kps = Bk[6 + hh][:CH, :].rearrange("p (i e) -> p i e", e=128)
                            vps = Bk[3 + hh][:CH, :].rearrange("p (i e) -> p i e", e=128)
                            S.op("dve", "tensor_tensor", reads=[bk(6 + hh), "kd"], writes=[("kdt", pb, hh)],
                                 out=kdt_[:CH, isl, :], in0=kps, in1=sc_b(kd), op=ALU.mult)
                            S.op("dve", "tensor_tensor", reads=[bk(6 + hh), "bg"], writes=[("kbt", hh)], out=kbt[:CH, isl, :],
                                 in0=kps, in1=sc_b(bg), op=ALU.mult)
                            S.op("dve", "tensor_tensor", reads=[bk(3 + hh), ("bt", d)], writes=[("vbt", hh)],
                                 out=vbt[:, isl, :], in0=vps, in1=sc_b(Bd), op=ALU.mult)
                        yield
                        MT_ = MT8[pb]
                        for i in range(BLK):
                            hh = i // 4
                            o_ = (i % 4) * 128
                            S.mm(Bk[5 + hh][:CH, o_:o_ + 128], t8["X8"][:, i, :], vbt[:, i, :], True, True,
                                 ["X8", ("vbt", hh)], [bk(5 + hh)])
                            S.mm(Bk[3 + hh][:CH, o_:o_ + 128], t8["X8"][:, i, :], kbt[:CH, i, :], True, True,
                                 ["X8", ("kbt", hh)], [bk(3 + hh)])
                        yield
                        for hh in range(2):
                            S.op("act", "activation", reads=[bk(5 + hh)], writes=[("u8", pb, hh)],
                                 out=u8_[:CH, hh * 4:hh * 4 + 4, :].rearrange("p i e -> p (i e)"), in_=Bk[5 + hh][:CH, :],
                                 func=AF.Copy)
                            S.op("act", "activation", reads=[bk(3 + hh)], writes=[("w8", hh)],
                                 out=w8[:CH, hh * 4:hh * 4 + 4, :].rearrange("p i e -> p (i e)"), in_=Bk[3 + hh][:CH, :],
                                 func=AF.Copy)
                        yield
                        for i in range(BLK):
                            hh = i // 4
                            o_ = (i % 4) * 128
                            S.mm(Bk[5 + hh][:, o_:o_ + 128], w8[:, i, :], kdt_[:, i, :], True, True,
                                 [("w8", hh), ("kdt", pb, hh)], [bk(5 + hh)])
                            S.mm(Bk[7][:, i * CH:(i + 1) * CH], w8[:, i, :], at_[:, i, :], True, True,
                                 [("w8", hh), ("attnT", pb)], [bk(7)])
                        yield
                        for hh in range(2):
                            S.op("act", "mul", reads=[bk(5 + hh)], writes=[("mtmp", hh)],
                                 out=mtmp[:, hh * 4:hh * 4 + 4, :].rearrange("p i e -> p (i e)"), in_=Bk[5 + hh][:], mul=-1.0)
                        S.op("dve", "tensor_tensor", reads=[bk(7), ("qd", pb)], writes=[("qd", pb)], out=qd_[:], in0=qd_[:],
                             in1=Bk[7][:], op=ALU.subtract)
                        yield
                        for i in range(BLK):
                            n = n0 + i
                            S.op("dve", "scalar_tensor_tensor", reads=[("mtmp", i // 4), "ident", "egt"], writes=[("MT", pb, i)],
                                 out=MT_[:, i, :], in0=ident[:], scalar=egt[:, n:n + 1], in1=mtmp[:, i, :], op0=ALU.mult,
                                 op1=ALU.add)
                            if i % 4 == 3:
                                yield

                    def scan(nb, pb, bi_, d=d, b=b):
                        n0 = nb * BLK
                        qd_, at_, kdt_, u8_, MT_ = qd[pb], attnT[pb], kdt[pb], u8[pb], MT8[pb]
                        ob, obk = o8[bi_ % 2], ("o8", bi_ % 2)
                        order = list(range(BLK)) if d == 0 else list(range(BLK))[::-1]
                        for i in order:
                            hh = i // 4
                            cur = sidx[0]
                            Sc, sk = Sst[cur], ("S", cur)
                            Sn, snk = Sst[1 - cur], ("S", 1 - cur)
                            sidx[0] = 1 - cur
                            S.mm(Bk[0][:, 0:128], MT_[:, i, :], Sc[:], True, False, [("MT", pb, i), sk], [bk(0)])
                            S.mm(Bk[0][:, 0:128], kdt_[:, i, :], u8_[:, i, :], False, True, [("kdt", pb, hh), ("u8", pb, hh)], [bk(0)])
                            S.mm(Bk[1][:CH, 0:128], qd_[:, i * CH:(i + 1) * CH], Sc[:], True, False, [("qd", pb), sk], [bk(1)])
                            S.mm(Bk[1][:CH, 0:128], at_[:, i, :], u8_[:, i, :], False, True, [("attnT", pb), ("u8", pb, hh)], [bk(1)])
                            yield
                            S.op("act", "activation", reads=[bk(0)], writes=[snk], out=Sn[:], in_=Bk[0][:, 0:128], func=AF.Copy)
                            S.op("dve", "tensor_copy", reads=[bk(1)], writes=[obk], out=ob[:, i, :], in_=Bk[1][:CH, 0:128])
                            for _ in range(SCAN_GAP):
                                yield
                        S.dma("sp", reads=[obk], writes=[("s_o", d, b, nb)], out=s_o[d, :, b, n0:n0 + BLK, :], in_=ob[:])

                    for _ in prep(blocks[0], 0):
                        pass
                    for bi_, nb in enumerate(blocks):
                        gens = [scan(nb, bi_ % 2, bi_)]
                        if bi_ + 1 < len(blocks):
                            gens.append(prep(blocks[bi_ + 1], (bi_ + 1) % 2))
                        interleave(gens)
        with C.scope():
            gnr = C.sb("gnr", [CH, 128], F32)
            S.dma("sp", writes=["gnr"], out=gnr[:], in_=gn)
            NB3 = 3
            of = [C.sb("of%d" % i, [CH, BLK, 128], F32) for i in range(NB3)]
            obb = [C.sb("ob%d" % i, [CH, BLK, 128], F32) for i in range(NB3)]
            zz = [C.sb("zz%d" % i, [CH, BLK, 128], F32) for i in range(NB3)]
            sq8 = [C.sb("sq8%d" % i, [CH, BLK, 128], F32) for i in range(NB3)]
            ss = [C.sb("ss%d" % i, [CH, BLK], F32) for i in range(NB3)]

            def out_block(it, b, nb):
                p = it % NB3
                n0 = nb * BLK
                S.dma("sp", reads=[("s_o", 0, b, nb)], writes=[("of", p)], out=of[p][:], in_=s_o[0, :, b, n0:n0 + BLK, :])
                S.dma("sp", reads=[("s_o", 1, b, nb)], writes=[("ob", p)], out=obb[p][:], in_=s_o[1, :, b, n0:n0 + BLK, :])
                S.dma("sp", writes=[("zz", p)], out=zz[p][:], in_=ztok[:, b, n0:n0 + BLK, :])
                yield
                S.op("dve", "tensor_tensor", reads=[("of", p), ("ob", p)], writes=[("of", p)], out=of[p][:], in0=of[p][:],
                     in1=obb[p][:], op=ALU.add)
                S.op("act", "activation", reads=[("zz", p)], writes=[("zz", p)], out=zz[p][:], in_=zz[p][:], func=AF.Silu)
                yield
                S.op("act", "activation", reads=[("of", p)], writes=[("sq8", p)], out=sq8[p][:], in_=of[p][:],
                     func=AF.Square)
                yield
                S.op("dve", "tensor_reduce", reads=[("sq8", p)], writes=[("ss", p)], out=ss[p][:], in_=sq8[p][:],
                     axis=AX.X, op=ALU.add)
                S.op("dve", "tensor_scalar", reads=[("ss", p)], writes=[("ss", p)], out=ss[p][:], in0=ss[p][:],
                     scalar1=1.0 / 128.0, scalar2=EPS, op0=ALU.mult, op1=ALU.add)
                yield
                S.op("act", "activation", reads=[("ss", p)], writes=[("ss", p)], out=ss[p][:], in_=ss[p][:], func=AF.Sqrt)
                yield
                S.op("dve", "reciprocal", reads=[("ss", p)], writes=[("ss", p)], out=ss[p][:], in_=ss[p][:])
                S.op("dve", "tensor_tensor", reads=[("of", p), ("ss", p)], writes=[("of", p)], out=of[p][:], in0=of[p][:],
                     in1=ss[p][:].unsqueeze(2).to_broadcast([CH, BLK, 128]), op=ALU.mult)
                S.op("dve", "tensor_tensor", reads=[("of", p), "gnr"], writes=[("of", p)], out=of[p][:], in0=of[p][:],
                     in1=gnr[:].unsqueeze(1).to_broadcast([CH, BLK, 128]), op=ALU.mult)
                S.op("dve", "tensor_tensor", reads=[("of", p), ("zz", p)], writes=[("of", p)], out=of[p][:], in0=of[p][:],
                     in1=zz[p][:], op=ALU.mult)
                S.dma("sp", reads=[("of", p)], is_out=True, out=ytok[:, b, n0:n0 + BLK, :], in_=of[p][:])
                yield

            pipeline((out_block(b * (N // BLK) + nb, b, nb) for b in range(BATCH) for nb in range(N // BLK)), NB3)
        S.replay()
    return nc


def build_scpre_prog(nout, T=TPC):
    nc = bass.Bass("TRN2", target_bir_lowering=False)
    xT = dram_in(nc, "xT", [D, T + 2])
    nrm = dram_in(nc, "nrm", [3, D])
    fw_in = dram_in(nc, "fw_in", [2, D, 2 * FF])
    fw_out = dram_in(nc, "fw_out", [2, FF, D])
    w_in = dram_in(nc, "w_in", [D, 3 * D])
    conv = dram_in(nc, "conv", [3, D])
    w_out = dram_in(nc, "w_out", [D, D])
    nrm2 = dram_in(nc, "nrm2", [2, D])
    fw_in2 = dram_in(nc, "fw_in2", [D, 2 * FF])
    fw_out2 = dram_in(nc, "fw_out2", [FF, D])
    w_in2 = dram_in(nc, "w_in2", [D, nout])
    b_in2 = dram_in(nc, "b_in2", [nout])
    xo = dram_out(nc, "xo", [D, T])
    uT = dram_out(nc, "uT", [nout, T])
    with contextlib.ExitStack() as es:
        C = Ctx(nc, es)
        emit_consts(C)
        x = C.sb("x", [128, KC, T + 2], F32)
        emit_load_x(C, x, xT, T + 2)
        grp = [(t, 1024) for t in range(0, T, 1024)]
        emit_ffn(C, x, grp + [(T, 2)], nrm[0], fw_in[0], fw_out[0], "f1")
        emit_sc_mixer(C, x, T, nrm[1], w_in, conv, w_out)
        emit_ffn(C, x, grp, nrm[2], fw_in[1], fw_out[1], "f2")
        emit_ffn(C, x, grp, nrm2[0], fw_in2, fw_out2, "f3")
        emit_store_x(C, x, xo, T)
        emit_proj(C, x, T, nrm2[1], w_in2, b_in2, nout, uT)
        C.S.replay()
    return nc


def build_postpre_prog(nout, T=TPC):
    nc = bass.Bass("TRN2", target_bir_lowering=False)
    xT = dram_in(nc, "xT", [D, T])
    yT = dram_in(nc, "yT", [D, T])
    w_out = dram_in(nc, "w_out", [D, D])
    b_out = dram_in(nc, "b_out", [D])
    nrm = dram_in(nc, "nrm", [D])
    fw_in = dram_in(nc, "fw_in", [D, 2 * FF])
    fw_out = dram_in(nc, "fw_out", [FF, D])
    nrm2 = dram_in(nc, "nrm2", [2, D])
    fw_in2 = dram_in(nc, "fw_in2", [D, 2 * FF])
    fw_out2 = dram_in(nc, "fw_out2", [FF, D])
    w_in2 = dram_in(nc, "w_in2", [D, nout])
    b_in2 = dram_in(nc, "b_in2", [nout])
    xo = dram_out(nc, "xo", [D, T])
    uT = dram_out(nc, "uT", [nout, T])
    with contextlib.ExitStack() as es:
        C = Ctx(nc, es)
        emit_consts(C)
        x = C.sb("x", [128, KC, T], F32)
        emit_load_x(C, x, xT, T)
        grp = [(t, 1024) for t in range(0, T, 1024)]
        emit_outproj(C, x, T, yT, w_out, b_out)
        emit_ffn(C, x, grp, nrm, fw_in, fw_out, "f2")
        emit_ffn(C, x, grp, nrm2[0], fw_in2, fw_out2, "f3")
        emit_store_x(C, x, xo, T)
        emit_proj(C, x, T, nrm2[1], w_in2, b_in2, nout, uT)
        C.S.replay()
    return nc


_PROGS = {}


def _prog(key, fn):
    if key not in _PROGS:
        _PROGS[key] = fn()
    return _PROGS[key]


def _run(nc, in_maps):
    res = run_bass_kernel_spmd(nc, in_maps, core_ids=list(range(NCORES)))
    return res.results


def _c(a):
    return np.ascontiguousarray(a, dtype=np.float32)


def _tok_shards_T(xf):
    return [_c(xf[c * TPC:(c + 1) * TPC].T) for c in range(NCORES)]


def _from_T(outs, name):
    return np.concatenate([r[name].T for r in outs], 0)


def _run_sc_layer(xf, p, li, j, final):
    nc = _prog(("sc", final), lambda: build_sc_prog(TPC, final))
    x3 = xf.reshape(BATCH, SEQ, D)
    zero = np.zeros((1, D), np.float32)
    in_maps = []
    for c in range(NCORES):
        b, s0 = divmod(c * TPC, SEQ)
        left = x3[b, s0 - 1:s0] if s0 > 0 else zero
        right = x3[b, s0 + TPC:s0 + TPC + 1] if s0 + TPC < SEQ else zero
        xs = np.concatenate([x3[b, s0:s0 + TPC], left, right], 0)
        in_maps.append({"xT": _c(xs.T), "nrm": _c(p["norms"][li]), "fw_in": _c(p["ffn_w_in"][li]),
                        "fw_out": _c(p["ffn_w_out"][li]), "w_in": _c(p["sc_w_in"][j]), "conv": _c(p["sc_conv"][j]),
                        "w_out": _c(p["sc_w_out"][j]), "gfin": _c(p["final_norm"])})
    return _from_T(_run(nc, in_maps), "yT")


def _run_pre(xf, p, li, w_in, b_in):
    nout = w_in.shape[1]
    nc = _prog(("pre", nout), lambda: build_pre_prog(nout, TPC))
    xs = _tok_shards_T(xf)
    in_maps = [{"xT": xs[c], "nrm": _c(p["norms"][li, 0:2]), "fw_in": _c(p["ffn_w_in"][li, 0]),
                "fw_out": _c(p["ffn_w_out"][li, 0]), "w_in": _c(w_in), "b_in": _c(b_in)} for c in range(NCORES)]
    outs = _run(nc, in_maps)
    x1 = _from_T(outs, "xo")
    u = np.concatenate([r["uT"] for r in outs], 1)
    return x1, u


def _run_post(xf, yfm, p, li, w_out, b_out):
    nc = _prog(("post",), lambda: build_post_prog(TPC))
    xs = _tok_shards_T(xf)
    in_maps = [{"xT": xs[c], "yT": _c(yfm[:, c * TPC:(c + 1) * TPC]), "w_out": _c(w_out), "b_out": _c(b_out),
                "nrm": _c(p["norms"][li, 2]), "fw_in": _c(p["ffn_w_in"][li, 1]), "fw_out": _c(p["ffn_w_out"][li, 1])}
               for c in range(NCORES)]
    return _from_T(_run(nc, in_maps), "xo")


def _run_hyena_core(u, p, j):
    nc = _prog(("hy",), build_hy_core_prog)
    cst, zT, trow, nad = hyena_consts()
    trow_rep = _c(np.broadcast_to(trow, (128, NFFT)))
    u4 = u.reshape(3, D, BATCH, SEQ)
    hc = p["hy_conv"][j].reshape(3, 3, D)
    cb = p["hy_conv_b"][j].reshape(3, D)
    w3 = p["hy_f_w3"][j].reshape(HY_ORD, 2, D)
    in_maps = []
    for c in range(NCORES):
        sl = slice(c * 128, (c + 1) * 128)
        in_maps.append({"u0": _c(u4[:, sl]), "convw": _c(hc[:, :, sl]), "convb": _c(cb[:, sl]), "dvec": _c(p["hy_d"][j][sl]),
                        "fw1": _c(p["hy_f_w1"][j]), "fb1": _c(p["hy_f_b1"][j]), "fw2": _c(p["hy_f_w2"][j]),
                        "fb2": _c(p["hy_f_b2"][j]), "fw3": _c(w3[:, :, sl]), "freq": _c(p["hy_f_freq"][j]), "cst": cst,
                        "zT": zT, "trow": trow_rep, "nad": _c(nad[sl])})
    outs = _run(nc, in_maps)
    return np.concatenate([r["yT"].reshape(128, BATCH * SEQ) for r in outs], 0)


def _run_gdn_core(u, p, j):
    nc = _prog(("gd",), build_gd_core_prog)
    mtri, mstrict, ident = gdn_consts()
    H = 8
    gcv = p["gd_conv"][j].reshape(3, 3, D)
    in_maps = []
    for h in range(NCORES):
        sl = slice(h * 128, (h + 1) * 128)
        qkv0 = u[0:3 * D].reshape(3, D, BATCH, SEQ)[:, sl]
        zfm = u[3 * D + h * 128: 3 * D + (h + 1) * 128]
        ztok = zfm.T.reshape(BATCH, NCH, CH, 128).transpose(2, 0, 1, 3)
        rows = [4 * D + 0 * H + h, 4 * D + 1 * H + h, 4 * D + 2 * H + 0 * H + h, 4 * D + 2 * H + 1 * H + h]
        abt = u[rows].reshape(4, BATCH, NCH, CH).transpose(0, 3, 1, 2)
        in_maps.append({"qkv0": _c(qkv0), "ztok": _c(ztok), "abt": _c(abt), "convw": _c(gcv[:, :, sl]),
                        "alog": _c(np.broadcast_to(p["gd_a_log"][j][:, h], (CH, 2))),
                        "dtb": _c(np.broadcast_to(p["gd_dt_bias"][j][:, h], (CH, 2))),
                        "gn": _c(np.broadcast_to(p["gd_norm"][j], (CH, 128))), "mtri": mtri, "mstrict": mstrict,
                        "ident": ident})
    outs = _run(nc, in_maps)
    return np.concatenate([r["ytok"].transpose(3, 1, 2, 0).reshape(128, BATCH * SEQ) for r in outs], 0)


def _run_sc_pre(xf, p, li, j, w_in2, b_in2):
    nout = w_in2.shape[1]
    nc = _prog(("scpre", nout), lambda: build_scpre_prog(nout, TPC))
    x3 = xf.reshape(BATCH, SEQ, D)
    zero = np.zeros((1, D), np.float32)
    in_maps = []
    for c in range(NCORES):
        b, s0 = divmod(c * TPC, SEQ)
        left = x3[b, s0 - 1:s0] if s0 > 0 else zero
        right = x3[b, s0 + TPC:s0 + TPC + 1] if s0 + TPC < SEQ else zero
        xs = np.concatenate([x3[b, s0:s0 + TPC], left, right], 0)
        in_maps.append({"xT": _c(xs.T), "nrm": _c(p["norms"][li]), "fw_in": _c(p["ffn_w_in"][li]),
                        "fw_out": _c(p["ffn_w_out"][li]), "w_in": _c(p["sc_w_in"][j]), "conv": _c(p["sc_conv"][j]),
                        "w_out": _c(p["sc_w_out"][j]), "nrm2": _c(p["norms"][li + 1, 0:2]),
                        "fw_in2": _c(p["ffn_w_in"][li + 1, 0]), "fw_out2": _c(p["ffn_w_out"][li + 1, 0]),
                        "w_in2": _c(w_in2), "b_in2": _c(b_in2)})
    outs = _run(nc, in_maps)
    return _from_T(outs, "xo"), np.concatenate([r["uT"] for r in outs], 1)


def _run_post_pre(xf, yfm, p, li, w_out, b_out, w_in2, b_in2):
    nout = w_in2.shape[1]
    nc = _prog(("postpre", nout), lambda: build_postpre_prog(nout, TPC))
    xs = _tok_shards_T(xf)
    in_maps = [{"xT": xs[c], "yT": _c(yfm[:, c * TPC:(c + 1) * TPC]), "w_out": _c(w_out), "b_out": _c(b_out),
                "nrm": _c(p["norms"][li, 2]), "fw_in": _c(p["ffn_w_in"][li, 1]), "fw_out": _c(p["ffn_w_out"][li, 1]),
                "nrm2": _c(p["norms"][li + 1, 0:2]), "fw_in2": _c(p["ffn_w_in"][li + 1, 0]),
                "fw_out2": _c(p["ffn_w_out"][li + 1, 0]), "w_in2": _c(w_in2), "b_in2": _c(b_in2)} for c in range(NCORES)]
    outs = _run(nc, in_maps)
    return _from_T(outs, "xo"), np.concatenate([r["uT"] for r in outs], 1)


def kernel(**inputs):
    p = {k: np.asarray(v, dtype=np.float32) for k, v in inputs.items()}
    xf = p["x"].reshape(BATCH * SEQ, D)
    nproj = p["gd_w_in"].shape[2]
    npad = ((nproj + 127) // 128) * 128
    gw_in = np.zeros((D, npad), np.float32)
    gw_in[:, :nproj] = p["gd_w_in"][0]
    xf, u = _run_sc_pre(xf, p, 0, 0, p["hy_w_in"][0], p["hy_b_in"][0])
    yfm = _run_hyena_core(u, p, 0)
    xf, u = _run_post_pre(xf, yfm, p, 1, p["hy_w_out"][0], p["hy_b_out"][0], gw_in, np.zeros((npad,), np.float32))
    yfm = _run_gdn_core(u, p, 0)
    xf = _run_post(xf, yfm, p, 2, p["gd_w_out"][0], np.zeros((D,), np.float32))
    xf = _run_sc_layer(xf, p, 3, 1, final=True)
    return np.ascontiguousarray(xf.reshape(BATCH, SEQ, D).astype(np.float32))
```
